# Optimizing a Trainium2 kernel written in Bass

```python
import math
import jax
import jax.numpy as jnp
from jax import lax
import numpy as np

D_MODEL = 1024
BATCH = 8
SEQ = 2048
DEPTH = 2
DEC_BATCH = 128
DEC_SEQ = 4
PAST_LEN = 16384
PAGE_SIZE = 128

N_MEM = 256
EPS = 1e-6
N_BRANCH = 3
BRANCH_W = D_MODEL // 2
POOL_WINDOWS = (2, 4, 8, 16)
POOL_GROUPS = len(POOL_WINDOWS)
POOL_GW = BRANCH_W // POOL_GROUPS
POOL_BUF = max(POOL_WINDOWS) - 1
DN_HEADS = 4
DN_HD = BRANCH_W // DN_HEADS
DN_CONV = 4
DN_CHUNK = 64
XA_HEADS = 4
XA_HD = BRANCH_W // XA_HEADS
PEER_HEADS = 8
PEER_NKEYS = 128
PEER_N = PEER_NKEYS * PEER_NKEYS
PEER_QD = 256
PEER_HALF = PEER_QD // 2
PEER_TOPK = 16
PEER_BLOCK = 128

OFF_POOL = 0
OFF_Q = OFF_POOL + BRANCH_W
OFF_Z = OFF_Q + 3 * BRANCH_W
OFF_BETA = OFF_Z + BRANCH_W
OFF_ALPHA = OFF_BETA + DN_HEADS
OFF_XQ = OFF_ALPHA + DN_HEADS
OFF_GATE = OFF_XQ + BRANCH_W
IN_COLS = OFF_GATE + N_BRANCH * D_MODEL

kernel_name = 'hybrid_pool_delta_peer_step'


def rmsnorm(x, g):
    xf = x.astype(jnp.float32)
    y = xf * lax.rsqrt(jnp.mean(xf * xf, axis=-1, keepdims=True) + EPS)
    return (y * g.astype(jnp.float32)).astype(x.dtype)


def l2norm(a):
    return a * lax.rsqrt(jnp.sum(a * a, axis=-1, keepdims=True) + EPS)


def pool_mixer(u, buf, start, w_grp, scale):
    B, T, _ = u.shape
    ext = jnp.concatenate([buf.astype(u.dtype), u], axis=1)
    cs = jnp.cumsum(ext.astype(jnp.float32), axis=1)
    cs = jnp.pad(cs, ((0, 0), (1, 0), (0, 0)))
    hi = cs[:, POOL_BUF + 1:POOL_BUF + 1 + T]
    pos = start + jnp.arange(T)
    means = []
    for gi, win in enumerate(POOL_WINDOWS):
        sl = slice(gi * POOL_GW, (gi + 1) * POOL_GW)
        lo = cs[:, POOL_BUF + 1 - win:POOL_BUF + 1 - win + T, sl]
        cnt = jnp.minimum(pos + 1, win).astype(jnp.float32)
        means.append((hi[..., sl] - lo) / cnt[None, :, None])
    d = (jnp.concatenate(means, axis=-1) - u.astype(jnp.float32)).astype(u.dtype)
    d = d.reshape(B, T, POOL_GROUPS, POOL_GW)
    y = jnp.einsum('btgc,gce->btge', d, w_grp).reshape(B, T, BRANCH_W) * scale
    return y, ext[:, -POOL_BUF:]


def short_conv(x, buf, w):
    T = x.shape[1]
    ext = jnp.concatenate([buf.astype(x.dtype), x], axis=1)
    y = ext[:, 0:T] * w[0]
    for j in range(1, DN_CONV):
        y = y + ext[:, j:j + T] * w[j]
    return jax.nn.silu(y), ext[:, -(DN_CONV - 1):]


def gated_delta(q, k, v, beta, g, s0):
    B, T, H, d = q.shape
    C = min(DN_CHUNK, T)
    pad = (-T) % C
    N = (T + pad) // C

    def to_chunks(a):
        a = jnp.pad(a, [(0, 0), (0, pad)] + [(0, 0)] * (a.ndim - 2))
        a = a.reshape((B, N, C) + a.shape[2:])
        return jnp.moveaxis(a, 3, 1)

    q, k, v, beta, g = (to_chunks(a) for a in (q, k, v, beta, g))
    q = q * (d ** -0.5)
    kb = k * beta[..., None]
    vb = v * beta[..., None]
    gc = jnp.cumsum(g, axis=-1)
    idx = jnp.arange(C)
    tril = idx[:, None] >= idx[None, :]
    strict = idx[:, None] > idx[None, :]
    diff = gc[..., :, None] - gc[..., None, :]
    decay = jnp.where(tril, jnp.exp(jnp.where(tril, diff, 0.0)), 0.0)
    A = jnp.where(strict, jnp.einsum('bhncd,bhnmd->bhncm', kb, k) * decay, 0.0)
    M = A + jnp.eye(C, dtype=A.dtype)
    u = lax.linalg.triangular_solve(M, vb, left_side=True, lower=True, unit_diagonal=True)
    w = lax.linalg.triangular_solve(M, kb * jnp.exp(gc)[..., None], left_side=True,
                                    lower=True, unit_diagonal=True)
    attn = jnp.where(tril, jnp.einsum('bhncd,bhnmd->bhncm', q, k) * decay, 0.0)
    qg = q * jnp.exp(gc)[..., None]
    glast = gc[..., -1]
    kd = k * jnp.exp(glast[..., None] - gc)[..., None]
    xs = tuple(jnp.moveaxis(a, 2, 0) for a in (u, w, attn, qg, kd, jnp.exp(glast)))

    def step(S, inp):
        u_n, w_n, at_n, qg_n, kd_n, gl_n = inp
        v_new = u_n - jnp.einsum('bhcd,bhde->bhce', w_n, S)
        o = jnp.einsum('bhcd,bhde->bhce', qg_n, S) + jnp.einsum('bhcm,bhme->bhce', at_n, v_new)
        S = S * gl_n[..., None, None] + jnp.einsum('bhcd,bhce->bhde', kd_n, v_new)
        return S, o

    s_fin, o = lax.scan(step, s0, xs)
    o = jnp.moveaxis(o, 0, 2).reshape(B, H, N * C, d)[:, :, :T]
    return jnp.transpose(o, (0, 2, 1, 3)), s_fin


def mem_kv(mem, g, w):
    B = mem.shape[0]
    kv = rmsnorm(mem, g) @ w
    k = kv[..., :BRANCH_W].reshape(B, -1, XA_HEADS, XA_HD)
    v = kv[..., BRANCH_W:].reshape(B, -1, XA_HEADS, XA_HD)
    return k, v


def mem_attend(xq, mk, mv):
    B, T, _ = xq.shape
    q = xq.reshape(B, T, XA_HEADS, XA_HD)
    s = jnp.einsum('bthd,bmhd->bhtm', q, mk.astype(xq.dtype)).astype(jnp.float32) * (XA_HD ** -0.5)
    p = jax.nn.softmax(s, axis=-1).astype(xq.dtype)
    o = jnp.einsum('bhtm,bmhd->bthd', p, mv.astype(xq.dtype))
    return o.reshape(B, T, BRANCH_W)


def mixer_block(x, start, mem_k, mem_v, pool_buf, conv_buf, s0, g_mix, w_in, w_conv,
                a_log, dt_bias, g_dn_out, w_pool_grp, pool_scale, w_branch, w_o):
    B, T, _ = x.shape
    f32 = jnp.float32
    h = rmsnorm(x, g_mix)
    proj = h @ w_in
    u_pool = proj[..., OFF_POOL:OFF_Q]
    qkv = proj[..., OFF_Q:OFF_Z]
    z = proj[..., OFF_Z:OFF_BETA]
    beta_raw = proj[..., OFF_BETA:OFF_ALPHA]
    alpha_raw = proj[..., OFF_ALPHA:OFF_XQ]
    xq = proj[..., OFF_XQ:OFF_GATE]
    gate_raw = proj[..., OFF_GATE:]
    y_pool, new_pool = pool_mixer(u_pool, pool_buf, start, w_pool_grp, pool_scale)
    qkv_c, new_conv = short_conv(qkv, conv_buf, w_conv)
    qkv_c = qkv_c.astype(f32).reshape(B, T, 3, DN_HEADS, DN_HD)
    q = l2norm(qkv_c[:, :, 0])
    k = l2norm(qkv_c[:, :, 1])
    v = qkv_c[:, :, 2]
    beta = jax.nn.sigmoid(beta_raw.astype(f32))
    g = -jnp.exp(a_log.astype(f32)) * jax.nn.softplus(alpha_raw.astype(f32) + dt_bias.astype(f32))
    o, s_new = gated_delta(q, k, v, beta, g, s0.astype(f32))
    o = o * lax.rsqrt(jnp.mean(o * o, axis=-1, keepdims=True) + EPS) * g_dn_out.astype(f32)
    o = o * jax.nn.silu(z.astype(f32).reshape(B, T, DN_HEADS, DN_HD))
    y_dn = o.reshape(B, T, BRANCH_W)
    y_mem = mem_attend(xq, mem_k, mem_v)
    br = jnp.stack([y_pool.astype(x.dtype), y_dn.astype(x.dtype), y_mem.astype(x.dtype)], axis=2)
    bproj = jnp.einsum('btnc,ncd->btnd', br, w_branch)
    gates = jax.nn.sigmoid(gate_raw.reshape(B, T, N_BRANCH, D_MODEL))
    merged = jnp.sum(gates * bproj, axis=2)
    return x + merged @ w_o, new_pool, new_conv, s_new


def peer_ffn(h, w_q, subkeys, U, V):
    B, T, D = h.shape
    q = (h @ w_q).reshape(B, T, PEER_HEADS, 2, PEER_HALF)
    s = jnp.einsum('bthpc,hpkc->bthpk', q, subkeys).astype(jnp.float32)
    s_top, i_top = lax.top_k(s, PEER_TOPK)
    ncand = PEER_TOPK * PEER_TOPK
    cand = (s_top[..., 0, :, None] + s_top[..., 1, None, :]).reshape(B, T, PEER_HEADS, ncand)
    cidx = (i_top[..., 0, :, None] * PEER_NKEYS + i_top[..., 1, None, :]).reshape(B, T, PEER_HEADS, ncand)
    best, pos = lax.top_k(cand, PEER_TOPK)
    idx = jnp.take_along_axis(cidx, pos, axis=-1)
    gate = jax.nn.softmax(best, axis=-1).astype(h.dtype)
    n = B * T
    blk = min(PEER_BLOCK, n)
    pad = (-n) % blk
    nb = (n + pad) // blk
    xt = jnp.pad(h.reshape(n, D), ((0, pad), (0, 0))).reshape(nb, blk, D)
    it = jnp.pad(idx.reshape(n, PEER_HEADS, PEER_TOPK), ((0, pad), (0, 0), (0, 0))).reshape(nb, blk, PEER_HEADS, PEER_TOPK)
    gt = jnp.pad(gate.reshape(n, PEER_HEADS, PEER_TOPK), ((0, pad), (0, 0), (0, 0))).reshape(nb, blk, PEER_HEADS, PEER_TOPK)

    def expert_block(args):
        xb, ib, gb = args
        a = jnp.einsum('td,thed->the', xb, jnp.take(U, ib, axis=0))
        wgt = gb * jax.nn.gelu(a)
        return jnp.einsum('the,thed->td', wgt, jnp.take(V, ib, axis=0))

    out = lax.map(expert_block, (xt, it, gt))
    return out.reshape(nb * blk, D)[:n].reshape(B, T, D)


def setup_inputs(seed: int = 0) -> dict:
    key = jax.random.key(seed)
    ks = iter(jax.random.split(key, 32))

    def nrm(shape, scale=1.0):
        return jax.random.normal(next(ks), shape, jnp.float32) * scale

    def gain(shape):
        return 1.0 + nrm(shape, 0.02)

    x_prompt = nrm((BATCH, SEQ, D_MODEL))
    x_sample = nrm((DEC_BATCH, DEC_SEQ, D_MODEL))
    state_pool = nrm((DEPTH, DEC_BATCH, POOL_BUF, BRANCH_W))
    state_conv = nrm((DEPTH, DEC_BATCH, DN_CONV - 1, 3 * BRANCH_W))
    state_delta = nrm((DEPTH, DEC_BATCH, DN_HEADS, DN_HD, DN_HD), 0.05)
    cache_mem_k = nrm((DEPTH, DEC_BATCH, N_MEM, XA_HEADS, XA_HD))
    cache_mem_v = nrm((DEPTH, DEC_BATCH, N_MEM, XA_HEADS, XA_HD))
    mem_prompt = nrm((BATCH, N_MEM, D_MODEL))
    g_mix = gain((DEPTH, D_MODEL))
    w_in = nrm((DEPTH, D_MODEL, IN_COLS), D_MODEL ** -0.5)
    w_conv = nrm((DEPTH, DN_CONV, 3 * BRANCH_W), DN_CONV ** -0.5)
    a_log = jnp.log(jax.random.uniform(next(ks), (DEPTH, DN_HEADS), jnp.float32, minval=1.0, maxval=16.0))
    dt = jnp.exp(jax.random.uniform(next(ks), (DEPTH, DN_HEADS), jnp.float32,
                                    minval=math.log(1e-3), maxval=math.log(1e-1)))
    dt_bias = dt + jnp.log(-jnp.expm1(-dt))
    g_dn_out = gain((DEPTH, DN_HD))
    w_pool_grp = nrm((DEPTH, POOL_GROUPS, POOL_GW, POOL_GW), POOL_GW ** -0.5)
    pool_scale = gain((DEPTH, BRANCH_W))
    g_mem = gain((DEPTH, D_MODEL))
    w_mem_kv = nrm((DEPTH, D_MODEL, 2 * BRANCH_W), D_MODEL ** -0.5)
    w_branch = nrm((DEPTH, N_BRANCH, BRANCH_W, D_MODEL), BRANCH_W ** -0.5)
    w_o = nrm((DEPTH, D_MODEL, D_MODEL), D_MODEL ** -0.5)
    g_ffn = gain((DEPTH, D_MODEL))
    w_peer_q = nrm((DEPTH, D_MODEL, PEER_HEADS * PEER_QD), D_MODEL ** -0.5)
    peer_subkeys = nrm((DEPTH, PEER_HEADS, 2, PEER_NKEYS, PEER_HALF), PEER_HALF ** -0.5)
    peer_u = nrm((DEPTH, PEER_N, D_MODEL), D_MODEL ** -0.5)
    peer_v = nrm((DEPTH, PEER_N, D_MODEL), (PEER_HEADS * PEER_TOPK) ** -0.5)
    g_final = gain((D_MODEL,))
    return {'x_prompt': x_prompt, 'x_sample': x_sample, 'state_pool': state_pool,
            'state_conv': state_conv, 'state_delta': state_delta, 'cache_mem_k': cache_mem_k,
            'cache_mem_v': cache_mem_v, 'mem_prompt': mem_prompt, 'g_mix': g_mix, 'w_in': w_in,
            'w_conv': w_conv, 'a_log': a_log, 'dt_bias': dt_bias, 'g_dn_out': g_dn_out,
            'w_pool_grp': w_pool_grp, 'pool_scale': pool_scale, 'g_mem': g_mem,
            'w_mem_kv': w_mem_kv, 'w_branch': w_branch, 'w_o': w_o, 'g_ffn': g_ffn,
            'w_peer_q': w_peer_q, 'peer_subkeys': peer_subkeys, 'peer_u': peer_u,
            'peer_v': peer_v, 'g_final': g_final}


def reference(x_prompt, x_sample, state_pool, state_conv, state_delta, cache_mem_k, cache_mem_v,
              mem_prompt, g_mix, w_in, w_conv, a_log, dt_bias, g_dn_out, w_pool_grp, pool_scale,
              g_mem, w_mem_kv, w_branch, w_o, g_ffn, w_peer_q, peer_subkeys, peer_u, peer_v, g_final):
    Bp = x_prompt.shape[0]
    xp = x_prompt
    xs = x_sample
    pool_p, conv_p, delta_p, mk_p, mv_p = [], [], [], [], []
    pool_s, conv_s, delta_s = [], [], []
    for l in range(DEPTH):
        mix_w = (g_mix[l], w_in[l], w_conv[l], a_log[l], dt_bias[l], g_dn_out[l],
                 w_pool_grp[l], pool_scale[l], w_branch[l], w_o[l])
        mk, mv = mem_kv(mem_prompt, g_mem[l], w_mem_kv[l])
        xp, pb, cb, sp = mixer_block(
            xp, 0, mk, mv,
            jnp.zeros((Bp, POOL_BUF, BRANCH_W), xp.dtype),
            jnp.zeros((Bp, DN_CONV - 1, 3 * BRANCH_W), xp.dtype),
            jnp.zeros((Bp, DN_HEADS, DN_HD, DN_HD), jnp.float32),
            *mix_w)
        xp = xp + peer_ffn(rmsnorm(xp, g_ffn[l]), w_peer_q[l], peer_subkeys[l], peer_u[l], peer_v[l])
        pool_p.append(pb)
        conv_p.append(cb)
        delta_p.append(sp.astype(xp.dtype))
        mk_p.append(mk)
        mv_p.append(mv)
        xs, pb, cb, ss = mixer_block(
            xs, PAST_LEN, cache_mem_k[l], cache_mem_v[l],
            state_pool[l], state_conv[l], state_delta[l], *mix_w)
        xs = xs + peer_ffn(rmsnorm(xs, g_ffn[l]), w_peer_q[l], peer_subkeys[l], peer_u[l], peer_v[l])
        pool_s.append(pb.astype(state_pool.dtype))
        conv_s.append(cb.astype(state_conv.dtype))
        delta_s.append(ss.astype(state_delta.dtype))
    y_prompt = rmsnorm(xp, g_final)
    y_sample = rmsnorm(xs, g_final)
    new_pool_p = jnp.stack(pool_p, axis=0)
    new_conv_p = jnp.stack(conv_p, axis=0)
    new_delta_p = jnp.stack(delta_p, axis=0)
    mem_k_p = jnp.stack(mk_p, axis=0)
    mem_v_p = jnp.stack(mv_p, axis=0)
    new_pool_s = jnp.stack(pool_s, axis=0)
    new_conv_s = jnp.stack(conv_s, axis=0)
    new_delta_s = jnp.stack(delta_s, axis=0)
    return (y_prompt, y_sample, new_pool_p, new_conv_p, new_delta_p, mem_k_p, mem_v_p,
            new_pool_s, new_conv_s, new_delta_s)
```

```python
import numpy as np
from contextlib import ExitStack
import concourse.bass as bass
import concourse.mybir as mybir
from concourse.bass_utils import run_bass_kernel_spmd

F32 = mybir.dt.float32
BF16 = mybir.dt.bfloat16
I32 = mybir.dt.int32
U32 = mybir.dt.uint32
ALU = mybir.AluOpType
AF = mybir.ActivationFunctionType
AX = mybir.AxisListType

NCORES = 8
D = 1024
T = 2048
NSQ = 16
TS = 64
DEPTH = 2
BW = 512
IN_COLS = 6152
OFF_Q = 512
OFF_Z = 2048
OFF_BA = 2560
OFF_XQ = 2568
OFF_GATE = 3080
EPS = 1e-6
NKEY = 128
NEXP = 16384
SEM_CH = 30000
NEG = -1.0e30
NG = 6


def CALL(name, *a, **k):
    return lambda e: getattr(e, name)(*a, **k)


class Op:
    __slots__ = ("eng", "fn", "deps", "is_dma", "key", "nparts", "seq", "signaled", "dma_val")


class Prog:
    ENGS = ("pe", "act", "dve", "pool", "sp")

    def __init__(self, nc):
        self.nc = nc
        self.ops = []
        self.last_w = {}
        self.readers = {}
        self.dma_cnt = {}
        self.dma_gen = {}
        self.psi = 0
        self.inames = {}

    def add(self, eng, fn, reads=(), writes=(), dma=0, key=None):
        op = Op()
        op.eng = eng
        op.fn = fn
        op.is_dma = dma > 0
        op.nparts = dma
        op.signaled = False
        op.seq = 0
        deps = set()
        excl = [r for r in reads if isinstance(r, str) and r[:2] == "ps" and r[2:].isdigit()]
        if excl:
            reads = [r for r in reads if r not in excl]
            writes = list(writes) + excl
        for r in reads:
            w = self.last_w.get(r)
            if w is not None:
                deps.add(w)
        for w_ in writes:
            w = self.last_w.get(w_)
            if w is not None:
                deps.add(w)
            for rd in self.readers.get(w_, ()):
                deps.add(rd)
        op.deps = deps
        for r in reads:
            self.readers.setdefault(r, []).append(op)
        for w_ in writes:
            self.last_w[w_] = op
            self.readers[w_] = []
        if op.is_dma:
            g = self.dma_gen.get(key, 0)
            c = self.dma_cnt.get((key, g), 0) + dma * 16
            if c > SEM_CH:
                g += 1
                self.dma_gen[key] = g
                c = dma * 16
            self.dma_cnt[(key, g)] = c
            op.key = (key, g)
            op.dma_val = c
        self.ops.append(op)
        return op

    def barrier(self):
        last = {}
        dmas = {}
        for op in self.ops:
            if op.is_dma:
                dmas[op.key] = op
            else:
                last[op.eng] = op
        deps = set(last.values()) | set(dmas.values())
        bops = []
        for e in ("pe", "act", "dve", "pool", "sp"):
            op = self.add(e, lambda en: en.nop(), ())
            op.deps = set(deps)
            bops.append(op)
        self.last_w = {}
        self.readers = {}

    def ps(self):
        lo = getattr(self, "ps_lo", 0)
        if self.psi < lo:
            self.psi = lo
        i = self.psi
        self.psi = self.psi + 1
        if self.psi >= 8:
            self.psi = lo
        return i

    def emit(self, es, maxops=None):
        nc = self.nc
        if maxops is not None:
            self.ops = self.ops[:maxops]
            cnt2 = {}
            for op in self.ops:
                if op.is_dma:
                    cnt2[op.key] = op.dma_val
            self.dma_cnt = cnt2
        for op in self.ops:
            nd = set()
            for d in op.deps:
                if (not d.is_dma) and (not op.is_dma) and d.eng == "pe" and op.eng == "pe":
                    continue
                nd.add(d)
                d.signaled = True
            op.deps = nd
        cnt = {e: 0 for e in self.ENGS}
        for op in self.ops:
            if not op.is_dma and op.signaled:
                cnt[op.eng] += 1
                op.seq = cnt[op.eng]
        eng_sems = {}
        for e in self.ENGS:
            n = (cnt[e] + SEM_CH - 1) // SEM_CH
            eng_sems[e] = [es.enter_context(nc.semaphore(f"s_{e}_{i}")) for i in range(n)]
        dma_sems = {}
        for i, k in enumerate(self.dma_cnt.keys()):
            dma_sems[k] = es.enter_context(nc.semaphore(f"d_{i}"))
        self.nsem = sum(len(v) for v in eng_sems.values()) + len(dma_sems)
        per_eng = {e: [o for o in self.ops if o.eng == e] for e in self.ENGS}
        block = es.enter_context(nc.Block())

        def run(engname, eobj):
            waited = {}

            def wait(sem, val):
                if waited.get(id(sem), 0) >= val:
                    return
                waited[id(sem)] = val
                eobj.wait_ge(sem, val)

            for op in per_eng[engname]:
                need = {}
                for d in op.deps:
                    if d.is_dma:
                        sem = dma_sems[d.key]
                        v = d.dma_val
                    else:
                        si = (d.seq - 1) // SEM_CH
                        sem = eng_sems[d.eng][si]
                        v = d.seq - si * SEM_CH
                    k = id(sem)
                    if k not in need or need[k][1] < v:
                        need[k] = (sem, v)
                for sem, v in need.values():
                    wait(sem, v)
                if op.is_dma:
                    sem = dma_sems[op.key]
                    insts = op.fn(eobj)
                    if not isinstance(insts, (list, tuple)):
                        insts = [insts]
                    assert len(insts) == op.nparts
                    for ins in insts:
                        ins.then_inc(sem, 16)
                else:
                    ins = op.fn(eobj)
                    try:
                        self.inames[ins.ins.name] = op
                    except Exception:
                        pass
                    if op.signaled:
                        si = (op.seq - 1) // SEM_CH
                        ins.then_inc(eng_sems[op.eng][si], 1)
            if engname == "sp":
                for k, c in self.dma_cnt.items():
                    wait(dma_sems[k], c)

        block.tensor(lambda e: run("pe", e))
        block.scalar(lambda e: run("act", e))
        block.vector(lambda e: run("dve", e))
        block.gpsimd(lambda e: run("pool", e))
        block.sync(lambda e: run("sp", e))


class Arena:
    def __init__(self, t, nf32):
        self.t = t
        self.n = nf32
        self.off = 0
        self.hw = 0

    def mark(self):
        return self.off

    def release(self, m):
        self.off = m

    def alloc(self, n, dt=F32):
        nf = n if dt in (F32, I32, U32) else (n + 1) // 2
        nf = (nf + 7) // 8 * 8
        a = self.off
        self.off += nf
        self.hw = max(self.hw, self.off)
        assert self.off <= self.n, f"arena overflow {self.off} > {self.n}"
        v = self.t[:, a:a + nf]
        if dt != F32:
            v = v.bitcast(dt)
        return v[:, 0:n]


def make_consts():
    c = np.zeros((128, 8, 128), np.float32)
    i = np.arange(128)
    c[:, 0, :] = np.eye(128)
    c[:, 1, :] = 1.0
    same64 = (i[:, None] // 64) == (i[None, :] // 64)
    same4 = ((i[:, None] // 4) == (i[None, :] // 4)) & (i[:, None] < 64) & (i[None, :] < 64)
    c[:, 2, :] = same64 & (i[:, None] <= i[None, :])
    c[:, 3, :] = same64 & (i[:, None] > i[None, :])
    c[:, 4, :] = same4 & (i[:, None] <= i[None, :])
    c[:, 5, :] = same4 & (i[:, None] > i[None, :])
    c[:, 6, 0:16] = (i[:, None] // 4) == np.arange(16)[None, :]
    for g, w in enumerate((2, 4, 8, 16)):
        c[:, 7, g * 16:(g + 1) * 16] = 1.0 / np.minimum(np.arange(16) + 1, w)
    c[:, 7, 64:80] = np.arange(16)[None, :]
    return c.reshape(128, 1024)


def _mixer_bufs(G, NT, sample):
    AR = G.AR
    B = type("B", (), {})()
    B.NT = NT
    B.hT = AR.alloc(8 * NT, BF16).rearrange("p (k t) -> p k t", k=8)
    B.junk = AR.alloc(1024, BF16)
    B.xbf = [AR.alloc(1024, BF16)] * 2
    B.ss = AR.alloc(8)
    B.stg = [(AR.alloc(2048), f"stg{i}") for i in range(2)]
    B.wbf = [(AR.alloc(2048, BF16), f"wbf{i}") for i in range(3)]
    B.pre = [AR.alloc(3 + NT) for _ in range(2)]
    B.cv = [AR.alloc(NT) for _ in range(2)]
    B.qkvc = AR.alloc(12 * NT).rearrange("p (c t) -> p c t", c=12)
    B.zs = AR.alloc(4 * NT).rearrange("p (c t) -> p c t", c=4)
    B.xqT = AR.alloc(4 * NT, BF16).rearrange("p (c t) -> p c t", c=4)
    B.yT = AR.alloc(12 * NT, BF16).rearrange("p (c t) -> p c t", c=12)
    B.macc = AR.alloc(4 * NT).rearrange("p (c t) -> p c t", c=4)
    B.mT = AR.alloc(8 * NT, BF16).rearrange("p (c t) -> p c t", c=8)
    B.sqb = [AR.alloc(NT) for _ in range(2)]
    B.rinv = [AR.alloc(NT) for _ in range(2)]
    B.sig = B.sqb
    B.prod = B.rinv
    B.dT = [AR.alloc(NT, BF16) for _ in range(2)]
    B.pA = AR.alloc(19 * 16 if sample else 15 + NT)
    B.pB = AR.alloc(19 * 16 if sample else 15 + NT)
    B.t16 = AR.alloc(16)
    B.ktok = AR.alloc(512).rearrange("p (h d) -> p h d", h=4)
    B.vtok = AR.alloc(512).rearrange("p (h d) -> p h d", h=4)
    B.ba = AR.alloc(32)
    B.gcc = AR.alloc(16)
    names = ["gL", "gcr", "egr", "dm", "t1", "dmT", "t2", "Pa", "Pb", "Qa", "Qb", "R", "u", "wT", "attnT", "vn", "qg", "kbg", "vb", "kd", "osq", "rr", "y1", "kdsc"]
    B.dw = [{}, {}]
    shared = ("gL", "dm", "dmT", "osq", "rr", "y1", "kdsc")
    for n in names:
        if n in shared:
            B.dw[0][n] = B.dw[1][n] = AR.alloc(128)
        else:
            B.dw[0][n] = AR.alloc(128)
            B.dw[1][n] = AR.alloc(128)
    B.dw_shared = shared
    B.pexp = [AR.alloc(256) for _ in range(2)]
    B.pn = [AR.alloc(256, BF16) for _ in range(2)]
    B.pT = [AR.alloc(256, BF16).rearrange("p (c t) -> p c t", c=2) for _ in range(2)]
    B.asm = AR.alloc(32)
    if not sample:
        B.uT = AR.alloc(4 * (15 + NT)).rearrange("p (c t) -> p c t", c=4)
        B.kTm = AR.alloc(4 * 256, BF16).rearrange("p (h m) -> p h m", h=4)
        B.vm = AR.alloc(2 * 512, BF16).rearrange("p (c n) -> p c n", c=2)
        B.hTm = B.hT
        qflat = B.qkvc.rearrange("p c t -> p (c t)")
        B.memx = qflat[:, 0:1024]
        B.kvrow = qflat[:, 1024:3072].rearrange("p (j n) -> p j n", j=2)
        B.rowbuf = qflat[:, 0:1536]
    else:
        B.uTs = AR.alloc(4 * 16 * 19).rearrange("p (c s e) -> p c s e", c=4, s=16)
        B.pre_s = [AR.alloc(16 * 7).rearrange("p (s e) -> p s e", s=16) for _ in range(2)]
        B.hist_s = AR.alloc(12 * 48).rearrange("p (c s e) -> p c s e", c=12, s=16)
        B.cvout = AR.alloc(12 * 48).rearrange("p (c s e) -> p c s e", c=12, s=16)
        B.ld1536 = AR.alloc(1536)
        B.ld512 = [AR.alloc(512) for _ in range(2)]
        mk_ = AR.mark()
        B.Sh = [AR.alloc(16 * 128).rearrange("p (s d) -> p s d", s=16) for _ in range(2)]
        B.rowbuf = B.Sh[0].rearrange("p s d -> p (s d)")[:, 0:1536]
        B.kdm = AR.alloc(16 * 128).rearrange("p (s d) -> p s d", s=16)
        B.wTm = AR.alloc(1088)
        B.o1 = AR.alloc(64)
        B.oTs = AR.alloc(64)
        hw_ = AR.mark()
        AR.release(mk_)
        B.xqm = AR.alloc(4 * 1088, BF16).rearrange("p (h r) -> p h r", h=4)
        B.kvs = [AR.alloc(1024) for _ in range(2)]
        B.kvb = [AR.alloc(1024, BF16).rearrange("p (c n) -> p c n", c=2) for _ in range(2)]
        B.kTs = [AR.alloc(1024, BF16).rearrange("p (h m) -> p h m", h=4) for _ in range(2)]
        B.pTall = [AR.alloc(256, BF16).rearrange("p (c t) -> p c t", c=2) for _ in range(4)]
        AR.release(max(hw_, AR.mark()))
    return B


def _mem_kv(G, B, l):
    P, PS, PSB, pk = G.P, G.PS, G.PSB, G.pk
    for j in range(2):
        P.add("sp", CALL("dma_start", out=B.memx[:, :], in_=G.memp[j * 128:(j + 1) * 128, :]), writes=["memx"], dma=1, key="memx")
        G.rms_rstd(B.memx[:, :], 128, "memx", B.junk, "junk", B.ss[:, 0:1], "ss0")
        xb = B.xbf[j % 2]
        P.add("act", CALL("activation", out=xb[:, :], in_=B.memx[:, :], func=AF.Copy, scale=B.ss[:, 0:1]),
              reads=["memx", "ss0"], writes=["xbf0"])
        G.to_featmajor(xb, 128, "xbf0", B.hTm, f"hTm{j}", slice(j * 128, (j + 1) * 128), gsb=G.gmem_sb, gkey="gmem")
    for g in range(4):
        wv, wk = G.load_w(G.w_mkv[l], 8, g * 256, 256, B.stg, B.wbf)
        for j in range(2):
            b = P.ps()
            for k in range(8):
                P.add("pe", CALL("matmul", PS(b, 256), lhsT=B.hTm[:, k, j * 128:(j + 1) * 128], rhs=wv[:, k, :], start=(k == 0), stop=(k == 7)),
                      reads=[f"hTm{j}", wk], writes=[pk(b)])
            P.add("act", CALL("activation", out=B.kvrow[:, j, g * 256:(g + 1) * 256], in_=PS(b, 256), func=AF.Copy),
                  reads=[pk(b)], writes=[f"kvrow{j}_{g}"])
            if g >= 2:
                P.add("dve", CALL("tensor_copy", out=B.vm[:, j, (g - 2) * 256:(g - 1) * 256], in_=PS(b, 256)),
                      reads=[pk(b)], writes=[f"vm{j}_{g}"])
        if g < 2:
            for cc in range(2):
                b = P.ps()
                for k in range(8):
                    P.add("pe", CALL("matmul", PS(b, 256), lhsT=wv[:, k, cc * 128:(cc + 1) * 128], rhs=B.hTm[:, k, :], start=(k == 0), stop=(k == 7)),
                          reads=["hTm0", "hTm1", wk], writes=[pk(b)])
                P.add("act", CALL("activation", out=B.kTm[:, g * 2 + cc, :], in_=PS(b, 256), func=AF.Copy),
                      reads=[pk(b)], writes=[f"kTm{g * 2 + cc}"])
    for j in range(2):
        P.add("pool", CALL("dma_start", out=G.o_mk[l, j * 128:(j + 1) * 128, :], in_=B.kvrow[:, j, 0:512]),
              reads=[f"kvrow{j}_0", f"kvrow{j}_1"], dma=1, key=f"o_mk{j}")
        P.add("pool", CALL("dma_start", out=G.o_mv[l, j * 128:(j + 1) * 128, :], in_=B.kvrow[:, j, 512:1024]),
              reads=[f"kvrow{j}_2", f"kvrow{j}_3"], dma=1, key=f"o_mv{j}")
    B.kTm_keys = [f"kTm{h}" for h in range(4)]
    B.vm_keys = [f"vm{j}_{g}" for j in range(2) for g in (2, 3)]


def _proj(G, B, l, wdram, krows, c0, nch, srcT, srckeys, NT, consume, chunk_w=128):
    P, PS, pk = G.P, G.PS, G.pk
    i = 0
    while i < nch:
        ng = min(2, nch - i)
        ncols = 128 * ng if chunk_w == 128 else chunk_w
        wv, wk = G.load_w(wdram, krows, c0 + i * 128, ncols, B.stg, B.wbf)
        for cc in range(ng):
            b = P.ps()
            for k in range(krows):
                P.add("pe", CALL("matmul", PS(b, NT)[0:chunk_w, :], lhsT=wv[:, k, cc * 128:cc * 128 + chunk_w], rhs=srcT[:, k, 0:NT],
                                                               start=(k == 0), stop=(k == krows - 1)),
                      reads=list(srckeys) + [wk], writes=[pk(b)])
            consume(i + cc, b)
        i += ng


def _delta_tile(G, B, l, j, np_, tsl, sample, S_ap=None):
    P, PS, PSB, pk = G.P, G.PS, G.PSB, G.pk
    LT = (G.Ltri4 if sample else G.Ltri)
    SLx = (G.SL4 if sample else G.SLm)
    I_ = G.identF
    ones = G.onesF
    r_ = slice(0, np_)
    for nm, c0, dst in (("ktok", 4, B.ktok), ("vtok", 8, B.vtok)):
        b = P.ps()
        for h in range(4):
            P.add("pe", CALL("transpose", out=PS(b, 128, h * 128)[r_, :], in_=B.qkvc[:, c0 + h, tsl], identity=I_),
                  reads=[f"qkvc{c0 + h}", "cst"], writes=[pk(b)])
        P.add("act", CALL("activation", out=dst[r_, :, :], in_=PS(b, 512)[r_, :].rearrange("p (h d) -> p h d", h=4), func=AF.Copy),
              reads=[pk(b)], writes=[nm])
    b = P.ps()
    P.add("pe", CALL("matmul", PS(b, 4)[r_, :], lhsT=LT[r_, r_], rhs=B.ba[r_, 8:12], start=True, stop=True), reads=["ba", "cst"], writes=[pk(b)])
    P.add("pe", CALL("matmul", PS(b, 4, 8)[r_, :], lhsT=LT[r_, r_], rhs=B.ba[r_, 8:12], start=True, stop=False), reads=["ba", "cst"], writes=[pk(b)])
    P.add("pe", CALL("matmul", PS(b, 4, 8)[r_, :], lhsT=SLx[r_, r_], rhs=B.ba[r_, 8:12], start=False, stop=True), reads=["ba", "cst"], writes=[pk(b)])
    P.add("dve", CALL("tensor_copy", out=B.gcc[r_, 0:4], in_=PS(b, 4)[r_, :]), reads=[pk(b)], writes=["gcc"])
    P.add("dve", CALL("tensor_copy", out=B.gcc[r_, 8:12], in_=PS(b, 4, 8)[r_, :]), reads=[pk(b)], writes=["gcc"])
    P.add("act", CALL("activation", out=B.gcc[r_, 4:8], in_=B.gcc[r_, 0:4], func=AF.Exp), reads=["gcc"], writes=["gcc"])
    for h in range(4):
        W = B.dw[h % 2]
        wn = lambda n, h=h: (f"dws_{n}" if n in B.dw_shared else f"dw{h % 2}_{n}")
        qT = B.qkvc[:, h, tsl]
        kT = B.qkvc[:, 4 + h, tsl]
        beta = B.ba[r_, h:h + 1]
        nbeta = B.ba[r_, 4 + h:5 + h]
        gcol = B.ba[r_, 8 + h:9 + h]
        gc_c = B.gcc[r_, h:h + 1]
        egc_c = B.gcc[r_, 4 + h:5 + h]
        gl_c = B.gcc[r_, 8 + h:9 + h]
        P.add("dve", CALL("tensor_scalar", out=W["gL"][r_, r_], in0=LT[r_, r_], scalar1=gcol, scalar2=0.0, op0=ALU.mult, op1=ALU.add),
              reads=["ba", "cst"], writes=[wn("gL")])
        b = P.ps()
        P.add("pe", CALL("matmul", PS(b, np_), lhsT=ones[r_, :], rhs=W["gL"][r_, r_], start=True, stop=True), reads=[wn("gL"), "cst"], writes=[pk(b)])
        P.add("act", CALL("activation", out=W["gcr"][:, r_], in_=PS(b, np_), func=AF.Copy), reads=[pk(b)], writes=[wn("gcr")])
        P.add("act", CALL("activation", out=W["egr"][:, r_], in_=PS(b, np_), func=AF.Exp), reads=[pk(b)], writes=[wn("egr")])
        P.add("dve", CALL("tensor_scalar", out=W["dm"][r_, r_], in0=W["gcr"][r_, r_], scalar1=gc_c, scalar2=0.0, op0=ALU.subtract, op1=ALU.max),
              reads=[wn("gcr"), "gcc"], writes=[wn("dm")])
        P.add("act", CALL("activation", out=W["dm"][r_, r_], in_=W["dm"][r_, r_], func=AF.Exp, scale=-1.0), reads=[wn("dm")], writes=[wn("dm")])
        P.add("pool", CALL("tensor_tensor", out=W["t1"][r_, r_], in0=W["dm"][r_, r_], in1=SLx[r_, r_], op=ALU.mult), reads=[wn("dm"), "cst"], writes=[wn("t1")])
        P.add("dve", CALL("tensor_scalar", out=W["dmT"][r_, r_], in0=W["gcr"][r_, r_], scalar1=gc_c, scalar2=0.0, op0=ALU.subtract, op1=ALU.min),
              reads=[wn("gcr"), "gcc"], writes=[wn("dmT")])
        P.add("act", CALL("activation", out=W["dmT"][r_, r_], in_=W["dmT"][r_, r_], func=AF.Exp), reads=[wn("dmT")], writes=[wn("dmT")])
        P.add("pool", CALL("tensor_tensor", out=W["t2"][r_, r_], in0=W["dmT"][r_, r_], in1=LT[r_, r_], op=ALU.mult), reads=[wn("dmT"), "cst"], writes=[wn("t2")])
        b = P.ps()
        P.add("pe", CALL("matmul", PS(b, np_)[r_, :], lhsT=kT, rhs=kT, start=True, stop=True), reads=[f"qkvc{4 + h}"], writes=[pk(b)])
        P.add("dve", CALL("scalar_tensor_tensor", out=W["Pa"][r_, r_], in0=PS(b, np_)[r_, :], scalar=nbeta, in1=W["t1"][r_, r_], op0=ALU.mult, op1=ALU.mult),
              reads=[pk(b), "ba", wn("t1")], writes=[wn("Pa")])
        b = P.ps()
        P.add("pe", CALL("transpose", out=PS(b, np_)[r_, :], in_=W["Pa"][r_, r_], identity=I_[r_, r_]), reads=[wn("Pa"), "cst"], writes=[pk(b)])
        P.add("act", CALL("activation", out=W["Qa"][r_, r_], in_=PS(b, np_)[r_, :], func=AF.Copy), reads=[pk(b)], writes=[wn("Qa")])
        P.add("dve", CALL("tensor_tensor", out=W["R"][r_, r_], in0=PS(b, np_)[r_, :], in1=I_[r_, r_], op=ALU.add), reads=[pk(b), "cst"], writes=[wn("R")])
        nst = 1 if sample else 5
        Pk, Qk, Pn, Qn = "Pa", "Qa", "Pb", "Qb"
        for k in range(nst):
            bP = P.ps()
            P.add("pe", CALL("matmul", PS(bP, np_)[r_, :], lhsT=W[Qk][r_, r_], rhs=W[Pk][r_, r_], start=True, stop=True),
                  reads=[wn(Pk), wn(Qk)], writes=[pk(bP)])
            P.add("act", CALL("activation", out=W[Pn][r_, r_], in_=PS(bP, np_)[r_, :], func=AF.Copy), reads=[pk(bP)], writes=[wn(Pn)])
            if k < nst - 1:
                bQ = P.ps()
                P.add("pe", CALL("matmul", PS(bQ, np_)[r_, :], lhsT=W[Pk][r_, r_], rhs=W[Qk][r_, r_], start=True, stop=True),
                      reads=[wn(Pk), wn(Qk)], writes=[pk(bQ)])
                P.add("dve", CALL("tensor_copy", out=W[Qn][r_, r_], in_=PS(bQ, np_)[r_, :]), reads=[pk(bQ)], writes=[wn(Qn)])
            bR = P.ps()
            P.add("pe", CALL("matmul", PS(bR, np_)[r_, :], lhsT=W[Pn][r_, r_], rhs=W["R"][r_, r_], start=True, stop=True),
                  reads=[wn(Pn), wn("R")], writes=[pk(bR)])
            P.add("dve", CALL("tensor_tensor", out=W["R"][r_, r_], in0=PS(bR, np_)[r_, :], in1=W["R"][r_, r_], op=ALU.add), reads=[pk(bR), wn("R")], writes=[wn("R")])
            Pk, Pn = Pn, Pk
            Qk, Qn = Qn, Qk
        P.add("dve", CALL("tensor_scalar", out=W["vb"][r_, :], in0=B.vtok[r_, h, :], scalar1=beta, scalar2=0.0, op0=ALU.mult, op1=ALU.add),
              reads=["vtok", "ba"], writes=[wn("vb")])
        P.add("dve", CALL("tensor_scalar", out=W["kbg"][r_, :], in0=B.ktok[r_, h, :], scalar1=beta, scalar2=egc_c, op0=ALU.mult, op1=ALU.mult),
              reads=["ktok", "ba", "gcc"], writes=[wn("kbg")])
        P.add("act", CALL("activation", out=W["kdsc"][r_, 0:1], in_=gc_c, func=AF.Exp, scale=-1.0, bias=gl_c), reads=["gcc"], writes=[wn("kdsc")])
        P.add("dve", CALL("tensor_scalar", out=W["kd"][r_, :], in0=B.ktok[r_, h, :], scalar1=W["kdsc"][r_, 0:1], scalar2=0.0, op0=ALU.mult, op1=ALU.add),
              reads=["ktok", wn("kdsc")], writes=[wn("kd")])
        P.add("pool", CALL("tensor_tensor", out=W["qg"][:, r_], in0=qT, in1=W["egr"][:, r_], op=ALU.mult), reads=[f"qkvc{h}", wn("egr")], writes=[wn("qg")])
        b = P.ps()
        P.add("pe", CALL("matmul", PS(b, 128)[r_, :], lhsT=W["R"][r_, r_], rhs=W["vb"][r_, :], start=True, stop=True), reads=[wn("R"), wn("vb")], writes=[pk(b)])
        P.add("act", CALL("activation", out=W["u"][r_, :], in_=PS(b, 128)[r_, :], func=AF.Copy), reads=[pk(b)], writes=[wn("u")])
        b = P.ps()
        P.add("pe", CALL("matmul", PS(b, np_), lhsT=W["kbg"][r_, :], rhs=W["R"][r_, r_], start=True, stop=True), reads=[wn("R"), wn("kbg")], writes=[pk(b)])
        P.add("act", CALL("activation", out=W["wT"][:, r_], in_=PS(b, np_), func=AF.Copy), reads=[pk(b)], writes=[wn("wT")])
        b = P.ps()
        P.add("pe", CALL("matmul", PS(b, np_)[r_, :], lhsT=kT, rhs=qT, start=True, stop=True), reads=[f"qkvc{4 + h}", f"qkvc{h}"], writes=[pk(b)])
        P.add("dve", CALL("tensor_tensor", out=W["attnT"][r_, r_], in0=PS(b, np_)[r_, :], in1=W["t2"][r_, r_], op=ALU.mult), reads=[pk(b), wn("t2")], writes=[wn("attnT")])
        if not sample:
            Sk = f"S{h}"
            Sh_ = G.Sst[:, h, :]
            bo = P.ps()
            for ci in range(2):
                rr_ = slice(ci * 64, ci * 64 + 64)
                bw = P.ps()
                P.add("pe", CALL("matmul", PS(bw, 128), lhsT=W["wT"][:, 0:128], rhs=Sh_, start=True, stop=True), reads=[wn("wT"), Sk], writes=[pk(bw)])
                P.add("dve", CALL("tensor_tensor", out=W["vn"][rr_, :], in0=W["u"][rr_, :], in1=PS(bw, 128)[rr_, :], op=ALU.subtract),
                      reads=[pk(bw), wn("u")], writes=[wn("vn") + str(ci)])
                P.add("pe", CALL("matmul", PS(bo, 64, ci * 64), lhsT=Sh_, rhs=W["qg"][:, rr_], start=True, stop=False), reads=[wn("qg"), Sk], writes=[pk(bo)])
                P.add("pe", CALL("matmul", PS(bo, 64, ci * 64), lhsT=W["vn"][rr_, :], rhs=W["attnT"][rr_, rr_], start=False, stop=True),
                      reads=[wn("vn") + str(ci), wn("attnT")], writes=[pk(bo)])
                bs = P.ps()
                P.add("pe", CALL("matmul", PS(bs, 128), lhsT=W["kd"][rr_, :], rhs=W["vn"][rr_, :], start=True, stop=True), reads=[wn("kd"), wn("vn") + str(ci)], writes=[pk(bs)])
                P.add("dve", CALL("scalar_tensor_tensor", out=Sh_, in0=Sh_, scalar=W["egr"][:, ci * 64 + 63:ci * 64 + 64], in1=PS(bs, 128), op0=ALU.mult, op1=ALU.add),
                      reads=[pk(bs), wn("egr"), Sk], writes=[Sk])
            o_ap = PS(bo, np_)
            o_key = pk(bo)
        else:
            Shb = B.Sh[h % 2]
            Sk = f"Sh{h % 2}"
            P.add("sp", CALL("dma_start", out=Shb[:, :, :], in_=G.st_delta[l, :, h].rearrange("s k v -> k s v")), writes=[Sk], dma=1, key=Sk)
            P.add("dve", CALL("tensor_copy", out=B.wTm[:, 0:1088].rearrange("p (s r) -> p s r", r=68)[:, :, 0:4], in_=W["wT"][:, 0:64].rearrange("p (s i) -> p s i", i=4)),
                  reads=[wn("wT")], writes=["wTm"])
            bw = P.ps()
            for s in range(16):
                P.add("pe", CALL("matmul", PS(bw, 128)[0:64, :], lhsT=B.wTm[:, s * 64:(s + 1) * 64], rhs=Shb[:, s, :], start=(s == 0), stop=(s == 15)),
                      reads=["wTm", Sk], writes=[pk(bw)])
            P.add("dve", CALL("tensor_tensor", out=W["vn"][0:64, :], in0=W["u"][0:64, :], in1=PS(bw, 128)[0:64, :], op=ALU.subtract), reads=[pk(bw), wn("u")], writes=[wn("vn") + "0"])
            bo = P.ps()
            for s in range(16):
                P.add("pe", CALL("matmul", PS(bo, 4, 4 * s), lhsT=Shb[:, s, :], rhs=W["qg"][:, 4 * s:4 * s + 4], start=True, stop=True),
                      reads=[wn("qg"), Sk], writes=[pk(bo)])
            P.add("act", CALL("activation", out=B.o1[:, 0:64], in_=PS(bo, 64), func=AF.Copy), reads=[pk(bo)], writes=["o1"])
            b2 = P.ps()
            P.add("pe", CALL("matmul", PS(b2, 64), lhsT=W["vn"][0:64, :], rhs=W["attnT"][0:64, 0:64], start=True, stop=True), reads=[wn("vn") + "0", wn("attnT")], writes=[pk(b2)])
            P.add("dve", CALL("tensor_tensor", out=B.oTs[:, 0:64], in0=PS(b2, 64), in1=B.o1[:, 0:64], op=ALU.add), reads=[pk(b2), "o1"], writes=["oTs"])
            P.add("pool", CALL("tensor_tensor", out=B.kdm[0:64, :, :], in0=W["kd"][0:64, :].unsqueeze(1).to_broadcast([64, 16, 128]),
                                                         in1=G.seqmask[0:64, :].unsqueeze(2).to_broadcast([64, 16, 128]), op=ALU.mult),
                  reads=[wn("kd"), "cst"], writes=["kdm"])
            for s in range(16):
                bs = P.ps()
                P.add("pe", CALL("matmul", PS(bs, 128), lhsT=B.kdm[0:64, s, :], rhs=W["vn"][0:64, :], start=True, stop=True), reads=["kdm", wn("vn") + "0"], writes=[pk(bs)])
                P.add("dve", CALL("scalar_tensor_tensor", out=Shb[:, s, :], in0=Shb[:, s, :], scalar=W["egr"][:, 4 * s + 3:4 * s + 4], in1=PS(bs, 128), op0=ALU.mult, op1=ALU.add),
                      reads=[pk(bs), wn("egr"), Sk], writes=[Sk])
            P.add("pool", CALL("dma_start", out=G.o_delta_s[l, :, h].rearrange("s k v -> k s v"), in_=Shb[:, :, :]), reads=[Sk], dma=1, key="o_" + Sk)
            o_ap = B.oTs[:, 0:64]
            o_key = "oTs"
        P.add("act", CALL("activation", out=W["osq"][:, r_], in_=o_ap, func=AF.Square), reads=[o_key], writes=[wn("osq")])
        bq = P.ps()
        P.add("pe", CALL("matmul", PS(bq, np_), lhsT=ones, rhs=W["osq"][:, r_], start=True, stop=True), reads=[wn("osq"), "cst"], writes=[pk(bq)])
        P.add("act", CALL("activation", out=W["rr"][:, r_], in_=PS(bq, np_), func=AF.Sqrt, scale=1.0 / 128, bias=G.epsb[:, 0:1]), reads=[pk(bq), "epsb"], writes=[wn("rr")])
        P.add("dve", CALL("reciprocal", out=W["rr"][:, r_], in_=W["rr"][:, r_]), reads=[wn("rr")], writes=[wn("rr")])
        P.add("dve", CALL("scalar_tensor_tensor", out=W["y1"][:, r_], in0=o_ap, scalar=G.gdn_sb[:, 0:1], in1=W["rr"][:, r_], op0=ALU.mult, op1=ALU.mult),
              reads=[o_key, "gdn", wn("rr")], writes=[wn("y1")])
        P.add("pool", CALL("tensor_tensor", out=B.yT[:, 4 + h, tsl], in0=W["y1"][:, r_], in1=B.zs[:, h, tsl], op=ALU.mult), reads=[wn("y1"), f"zs{h}"], writes=[f"yT{4 + h}_{j}"])


def _attn_softmax(G, B, sc_ap, sc_key, np_, h, out_writes):
    P, PS, PSB, pk = G.P, G.PS, G.PSB, G.pk
    r_ = slice(0, np_)
    i = h % 2
    sc = 128.0 ** -0.5
    mx = B.asm[r_, 4 * i:4 * i + 1]
    nmx = B.asm[r_, 4 * i + 1:4 * i + 2]
    rs = B.asm[r_, 4 * i + 2:4 * i + 3]
    ak = f"asm{i}"
    P.add("dve", CALL("reduce_max", out=mx, in_=sc_ap, axis=AX.X), reads=[sc_key], writes=[ak])
    P.add("dve", CALL("tensor_scalar", out=nmx, in0=mx, scalar1=-sc, scalar2=0.0, op0=ALU.mult, op1=ALU.add), reads=[ak], writes=[ak])
    P.add("act", CALL("activation", out=B.pexp[i][r_, :], in_=sc_ap, func=AF.Exp, scale=sc, bias=nmx, accum_out=rs), reads=[sc_key, ak], writes=[f"pexp{i}", ak])
    P.add("dve", CALL("reciprocal", out=rs, in_=rs), reads=[ak], writes=[ak])
    P.add("dve", CALL("tensor_scalar", out=B.pn[i][r_, :], in0=B.pexp[i][r_, :], scalar1=rs, scalar2=0.0, op0=ALU.mult, op1=ALU.add), reads=[f"pexp{i}", ak], writes=[f"pn{i}"])
    bt = P.ps()
    for mc in range(2):
        P.add("pe", CALL("transpose", out=PSB(bt, np_, mc * 128), in_=B.pn[i][r_, mc * 128:(mc + 1) * 128], identity=G.identB[r_, r_]),
              reads=[f"pn{i}", "identB"], writes=[pk(bt)])
    P.add("act", CALL("activation", out=B.pT[i][:, :, r_], in_=PSB(bt, 256).rearrange("p (c t) -> p c t", c=2)[:, :, r_], func=AF.Copy), reads=[pk(bt)], writes=[f"pT{i}"])
    return B.pT[i], f"pT{i}"


def _mixer_st(G, B, l, st, tiles, sample):
    P, PS, PSB, pk = G.P, G.PS, G.PSB, G.pk
    NT = sum(t[1] for t in tiles)
    nt = len(tiles)
    hkeys = [f"hT{j}" for j in range(nt)]
    for j, (x_ap, np_, xkey) in enumerate(tiles):
        G.rms_rstd(x_ap, np_, xkey, B.junk, "junk", B.ss[:, 0:1], "ss0")
        xb = B.xbf[j % 2]
        P.add("act", CALL("activation", out=xb[0:np_, :], in_=x_ap, func=AF.Copy, scale=B.ss[0:np_, 0:1]),
              reads=[xkey, "ss0"], writes=["xbf0"])
        G.to_featmajor(xb, np_, "xbf0", B.hT, hkeys[j], slice(j * 128, j * 128 + np_), gsb=G.gmix_sb, gkey="gmix")

    win = (2, 4, 8, 16)
    if not sample:
        L = 15 + NT
        if st == 0:
            P.add("pool", CALL("memset", B.uT[:, :, 0:15], 0.0), writes=[f"uT{c}" for c in range(4)])

        def pool_consume(c, b):
            P.add("act", CALL("activation", out=B.uT[:, c, 15:L], in_=PS(b, NT), func=AF.Copy), reads=[pk(b)], writes=[f"uT{c}"])
            a = B.uT[:, c, :]
            bufs = [(B.pA, "pA"), (B.pB, "pB")]
            src, skey = a, f"uT{c}"
            sh = 1
            for s_ in range(c + 1):
                dst, dkey = bufs[s_ % 2]
                lo = 2 * sh - 1
                P.add("dve", CALL("tensor_tensor", out=dst[:, lo:L], in0=src[:, lo:L], in1=src[:, lo - sh:L - sh], op=ALU.add),
                      reads=[skey], writes=[dkey])
                src, skey = dst, dkey
                sh *= 2
            dT = B.dT[c % 2]
            dk = f"dT{c % 2}"
            P.add("dve", CALL("scalar_tensor_tensor", out=dT[:, 0:NT], in0=src[:, 15:L], scalar=1.0 / win[c], in1=a[:, 15:L], op0=ALU.mult, op1=ALU.subtract),
                  reads=[skey, f"uT{c}"], writes=[dk])
            if st == 0:
                P.add("dve", CALL("tensor_tensor", out=B.t16[:, 0:16], in0=src[:, 15:31], in1=G.rcnt[:, c * 16:(c + 1) * 16], op=ALU.mult), reads=[skey, "cst"], writes=["t16"])
                P.add("dve", CALL("tensor_tensor", out=dT[:, 0:16], in0=B.t16[:, 0:16], in1=a[:, 15:31], op=ALU.subtract), reads=["t16", f"uT{c}", dk], writes=[dk])
            b2 = P.ps()
            P.add("pe", CALL("matmul", PS(b2, NT), lhsT=G.wgrp_b[:, c, :], rhs=dT[:, 0:NT], start=True, stop=True), reads=[dk, "wgrp_b"], writes=[pk(b2)])
            P.add("act", CALL("activation", out=B.yT[:, c, 0:NT], in_=PS(b2, NT), func=AF.Copy, scale=G.psc_sb[:, c:c + 1]), reads=[pk(b2), "psc"], writes=[f"yT{c}_all"])
        _proj(G, B, l, G.w_in[l], 8, 0, 4, B.hT, hkeys, NT, pool_consume)
        P.add("pool", CALL("tensor_copy", out=B.uT[:, :, 0:15], in_=B.uT[:, :, NT:NT + 15]), writes=[f"uT{c}" for c in range(4)])
    else:
        def pool_consume(c, b):
            P.add("act", CALL("activation", out=B.uTs[:, c, :, 15:19], in_=PS(b, 64).rearrange("p (s i) -> p s i", i=4), func=AF.Copy), reads=[pk(b)], writes=[f"uTs{c}"])
            a = B.uTs[:, c, :, :]
            pA = B.pA[:, 0:304].rearrange("p (s e) -> p s e", e=19)
            pB = B.pB[:, 0:304].rearrange("p (s e) -> p s e", e=19)
            bufs = [(pA, "pA"), (pB, "pB")]
            src, skey = a, f"uTs{c}"
            sh = 1
            for s_ in range(c + 1):
                dst, dkey = bufs[s_ % 2]
                lo = 2 * sh - 1
                P.add("dve", CALL("tensor_tensor", out=dst[:, :, lo:19], in0=src[:, :, lo:19], in1=src[:, :, lo - sh:19 - sh], op=ALU.add),
                      reads=[skey], writes=[dkey])
                src, skey = dst, dkey
                sh *= 2
            dT = B.dT[c % 2]
            dk = f"dT{c % 2}"
            P.add("dve", CALL("scalar_tensor_tensor", out=dT[:, 0:64].rearrange("p (s i) -> p s i", i=4), in0=src[:, :, 15:19], scalar=1.0 / win[c], in1=a[:, :, 15:19], op0=ALU.mult, op1=ALU.subtract),
                  reads=[skey, f"uTs{c}"], writes=[dk])
            b2 = P.ps()
            P.add("pe", CALL("matmul", PS(b2, NT), lhsT=G.wgrp_b[:, c, :], rhs=dT[:, 0:NT], start=True, stop=True), reads=[dk, "wgrp_b"], writes=[pk(b2)])
            P.add("act", CALL("activation", out=B.yT[:, c, 0:NT], in_=PS(b2, NT), func=AF.Copy, scale=G.psc_sb[:, c:c + 1]), reads=[pk(b2), "psc"], writes=[f"yT{c}_all"])
        _proj(G, B, l, G.w_in[l], 8, 0, 4, B.hT, hkeys, NT, pool_consume)

    def qkv_consume(c, b):
        if not sample:
            pre = B.pre[c % 2]
            pkey = f"pre{c % 2}"
            cv = B.cv[c % 2]
            ckey = f"cv{c % 2}"
            P.add("pool", CALL("tensor_copy", out=pre[:, 0:3], in_=G.hist[:, c, :]), reads=[f"hist{c}"], writes=[pkey + "h"])
            P.add("act", CALL("activation", out=pre[:, 3:3 + NT], in_=PS(b, NT), func=AF.Copy), reads=[pk(b)], writes=[pkey])
            P.add("dve", CALL("tensor_scalar", out=cv[:, 0:NT], in0=pre[:, 0:NT], scalar1=G.wconv_sb[:, c, 0:1], scalar2=0.0, op0=ALU.mult, op1=ALU.add),
                  reads=[pkey, pkey + "h", "wconv"], writes=[ckey])
            for j in range(1, 4):
                P.add("dve", CALL("scalar_tensor_tensor", out=cv[:, 0:NT], in0=pre[:, j:j + NT], scalar=G.wconv_sb[:, c, j:j + 1], in1=cv[:, 0:NT], op0=ALU.mult, op1=ALU.add),
                      reads=[pkey, pkey + "h", "wconv", ckey], writes=[ckey])
            P.add("pool", CALL("tensor_copy", out=G.hist[:, c, :], in_=pre[:, NT:NT + 3]), reads=[pkey], writes=[f"hist{c}"])
            P.add("act", CALL("activation", out=B.qkvc[:, c, 0:NT], in_=cv[:, 0:NT], func=AF.Silu), reads=[ckey], writes=[f"qkvc{c}"])
        else:
            pre = B.pre_s[c % 2]
            pkey = f"pre{c % 2}"
            cv = B.cv[c % 2][:, 0:64].rearrange("p (s i) -> p s i", i=4)
            ckey = f"cv{c % 2}"
            P.add("pool", CALL("tensor_copy", out=pre[:, :, 0:3], in_=B.hist_s[:, c, :, :]), reads=[f"hist_s{c // 4}"], writes=[pkey + "h"])
            P.add("act", CALL("activation", out=pre[:, :, 3:7], in_=PS(b, 64).rearrange("p (s i) -> p s i", i=4), func=AF.Copy), reads=[pk(b)], writes=[pkey])
            P.add("dve", CALL("tensor_scalar", out=cv, in0=pre[:, :, 0:4], scalar1=G.wconv_sb[:, c, 0:1], scalar2=0.0, op0=ALU.mult, op1=ALU.add),
                  reads=[pkey, pkey + "h", "wconv"], writes=[ckey])
            for j in range(1, 4):
                P.add("dve", CALL("scalar_tensor_tensor", out=cv, in0=pre[:, :, j:j + 4], scalar=G.wconv_sb[:, c, j:j + 1], in1=cv, op0=ALU.mult, op1=ALU.add),
                      reads=[pkey, pkey + "h", "wconv", ckey], writes=[ckey])
            P.add("pool", CALL("tensor_copy", out=B.cvout[:, c, :, :], in_=pre[:, :, 4:7]), reads=[pkey], writes=[f"cvout{c // 4}"])
            P.add("act", CALL("activation", out=B.qkvc[:, c, 0:NT], in_=B.cv[c % 2][:, 0:64], func=AF.Silu), reads=[ckey], writes=[f"qkvc{c}"])
    _proj(G, B, l, G.w_in[l], 8, OFF_Q, 12, B.hT, hkeys, NT, qkv_consume)

    for c in range(8):
        i = c % 2
        P.add("act", CALL("activation", out=B.sqb[i][:, 0:NT], in_=B.qkvc[:, c, 0:NT], func=AF.Square), reads=[f"qkvc{c}"], writes=[f"sqb{i}"])
        b = P.ps()
        P.add("pe", CALL("matmul", PS(b, NT), lhsT=G.onesF, rhs=B.sqb[i][:, 0:NT], start=True, stop=True), reads=[f"sqb{i}", "cst"], writes=[pk(b)])
        P.add("act", CALL("activation", out=B.rinv[i][:, 0:NT], in_=PS(b, NT), func=AF.Sqrt, scale=1.0, bias=G.epsb[:, 0:1]), reads=[pk(b), "epsb"], writes=[f"rinv{i}"])
        P.add("dve", CALL("reciprocal", out=B.rinv[i][:, 0:NT], in_=B.rinv[i][:, 0:NT]), reads=[f"rinv{i}"], writes=[f"rinv{i}"])
        scl = (128.0 ** -0.5) if c < 4 else 1.0
        P.add("dve", CALL("scalar_tensor_tensor", out=B.qkvc[:, c, 0:NT], in0=B.rinv[i][:, 0:NT], scalar=scl, in1=B.qkvc[:, c, 0:NT], op0=ALU.mult, op1=ALU.mult),
              reads=[f"rinv{i}", f"qkvc{c}"], writes=[f"qkvc{c}"])

    def z_consume(c, b):
        P.add("act", CALL("activation", out=B.zs[:, c, 0:NT], in_=PS(b, NT), func=AF.Silu), reads=[pk(b)], writes=[f"zs{c}"])
    _proj(G, B, l, G.w_in[l], 8, OFF_Z, 4, B.hT, hkeys, NT, z_consume)

    def xq_consume(c, b):
        P.add("act", CALL("activation", out=B.xqT[:, c, 0:NT], in_=PS(b, NT), func=AF.Copy), reads=[pk(b)], writes=[f"xqT{c}"])
    _proj(G, B, l, G.w_in[l], 8, OFF_XQ, 4, B.hT, hkeys, NT, xq_consume)

    wba, wbak = G.load_w(G.w_in[l], 8, OFF_BA, 8, B.stg, B.wbf)

    if sample:
        _sample_attn(G, B, l)
        P.barrier()
        P.add("pool", CALL("memset", B.wTm[:, :], 0.0), writes=["wTm"])
    def per_tile(j, x_ap, np_, xkey):
        r_ = slice(0, np_)
        tsl = slice(j * 128, j * 128 + np_)
        b = P.ps()
        for k in range(8):
            P.add("pe", CALL("matmul", PS(b, 8)[r_, :], lhsT=B.hT[:, k, tsl], rhs=wba[:, k, :], start=(k == 0), stop=(k == 7)), reads=[hkeys[j], wbak], writes=[pk(b)])
        ba = B.ba
        P.add("act", CALL("activation", out=ba[r_, 0:4], in_=PS(b, 4)[r_, :], func=AF.Sigmoid), reads=[pk(b)], writes=["ba"])
        P.add("dve", CALL("tensor_scalar", out=ba[r_, 4:8], in0=ba[r_, 0:4], scalar1=-1.0, scalar2=0.0, op0=ALU.mult, op1=ALU.add), reads=["ba"], writes=["ba"])
        P.add("dve", CALL("tensor_tensor", out=ba[r_, 12:16], in0=PS(b, 4, 4)[r_, :], in1=G.dtb_b[r_, :], op=ALU.add), reads=[pk(b), "dtb"], writes=["ba"])
        P.add("dve", CALL("scalar_tensor_tensor", out=ba[r_, 16:20], in0=ba[r_, 12:16], scalar=-1.0, in1=ba[r_, 12:16], op0=ALU.mult, op1=ALU.max), reads=["ba"], writes=["ba"])
        P.add("act", CALL("activation", out=ba[r_, 16:20], in_=ba[r_, 16:20], func=AF.Exp, scale=-1.0), reads=["ba"], writes=["ba"])
        P.add("act", CALL("activation", out=ba[r_, 16:20], in_=ba[r_, 16:20], func=AF.Ln, scale=1.0, bias=G.onesF[r_, 0:1]), reads=["ba", "cst"], writes=["ba"])
        P.add("dve", CALL("scalar_tensor_tensor", out=ba[r_, 12:16], in0=ba[r_, 12:16], scalar=0.0, in1=ba[r_, 16:20], op0=ALU.max, op1=ALU.add), reads=["ba"], writes=["ba"])
        P.add("dve", CALL("tensor_tensor", out=ba[r_, 8:12], in0=ba[r_, 12:16], in1=G.alog_b[r_, :], op=ALU.mult), reads=["ba", "alog"], writes=["ba"])
        _delta_tile(G, B, l, j, np_, tsl, sample)
        if not sample:
            for h in range(4):
                bs_ = P.ps()
                P.add("pe", CALL("matmul", PS(bs_, 256)[r_, :], lhsT=B.xqT[:, h, tsl], rhs=B.kTm[:, h, :], start=True, stop=True),
                      reads=[f"xqT{h}", f"kTm{h}"], writes=[pk(bs_)])
                pT, pTk = _attn_softmax(G, B, PS(bs_, 256)[r_, :], pk(bs_), np_, h, None)
                bo = P.ps()
                for mc in range(2):
                    P.add("pe", CALL("matmul", PS(bo, np_), lhsT=B.vm[:, mc, h * 128:(h + 1) * 128], rhs=pT[:, mc, r_], start=(mc == 0), stop=(mc == 1)),
                          reads=[pTk] + B.vm_keys, writes=[pk(bo)])
                P.add("act", CALL("activation", out=B.yT[:, 8 + h, tsl], in_=PS(bo, np_), func=AF.Copy), reads=[pk(bo)], writes=[f"yT{8 + h}_{j}"])

    for j_, (x_ap_, np__, xkey_) in enumerate(tiles):
        per_tile(j_, x_ap_, np__, xkey_)

    ykeys = [[f"yT{c}_all" for c in range(4)], [f"yT{4 + h}_{j}" for h in range(4) for j in range(nt)], [f"yT{8 + h}_{j}" for h in range(4) for j in range(nt)]]
    if sample:
        ykeys[2] = [f"yT{8 + h}_0" for h in range(4)]

    for half in range(2):
        for n in range(3):
            wg0, wgk0 = G.load_w(G.w_in[l], 8, OFF_GATE + n * 1024 + half * 512, 256, B.stg, B.wbf)
            wb_, wbk = G.load_w(G.w_br[l, n], 4, half * 512, 512, B.stg, B.wbf)
            wg1, wgk1 = G.load_w(G.w_in[l], 8, OFF_GATE + n * 1024 + half * 512 + 256, 256, B.stg, B.wbf)
            for jj in range(4):
                wg, wgk = (wg0, wgk0) if jj < 2 else (wg1, wgk1)
                cc = jj % 2
                i = jj % 2
                bg = P.ps()
                for k in range(8):
                    P.add("pe", CALL("matmul", PS(bg, NT), lhsT=wg[:, k, cc * 128:(cc + 1) * 128], rhs=B.hT[:, k, 0:NT], start=(k == 0), stop=(k == 7)),
                          reads=hkeys + [wgk], writes=[pk(bg)])
                P.add("act", CALL("activation", out=B.sig[i][:, 0:NT], in_=PS(bg, NT), func=AF.Sigmoid), reads=[pk(bg)], writes=[f"sqb{i}"])
                bb = P.ps()
                for c in range(4):
                    P.add("pe", CALL("matmul", PS(bb, NT), lhsT=wb_[:, c, jj * 128:(jj + 1) * 128], rhs=B.yT[:, n * 4 + c, 0:NT], start=(c == 0), stop=(c == 3)),
                          reads=ykeys[n] + [wbk], writes=[pk(bb)])
                if n == 0:
                    P.add("dve", CALL("tensor_tensor", out=B.macc[:, jj, 0:NT], in0=PS(bb, NT), in1=B.sig[i][:, 0:NT], op=ALU.mult), reads=[pk(bb), f"sqb{i}"], writes=[f"macc{jj}"])
                else:
                    P.add("dve", CALL("tensor_tensor", out=B.prod[i][:, 0:NT], in0=PS(bb, NT), in1=B.sig[i][:, 0:NT], op=ALU.mult), reads=[pk(bb), f"sqb{i}"], writes=[f"rinv{i}"])
                    if n == 1:
                        P.add("pool", CALL("tensor_tensor", out=B.macc[:, jj, 0:NT], in0=B.macc[:, jj, 0:NT], in1=B.prod[i][:, 0:NT], op=ALU.add), reads=[f"rinv{i}", f"macc{jj}"], writes=[f"macc{jj}"])
                    else:
                        P.add("pool", CALL("tensor_tensor", out=B.mT[:, half * 4 + jj, 0:NT], in0=B.macc[:, jj, 0:NT], in1=B.prod[i][:, 0:NT], op=ALU.add),
                              reads=[f"rinv{i}", f"macc{jj}"], writes=[f"mT{half * 4 + jj}"])

    mkeys = [f"mT{c}" for c in range(8)]
    for q in range(4):
        wo, wok = G.load_w(G.w_o[l], 8, q * 256, 256, B.stg, B.wbf)
        for j, (x_ap, np_, xkey) in enumerate(tiles):
            tsl = slice(j * 128, j * 128 + np_)
            b = P.ps()
            for k in range(8):
                P.add("pe", CALL("matmul", PS(b, 256)[0:np_, :], lhsT=B.mT[:, k, tsl], rhs=wo[:, k, :], start=(k == 0), stop=(k == 7)),
                      reads=mkeys + [wok], writes=[pk(b)])
            xo = x_ap[:, q * 256:(q + 1) * 256]
            P.add("dve", CALL("tensor_tensor", out=xo, in0=xo, in1=PS(b, 256)[0:np_, :], op=ALU.add), reads=[pk(b), xkey], writes=[xkey])


def _sample_prep(G, B, l):
    P, PS, PSB, pk = G.P, G.PS, G.PSB, G.pk
    I_ = G.identF
    P.add("pool", CALL("memset", B.xqm[:, :, :], 0.0), writes=["xqm"])
    P.add("sp", CALL("dma_start", out=B.ld1536[0:48, :], in_=G.st_conv[l]), writes=["ld1536"], dma=1, key="ld1536")
    for g in range(3):
        b = P.ps()
        for cc in range(4):
            c = g * 4 + cc
            P.add("pe", CALL("transpose", out=PS(b, 48, cc * 48), in_=B.ld1536[0:48, c * 128:(c + 1) * 128], identity=I_[0:48, 0:48]), reads=["ld1536", "cst"], writes=[pk(b)])
        P.add("act", CALL("activation", out=B.hist_s[:, g * 4:(g + 1) * 4, :, :].rearrange("p c s e -> p (c s e)"), in_=PS(b, 192), func=AF.Copy), reads=[pk(b)], writes=[f"hist_s{g}"])
    for hf in range(2):
        ld = B.ld512[hf]
        P.add("sp", CALL("dma_start", out=ld[0:120, :], in_=G.st_pool[l, hf * 120:(hf + 1) * 120, :]), writes=[f"ld512{hf}"], dma=1, key=f"ld512{hf}")
        b = P.ps()
        for c in range(4):
            P.add("pe", CALL("transpose", out=PS(b, 120, c * 120), in_=ld[0:120, c * 128:(c + 1) * 128], identity=I_[0:120, 0:120]), reads=[f"ld512{hf}", "cst"], writes=[pk(b)])
        for c in range(4):
            P.add("act", CALL("activation", out=B.uTs[:, c, hf * 8:(hf + 1) * 8, 0:15], in_=PS(b, 120, c * 120).rearrange("p (s e) -> p s e", e=15), func=AF.Copy),
                  reads=[pk(b)], writes=[f"uTs{c}"])


def _sample_attn(G, B, l):
    P, PS, PSB, pk = G.P, G.PS, G.PSB, G.pk
    for h in range(4):
        P.add("dve", CALL("tensor_copy", out=B.xqm[:, h, 0:1088].rearrange("p (s r) -> p s r", r=68)[:, :, 0:4], in_=B.xqT[:, h, 0:64].rearrange("p (s i) -> p s i", i=4)),
              reads=[f"xqT{h}"], writes=["xqm"])
    for s in range(16):
        i = s % 2
        st_, stk = B.kvs[i], f"kvs{i}"
        P.add("sp", CALL("dma_start", out=st_[:, :].rearrange("p (c n) -> p c n", c=2), in_=G.c_k[l, s].rearrange("(c p) n -> p c n", p=128)), writes=[stk], dma=1, key=stk)
        P.add("act", CALL("activation", out=B.kvb[i][:, :, :].rearrange("p c n -> p (c n)"), in_=st_[:, :], func=AF.Copy), reads=[stk], writes=[f"kvb{i}"])
        bt = 4 + (s % 4)
        for h in range(4):
            for mc in range(2):
                P.add("pe", CALL("transpose", out=PSB(bt, 128, h * 256 + mc * 128), in_=B.kvb[i][:, mc, h * 128:(h + 1) * 128], identity=G.identB),
                      reads=[f"kvb{i}", "identB"], writes=[pk(bt)])
        P.add("dve", CALL("tensor_copy", out=B.kTs[i][:, :, :].rearrange("p h m -> p (h m)"), in_=PSB(bt, 1024)), reads=[pk(bt)], writes=[f"kTs{i}"])
        for h in range(4):
            P.add("pe", CALL("matmul", PS(h, 256)[0:64, :], lhsT=B.xqm[:, h, s * 64:(s + 1) * 64], rhs=B.kTs[i][:, h, :], start=(s == 0), stop=(s == 15)),
                  reads=["xqm", f"kTs{i}"], writes=[pk(h)])
    pTs = []
    for h in range(4):
        pT, pTk = _attn_softmax(G, B, PS(h, 256)[0:64, :], pk(h), 64, h, None)
        dst = B.pTall[h]
        P.add("pool", CALL("tensor_copy", out=dst[:, :, 0:64], in_=pT[:, :, 0:64]), reads=[pTk], writes=[f"pTall{h}"])
        pTs.append(dst)
    for s in range(16):
        i = s % 2
        st_, stk = B.kvs[i], f"kvs{i}"
        P.add("sp", CALL("dma_start", out=st_[:, :].rearrange("p (c n) -> p c n", c=2), in_=G.c_v[l, s].rearrange("(c p) n -> p c n", p=128)), writes=[stk], dma=1, key=stk)
        P.add("act", CALL("activation", out=B.kvb[i][:, :, :].rearrange("p c n -> p (c n)"), in_=st_[:, :], func=AF.Copy), reads=[stk], writes=[f"kvb{i}"])
        for h in range(4):
            for mc in range(2):
                P.add("pe", CALL("matmul", PS(h, 4, 4 * s), lhsT=B.kvb[i][:, mc, h * 128:(h + 1) * 128], rhs=pTs[h][:, mc, 4 * s:4 * s + 4], start=(mc == 0), stop=(mc == 1)),
                      reads=[f"kvb{i}", f"pTall{h}"], writes=[pk(h)])
    for h in range(4):
        P.add("act", CALL("activation", out=B.yT[:, 8 + h, 0:64], in_=PS(h, 64), func=AF.Copy), reads=[pk(h)], writes=[f"yT{8 + h}_0"])
    P.psi = 4


def mixer_phase(G, l):
    P, AR, PS, pk = G.P, G.AR, G.PS, G.pk
    I_ = G.identF
    P.barrier()
    m0 = AR.mark()
    Bp = _mixer_bufs(G, 256, False)
    P.add("pool", CALL("memset", G.hist[:, :, :], 0.0), writes=[f"hist{c}" for c in range(12)])
    P.add("pool", CALL("memset", G.Sst[:, :, :], 0.0), writes=[f"S{h}" for h in range(4)])
    P.add("sp", CALL("dma_start", out=Bp.stg[0][0][:, 0:512].rearrange("p (g e) -> p g e", g=4), in_=G.w_grp[l].rearrange("g c e -> c g e")), writes=["stg0"], dma=1, key="stg0")
    P.add("act", CALL("activation", out=G.wgrp_b[:, :, :], in_=Bp.stg[0][0][:, 0:512].rearrange("p (g e) -> p g e", g=4), func=AF.Copy), reads=["stg0"], writes=["wgrp_b"])
    _mem_kv(G, Bp, l)
    P.barrier()
    nst = G.n_st if hasattr(G, "n_st") else 8
    for st in range(nst):
        tiles = [(G.xp[:, st * 2 + j, :], 128, f"xp{st * 2 + j}") for j in range(2)]
        _mixer_st(G, Bp, l, st, tiles, False)
    P.barrier()
    b = P.ps()
    for c in range(4):
        P.add("pe", CALL("transpose", out=PS(b, 128, c * 128)[0:15, :], in_=Bp.uT[:, c, 0:15], identity=I_), reads=[f"uT{c}", "cst"], writes=[pk(b)])
    P.add("act", CALL("activation", out=Bp.rowbuf[0:15, 0:512], in_=PS(b, 512)[0:15, :], func=AF.Copy), reads=[pk(b)], writes=["rowbuf"])
    P.add("pool", CALL("dma_start", out=G.o_pool_p[l], in_=Bp.rowbuf[0:15, 0:512]), reads=["rowbuf"], dma=1, key="o_pool_p")
    for g in range(3):
        b = P.ps()
        for cc in range(4):
            c = g * 4 + cc
            P.add("pe", CALL("transpose", out=PS(b, 128, cc * 128)[0:3, :], in_=G.hist[:, c, :], identity=I_), reads=[f"hist{c}", "cst"], writes=[pk(b)])
        P.add("act", CALL("activation", out=Bp.rowbuf[0:3, g * 512:(g + 1) * 512], in_=PS(b, 512)[0:3, :], func=AF.Copy), reads=[pk(b)], writes=["rowbuf"])
    P.add("pool", CALL("dma_start", out=G.o_conv_p[l], in_=Bp.rowbuf[0:3, 0:1536]), reads=["rowbuf"], dma=1, key="o_conv_p")
    P.add("pool", CALL("dma_start", out=G.o_delta_p[l].rearrange("h k v -> k h v"), in_=G.Sst[:, :, :]), reads=[f"S{h}" for h in range(4)], dma=1, key="o_delta_p")
    P.barrier()
    AR.release(m0)
    if getattr(G, "skip_sample", False):
        return
    Bs = _mixer_bufs(G, 64, True)
    _sample_prep(G, Bs, l)
    _mixer_st(G, Bs, l, 0, [(G.xs[0:TS, :], TS, "xs")], True)
    P.barrier()
    for hf in range(2):
        b = P.ps()
        for c in range(4):
            P.add("dve", CALL("tensor_copy", out=Bs.pA[:, 0:120].rearrange("p (s e) -> p s e", e=15), in_=Bs.uTs[:, c, hf * 8:(hf + 1) * 8, 4:19]), reads=[f"uTs{c}"], writes=["pA"])
            P.add("pe", CALL("transpose", out=PS(b, 128, c * 128)[0:120, :], in_=Bs.pA[:, 0:120], identity=I_), reads=["pA", "cst"], writes=[pk(b)])
        P.add("act", CALL("activation", out=Bs.rowbuf[0:120, 0:512], in_=PS(b, 512)[0:120, :], func=AF.Copy), reads=[pk(b)], writes=["rowbuf"])
        P.add("pool", CALL("dma_start", out=G.o_pool_s[l, hf * 120:(hf + 1) * 120, :], in_=Bs.rowbuf[0:120, 0:512]), reads=["rowbuf"], dma=1, key="o_pool_s")
    for g in range(3):
        b = P.ps()
        for cc in range(4):
            c = g * 4 + cc
            P.add("pe", CALL("transpose", out=PS(b, 128, cc * 128)[0:48, :], in_=Bs.cvout[:, c, :, :].rearrange("p s e -> p (s e)"), identity=I_), reads=[f"cvout{g}", "cst"], writes=[pk(b)])
        P.add("act", CALL("activation", out=Bs.ld1536[0:48, g * 512:(g + 1) * 512], in_=PS(b, 512)[0:48, :], func=AF.Copy), reads=[pk(b)], writes=["ld1536"])
    P.add("pool", CALL("dma_start", out=G.o_conv_s[l], in_=Bs.ld1536[0:48, :]), reads=["ld1536"], dma=1, key="o_conv_s")
    P.barrier()
    AR.release(m0)


def peer_phase(G, l):
    P, AR, PS, PSB, pk = G.P, G.AR, G.PS, G.PSB, G.pk
    P.barrier()
    m0 = AR.mark()
    gsblk = AR.alloc(NG * 1024)
    gs = [gsblk[:, i * 1024:(i + 1) * 1024] for i in range(NG)]
    wq = AR.alloc(8 * 2048, BF16).rearrange("p (k n) -> p k n", k=8)
    skT = AR.alloc(16 * 128, BF16).rearrange("p (j n) -> p j n", j=16)
    skb = AR.alloc(16 * 128, BF16).rearrange("p (j n) -> p j n", j=16)
    hn = AR.alloc(1024)
    hnb = AR.alloc(1024, BF16)
    hnT = AR.alloc(8 * 128, BF16).rearrange("p (k t) -> p k t", k=8)
    qT = AR.alloc(16 * 128, BF16).rearrange("p (j t) -> p j t", j=16)
    s1 = AR.alloc(2048)
    s2 = AR.alloc(2048)
    oh = AR.alloc(2048)
    top = AR.alloc(256).rearrange("p (j a) -> p j a", j=16)
    topi = AR.alloc(256, U32).rearrange("p (j a) -> p j a", j=16)
    topif = AR.alloc(256).rearrange("p (h t a) -> p h t a", h=8, t=2)
    best = AR.alloc(128).rearrange("p (h k) -> p h k", h=8)
    pos = AR.alloc(128, U32).rearrange("p (h k) -> p h k", h=8)
    pint = AR.alloc(128, U32).rearrange("p (h k) -> p h k", h=8)
    paf = AR.alloc(128).rearrange("p (h k) -> p h k", h=8)
    pbf = AR.alloc(128).rearrange("p (h k) -> p h k", h=8)
    I1 = AR.alloc(128).rearrange("p (h k) -> p h k", h=8)
    I2 = AR.alloc(128).rearrange("p (h k) -> p h k", h=8)
    idxf = AR.alloc(128)
    idx = AR.alloc(128, I32)
    gate = AR.alloc(128).rearrange("p (h k) -> p h k", h=8)
    gsum = AR.alloc(8)
    av = AR.alloc(128)
    tg = AR.alloc(128)
    wgt = AR.alloc(128)
    acc = [AR.alloc(1024) for _ in range(2)]
    jb = AR.alloc(1024, BF16)
    ss = AR.alloc(8)

    stgA = (gsblk[:, 0:2048], ["gs0", "gs1"])
    stgB = (gsblk[:, 2048:4096], ["gs2", "gs3"])
    for g in range(8):
        sv, skeys = (stgA, stgB)[g % 2]
        svv = sv.rearrange("p (k n) -> p k n", k=8)
        P.add("sp", CALL("dma_start", out=svv, in_=G.w_pq[l].rearrange("(k p) n -> p k n", p=128)[:, :, g * 256:(g + 1) * 256]), writes=skeys, dma=1, key="pq" + skeys[0])
        if g % 2 == 0:
            P.add("act", CALL("activation", out=wq[:, :, g * 256:(g + 1) * 256], in_=svv, func=AF.Copy), reads=skeys, writes=[f"wq{g}"])
        else:
            P.add("dve", CALL("tensor_copy", out=wq[:, :, g * 256:(g + 1) * 256], in_=svv), reads=skeys, writes=[f"wq{g}"])
    wqkeys = [f"wq{g}" for g in range(8)]
    sks = gsblk[:, 4096:6144].rearrange("p (j c) -> p j c", j=16)
    P.add("sp", CALL("dma_start", out=sks, in_=G.subk[l].rearrange("j k c -> k j c")), writes=["gs4", "gs5"], dma=1, key="sks")
    P.add("act", CALL("activation", out=skb[:, :, :], in_=sks, func=AF.Copy), reads=["gs4", "gs5"], writes=["skb"])
    for hf in range(2):
        b = P.ps()
        for jj in range(8):
            j = hf * 8 + jj
            P.add("pe", CALL("transpose", out=PSB(b, 128, jj * 128), in_=skb[:, j, :], identity=G.identB), reads=["skb", "identB"], writes=[pk(b)])
        P.add("act", CALL("activation", out=skT[:, hf * 8:(hf + 1) * 8, :].rearrange("p j n -> p (j n)"), in_=PSB(b, 1024), func=AF.Copy), reads=[pk(b)], writes=[f"skT{hf}"])

    tiles = [(G.xp[:, t, :], 128, f"xp{t}") for t in range(16)] + [(G.xs[0:TS, :], TS, "xs")]
    if hasattr(G, "peer_tiles"):
        tiles = [tiles[i] for i in G.peer_tiles]
    gst = {"gi": 0}

    def do_tile(x_ap, np_, xkey):
        r_ = slice(0, np_)
        G.rms_rstd(x_ap, np_, xkey, jb, "jb", ss[:, 0:1], "pss")
        P.add("dve", CALL("scalar_tensor_tensor", out=hn[r_, :], in0=x_ap, scalar=ss[r_, 0:1], in1=G.gb_ffn[r_, :], op0=ALU.mult, op1=ALU.mult),
              reads=[xkey, "pss", "gb_ffn"], writes=["hn"])
        P.add("act", CALL("activation", out=hnb[r_, :], in_=hn[r_, :], func=AF.Copy), reads=["hn"], writes=["hnb"])
        G.to_featmajor(hnb, np_, "hnb", hnT, "hnT", slice(0, np_))
        for j in range(16):
            b = P.ps()
            for k in range(8):
                P.add("pe", CALL("matmul", PS(b, np_), lhsT=wq[:, k, j * 128:(j + 1) * 128], rhs=hnT[:, k, r_], start=(k == 0), stop=(k == 7)),
                      reads=["hnT", wqkeys[j // 2]], writes=[pk(b)])
            if j % 2 == 0:
                P.add("act", CALL("activation", out=qT[:, j, r_], in_=PS(b, np_), func=AF.Copy), reads=[pk(b)], writes=[f"qT{j}"])
            else:
                P.add("dve", CALL("tensor_copy", out=qT[:, j, r_], in_=PS(b, np_)), reads=[pk(b)], writes=[f"qT{j}"])
        for q in range(4):
            b = P.ps()
            for jj in range(4):
                j = q * 4 + jj
                P.add("pe", CALL("matmul", PS(b, 128, jj * 128)[r_, :], lhsT=qT[:, j, r_], rhs=skT[:, j, :], start=True, stop=True),
                      reads=[f"qT{j}", f"skT{j // 8}"], writes=[pk(b)])
            P.add("act", CALL("activation", out=s1[r_, q * 512:(q + 1) * 512], in_=PS(b, 512)[r_, :], func=AF.Copy), reads=[pk(b)], writes=[f"s1_{q}"])
        s1v = s1[:, :].rearrange("p (j n) -> p j n", j=16)
        s2v = s2[:, :].rearrange("p (j n) -> p j n", j=16)
        for j in range(16):
            sk_ = f"s1_{j // 4}"
            P.add("dve", CALL("max", out=top[r_, j, 0:8], in_=s1v[r_, j, :]), reads=[sk_], writes=[f"top{j}a"])
            P.add("dve", CALL("max_index", out=topi[r_, j, 0:8], in_max=top[r_, j, 0:8], in_values=s1v[r_, j, :]), reads=[sk_, f"top{j}a"], writes=[f"topi{j}a"])
            P.add("dve", CALL("match_replace", out=s2v[r_, j, :], in_to_replace=top[r_, j, 0:8], in_values=s1v[r_, j, :], imm_value=NEG), reads=[sk_, f"top{j}a"], writes=[f"s2_{j}"])
            P.add("dve", CALL("max", out=top[r_, j, 8:16], in_=s2v[r_, j, :]), reads=[f"s2_{j}"], writes=[f"top{j}b"])
            P.add("dve", CALL("max_index", out=topi[r_, j, 8:16], in_max=top[r_, j, 8:16], in_values=s2v[r_, j, :]), reads=[f"s2_{j}", f"top{j}b"], writes=[f"topi{j}b"])
        allt = [f"top{j}{x}" for j in range(16) for x in "ab"]
        alli = [f"topi{j}{x}" for j in range(16) for x in "ab"]
        P.add("dve", CALL("tensor_copy", out=topif[r_, :, :, :].rearrange("p h t a -> p (h t a)"), in_=topi[r_, :, :].rearrange("p j a -> p (j a)")), reads=alli, writes=["topif"])
        topv = top[:, :, :].rearrange("p (h t) a -> p h t a", t=2)
        cand = s1[:, :].rearrange("p (h a b) -> p h a b", h=8, a=16)
        cand2 = s2[:, :].rearrange("p (h n) -> p h n", h=8)
        candf = s1[:, :].rearrange("p (h n) -> p h n", h=8)
        for h in range(8):
            P.add("dve", CALL("tensor_tensor", out=cand[r_, h, :, :], in0=topv[r_, h, 0, :].unsqueeze(2).to_broadcast([np_, 16, 16]),
                                                        in1=topv[r_, h, 1, :].unsqueeze(1).to_broadcast([np_, 16, 16]), op=ALU.add),
                  reads=allt, writes=[f"s1_{h // 2}"])
            P.add("dve", CALL("max", out=best[r_, h, 0:8], in_=candf[r_, h, :]), reads=[f"s1_{h // 2}"], writes=[f"best{h}a"])
            P.add("dve", CALL("max_index", out=pos[r_, h, 0:8], in_max=best[r_, h, 0:8], in_values=candf[r_, h, :]), reads=[f"s1_{h // 2}", f"best{h}a"], writes=[f"pos{h}a"])
            P.add("dve", CALL("match_replace", out=cand2[r_, h, :], in_to_replace=best[r_, h, 0:8], in_values=candf[r_, h, :], imm_value=NEG),
                  reads=[f"s1_{h // 2}", f"best{h}a"], writes=[f"s2_{2 * h}", f"s2_{2 * h + 1}"])
            P.add("dve", CALL("max", out=best[r_, h, 8:16], in_=cand2[r_, h, :]), reads=[f"s2_{2 * h}", f"s2_{2 * h + 1}"], writes=[f"best{h}b"])
            P.add("dve", CALL("max_index", out=pos[r_, h, 8:16], in_max=best[r_, h, 8:16], in_values=cand2[r_, h, :]), reads=[f"s2_{2 * h}", f"s2_{2 * h + 1}", f"best{h}b"], writes=[f"pos{h}b"])
        allb = [f"best{h}{x}" for h in range(8) for x in "ab"]
        allp = [f"pos{h}{x}" for h in range(8) for x in "ab"]
        P.add("dve", CALL("tensor_single_scalar", out=pint[r_, :, :], in_=pos[r_, :, :], scalar=4, op=ALU.logical_shift_right), reads=allp, writes=["pint"])
        P.add("dve", CALL("tensor_copy", out=paf[r_, :, :], in_=pint[r_, :, :]), reads=["pint"], writes=["paf"])
        P.add("dve", CALL("tensor_single_scalar", out=pint[r_, :, :], in_=pos[r_, :, :], scalar=15, op=ALU.bitwise_and), reads=allp + ["paf"], writes=["pint"])
        P.add("dve", CALL("tensor_copy", out=pbf[r_, :, :], in_=pint[r_, :, :]), reads=["pint"], writes=["pbf"])
        ohv = oh[:, :].rearrange("p (h k a) -> p h k a", h=8, k=16)
        for (pf, pfk, tsel, Iout, Ik) in ((paf, "paf", 0, I1, "I1"), (pbf, "pbf", 1, I2, "I2")):
            for h in range(8):
                P.add("dve", CALL("tensor_tensor", out=ohv[r_, h, :, :], in0=pf[r_, h, :].unsqueeze(2).to_broadcast([np_, 16, 16]),
                                                                   in1=G.iota16[r_, :].unsqueeze(1).to_broadcast([np_, 16, 16]), op=ALU.is_equal),
                      reads=[pfk, "cst"], writes=[f"oh{h}"])
                P.add("dve", CALL("tensor_tensor", out=ohv[r_, h, :, :], in0=ohv[r_, h, :, :], in1=topif[r_, h, tsel, :].unsqueeze(1).to_broadcast([np_, 16, 16]), op=ALU.mult),
                      reads=[f"oh{h}", "topif"], writes=[f"oh{h}"])
            P.add("dve", CALL("tensor_reduce", out=Iout[r_, :, :], in_=ohv[r_, :, :, :], axis=AX.X, op=ALU.add), reads=[f"oh{h}" for h in range(8)], writes=[Ik])
        P.add("dve", CALL("scalar_tensor_tensor", out=idxf[r_, :], in0=I1[r_, :, :].rearrange("p h k -> p (h k)"), scalar=128.0, in1=I2[r_, :, :].rearrange("p h k -> p (h k)"), op0=ALU.mult, op1=ALU.add),
              reads=["I1", "I2"], writes=["idxf"])
        if l > 0:
            P.add("dve", CALL("tensor_scalar", out=idxf[r_, :], in0=idxf[r_, :], scalar1=float(l * NEXP), scalar2=0.0, op0=ALU.add, op1=ALU.add), reads=["idxf"], writes=["idxf"])
        P.add("dve", CALL("tensor_copy", out=idx[r_, :], in_=idxf[r_, :]), reads=["idxf"], writes=["idx"])
        P.add("dve", CALL("tensor_tensor", out=gate[r_, :, :], in0=best[r_, :, :], in1=best[r_, :, 0:1].to_broadcast([np_, 8, 16]), op=ALU.subtract), reads=allb, writes=["gate"])
        P.add("act", CALL("activation", out=gate[r_, :, :], in_=gate[r_, :, :], func=AF.Exp), reads=["gate"], writes=["gate"])
        P.add("dve", CALL("tensor_reduce", out=gsum[r_, 0:8], in_=gate[r_, :, :], axis=AX.X, op=ALU.add), reads=["gate"], writes=["gsum"])
        P.add("dve", CALL("reciprocal", out=gsum[r_, 0:8], in_=gsum[r_, 0:8]), reads=["gsum"], writes=["gsum"])
        P.add("dve", CALL("tensor_tensor", out=gate[r_, :, :], in0=gate[r_, :, :], in1=gsum[r_, 0:8].unsqueeze(2).to_broadcast([np_, 8, 16]), op=ALU.mult), reads=["gate", "gsum"], writes=["gate"])
        nsl = getattr(G, "peer_slots", 128)
        for sl in range(nsl):
            i = gst["gi"] % NG
            gst["gi"] += 1
            P.add("pool", CALL("indirect_dma_start", out=gs[i][r_, :], out_offset=None, in_=G.p_u, in_offset=bass.IndirectOffsetOnAxis(ap=idx[r_, sl:sl + 1], axis=0)),
                  reads=["idx"], writes=[f"gs{i}"], dma=1, key=f"gs{i}")
            P.add("dve", CALL("scalar_tensor_tensor", out=gs[i][r_, :], in0=gs[i][r_, :], scalar=1.0, in1=hn[r_, :], op0=ALU.mult, op1=ALU.mult, accum_out=av[r_, sl:sl + 1]),
                  reads=["hn"], writes=[f"gs{i}", f"av{sl}"])
        avk = [f"av{sl}" for sl in range(nsl)]
        if nsl < 128:
            P.add("pool", CALL("memset", av[r_, nsl:128], 0.0), writes=["avz"])
            avk.append("avz")
        P.add("dve", CALL("tensor_tensor", out=tg[r_, :], in0=av[r_, :], in1=av[r_, :], op=ALU.mult), reads=avk, writes=["tg"])
        P.add("dve", CALL("tensor_scalar", out=tg[r_, :], in0=tg[r_, :], scalar1=0.044715, scalar2=1.0, op0=ALU.mult, op1=ALU.add), reads=["tg"], writes=["tg"])
        P.add("dve", CALL("tensor_tensor", out=tg[r_, :], in0=tg[r_, :], in1=av[r_, :], op=ALU.mult), reads=["tg"] + avk, writes=["tg"])
        P.add("act", CALL("activation", out=tg[r_, :], in_=tg[r_, :], func=AF.Sigmoid, scale=1.5957691216057308), reads=["tg"], writes=["tg"])
        P.add("dve", CALL("tensor_tensor", out=tg[r_, :], in0=tg[r_, :], in1=av[r_, :], op=ALU.mult), reads=["tg"] + avk, writes=["tg"])
        P.add("dve", CALL("tensor_tensor", out=wgt[r_, :], in0=tg[r_, :], in1=gate[r_, :, :].rearrange("p h k -> p (h k)"), op=ALU.mult), reads=["tg", "gate"], writes=["wgt"])
        for sl in range(nsl):
            i = gst["gi"] % NG
            gst["gi"] += 1
            a_ = acc[sl % 2]
            ak = f"acc{sl % 2}"
            P.add("pool", CALL("indirect_dma_start", out=gs[i][r_, :], out_offset=None, in_=G.p_v, in_offset=bass.IndirectOffsetOnAxis(ap=idx[r_, sl:sl + 1], axis=0)),
                  reads=["idx"], writes=[f"gs{i}"], dma=1, key=f"gs{i}")
            if sl < 2:
                P.add("dve", CALL("tensor_scalar", out=a_[r_, :], in0=gs[i][r_, :], scalar1=wgt[r_, sl:sl + 1], scalar2=0.0, op0=ALU.mult, op1=ALU.add),
                      reads=[f"gs{i}", "wgt"], writes=[ak])
            else:
                P.add("dve", CALL("scalar_tensor_tensor", out=a_[r_, :], in0=gs[i][r_, :], scalar=wgt[r_, sl:sl + 1], in1=a_[r_, :], op0=ALU.mult, op1=ALU.add),
                      reads=[f"gs{i}", "wgt", ak], writes=[ak])
        P.add("dve", CALL("tensor_tensor", out=x_ap, in0=x_ap, in1=acc[0][r_, :], op=ALU.add), reads=["acc0", xkey], writes=[xkey])
        P.add("dve", CALL("tensor_tensor", out=x_ap, in0=x_ap, in1=acc[1][r_, :], op=ALU.add), reads=["acc1", xkey], writes=[xkey])
    for (x_ap_, np__, xkey_) in tiles:
        do_tile(x_ap_, np__, xkey_)
    P.barrier()
    AR.release(m0)


def final_phase(G):
    P, AR = G.P, G.AR
    m0 = AR.mark()
    ob = [AR.alloc(1024) for _ in range(2)]
    jb = AR.alloc(1024, BF16)
    ss = AR.alloc(8)
    gb_fin = AR.alloc(1024)
    P.add("sp", CALL("dma_start", out=gb_fin[:, :], in_=G.g_fin.partition_broadcast(128)), writes=["gb_fin"], dma=1, key="gb_fin")
    tiles = [(G.xp[:, t, :], 128, f"xp{t}", G.y_p[t * 128:(t + 1) * 128, :]) for t in range(16)] + [(G.xs[0:TS, :], TS, "xs", G.y_s)]
    for n, (x_ap, np_, xkey, o_ap) in enumerate(tiles):
        r_ = slice(0, np_)
        o = ob[n % 2]
        G.rms_rstd(x_ap, np_, xkey, jb, "jb", ss[:, 0:1], "fss")
        P.add("dve", CALL("scalar_tensor_tensor", out=o[r_, :], in0=x_ap, scalar=ss[r_, 0:1], in1=gb_fin[r_, :], op0=ALU.mult, op1=ALU.mult),
              reads=[xkey, "fss", "gb_fin"], writes=[f"ob{n % 2}"])
        P.add("sp", CALL("dma_start", out=o_ap, in_=o[r_, :]), reads=[f"ob{n % 2}"], dma=1, key=f"ob{n % 2}")
    AR.release(m0)

def build(dbg=(), stop=None, **opts):
    build.opts = opts
    nc = bass.Bass("TRN2", target_bir_lowering=False)
    es = ExitStack()
    with es:
        _build(nc, es, dbg, stop)
    return nc


def _build(nc, es, dbg, stop):
    def din(name, shape, dt=F32):
        return nc.dram_tensor(name, list(shape), dt, kind="ExternalInput").ap()

    def dout(name, shape, dt=F32):
        return nc.dram_tensor(name, list(shape), dt, kind="ExternalOutput").ap()

    x_p = din("x_p", [T, D])
    x_s = din("x_s", [TS, D])
    st_pool = din("st_pool", [DEPTH, NSQ * 15, BW])
    st_conv = din("st_conv", [DEPTH, NSQ * 3, 3 * BW])
    st_delta = din("st_delta", [DEPTH, NSQ, 4, 128, 128])
    c_k = din("c_k", [DEPTH, NSQ, 256, BW])
    c_v = din("c_v", [DEPTH, NSQ, 256, BW])
    memp = din("memp", [256, D])
    g_mix = din("g_mix", [DEPTH, D])
    w_in = din("w_in", [DEPTH, D, IN_COLS])
    w_conv = din("w_conv", [DEPTH, 4, 3 * BW])
    a_log = din("a_log", [DEPTH, 4])
    dt_bias = din("dt_bias", [DEPTH, 4])
    g_dn = din("g_dn", [DEPTH, 128])
    w_grp = din("w_grp", [DEPTH, 4, 128, 128])
    p_scale = din("p_scale", [DEPTH, BW])
    g_mem = din("g_mem", [DEPTH, D])
    w_mkv = din("w_mkv", [DEPTH, D, 2 * BW])
    w_br = din("w_br", [DEPTH, 3, BW, D])
    w_o = din("w_o", [DEPTH, D, D])
    g_ffn = din("g_ffn", [DEPTH, D])
    w_pq = din("w_pq", [DEPTH, D, 2048])
    subk = din("subk", [DEPTH, 16, 128, 128])
    p_u = din("p_u", [DEPTH * NEXP, D])
    p_v = din("p_v", [DEPTH * NEXP, D])
    g_fin = din("g_fin", [D])
    consts = din("consts", [128, 1024])

    y_p = dout("y_p", [T, D])
    y_s = dout("y_s", [TS, D])
    o_pool_p = dout("o_pool_p", [DEPTH, 15, BW])
    o_conv_p = dout("o_conv_p", [DEPTH, 3, 3 * BW])
    o_delta_p = dout("o_delta_p", [DEPTH, 4, 128, 128])
    o_mk = dout("o_mk", [DEPTH, 256, BW])
    o_mv = dout("o_mv", [DEPTH, 256, BW])
    o_pool_s = dout("o_pool_s", [DEPTH, NSQ * 15, BW])
    o_conv_s = dout("o_conv_s", [DEPTH, NSQ * 3, 3 * BW])
    o_delta_s = dout("o_delta_s", [DEPTH, NSQ, 4, 128, 128])

    P = Prog(nc)
    dbg_outs = {}

    def sb(name, shape, dt=F32):
        return es.enter_context(nc.sbuf_tensor(name, shape, dt))

    xp = sb("xp", [128, 16, D])
    xs = sb("xs", [128, D])
    cst = sb("cst", [128, 8, 128])
    identB_t = sb("identB", [128, 128], BF16)
    identB = identB_t[:, :]
    gmix_sb = sb("gmix_sb", [128, 8])
    gmem_sb = sb("gmem_sb", [128, 8])
    gb_ffn = sb("gb_ffn", [128, D])
    wconv_sb = sb("wconv_sb", [128, 12, 4])
    alog_b = sb("alog_b", [128, 4])
    dtb_b = sb("dtb_b", [128, 4])
    gdn_sb = sb("gdn_sb", [128, 1])
    psc_sb = sb("psc_sb", [128, 4])
    wgrp_b = sb("wgrp_b", [128, 4, 128], BF16)
    Sst = sb("Sst", [128, 4, 128])
    hist = sb("hist", [128, 12, 3])
    epsb = sb("epsb", [128, 1])
    ARENA_N = 32000
    arena_t = sb("arena", [128, ARENA_N])
    AR = Arena(arena_t, ARENA_N)
    psum = es.enter_context(nc.psum_tensor("psum", [128, 4096], F32))

    identF = cst[:, 0, :]
    onesF = cst[:, 1, :]
    Ltri = cst[:, 2, :]
    SLm = cst[:, 3, :]
    Ltri4 = cst[:, 4, :]
    SL4 = cst[:, 5, :]
    seqmask = cst[:, 6, 0:16]
    rcnt = cst[:, 7, 0:64]
    iota16 = cst[:, 7, 64:80]

    def PS(i, n=512, off=0):
        return psum[:, i * 512 + off:i * 512 + off + n]

    def PSB(i, n=1024, off=0):
        return psum[:, i * 512:(i + 1) * 512].bitcast(BF16)[:, off:off + n]

    def pk(i):
        return f"ps{i}"

    def dump(name, ap, reads, shape, view=None, **kw):
        if name not in dbg:
            return
        o = dout("dbg_" + name, shape, ap.dtype)
        dbg_outs[name] = o
        if view:
            o = o.rearrange(view, **kw)
        P.add("sp", CALL("dma_start", out=o, in_=ap), reads=reads, dma=1, key="dbg_" + name)

    P.add("sp", CALL("dma_start", out=cst[:].rearrange("p a b -> p (a b)"), in_=consts), writes=["cst"], dma=1, key="cst")
    P.add("pool", CALL("memset", epsb[:], EPS), writes=["epsb"])
    P.add("act", CALL("activation", out=identB, in_=identF, func=AF.Copy), reads=["cst"], writes=["identB"])
    for i in range(16):
        P.add("sp", CALL("dma_start", out=xp[:, i, :], in_=x_p[i * 128:(i + 1) * 128, :]), writes=[f"xp{i}"], dma=1, key=f"xp{i}")
    P.add("sp", CALL("dma_start", out=xs[0:TS, :], in_=x_s), writes=["xs"], dma=1, key="xs")

    def rms_rstd(x_ap, np_, xkey, junk, junk_key, ss, sskey):
        P.add("act", CALL("activation", out=junk[0:np_, :], in_=x_ap, func=AF.Square, accum_out=ss[0:np_, :]),
              reads=[xkey], writes=[junk_key, sskey])
        P.add("act", CALL("activation", out=ss[0:np_, :], in_=ss[0:np_, :], func=AF.Sqrt, scale=1.0 / D, bias=epsb[0:np_, :]),
              reads=[sskey, "epsb"], writes=[sskey])
        P.add("dve", CALL("reciprocal", out=ss[0:np_, :], in_=ss[0:np_, :]), reads=[sskey], writes=[sskey])

    def to_featmajor(src_bf, np_, srckey, dst, dstkey_fn, tsl, gsb=None, gkey=None):
        b = P.ps()
        for k in range(8):
            P.add("pe", CALL("transpose", out=PSB(b)[:, k * 128:k * 128 + np_], in_=src_bf[0:np_, k * 128:(k + 1) * 128], identity=identB[0:np_, 0:np_]),
                  reads=[srckey, "identB"], writes=[pk(b)])
        src = PSB(b).rearrange("p (k t) -> p k t", k=8)[:, :, 0:np_]
        if gsb is None:
            P.add("act", CALL("activation", out=dst[:, :, tsl], in_=src, func=AF.Copy), reads=[pk(b)], writes=[dstkey_fn])
        else:
            P.add("dve", CALL("tensor_tensor", out=dst[:, :, tsl], in0=src, in1=gsb[:, :].unsqueeze(2).to_broadcast([128, 8, np_]), op=ALU.mult),
                  reads=[pk(b), gkey], writes=[dstkey_fn])

    wst = {"i": 0, "s": 0}

    def load_w(dram2d, krows, c0, ncols, stg, wbf, caster=None):
        i = wst["i"] % len(wbf)
        wst["i"] += 1
        si = wst["s"] % len(stg)
        wst["s"] += 1
        s_ap, s_key = stg[si]
        b_ap, b_key = wbf[i]
        sv = s_ap[:, 0:krows * ncols].rearrange("p (k n) -> p k n", k=krows)
        bv = b_ap[:, 0:krows * ncols].rearrange("p (k n) -> p k n", k=krows)
        src = dram2d.rearrange("(k p) n -> p k n", p=128)[:, :, c0:c0 + ncols]
        P.add("sp", CALL("dma_start", out=sv, in_=src), writes=[s_key], dma=1, key=s_key)
        eng = caster or ("act" if (wst["i"] % 2 == 0) else "pool")
        if eng == "act":
            P.add("act", CALL("activation", out=bv, in_=sv, func=AF.Copy), reads=[s_key], writes=[b_key])
        else:
            P.add(eng, CALL("tensor_copy", out=bv, in_=sv), reads=[s_key], writes=[b_key])
        return bv, b_key

    def layer_params(l):
        P.add("sp", CALL("dma_start", out=gmix_sb[:], in_=g_mix[l].rearrange("(k p) -> p k", p=128), allow_slow_non_contiguous=True), writes=["gmix"], dma=1, key="gmix")
        P.add("sp", CALL("dma_start", out=gmem_sb[:], in_=g_mem[l].rearrange("(k p) -> p k", p=128), allow_slow_non_contiguous=True), writes=["gmem"], dma=1, key="gmem")
        P.add("sp", lambda e: [e.dma_start(out=wconv_sb[:, :, j], in_=w_conv[l, j].rearrange("(c p) -> p c", p=128), allow_slow_non_contiguous=True) for j in range(4)],
              writes=["wconv"], dma=4, key="wconv")
        P.add("sp", CALL("dma_start", out=gdn_sb[:], in_=g_dn[l].rearrange("(p o) -> p o", o=1), allow_slow_non_contiguous=True), writes=["gdn"], dma=1, key="gdn")
        P.add("sp", CALL("dma_start", out=psc_sb[:], in_=p_scale[l].rearrange("(g p) -> p g", p=128), allow_slow_non_contiguous=True), writes=["psc"], dma=1, key="psc")
        P.add("sp", CALL("dma_start", out=gb_ffn[:], in_=g_ffn[l].partition_broadcast(128)), writes=["gb_ffn"], dma=1, key="gb_ffn")
        P.add("sp", CALL("dma_start", out=alog_b[:], in_=a_log[l].partition_broadcast(128)), writes=["alog"], dma=1, key="alog")
        P.add("sp", CALL("dma_start", out=dtb_b[:], in_=dt_bias[l].partition_broadcast(128)), writes=["dtb"], dma=1, key="dtb")
        P.add("act", CALL("activation", out=alog_b[:], in_=alog_b[:], func=AF.Exp), reads=["alog"], writes=["alog"])
        P.add("dve", CALL("tensor_scalar", out=alog_b[:], in0=alog_b[:], scalar1=-1.0, scalar2=0.0, op0=ALU.mult, op1=ALU.add), reads=["alog"], writes=["alog"])

    G = type("G", (), {})()
    for k_, v_ in list(locals().items()):
        setattr(G, k_, v_)
    for k_, v_ in build.opts.items():
        setattr(G, k_, v_)

    for l in range(DEPTH):
        layer_params(l)
        mixer_phase(G, l)
        dump(f"xp_m{l}", xp[:, :, :], [f"xp{i}" for i in range(16)], [T, D], "(t p) d -> p t d", p=128)
        dump(f"xs_m{l}", xs[0:TS, :], ["xs"], [TS, D])
        if stop == ("mixer", l):
            break
        peer_phase(G, l)
        dump(f"xp_p{l}", xp[:, :, :], [f"xp{i}" for i in range(16)], [T, D], "(t p) d -> p t d", p=128)
        dump(f"xs_p{l}", xs[0:TS, :], ["xs"], [TS, D])
        if stop == ("peer", l):
            break
    else:
        final_phase(G)
    P.emit(es, maxops=build.opts.get('maxops'))
    G.P = P
    build.last = G


def _shard_inputs(inp):
    f = lambda a: np.ascontiguousarray(np.asarray(a, dtype=np.float32))
    shared = dict(
        g_mix=f(inp["g_mix"]), w_in=f(inp["w_in"]), w_conv=f(inp["w_conv"]), a_log=f(inp["a_log"]), dt_bias=f(inp["dt_bias"]),
        g_dn=f(inp["g_dn_out"]), w_grp=f(inp["w_pool_grp"]), p_scale=f(inp["pool_scale"]), g_mem=f(inp["g_mem"]),
        w_mkv=f(inp["w_mem_kv"]), w_br=f(inp["w_branch"]), w_o=f(inp["w_o"]), g_ffn=f(inp["g_ffn"]), w_pq=f(inp["w_peer_q"]),
        subk=f(inp["peer_subkeys"]).reshape(DEPTH, 16, 128, 128), p_u=f(inp["peer_u"]).reshape(DEPTH * NEXP, D),
        p_v=f(inp["peer_v"]).reshape(DEPTH * NEXP, D), g_fin=f(inp["g_final"]), consts=make_consts())
    maps = []
    for c in range(NCORES):
        sl = slice(c * NSQ, (c + 1) * NSQ)
        m = dict(shared)
        m["x_p"] = f(inp["x_prompt"][c])
        m["x_s"] = f(inp["x_sample"][sl]).reshape(TS, D)
        m["st_pool"] = f(inp["state_pool"][:, sl]).reshape(DEPTH, NSQ * 15, BW)
        m["st_conv"] = f(inp["state_conv"][:, sl]).reshape(DEPTH, NSQ * 3, 3 * BW)
        m["st_delta"] = f(inp["state_delta"][:, sl])
        m["c_k"] = f(inp["cache_mem_k"][:, sl]).reshape(DEPTH, NSQ, 256, BW)
        m["c_v"] = f(inp["cache_mem_v"][:, sl]).reshape(DEPTH, NSQ, 256, BW)
        m["memp"] = f(inp["mem_prompt"][c])
        maps.append(m)
    return maps


def _gather_outputs(res):
    R = res.results
    cat = lambda k, ax: np.concatenate([np.asarray(r[k]) for r in R], axis=ax)
    stk = lambda k, ax: np.stack([np.asarray(r[k]) for r in R], axis=ax)
    y_p = stk("y_p", 0)
    y_s = cat("y_s", 0).reshape(NCORES * NSQ, 4, D)
    pool_p = stk("o_pool_p", 1)
    conv_p = stk("o_conv_p", 1)
    delta_p = stk("o_delta_p", 1)
    mk = stk("o_mk", 1).reshape(DEPTH, NCORES, 256, 4, 128)
    mv = stk("o_mv", 1).reshape(DEPTH, NCORES, 256, 4, 128)
    pool_s = cat("o_pool_s", 1).reshape(DEPTH, NCORES * NSQ, 15, BW)
    conv_s = cat("o_conv_s", 1).reshape(DEPTH, NCORES * NSQ, 3, 3 * BW)
    delta_s = cat("o_delta_s", 1)
    outs = (y_p, y_s, pool_p, conv_p, delta_p, mk, mv, pool_s, conv_s, delta_s)
    return tuple(np.ascontiguousarray(o, dtype=np.float32) for o in outs)


def kernel(**inputs):
    maps = _shard_inputs(inputs)
    nc = build()
    res = run_bass_kernel_spmd(nc, maps, core_ids=list(range(NCORES)))
    return _gather_outputs(res)
```

```python
import numpy as np
from contextlib import ExitStack
import concourse.bass as bass
import concourse.mybir as mybir
from concourse.bass_utils import run_bass_kernel_spmd

F32 = mybir.dt.float32
BF16 = mybir.dt.bfloat16
I32 = mybir.dt.int32
U32 = mybir.dt.uint32
ALU = mybir.AluOpType
AF = mybir.ActivationFunctionType
AX = mybir.AxisListType

NCORES = 8
D = 1024
T = 2048
NSQ = 16
TS = 64
DEPTH = 2
BW = 512
IN_COLS = 6152
OFF_Q = 512
OFF_Z = 2048
OFF_BA = 2560
OFF_XQ = 2568
OFF_GATE = 3080
EPS = 1e-6
NKEY = 128
NEXP = 16384
SEM_CH = 30000
NEG = -1.0e30
NG = 6


def CALL(name, *a, **k):
    return lambda e: getattr(e, name)(*a, **k)


class Op:
    __slots__ = ("eng", "fn", "deps", "is_dma", "key", "nparts", "seq", "signaled", "dma_val")


class Prog:
    ENGS = ("pe", "act", "dve", "pool", "sp")

    def __init__(self, nc):
        self.nc = nc
        self.ops = []
        self.last_w = {}
        self.readers = {}
        self.dma_cnt = {}
        self.dma_gen = {}
        self.psi = 0
        self.inames = {}

    def begin_record(self, banks):
        self.rec = []
        self.ps_banks = list(banks)
        self.ps_bi = 0

    def end_record(self):
        r = self.rec
        self.rec = None
        self.ps_banks = None
        return r

    def replay(self, recs):
        n = max(len(r) for r in recs)
        for i in range(n):
            for r in recs:
                if i < len(r):
                    self.add(*r[i][0], **r[i][1])

    def add(self, eng, fn, reads=(), writes=(), dma=0, key=None):
        if getattr(self, "rec", None) is not None:
            self.rec.append(((eng, fn), dict(reads=list(reads), writes=list(writes), dma=dma, key=key)))
            return None
        op = Op()
        op.eng = eng
        op.fn = fn
        op.is_dma = dma > 0
        op.nparts = dma
        op.signaled = False
        op.seq = 0
        deps = set()
        excl = [r for r in reads if isinstance(r, str) and r[:2] == "ps" and r[2:].isdigit()]
        if excl:
            reads = [r for r in reads if r not in excl]
            writes = list(writes) + excl
        for r in reads:
            w = self.last_w.get(r)
            if w is not None:
                deps.add(w)
        for w_ in writes:
            w = self.last_w.get(w_)
            if w is not None:
                deps.add(w)
            for rd in self.readers.get(w_, ()):
                deps.add(rd)
        op.deps = deps
        for r in reads:
            self.readers.setdefault(r, []).append(op)
        for w_ in writes:
            self.last_w[w_] = op
            self.readers[w_] = []
        if op.is_dma:
            g = self.dma_gen.get(key, 0)
            c = self.dma_cnt.get((key, g), 0) + dma * 16
            if c > SEM_CH:
                g += 1
                self.dma_gen[key] = g
                c = dma * 16
            self.dma_cnt[(key, g)] = c
            op.key = (key, g)
            op.dma_val = c
        self.ops.append(op)
        return op

    def barrier(self):
        last = {}
        dmas = {}
        for op in self.ops:
            if op.is_dma:
                dmas[op.key] = op
            else:
                last[op.eng] = op
        deps = set(last.values()) | set(dmas.values())
        bops = []
        for e in ("pe", "act", "dve", "pool", "sp"):
            op = self.add(e, lambda en: en.nop(), ())
            op.deps = set(deps)
            bops.append(op)
        self.last_w = {}
        self.readers = {}

    def ps(self):
        if getattr(self, "ps_banks", None):
            i = self.ps_banks[self.ps_bi % len(self.ps_banks)]
            self.ps_bi += 1
            return i
        lo = getattr(self, "ps_lo", 0)
        if self.psi < lo:
            self.psi = lo
        i = self.psi
        self.psi = self.psi + 1
        if self.psi >= 8:
            self.psi = lo
        return i

    def emit(self, es, maxops=None):
        nc = self.nc
        if maxops is not None:
            self.ops = self.ops[:maxops]
            cnt2 = {}
            for op in self.ops:
                if op.is_dma:
                    cnt2[op.key] = op.dma_val
            self.dma_cnt = cnt2
        for op in self.ops:
            nd = set()
            for d in op.deps:
                if (not d.is_dma) and (not op.is_dma) and d.eng == "pe" and op.eng == "pe":
                    continue
                nd.add(d)
                d.signaled = True
            op.deps = nd
        cnt = {e: 0 for e in self.ENGS}
        for op in self.ops:
            if not op.is_dma and op.signaled:
                cnt[op.eng] += 1
                op.seq = cnt[op.eng]
        eng_sems = {}
        for e in self.ENGS:
            n = (cnt[e] + SEM_CH - 1) // SEM_CH
            eng_sems[e] = [es.enter_context(nc.semaphore(f"s_{e}_{i}")) for i in range(n)]
        dma_sems = {}
        for i, k in enumerate(self.dma_cnt.keys()):
            dma_sems[k] = es.enter_context(nc.semaphore(f"d_{i}"))
        self.nsem = sum(len(v) for v in eng_sems.values()) + len(dma_sems)
        per_eng = {e: [o for o in self.ops if o.eng == e] for e in self.ENGS}
        block = es.enter_context(nc.Block())

        def run(engname, eobj):
            waited = {}

            def wait(sem, val):
                if waited.get(id(sem), 0) >= val:
                    return
                waited[id(sem)] = val
                eobj.wait_ge(sem, val)

            for op in per_eng[engname]:
                need = {}
                for d in op.deps:
                    if d.is_dma:
                        sem = dma_sems[d.key]
                        v = d.dma_val
                    else:
                        si = (d.seq - 1) // SEM_CH
                        sem = eng_sems[d.eng][si]
                        v = d.seq - si * SEM_CH
                    k = id(sem)
                    if k not in need or need[k][1] < v:
                        need[k] = (sem, v)
                for sem, v in need.values():
                    wait(sem, v)
                if op.is_dma:
                    sem = dma_sems[op.key]
                    insts = op.fn(eobj)
                    if not isinstance(insts, (list, tuple)):
                        insts = [insts]
                    assert len(insts) == op.nparts
                    for ins in insts:
                        ins.then_inc(sem, 16)
                else:
                    ins = op.fn(eobj)
                    try:
                        self.inames[ins.ins.name] = op
                    except Exception:
                        pass
                    if op.signaled:
                        si = (op.seq - 1) // SEM_CH
                        ins.then_inc(eng_sems[op.eng][si], 1)
            if engname == "sp":
                for k, c in self.dma_cnt.items():
                    wait(dma_sems[k], c)

        block.tensor(lambda e: run("pe", e))
        block.scalar(lambda e: run("act", e))
        block.vector(lambda e: run("dve", e))
        block.gpsimd(lambda e: run("pool", e))
        block.sync(lambda e: run("sp", e))


class Arena:
    def __init__(self, t, nf32):
        self.t = t
        self.n = nf32
        self.off = 0
        self.hw = 0

    def mark(self):
        return self.off

    def release(self, m):
        self.off = m

    def alloc(self, n, dt=F32):
        nf = n if dt in (F32, I32, U32) else (n + 1) // 2
        nf = (nf + 7) // 8 * 8
        a = self.off
        self.off += nf
        self.hw = max(self.hw, self.off)
        assert self.off <= self.n, f"arena overflow {self.off} > {self.n}"
        v = self.t[:, a:a + nf]
        if dt != F32:
            v = v.bitcast(dt)
        return v[:, 0:n]


def make_consts():
    c = np.zeros((128, 8, 128), np.float32)
    i = np.arange(128)
    c[:, 0, :] = np.eye(128)
    c[:, 1, :] = 1.0
    same64 = (i[:, None] // 64) == (i[None, :] // 64)
    same4 = ((i[:, None] // 4) == (i[None, :] // 4)) & (i[:, None] < 64) & (i[None, :] < 64)
    c[:, 2, :] = same64 & (i[:, None] <= i[None, :])
    c[:, 3, :] = same64 & (i[:, None] > i[None, :])
    c[:, 4, :] = same4 & (i[:, None] <= i[None, :])
    c[:, 5, :] = same4 & (i[:, None] > i[None, :])
    c[:, 6, 0:16] = (i[:, None] // 4) == np.arange(16)[None, :]
    for g, w in enumerate((2, 4, 8, 16)):
        c[:, 7, g * 16:(g + 1) * 16] = 1.0 / np.minimum(np.arange(16) + 1, w)
    c[:, 7, 64:80] = np.arange(16)[None, :]
    return c.reshape(128, 1024)


def _mixer_bufs(G, NT, sample):
    AR = G.AR
    B = type("B", (), {})()
    B.NT = NT
    B.hT = AR.alloc(8 * NT, BF16).rearrange("p (k t) -> p k t", k=8)
    B.xbf = [AR.alloc(1024, BF16)] * 2
    B.junk = B.xbf[0]
    B.ss = AR.alloc(8)
    B.stg = [(AR.alloc(2048), f"stg{i}") for i in range(2)]
    B.wbf = [(AR.alloc(2048, BF16), f"wbf{i}") for i in range(3)]
    B.pre = [AR.alloc(3 + NT) for _ in range(2)]
    B.cv = [AR.alloc(NT) for _ in range(2)]
    B.qkvc = AR.alloc(12 * NT).rearrange("p (c t) -> p c t", c=12)
    B.zs = AR.alloc(4 * NT).rearrange("p (c t) -> p c t", c=4)
    B.xqT = AR.alloc(4 * NT, BF16).rearrange("p (c t) -> p c t", c=4)
    B.yT = AR.alloc(12 * NT, BF16).rearrange("p (c t) -> p c t", c=12)
    B.macc = AR.alloc(4 * NT).rearrange("p (c t) -> p c t", c=4)
    B.mT = AR.alloc(8 * NT, BF16).rearrange("p (c t) -> p c t", c=8)
    B.sqb = [AR.alloc(NT) for _ in range(2)]
    B.rinv = [AR.alloc(NT) for _ in range(2)]
    B.sig = B.sqb
    B.prod = B.rinv
    B.dT = [AR.alloc(NT, BF16) for _ in range(2)]
    B.pA = AR.alloc(19 * 16 if sample else 15 + NT)
    B.pB = AR.alloc(19 * 16 if sample else 15 + NT)
    B.t16 = AR.alloc(16)
    B.ktok = AR.alloc(512).rearrange("p (h d) -> p h d", h=4)
    B.vtok = AR.alloc(512).rearrange("p (h d) -> p h d", h=4)
    B.ba = AR.alloc(32)
    B.gcc = AR.alloc(16)
    names = ["gL", "gcr", "egr", "dm", "t1", "dmT", "t2", "Pa", "Pb", "Qa", "Qb", "R", "u", "wT", "attnT", "vn", "qg", "kbg", "vb", "kd", "osq", "rr", "y1", "kdsc"]
    B.dw = [{}, {}]
    shared = ("gL", "dm", "dmT", "osq", "rr", "y1", "kdsc") if sample else ()
    for n in names:
        if n in shared:
            B.dw[0][n] = B.dw[1][n] = AR.alloc(128)
        else:
            B.dw[0][n] = AR.alloc(128)
            B.dw[1][n] = AR.alloc(128)
    B.dw_shared = shared
    B.pexp = [AR.alloc(256) for _ in range(2)]
    B.pn = [AR.alloc(256, BF16) for _ in range(2)]
    B.pT = [AR.alloc(256, BF16).rearrange("p (c t) -> p c t", c=2) for _ in range(2)]
    B.asm = AR.alloc(32)
    if not sample:
        B.uT = AR.alloc(4 * (15 + NT)).rearrange("p (c t) -> p c t", c=4)
        B.kTm = AR.alloc(4 * 256, BF16).rearrange("p (h m) -> p h m", h=4)
        B.vm = AR.alloc(2 * 512, BF16).rearrange("p (c n) -> p c n", c=2)
        B.hTm = B.hT
        qflat = B.qkvc.rearrange("p c t -> p (c t)")
        B.memx = qflat[:, 0:1024]
        B.kvrow = qflat[:, 1024:3072].rearrange("p (j n) -> p j n", j=2)
        B.rowbuf = qflat[:, 0:1536]
    else:
        B.uTs = AR.alloc(4 * 16 * 19).rearrange("p (c s e) -> p c s e", c=4, s=16)
        B.pre_s = [AR.alloc(16 * 7).rearrange("p (s e) -> p s e", s=16) for _ in range(2)]
        B.hist_s = AR.alloc(12 * 48).rearrange("p (c s e) -> p c s e", c=12, s=16)
        B.cvout = AR.alloc(12 * 48).rearrange("p (c s e) -> p c s e", c=12, s=16)
        B.ld1536 = AR.alloc(1536)
        B.ld512 = [AR.alloc(512) for _ in range(2)]
        mk_ = AR.mark()
        B.Sh = [AR.alloc(16 * 128).rearrange("p (s d) -> p s d", s=16) for _ in range(2)]
        B.rowbuf = B.Sh[0].rearrange("p s d -> p (s d)")[:, 0:1536]
        B.kdm = AR.alloc(16 * 128).rearrange("p (s d) -> p s d", s=16)
        B.wTm = AR.alloc(1088)
        B.o1 = AR.alloc(64)
        B.oTs = AR.alloc(64)
        hw_ = AR.mark()
        AR.release(mk_)
        B.xqm = AR.alloc(4 * 1088, BF16).rearrange("p (h r) -> p h r", h=4)
        B.kvs = [AR.alloc(1024) for _ in range(2)]
        B.kvb = [AR.alloc(1024, BF16).rearrange("p (c n) -> p c n", c=2) for _ in range(2)]
        B.kTs = [AR.alloc(1024, BF16).rearrange("p (h m) -> p h m", h=4) for _ in range(2)]
        B.pTall = [AR.alloc(256, BF16).rearrange("p (c t) -> p c t", c=2) for _ in range(4)]
        AR.release(max(hw_, AR.mark()))
    return B


def _mem_kv(G, B, l):
    P, PS, PSB, pk = G.P, G.PS, G.PSB, G.pk
    for j in range(2):
        P.add("sp", CALL("dma_start", out=B.memx[:, :], in_=G.memp[j * 128:(j + 1) * 128, :]), writes=["memx"], dma=1, key="memx")
        G.rms_rstd(B.memx[:, :], 128, "memx", B.junk, "xbf0", B.ss[:, 0:1], "ss0")
        xb = B.xbf[j % 2]
        P.add("act", CALL("activation", out=xb[:, :], in_=B.memx[:, :], func=AF.Copy, scale=B.ss[:, 0:1]),
              reads=["memx", "ss0"], writes=["xbf0"])
        G.to_featmajor(xb, 128, "xbf0", B.hTm, f"hTm{j}", slice(j * 128, (j + 1) * 128), gsb=G.gmem_sb, gkey="gmem")
    for g in range(4):
        wv, wk = G.load_w(G.w_mkv[l], 8, g * 256, 256, B.stg, B.wbf)
        for j in range(2):
            b = P.ps()
            for k in range(8):
                P.add("pe", CALL("matmul", PS(b, 256), lhsT=B.hTm[:, k, j * 128:(j + 1) * 128], rhs=wv[:, k, :], start=(k == 0), stop=(k == 7)),
                      reads=[f"hTm{j}", wk], writes=[pk(b)])
            P.add("act", CALL("activation", out=B.kvrow[:, j, g * 256:(g + 1) * 256], in_=PS(b, 256), func=AF.Copy),
                  reads=[pk(b)], writes=[f"kvrow{j}_{g}"])
            if g >= 2:
                P.add("dve", CALL("tensor_copy", out=B.vm[:, j, (g - 2) * 256:(g - 1) * 256], in_=PS(b, 256)),
                      reads=[pk(b)], writes=[f"vm{j}_{g}"])
        if g < 2:
            for cc in range(2):
                b = P.ps()
                for k in range(8):
                    P.add("pe", CALL("matmul", PS(b, 256), lhsT=wv[:, k, cc * 128:(cc + 1) * 128], rhs=B.hTm[:, k, :], start=(k == 0), stop=(k == 7)),
                          reads=["hTm0", "hTm1", wk], writes=[pk(b)])
                P.add("act", CALL("activation", out=B.kTm[:, g * 2 + cc, :], in_=PS(b, 256), func=AF.Copy),
                      reads=[pk(b)], writes=[f"kTm{g * 2 + cc}"])
    for j in range(2):
        P.add("pool", CALL("dma_start", out=G.o_mk[l, j * 128:(j + 1) * 128, :], in_=B.kvrow[:, j, 0:512]),
              reads=[f"kvrow{j}_0", f"kvrow{j}_1"], dma=1, key=f"o_mk{j}")
        P.add("pool", CALL("dma_start", out=G.o_mv[l, j * 128:(j + 1) * 128, :], in_=B.kvrow[:, j, 512:1024]),
              reads=[f"kvrow{j}_2", f"kvrow{j}_3"], dma=1, key=f"o_mv{j}")
    B.kTm_keys = [f"kTm{h}" for h in range(4)]
    B.vm_keys = [f"vm{j}_{g}" for j in range(2) for g in (2, 3)]


def _proj(G, B, l, wdram, krows, c0, nch, srcT, srckeys, NT, consume, chunk_w=128):
    P, PS, pk = G.P, G.PS, G.pk
    i = 0
    while i < nch:
        ng = min(2, nch - i)
        ncols = 128 * ng if chunk_w == 128 else chunk_w
        wv, wk = G.load_w(wdram, krows, c0 + i * 128, ncols, B.stg, B.wbf)
        for cc in range(ng):
            b = P.ps()
            for k in range(krows):
                P.add("pe", CALL("matmul", PS(b, NT)[0:chunk_w, :], lhsT=wv[:, k, cc * 128:cc * 128 + chunk_w], rhs=srcT[:, k, 0:NT],
                                                               start=(k == 0), stop=(k == krows - 1)),
                      reads=list(srckeys) + [wk], writes=[pk(b)])
            consume(i + cc, b)
        i += ng


def _delta_tile(G, B, l, j, np_, tsl, sample, S_ap=None):
    P, PS, PSB, pk = G.P, G.PS, G.PSB, G.pk
    LT = (G.Ltri4 if sample else G.Ltri)
    SLx = (G.SL4 if sample else G.SLm)
    I_ = G.identF
    ones = G.onesF
    r_ = slice(0, np_)
    for nm, c0, dst in (("ktok", 4, B.ktok), ("vtok", 8, B.vtok)):
        b = P.ps()
        for h in range(4):
            P.add("pe", CALL("transpose", out=PS(b, 128, h * 128)[r_, :], in_=B.qkvc[:, c0 + h, tsl], identity=I_),
                  reads=[f"qkvc{c0 + h}", "cst"], writes=[pk(b)])
        P.add("act", CALL("activation", out=dst[r_, :, :], in_=PS(b, 512)[r_, :].rearrange("p (h d) -> p h d", h=4), func=AF.Copy),
              reads=[pk(b)], writes=[nm])
    b = P.ps()
    P.add("pe", CALL("matmul", PS(b, 4)[r_, :], lhsT=LT[r_, r_], rhs=B.ba[r_, 8:12], start=True, stop=True), reads=["ba", "cst"], writes=[pk(b)])
    P.add("pe", CALL("matmul", PS(b, 4, 8)[r_, :], lhsT=LT[r_, r_], rhs=B.ba[r_, 8:12], start=True, stop=False), reads=["ba", "cst"], writes=[pk(b)])
    P.add("pe", CALL("matmul", PS(b, 4, 8)[r_, :], lhsT=SLx[r_, r_], rhs=B.ba[r_, 8:12], start=False, stop=True), reads=["ba", "cst"], writes=[pk(b)])
    P.add("dve", CALL("tensor_copy", out=B.gcc[r_, 0:4], in_=PS(b, 4)[r_, :]), reads=[pk(b)], writes=["gcc"])
    P.add("dve", CALL("tensor_copy", out=B.gcc[r_, 8:12], in_=PS(b, 4, 8)[r_, :]), reads=[pk(b)], writes=["gcc"])
    P.add("act", CALL("activation", out=B.gcc[r_, 4:8], in_=B.gcc[r_, 0:4], func=AF.Exp), reads=["gcc"], writes=["gcc"])
    def head_body(h):
        W = B.dw[h % 2]
        wn = lambda n, h=h: (f"dws_{n}" if n in B.dw_shared else f"dw{h % 2}_{n}")
        qT = B.qkvc[:, h, tsl]
        kT = B.qkvc[:, 4 + h, tsl]
        beta = B.ba[r_, h:h + 1]
        nbeta = B.ba[r_, 4 + h:5 + h]
        gcol = B.ba[r_, 8 + h:9 + h]
        gc_c = B.gcc[r_, h:h + 1]
        egc_c = B.gcc[r_, 4 + h:5 + h]
        gl_c = B.gcc[r_, 8 + h:9 + h]
        P.add("dve", CALL("tensor_scalar", out=W["gL"][r_, r_], in0=LT[r_, r_], scalar1=gcol, scalar2=0.0, op0=ALU.mult, op1=ALU.add),
              reads=["ba", "cst"], writes=[wn("gL")])
        b = P.ps()
        P.add("pe", CALL("matmul", PS(b, np_), lhsT=ones[r_, :], rhs=W["gL"][r_, r_], start=True, stop=True), reads=[wn("gL"), "cst"], writes=[pk(b)])
        P.add("act", CALL("activation", out=W["gcr"][:, r_], in_=PS(b, np_), func=AF.Copy), reads=[pk(b)], writes=[wn("gcr")])
        P.add("act", CALL("activation", out=W["egr"][:, r_], in_=PS(b, np_), func=AF.Exp), reads=[pk(b)], writes=[wn("egr")])
        P.add("dve", CALL("tensor_scalar", out=W["dm"][r_, r_], in0=W["gcr"][r_, r_], scalar1=gc_c, scalar2=0.0, op0=ALU.subtract, op1=ALU.max),
              reads=[wn("gcr"), "gcc"], writes=[wn("dm")])
        P.add("act", CALL("activation", out=W["dm"][r_, r_], in_=W["dm"][r_, r_], func=AF.Exp, scale=-1.0), reads=[wn("dm")], writes=[wn("dm")])
        P.add("pool", CALL("tensor_tensor", out=W["t1"][r_, r_], in0=W["dm"][r_, r_], in1=SLx[r_, r_], op=ALU.mult), reads=[wn("dm"), "cst"], writes=[wn("t1")])
        P.add("dve", CALL("tensor_scalar", out=W["dmT"][r_, r_], in0=W["gcr"][r_, r_], scalar1=gc_c, scalar2=0.0, op0=ALU.subtract, op1=ALU.min),
              reads=[wn("gcr"), "gcc"], writes=[wn("dmT")])
        P.add("act", CALL("activation", out=W["dmT"][r_, r_], in_=W["dmT"][r_, r_], func=AF.Exp), reads=[wn("dmT")], writes=[wn("dmT")])
        P.add("pool", CALL("tensor_tensor", out=W["t2"][r_, r_], in0=W["dmT"][r_, r_], in1=LT[r_, r_], op=ALU.mult), reads=[wn("dmT"), "cst"], writes=[wn("t2")])
        b = P.ps()
        P.add("pe", CALL("matmul", PS(b, np_)[r_, :], lhsT=kT, rhs=kT, start=True, stop=True), reads=[f"qkvc{4 + h}"], writes=[pk(b)])
        P.add("dve", CALL("scalar_tensor_tensor", out=W["Pa"][r_, r_], in0=PS(b, np_)[r_, :], scalar=nbeta, in1=W["t1"][r_, r_], op0=ALU.mult, op1=ALU.mult),
              reads=[pk(b), "ba", wn("t1")], writes=[wn("Pa")])
        b = P.ps()
        P.add("pe", CALL("transpose", out=PS(b, np_)[r_, :], in_=W["Pa"][r_, r_], identity=I_[r_, r_]), reads=[wn("Pa"), "cst"], writes=[pk(b)])
        P.add("act", CALL("activation", out=W["Qa"][r_, r_], in_=PS(b, np_)[r_, :], func=AF.Copy), reads=[pk(b)], writes=[wn("Qa")])
        P.add("dve", CALL("tensor_tensor", out=W["R"][r_, r_], in0=PS(b, np_)[r_, :], in1=I_[r_, r_], op=ALU.add), reads=[pk(b), "cst"], writes=[wn("R")])
        nst = 1 if sample else 5
        Pk, Qk, Pn, Qn = "Pa", "Qa", "Pb", "Qb"
        for k in range(nst):
            bP = P.ps()
            P.add("pe", CALL("matmul", PS(bP, np_)[r_, :], lhsT=W[Qk][r_, r_], rhs=W[Pk][r_, r_], start=True, stop=True),
                  reads=[wn(Pk), wn(Qk)], writes=[pk(bP)])
            P.add("act", CALL("activation", out=W[Pn][r_, r_], in_=PS(bP, np_)[r_, :], func=AF.Copy), reads=[pk(bP)], writes=[wn(Pn)])
            if k < nst - 1:
                bQ = P.ps()
                P.add("pe", CALL("matmul", PS(bQ, np_)[r_, :], lhsT=W[Pk][r_, r_], rhs=W[Qk][r_, r_], start=True, stop=True),
                      reads=[wn(Pk), wn(Qk)], writes=[pk(bQ)])
                P.add("dve", CALL("tensor_copy", out=W[Qn][r_, r_], in_=PS(bQ, np_)[r_, :]), reads=[pk(bQ)], writes=[wn(Qn)])
            bR = P.ps()
            P.add("pe", CALL("matmul", PS(bR, np_)[r_, :], lhsT=W[Pn][r_, r_], rhs=W["R"][r_, r_], start=True, stop=True),
                  reads=[wn(Pn), wn("R")], writes=[pk(bR)])
            P.add("dve", CALL("tensor_tensor", out=W["R"][r_, r_], in0=PS(bR, np_)[r_, :], in1=W["R"][r_, r_], op=ALU.add), reads=[pk(bR), wn("R")], writes=[wn("R")])
            Pk, Pn = Pn, Pk
            Qk, Qn = Qn, Qk
        P.add("dve", CALL("tensor_scalar", out=W["vb"][r_, :], in0=B.vtok[r_, h, :], scalar1=beta, scalar2=0.0, op0=ALU.mult, op1=ALU.add),
              reads=["vtok", "ba"], writes=[wn("vb")])
        P.add("dve", CALL("tensor_scalar", out=W["kbg"][r_, :], in0=B.ktok[r_, h, :], scalar1=beta, scalar2=egc_c, op0=ALU.mult, op1=ALU.mult),
              reads=["ktok", "ba", "gcc"], writes=[wn("kbg")])
        P.add("act", CALL("activation", out=W["kdsc"][r_, 0:1], in_=gc_c, func=AF.Exp, scale=-1.0, bias=gl_c), reads=["gcc"], writes=[wn("kdsc")])
        P.add("dve", CALL("tensor_scalar", out=W["kd"][r_, :], in0=B.ktok[r_, h, :], scalar1=W["kdsc"][r_, 0:1], scalar2=0.0, op0=ALU.mult, op1=ALU.add),
              reads=["ktok", wn("kdsc")], writes=[wn("kd")])
        P.add("pool", CALL("tensor_tensor", out=W["qg"][:, r_], in0=qT, in1=W["egr"][:, r_], op=ALU.mult), reads=[f"qkvc{h}", wn("egr")], writes=[wn("qg")])
        b = P.ps()
        P.add("pe", CALL("matmul", PS(b, 128)[r_, :], lhsT=W["R"][r_, r_], rhs=W["vb"][r_, :], start=True, stop=True), reads=[wn("R"), wn("vb")], writes=[pk(b)])
        P.add("act", CALL("activation", out=W["u"][r_, :], in_=PS(b, 128)[r_, :], func=AF.Copy), reads=[pk(b)], writes=[wn("u")])
        b = P.ps()
        P.add("pe", CALL("matmul", PS(b, np_), lhsT=W["kbg"][r_, :], rhs=W["R"][r_, r_], start=True, stop=True), reads=[wn("R"), wn("kbg")], writes=[pk(b)])
        P.add("act", CALL("activation", out=W["wT"][:, r_], in_=PS(b, np_), func=AF.Copy), reads=[pk(b)], writes=[wn("wT")])
        b = P.ps()
        P.add("pe", CALL("matmul", PS(b, np_)[r_, :], lhsT=kT, rhs=qT, start=True, stop=True), reads=[f"qkvc{4 + h}", f"qkvc{h}"], writes=[pk(b)])
        P.add("dve", CALL("tensor_tensor", out=W["attnT"][r_, r_], in0=PS(b, np_)[r_, :], in1=W["t2"][r_, r_], op=ALU.mult), reads=[pk(b), wn("t2")], writes=[wn("attnT")])
        if not sample:
            Sk = f"S{h}"
            Sh_ = G.Sst[:, h, :]
            bo = P.ps()
            for ci in range(2):
                rr_ = slice(ci * 64, ci * 64 + 64)
                bw = P.ps()
                P.add("pe", CALL("matmul", PS(bw, 128), lhsT=W["wT"][:, 0:128], rhs=Sh_, start=True, stop=True), reads=[wn("wT"), Sk], writes=[pk(bw)])
                P.add("dve", CALL("tensor_tensor", out=W["vn"][rr_, :], in0=W["u"][rr_, :], in1=PS(bw, 128)[rr_, :], op=ALU.subtract),
                      reads=[pk(bw), wn("u")], writes=[wn("vn") + str(ci)])
                P.add("pe", CALL("matmul", PS(bo, 64, 256 + ci * 64), lhsT=Sh_, rhs=W["qg"][:, rr_], start=True, stop=False), reads=[wn("qg"), Sk], writes=[pk(bo)])
                P.add("pe", CALL("matmul", PS(bo, 64, 256 + ci * 64), lhsT=W["vn"][rr_, :], rhs=W["attnT"][rr_, rr_], start=False, stop=True),
                      reads=[wn("vn") + str(ci), wn("attnT")], writes=[pk(bo)])
                bs = P.ps()
                P.add("pe", CALL("matmul", PS(bs, 128), lhsT=W["kd"][rr_, :], rhs=W["vn"][rr_, :], start=True, stop=True), reads=[wn("kd"), wn("vn") + str(ci)], writes=[pk(bs)])
                P.add("dve", CALL("scalar_tensor_tensor", out=Sh_, in0=Sh_, scalar=W["egr"][:, ci * 64 + 63:ci * 64 + 64], in1=PS(bs, 128), op0=ALU.mult, op1=ALU.add),
                      reads=[pk(bs), wn("egr"), Sk], writes=[Sk])
            o_ap = PS(bo, np_, 256)
            o_key = pk(bo)
        else:
            Shb = B.Sh[h % 2]
            Sk = f"Sh{h % 2}"
            P.add("sp", CALL("dma_start", out=Shb[:, :, :], in_=G.st_delta[l, :, h].rearrange("s k v -> k s v")), writes=[Sk], dma=1, key=Sk)
            P.add("dve", CALL("tensor_copy", out=B.wTm[:, 0:1088].rearrange("p (s r) -> p s r", r=68)[:, :, 0:4], in_=W["wT"][:, 0:64].rearrange("p (s i) -> p s i", i=4)),
                  reads=[wn("wT")], writes=["wTm"])
            bw = P.ps()
            for s in range(16):
                P.add("pe", CALL("matmul", PS(bw, 128)[0:64, :], lhsT=B.wTm[:, s * 64:(s + 1) * 64], rhs=Shb[:, s, :], start=(s == 0), stop=(s == 15)),
                      reads=["wTm", Sk], writes=[pk(bw)])
            P.add("dve", CALL("tensor_tensor", out=W["vn"][0:64, :], in0=W["u"][0:64, :], in1=PS(bw, 128)[0:64, :], op=ALU.subtract), reads=[pk(bw), wn("u")], writes=[wn("vn") + "0"])
            bo = P.ps()
            for s in range(16):
                P.add("pe", CALL("matmul", PS(bo, 4, 4 * s), lhsT=Shb[:, s, :], rhs=W["qg"][:, 4 * s:4 * s + 4], start=True, stop=True),
                      reads=[wn("qg"), Sk], writes=[pk(bo)])
            P.add("act", CALL("activation", out=B.o1[:, 0:64], in_=PS(bo, 64), func=AF.Copy), reads=[pk(bo)], writes=["o1"])
            b2 = P.ps()
            P.add("pe", CALL("matmul", PS(b2, 64), lhsT=W["vn"][0:64, :], rhs=W["attnT"][0:64, 0:64], start=True, stop=True), reads=[wn("vn") + "0", wn("attnT")], writes=[pk(b2)])
            P.add("dve", CALL("tensor_tensor", out=B.oTs[:, 0:64], in0=PS(b2, 64), in1=B.o1[:, 0:64], op=ALU.add), reads=[pk(b2), "o1"], writes=["oTs"])
            P.add("pool", CALL("tensor_tensor", out=B.kdm[0:64, :, :], in0=W["kd"][0:64, :].unsqueeze(1).to_broadcast([64, 16, 128]),
                                                         in1=G.seqmask[0:64, :].unsqueeze(2).to_broadcast([64, 16, 128]), op=ALU.mult),
                  reads=[wn("kd"), "cst"], writes=["kdm"])
            for s in range(16):
                bs = P.ps()
                P.add("pe", CALL("matmul", PS(bs, 128), lhsT=B.kdm[0:64, s, :], rhs=W["vn"][0:64, :], start=True, stop=True), reads=["kdm", wn("vn") + "0"], writes=[pk(bs)])
                P.add("dve", CALL("scalar_tensor_tensor", out=Shb[:, s, :], in0=Shb[:, s, :], scalar=W["egr"][:, 4 * s + 3:4 * s + 4], in1=PS(bs, 128), op0=ALU.mult, op1=ALU.add),
                      reads=[pk(bs), wn("egr"), Sk], writes=[Sk])
            P.add("pool", CALL("dma_start", out=G.o_delta_s[l, :, h].rearrange("s k v -> k s v"), in_=Shb[:, :, :]), reads=[Sk], dma=1, key="o_" + Sk)
            o_ap = B.oTs[:, 0:64]
            o_key = "oTs"
        P.add("act", CALL("activation", out=W["osq"][:, r_], in_=o_ap, func=AF.Square), reads=[o_key], writes=[wn("osq")])
        bq = P.ps()
        P.add("pe", CALL("matmul", PS(bq, np_), lhsT=ones, rhs=W["osq"][:, r_], start=True, stop=True), reads=[wn("osq"), "cst"], writes=[pk(bq)])
        P.add("act", CALL("activation", out=W["rr"][:, r_], in_=PS(bq, np_), func=AF.Sqrt, scale=1.0 / 128, bias=G.epsb[:, 0:1]), reads=[pk(bq), "epsb"], writes=[wn("rr")])
        P.add("dve", CALL("reciprocal", out=W["rr"][:, r_], in_=W["rr"][:, r_]), reads=[wn("rr")], writes=[wn("rr")])
        P.add("dve", CALL("scalar_tensor_tensor", out=W["y1"][:, r_], in0=o_ap, scalar=G.gdn_sb[:, 0:1], in1=W["rr"][:, r_], op0=ALU.mult, op1=ALU.mult),
              reads=[o_key, "gdn", wn("rr")], writes=[wn("y1")])
        P.add("pool", CALL("tensor_tensor", out=B.yT[:, 4 + h, tsl], in0=W["y1"][:, r_], in1=B.zs[:, h, tsl], op=ALU.mult), reads=[wn("y1"), f"zs{h}"], writes=[f"yT{4 + h}_{j}"])

    if sample:
        for h in range(4):
            head_body(h)
    else:
        for h0 in (0, 2):
            recs = []
            for hh, banks in ((h0, (0, 1, 2, 3)), (h0 + 1, (4, 5, 6, 7))):
                P.begin_record(banks)
                head_body(hh)
                recs.append(P.end_record())
            P.replay(recs)


def _attn_softmax(G, B, sc_ap, sc_key, np_, h, out_writes):
    P, PS, PSB, pk = G.P, G.PS, G.PSB, G.pk
    r_ = slice(0, np_)
    i = h % 2
    sc = 128.0 ** -0.5
    mx = B.asm[r_, 4 * i:4 * i + 1]
    nmx = B.asm[r_, 4 * i + 1:4 * i + 2]
    rs = B.asm[r_, 4 * i + 2:4 * i + 3]
    ak = f"asm{i}"
    P.add("dve", CALL("reduce_max", out=mx, in_=sc_ap, axis=AX.X), reads=[sc_key], writes=[ak])
    P.add("dve", CALL("tensor_scalar", out=nmx, in0=mx, scalar1=-sc, scalar2=0.0, op0=ALU.mult, op1=ALU.add), reads=[ak], writes=[ak])
    P.add("act", CALL("activation", out=B.pexp[i][r_, :], in_=sc_ap, func=AF.Exp, scale=sc, bias=nmx, accum_out=rs), reads=[sc_key, ak], writes=[f"pexp{i}", ak])
    P.add("dve", CALL("reciprocal", out=rs, in_=rs), reads=[ak], writes=[ak])
    P.add("dve", CALL("tensor_scalar", out=B.pn[i][r_, :], in0=B.pexp[i][r_, :], scalar1=rs, scalar2=0.0, op0=ALU.mult, op1=ALU.add), reads=[f"pexp{i}", ak], writes=[f"pn{i}"])
    bt = P.ps()
    for mc in range(2):
        P.add("pe", CALL("transpose", out=PSB(bt, np_, mc * 128), in_=B.pn[i][r_, mc * 128:(mc + 1) * 128], identity=G.identB[r_, r_]),
              reads=[f"pn{i}", "identB"], writes=[pk(bt)])
    P.add("act", CALL("activation", out=B.pT[i][:, :, r_], in_=PSB(bt, 256).rearrange("p (c t) -> p c t", c=2)[:, :, r_], func=AF.Copy), reads=[pk(bt)], writes=[f"pT{i}"])
    return B.pT[i], f"pT{i}"


def _mixer_st(G, B, l, st, tiles, sample):
    P, PS, PSB, pk = G.P, G.PS, G.PSB, G.pk
    NT = sum(t[1] for t in tiles)
    nt = len(tiles)
    hkeys = [f"hT{j}" for j in range(nt)]
    for j, (x_ap, np_, xkey) in enumerate(tiles):
        G.rms_rstd(x_ap, np_, xkey, B.junk, "xbf0", B.ss[:, 0:1], "ss0")
        xb = B.xbf[j % 2]
        P.add("act", CALL("activation", out=xb[0:np_, :], in_=x_ap, func=AF.Copy, scale=B.ss[0:np_, 0:1]),
              reads=[xkey, "ss0"], writes=["xbf0"])
        G.to_featmajor(xb, np_, "xbf0", B.hT, hkeys[j], slice(j * 128, j * 128 + np_), gsb=G.gmix_sb, gkey="gmix")

    win = (2, 4, 8, 16)
    if not sample:
        L = 15 + NT
        if st == 0:
            P.add("pool", CALL("memset", B.uT[:, :, 0:15], 0.0), writes=[f"uT{c}" for c in range(4)])

        def pool_consume(c, b):
            P.add("act", CALL("activation", out=B.uT[:, c, 15:L], in_=PS(b, NT), func=AF.Copy), reads=[pk(b)], writes=[f"uT{c}"])
            a = B.uT[:, c, :]
            bufs = [(B.pA, "pA"), (B.pB, "pB")]
            src, skey = a, f"uT{c}"
            sh = 1
            for s_ in range(c + 1):
                dst, dkey = bufs[s_ % 2]
                lo = 2 * sh - 1
                P.add("dve", CALL("tensor_tensor", out=dst[:, lo:L], in0=src[:, lo:L], in1=src[:, lo - sh:L - sh], op=ALU.add),
                      reads=[skey], writes=[dkey])
                src, skey = dst, dkey
                sh *= 2
            dT = B.dT[c % 2]
            dk = f"dT{c % 2}"
            P.add("dve", CALL("scalar_tensor_tensor", out=dT[:, 0:NT], in0=src[:, 15:L], scalar=1.0 / win[c], in1=a[:, 15:L], op0=ALU.mult, op1=ALU.subtract),
                  reads=[skey, f"uT{c}"], writes=[dk])
            if st == 0:
                P.add("dve", CALL("tensor_tensor", out=B.t16[:, 0:16], in0=src[:, 15:31], in1=G.rcnt[:, c * 16:(c + 1) * 16], op=ALU.mult), reads=[skey, "cst"], writes=["t16"])
                P.add("dve", CALL("tensor_tensor", out=dT[:, 0:16], in0=B.t16[:, 0:16], in1=a[:, 15:31], op=ALU.subtract), reads=["t16", f"uT{c}", dk], writes=[dk])
            b2 = P.ps()
            P.add("pe", CALL("matmul", PS(b2, NT), lhsT=G.wgrp_b[:, c, :], rhs=dT[:, 0:NT], start=True, stop=True), reads=[dk, "wgrp_b"], writes=[pk(b2)])
            P.add("act", CALL("activation", out=B.yT[:, c, 0:NT], in_=PS(b2, NT), func=AF.Copy, scale=G.psc_sb[:, c:c + 1]), reads=[pk(b2), "psc"], writes=[f"yT{c}_all"])
        _proj(G, B, l, G.w_in[l], 8, 0, 4, B.hT, hkeys, NT, pool_consume)
        P.add("pool", CALL("tensor_copy", out=B.uT[:, :, 0:15], in_=B.uT[:, :, NT:NT + 15]), writes=[f"uT{c}" for c in range(4)])
    else:
        def pool_consume(c, b):
            P.add("act", CALL("activation", out=B.uTs[:, c, :, 15:19], in_=PS(b, 64).rearrange("p (s i) -> p s i", i=4), func=AF.Copy), reads=[pk(b)], writes=[f"uTs{c}"])
            a = B.uTs[:, c, :, :]
            pA = B.pA[:, 0:304].rearrange("p (s e) -> p s e", e=19)
            pB = B.pB[:, 0:304].rearrange("p (s e) -> p s e", e=19)
            bufs = [(pA, "pA"), (pB, "pB")]
            src, skey = a, f"uTs{c}"
            sh = 1
            for s_ in range(c + 1):
                dst, dkey = bufs[s_ % 2]
                lo = 2 * sh - 1
                P.add("dve", CALL("tensor_tensor", out=dst[:, :, lo:19], in0=src[:, :, lo:19], in1=src[:, :, lo - sh:19 - sh], op=ALU.add),
                      reads=[skey], writes=[dkey])
                src, skey = dst, dkey
                sh *= 2
            dT = B.dT[c % 2]
            dk = f"dT{c % 2}"
            P.add("dve", CALL("scalar_tensor_tensor", out=dT[:, 0:64].rearrange("p (s i) -> p s i", i=4), in0=src[:, :, 15:19], scalar=1.0 / win[c], in1=a[:, :, 15:19], op0=ALU.mult, op1=ALU.subtract),
                  reads=[skey, f"uTs{c}"], writes=[dk])
            b2 = P.ps()
            P.add("pe", CALL("matmul", PS(b2, NT), lhsT=G.wgrp_b[:, c, :], rhs=dT[:, 0:NT], start=True, stop=True), reads=[dk, "wgrp_b"], writes=[pk(b2)])
            P.add("act", CALL("activation", out=B.yT[:, c, 0:NT], in_=PS(b2, NT), func=AF.Copy, scale=G.psc_sb[:, c:c + 1]), reads=[pk(b2), "psc"], writes=[f"yT{c}_all"])
        _proj(G, B, l, G.w_in[l], 8, 0, 4, B.hT, hkeys, NT, pool_consume)

    def qkv_consume(c, b):
        if not sample:
            pre = B.pre[c % 2]
            pkey = f"pre{c % 2}"
            cv = B.cv[c % 2]
            ckey = f"cv{c % 2}"
            P.add("pool", CALL("tensor_copy", out=pre[:, 0:3], in_=G.hist[:, c, :]), reads=[f"hist{c}"], writes=[pkey + "h"])
            P.add("act", CALL("activation", out=pre[:, 3:3 + NT], in_=PS(b, NT), func=AF.Copy), reads=[pk(b)], writes=[pkey])
            P.add("dve", CALL("tensor_scalar", out=cv[:, 0:NT], in0=pre[:, 0:NT], scalar1=G.wconv_sb[:, c, 0:1], scalar2=0.0, op0=ALU.mult, op1=ALU.add),
                  reads=[pkey, pkey + "h", "wconv"], writes=[ckey])
            for j in range(1, 4):
                P.add("dve", CALL("scalar_tensor_tensor", out=cv[:, 0:NT], in0=pre[:, j:j + NT], scalar=G.wconv_sb[:, c, j:j + 1], in1=cv[:, 0:NT], op0=ALU.mult, op1=ALU.add),
                      reads=[pkey, pkey + "h", "wconv", ckey], writes=[ckey])
            P.add("pool", CALL("tensor_copy", out=G.hist[:, c, :], in_=pre[:, NT:NT + 3]), reads=[pkey], writes=[f"hist{c}"])
            P.add("act", CALL("activation", out=B.qkvc[:, c, 0:NT], in_=cv[:, 0:NT], func=AF.Silu), reads=[ckey], writes=[f"qkvc{c}"])
        else:
            pre = B.pre_s[c % 2]
            pkey = f"pre{c % 2}"
            cv = B.cv[c % 2][:, 0:64].rearrange("p (s i) -> p s i", i=4)
            ckey = f"cv{c % 2}"
            P.add("pool", CALL("tensor_copy", out=pre[:, :, 0:3], in_=B.hist_s[:, c, :, :]), reads=[f"hist_s{c // 4}"], writes=[pkey + "h"])
            P.add("act", CALL("activation", out=pre[:, :, 3:7], in_=PS(b, 64).rearrange("p (s i) -> p s i", i=4), func=AF.Copy), reads=[pk(b)], writes=[pkey])
            P.add("dve", CALL("tensor_scalar", out=cv, in0=pre[:, :, 0:4], scalar1=G.wconv_sb[:, c, 0:1], scalar2=0.0, op0=ALU.mult, op1=ALU.add),
                  reads=[pkey, pkey + "h", "wconv"], writes=[ckey])
            for j in range(1, 4):
                P.add("dve", CALL("scalar_tensor_tensor", out=cv, in0=pre[:, :, j:j + 4], scalar=G.wconv_sb[:, c, j:j + 1], in1=cv, op0=ALU.mult, op1=ALU.add),
                      reads=[pkey, pkey + "h", "wconv", ckey], writes=[ckey])
            P.add("pool", CALL("tensor_copy", out=B.cvout[:, c, :, :], in_=pre[:, :, 4:7]), reads=[pkey], writes=[f"cvout{c // 4}"])
            P.add("act", CALL("activation", out=B.qkvc[:, c, 0:NT], in_=B.cv[c % 2][:, 0:64], func=AF.Silu), reads=[ckey], writes=[f"qkvc{c}"])
    _proj(G, B, l, G.w_in[l], 8, OFF_Q, 12, B.hT, hkeys, NT, qkv_consume)

    for c in range(8):
        i = c % 2
        P.add("act", CALL("activation", out=B.sqb[i][:, 0:NT], in_=B.qkvc[:, c, 0:NT], func=AF.Square), reads=[f"qkvc{c}"], writes=[f"sqb{i}"])
        b = P.ps()
        P.add("pe", CALL("matmul", PS(b, NT), lhsT=G.onesF, rhs=B.sqb[i][:, 0:NT], start=True, stop=True), reads=[f"sqb{i}", "cst"], writes=[pk(b)])
        P.add("act", CALL("activation", out=B.rinv[i][:, 0:NT], in_=PS(b, NT), func=AF.Sqrt, scale=1.0, bias=G.epsb[:, 0:1]), reads=[pk(b), "epsb"], writes=[f"rinv{i}"])
        P.add("dve", CALL("reciprocal", out=B.rinv[i][:, 0:NT], in_=B.rinv[i][:, 0:NT]), reads=[f"rinv{i}"], writes=[f"rinv{i}"])
        scl = (128.0 ** -0.5) if c < 4 else 1.0
        P.add("dve", CALL("scalar_tensor_tensor", out=B.qkvc[:, c, 0:NT], in0=B.rinv[i][:, 0:NT], scalar=scl, in1=B.qkvc[:, c, 0:NT], op0=ALU.mult, op1=ALU.mult),
              reads=[f"rinv{i}", f"qkvc{c}"], writes=[f"qkvc{c}"])

    def z_consume(c, b):
        P.add("act", CALL("activation", out=B.zs[:, c, 0:NT], in_=PS(b, NT), func=AF.Silu), reads=[pk(b)], writes=[f"zs{c}"])
    _proj(G, B, l, G.w_in[l], 8, OFF_Z, 4, B.hT, hkeys, NT, z_consume)

    def xq_consume(c, b):
        P.add("act", CALL("activation", out=B.xqT[:, c, 0:NT], in_=PS(b, NT), func=AF.Copy), reads=[pk(b)], writes=[f"xqT{c}"])
    _proj(G, B, l, G.w_in[l], 8, OFF_XQ, 4, B.hT, hkeys, NT, xq_consume)

    wba, wbak = G.load_w(G.w_in[l], 8, OFF_BA, 8, B.stg, B.wbf)

    if sample:
        _sample_attn(G, B, l)
        P.barrier()
        P.add("pool", CALL("memset", B.wTm[:, :], 0.0), writes=["wTm"])
    def per_tile(j, x_ap, np_, xkey):
        r_ = slice(0, np_)
        tsl = slice(j * 128, j * 128 + np_)
        b = P.ps()
        for k in range(8):
            P.add("pe", CALL("matmul", PS(b, 8)[r_, :], lhsT=B.hT[:, k, tsl], rhs=wba[:, k, :], start=(k == 0), stop=(k == 7)), reads=[hkeys[j], wbak], writes=[pk(b)])
        ba = B.ba
        P.add("act", CALL("activation", out=ba[r_, 0:4], in_=PS(b, 4)[r_, :], func=AF.Sigmoid), reads=[pk(b)], writes=["ba"])
        P.add("dve", CALL("tensor_scalar", out=ba[r_, 4:8], in0=ba[r_, 0:4], scalar1=-1.0, scalar2=0.0, op0=ALU.mult, op1=ALU.add), reads=["ba"], writes=["ba"])
        P.add("dve", CALL("tensor_tensor", out=ba[r_, 12:16], in0=PS(b, 4, 4)[r_, :], in1=G.dtb_b[r_, :], op=ALU.add), reads=[pk(b), "dtb"], writes=["ba"])
        P.add("dve", CALL("scalar_tensor_tensor", out=ba[r_, 16:20], in0=ba[r_, 12:16], scalar=-1.0, in1=ba[r_, 12:16], op0=ALU.mult, op1=ALU.max), reads=["ba"], writes=["ba"])
        P.add("act", CALL("activation", out=ba[r_, 16:20], in_=ba[r_, 16:20], func=AF.Exp, scale=-1.0), reads=["ba"], writes=["ba"])
        P.add("act", CALL("activation", out=ba[r_, 16:20], in_=ba[r_, 16:20], func=AF.Ln, scale=1.0, bias=G.onesF[r_, 0:1]), reads=["ba", "cst"], writes=["ba"])
        P.add("dve", CALL("scalar_tensor_tensor", out=ba[r_, 12:16], in0=ba[r_, 12:16], scalar=0.0, in1=ba[r_, 16:20], op0=ALU.max, op1=ALU.add), reads=["ba"], writes=["ba"])
        P.add("dve", CALL("tensor_tensor", out=ba[r_, 8:12], in0=ba[r_, 12:16], in1=G.alog_b[r_, :], op=ALU.mult), reads=["ba", "alog"], writes=["ba"])
        _delta_tile(G, B, l, j, np_, tsl, sample)
        if not sample:
            for h in range(4):
                bs_ = P.ps()
                P.add("pe", CALL("matmul", PS(bs_, 256)[r_, :], lhsT=B.xqT[:, h, tsl], rhs=B.kTm[:, h, :], start=True, stop=True),
                      reads=[f"xqT{h}", f"kTm{h}"], writes=[pk(bs_)])
                pT, pTk = _attn_softmax(G, B, PS(bs_, 256)[r_, :], pk(bs_), np_, h, None)
                bo = P.ps()
                for mc in range(2):
                    P.add("pe", CALL("matmul", PS(bo, np_), lhsT=B.vm[:, mc, h * 128:(h + 1) * 128], rhs=pT[:, mc, r_], start=(mc == 0), stop=(mc == 1)),
                          reads=[pTk] + B.vm_keys, writes=[pk(bo)])
                P.add("act", CALL("activation", out=B.yT[:, 8 + h, tsl], in_=PS(bo, np_), func=AF.Copy), reads=[pk(bo)], writes=[f"yT{8 + h}_{j}"])

    for j_, (x_ap_, np__, xkey_) in enumerate(tiles):
        per_tile(j_, x_ap_, np__, xkey_)

    ykeys = [[f"yT{c}_all" for c in range(4)], [f"yT{4 + h}_{j}" for h in range(4) for j in range(nt)], [f"yT{8 + h}_{j}" for h in range(4) for j in range(nt)]]
    if sample:
        ykeys[2] = [f"yT{8 + h}_0" for h in range(4)]

    for half in range(2):
        for n in range(3):
            wg0, wgk0 = G.load_w(G.w_in[l], 8, OFF_GATE + n * 1024 + half * 512, 256, B.stg, B.wbf)
            wb_, wbk = G.load_w(G.w_br[l, n], 4, half * 512, 512, B.stg, B.wbf)
            wg1, wgk1 = G.load_w(G.w_in[l], 8, OFF_GATE + n * 1024 + half * 512 + 256, 256, B.stg, B.wbf)
            for jj in range(4):
                wg, wgk = (wg0, wgk0) if jj < 2 else (wg1, wgk1)
                cc = jj % 2
                i = jj % 2
                bg = P.ps()
                for k in range(8):
                    P.add("pe", CALL("matmul", PS(bg, NT), lhsT=wg[:, k, cc * 128:(cc + 1) * 128], rhs=B.hT[:, k, 0:NT], start=(k == 0), stop=(k == 7)),
                          reads=hkeys + [wgk], writes=[pk(bg)])
                P.add("act", CALL("activation", out=B.sig[i][:, 0:NT], in_=PS(bg, NT), func=AF.Sigmoid), reads=[pk(bg)], writes=[f"sqb{i}"])
                bb = P.ps()
                for c in range(4):
                    P.add("pe", CALL("matmul", PS(bb, NT), lhsT=wb_[:, c, jj * 128:(jj + 1) * 128], rhs=B.yT[:, n * 4 + c, 0:NT], start=(c == 0), stop=(c == 3)),
                          reads=ykeys[n] + [wbk], writes=[pk(bb)])
                if n == 0:
                    P.add("dve", CALL("tensor_tensor", out=B.macc[:, jj, 0:NT], in0=PS(bb, NT), in1=B.sig[i][:, 0:NT], op=ALU.mult), reads=[pk(bb), f"sqb{i}"], writes=[f"macc{jj}"])
                else:
                    P.add("dve", CALL("tensor_tensor", out=B.prod[i][:, 0:NT], in0=PS(bb, NT), in1=B.sig[i][:, 0:NT], op=ALU.mult), reads=[pk(bb), f"sqb{i}"], writes=[f"rinv{i}"])
                    if n == 1:
                        P.add("pool", CALL("tensor_tensor", out=B.macc[:, jj, 0:NT], in0=B.macc[:, jj, 0:NT], in1=B.prod[i][:, 0:NT], op=ALU.add), reads=[f"rinv{i}", f"macc{jj}"], writes=[f"macc{jj}"])
                    else:
                        P.add("pool", CALL("tensor_tensor", out=B.mT[:, half * 4 + jj, 0:NT], in0=B.macc[:, jj, 0:NT], in1=B.prod[i][:, 0:NT], op=ALU.add),
                              reads=[f"rinv{i}", f"macc{jj}"], writes=[f"mT{half * 4 + jj}"])

    mkeys = [f"mT{c}" for c in range(8)]
    for q in range(4):
        wo, wok = G.load_w(G.w_o[l], 8, q * 256, 256, B.stg, B.wbf)
        for j, (x_ap, np_, xkey) in enumerate(tiles):
            tsl = slice(j * 128, j * 128 + np_)
            b = P.ps()
            for k in range(8):
                P.add("pe", CALL("matmul", PS(b, 256)[0:np_, :], lhsT=B.mT[:, k, tsl], rhs=wo[:, k, :], start=(k == 0), stop=(k == 7)),
                      reads=mkeys + [wok], writes=[pk(b)])
            xo = x_ap[:, q * 256:(q + 1) * 256]
            P.add("dve", CALL("tensor_tensor", out=xo, in0=xo, in1=PS(b, 256)[0:np_, :], op=ALU.add), reads=[pk(b), xkey], writes=[xkey])


def _sample_prep(G, B, l):
    P, PS, PSB, pk = G.P, G.PS, G.PSB, G.pk
    I_ = G.identF
    P.add("pool", CALL("memset", B.xqm[:, :, :], 0.0), writes=["xqm"])
    P.add("sp", CALL("dma_start", out=B.ld1536[0:48, :], in_=G.st_conv[l]), writes=["ld1536"], dma=1, key="ld1536")
    for g in range(3):
        b = P.ps()
        for cc in range(4):
            c = g * 4 + cc
            P.add("pe", CALL("transpose", out=PS(b, 48, cc * 48), in_=B.ld1536[0:48, c * 128:(c + 1) * 128], identity=I_[0:48, 0:48]), reads=["ld1536", "cst"], writes=[pk(b)])
        P.add("act", CALL("activation", out=B.hist_s[:, g * 4:(g + 1) * 4, :, :].rearrange("p c s e -> p (c s e)"), in_=PS(b, 192), func=AF.Copy), reads=[pk(b)], writes=[f"hist_s{g}"])
    for hf in range(2):
        ld = B.ld512[hf]
        P.add("sp", CALL("dma_start", out=ld[0:120, :], in_=G.st_pool[l, hf * 120:(hf + 1) * 120, :]), writes=[f"ld512{hf}"], dma=1, key=f"ld512{hf}")
        b = P.ps()
        for c in range(4):
            P.add("pe", CALL("transpose", out=PS(b, 120, c * 120), in_=ld[0:120, c * 128:(c + 1) * 128], identity=I_[0:120, 0:120]), reads=[f"ld512{hf}", "cst"], writes=[pk(b)])
        for c in range(4):
            P.add("act", CALL("activation", out=B.uTs[:, c, hf * 8:(hf + 1) * 8, 0:15], in_=PS(b, 120, c * 120).rearrange("p (s e) -> p s e", e=15), func=AF.Copy),
                  reads=[pk(b)], writes=[f"uTs{c}"])


def _sample_attn(G, B, l):
    P, PS, PSB, pk = G.P, G.PS, G.PSB, G.pk
    for h in range(4):
        P.add("dve", CALL("tensor_copy", out=B.xqm[:, h, 0:1088].rearrange("p (s r) -> p s r", r=68)[:, :, 0:4], in_=B.xqT[:, h, 0:64].rearrange("p (s i) -> p s i", i=4)),
              reads=[f"xqT{h}"], writes=["xqm"])
    for s in range(16):
        i = s % 2
        st_, stk = B.kvs[i], f"kvs{i}"
        P.add("sp", CALL("dma_start", out=st_[:, :].rearrange("p (c n) -> p c n", c=2), in_=G.c_k[l, s].rearrange("(c p) n -> p c n", p=128)), writes=[stk], dma=1, key=stk)
        P.add("act", CALL("activation", out=B.kvb[i][:, :, :].rearrange("p c n -> p (c n)"), in_=st_[:, :], func=AF.Copy), reads=[stk], writes=[f"kvb{i}"])
        bt = 4 + (s % 4)
        for h in range(4):
            for mc in range(2):
                P.add("pe", CALL("transpose", out=PSB(bt, 128, h * 256 + mc * 128), in_=B.kvb[i][:, mc, h * 128:(h + 1) * 128], identity=G.identB),
                      reads=[f"kvb{i}", "identB"], writes=[pk(bt)])
        P.add("dve", CALL("tensor_copy", out=B.kTs[i][:, :, :].rearrange("p h m -> p (h m)"), in_=PSB(bt, 1024)), reads=[pk(bt)], writes=[f"kTs{i}"])
        for h in range(4):
            P.add("pe", CALL("matmul", PS(h, 256)[0:64, :], lhsT=B.xqm[:, h, s * 64:(s + 1) * 64], rhs=B.kTs[i][:, h, :], start=(s == 0), stop=(s == 15)),
                  reads=["xqm", f"kTs{i}"], writes=[pk(h)])
    pTs = []
    for h in range(4):
        pT, pTk = _attn_softmax(G, B, PS(h, 256)[0:64, :], pk(h), 64, h, None)
        dst = B.pTall[h]
        P.add("pool", CALL("tensor_copy", out=dst[:, :, 0:64], in_=pT[:, :, 0:64]), reads=[pTk], writes=[f"pTall{h}"])
        pTs.append(dst)
    for s in range(16):
        i = s % 2
        st_, stk = B.kvs[i], f"kvs{i}"
        P.add("sp", CALL("dma_start", out=st_[:, :].rearrange("p (c n) -> p c n", c=2), in_=G.c_v[l, s].rearrange("(c p) n -> p c n", p=128)), writes=[stk], dma=1, key=stk)
        P.add("act", CALL("activation", out=B.kvb[i][:, :, :].rearrange("p c n -> p (c n)"), in_=st_[:, :], func=AF.Copy), reads=[stk], writes=[f"kvb{i}"])
        for h in range(4):
            for mc in range(2):
                P.add("pe", CALL("matmul", PS(h, 4, 4 * s), lhsT=B.kvb[i][:, mc, h * 128:(h + 1) * 128], rhs=pTs[h][:, mc, 4 * s:4 * s + 4], start=(mc == 0), stop=(mc == 1)),
                      reads=[f"kvb{i}", f"pTall{h}"], writes=[pk(h)])
    for h in range(4):
        P.add("act", CALL("activation", out=B.yT[:, 8 + h, 0:64], in_=PS(h, 64), func=AF.Copy), reads=[pk(h)], writes=[f"yT{8 + h}_0"])
    P.psi = 4


def mixer_phase(G, l):
    P, AR, PS, pk = G.P, G.AR, G.PS, G.pk
    I_ = G.identF
    P.barrier()
    m0 = AR.mark()
    Bp = _mixer_bufs(G, 256, False)
    P.add("pool", CALL("memset", G.hist[:, :, :], 0.0), writes=[f"hist{c}" for c in range(12)])
    P.add("pool", CALL("memset", G.Sst[:, :, :], 0.0), writes=[f"S{h}" for h in range(4)])
    P.add("sp", CALL("dma_start", out=Bp.stg[0][0][:, 0:512].rearrange("p (g e) -> p g e", g=4), in_=G.w_grp[l].rearrange("g c e -> c g e")), writes=["stg0"], dma=1, key="stg0")
    P.add("act", CALL("activation", out=G.wgrp_b[:, :, :], in_=Bp.stg[0][0][:, 0:512].rearrange("p (g e) -> p g e", g=4), func=AF.Copy), reads=["stg0"], writes=["wgrp_b"])
    _mem_kv(G, Bp, l)
    P.barrier()
    nst = G.n_st if hasattr(G, "n_st") else 8
    for st in range(nst):
        tiles = [(G.xp[:, st * 2 + j, :], 128, f"xp{st * 2 + j}") for j in range(2)]
        _mixer_st(G, Bp, l, st, tiles, False)
    P.barrier()
    b = P.ps()
    for c in range(4):
        P.add("pe", CALL("transpose", out=PS(b, 128, c * 128)[0:15, :], in_=Bp.uT[:, c, 0:15], identity=I_), reads=[f"uT{c}", "cst"], writes=[pk(b)])
    P.add("act", CALL("activation", out=Bp.rowbuf[0:15, 0:512], in_=PS(b, 512)[0:15, :], func=AF.Copy), reads=[pk(b)], writes=["rowbuf"])
    P.add("pool", CALL("dma_start", out=G.o_pool_p[l], in_=Bp.rowbuf[0:15, 0:512]), reads=["rowbuf"], dma=1, key="o_pool_p")
    for g in range(3):
        b = P.ps()
        for cc in range(4):
            c = g * 4 + cc
            P.add("pe", CALL("transpose", out=PS(b, 128, cc * 128)[0:3, :], in_=G.hist[:, c, :], identity=I_), reads=[f"hist{c}", "cst"], writes=[pk(b)])
        P.add("act", CALL("activation", out=Bp.rowbuf[0:3, g * 512:(g + 1) * 512], in_=PS(b, 512)[0:3, :], func=AF.Copy), reads=[pk(b)], writes=["rowbuf"])
    P.add("pool", CALL("dma_start", out=G.o_conv_p[l], in_=Bp.rowbuf[0:3, 0:1536]), reads=["rowbuf"], dma=1, key="o_conv_p")
    P.add("pool", CALL("dma_start", out=G.o_delta_p[l].rearrange("h k v -> k h v"), in_=G.Sst[:, :, :]), reads=[f"S{h}" for h in range(4)], dma=1, key="o_delta_p")
    P.barrier()
    AR.release(m0)
    if getattr(G, "skip_sample", False):
        return
    Bs = _mixer_bufs(G, 64, True)
    _sample_prep(G, Bs, l)
    _mixer_st(G, Bs, l, 0, [(G.xs[0:TS, :], TS, "xs")], True)
    P.barrier()
    for hf in range(2):
        b = P.ps()
        for c in range(4):
            P.add("dve", CALL("tensor_copy", out=Bs.pA[:, 0:120].rearrange("p (s e) -> p s e", e=15), in_=Bs.uTs[:, c, hf * 8:(hf + 1) * 8, 4:19]), reads=[f"uTs{c}"], writes=["pA"])
            P.add("pe", CALL("transpose", out=PS(b, 128, c * 128)[0:120, :], in_=Bs.pA[:, 0:120], identity=I_), reads=["pA", "cst"], writes=[pk(b)])
        P.add("act", CALL("activation", out=Bs.rowbuf[0:120, 0:512], in_=PS(b, 512)[0:120, :], func=AF.Copy), reads=[pk(b)], writes=["rowbuf"])
        P.add("pool", CALL("dma_start", out=G.o_pool_s[l, hf * 120:(hf + 1) * 120, :], in_=Bs.rowbuf[0:120, 0:512]), reads=["rowbuf"], dma=1, key="o_pool_s")
    for g in range(3):
        b = P.ps()
        for cc in range(4):
            c = g * 4 + cc
            P.add("pe", CALL("transpose", out=PS(b, 128, cc * 128)[0:48, :], in_=Bs.cvout[:, c, :, :].rearrange("p s e -> p (s e)"), identity=I_), reads=[f"cvout{g}", "cst"], writes=[pk(b)])
        P.add("act", CALL("activation", out=Bs.ld1536[0:48, g * 512:(g + 1) * 512], in_=PS(b, 512)[0:48, :], func=AF.Copy), reads=[pk(b)], writes=["ld1536"])
    P.add("pool", CALL("dma_start", out=G.o_conv_s[l], in_=Bs.ld1536[0:48, :]), reads=["ld1536"], dma=1, key="o_conv_s")
    P.barrier()
    AR.release(m0)


def peer_phase(G, l):
    P, AR, PS, PSB, pk = G.P, G.AR, G.PS, G.PSB, G.pk
    P.barrier()
    P.ps_lo = 2
    m0 = AR.mark()
    gsblk = AR.alloc(NG * 1024)
    gs = [gsblk[:, i * 1024:(i + 1) * 1024] for i in range(NG)]
    wq = AR.alloc(8 * 2048, BF16).rearrange("p (k n) -> p k n", k=8)
    skT = AR.alloc(16 * 128, BF16).rearrange("p (j n) -> p j n", j=16)
    skb = AR.alloc(16 * 128, BF16).rearrange("p (j n) -> p j n", j=16)
    hn = AR.alloc(1024)
    hnb = AR.alloc(1024, BF16)
    hnT = AR.alloc(8 * 128, BF16).rearrange("p (k t) -> p k t", k=8)
    qT = AR.alloc(16 * 128, BF16).rearrange("p (j t) -> p j t", j=16)
    s1 = AR.alloc(2048)
    s2 = AR.alloc(2048)
    oh = AR.alloc(2048)
    top = AR.alloc(256).rearrange("p (j a) -> p j a", j=16)
    topi = AR.alloc(256, U32).rearrange("p (j a) -> p j a", j=16)
    topif = AR.alloc(256).rearrange("p (h t a) -> p h t a", h=8, t=2)
    best = AR.alloc(128).rearrange("p (h k) -> p h k", h=8)
    pos = AR.alloc(128, U32).rearrange("p (h k) -> p h k", h=8)
    pint = AR.alloc(128, U32).rearrange("p (h k) -> p h k", h=8)
    paf = AR.alloc(128).rearrange("p (h k) -> p h k", h=8)
    pbf = AR.alloc(128).rearrange("p (h k) -> p h k", h=8)
    I1 = AR.alloc(128).rearrange("p (h k) -> p h k", h=8)
    I2 = AR.alloc(128).rearrange("p (h k) -> p h k", h=8)
    idxf = AR.alloc(128)
    idx2 = [AR.alloc(128, I32) for _ in range(2)]
    gate = AR.alloc(128).rearrange("p (h k) -> p h k", h=8)
    gsum = AR.alloc(8)
    av = AR.alloc(128)
    tg = AR.alloc(128)
    wgt2 = [AR.alloc(128) for _ in range(2)]
    Dg = [AR.alloc(8 * 128, BF16).rearrange("p (j m) -> p j m", j=8) for _ in range(2)]
    vbf = [AR.alloc(1024, BF16) for _ in range(2)]
    jb = AR.alloc(1024, BF16)
    ss = AR.alloc(8)

    stgA = (gsblk[:, 0:2048], ["gs0", "gs1"])
    stgB = (gsblk[:, 2048:4096], ["gs2", "gs3"])
    for g in range(8):
        sv, skeys = (stgA, stgB)[g % 2]
        svv = sv.rearrange("p (k n) -> p k n", k=8)
        P.add("sp", CALL("dma_start", out=svv, in_=G.w_pq[l].rearrange("(k p) n -> p k n", p=128)[:, :, g * 256:(g + 1) * 256]), writes=skeys, dma=1, key="pq" + skeys[0])
        if g % 2 == 0:
            P.add("act", CALL("activation", out=wq[:, :, g * 256:(g + 1) * 256], in_=svv, func=AF.Copy), reads=skeys, writes=[f"wq{g}"])
        else:
            P.add("dve", CALL("tensor_copy", out=wq[:, :, g * 256:(g + 1) * 256], in_=svv), reads=skeys, writes=[f"wq{g}"])
    wqkeys = [f"wq{g}" for g in range(8)]
    sks = gsblk[:, 4096:6144].rearrange("p (j c) -> p j c", j=16)
    P.add("sp", CALL("dma_start", out=sks, in_=G.subk[l].rearrange("j k c -> k j c")), writes=["gs4", "gs5"], dma=1, key="sks")
    P.add("act", CALL("activation", out=skb[:, :, :], in_=sks, func=AF.Copy), reads=["gs4", "gs5"], writes=["skb"])
    for hf in range(2):
        b = P.ps()
        for jj in range(8):
            j = hf * 8 + jj
            P.add("pe", CALL("transpose", out=PSB(b, 128, jj * 128), in_=skb[:, j, :], identity=G.identB), reads=["skb", "identB"], writes=[pk(b)])
        P.add("act", CALL("activation", out=skT[:, hf * 8:(hf + 1) * 8, :].rearrange("p j n -> p (j n)"), in_=PSB(b, 1024), func=AF.Copy), reads=[pk(b)], writes=[f"skT{hf}"])

    tiles = [(G.xp[:, t, :], 128, f"xp{t}") for t in range(16)] + [(G.xs[0:TS, :], TS, "xs")]
    if hasattr(G, "peer_tiles"):
        tiles = [tiles[i] for i in G.peer_tiles]
    gst = {"gi": 0, "gv": 0}

    def part_topk(x_ap, np_, xkey, par):
        r_ = slice(0, np_)
        idx = idx2[par]
        ik = f"idx{par}"
        G.rms_rstd(x_ap, np_, xkey, jb, "jb", ss[:, 0:1], "pss")
        P.add("dve", CALL("scalar_tensor_tensor", out=hn[r_, :], in0=x_ap, scalar=ss[r_, 0:1], in1=G.gb_ffn[r_, :], op0=ALU.mult, op1=ALU.mult),
              reads=[xkey, "pss", "gb_ffn"], writes=["hn"])
        P.add("act", CALL("activation", out=hnb[r_, :], in_=hn[r_, :], func=AF.Copy), reads=["hn"], writes=["hnb"])
        G.to_featmajor(hnb, np_, "hnb", hnT, "hnT", slice(0, np_))
        for j in range(16):
            b = P.ps()
            for k in range(8):
                P.add("pe", CALL("matmul", PS(b, np_), lhsT=wq[:, k, j * 128:(j + 1) * 128], rhs=hnT[:, k, r_], start=(k == 0), stop=(k == 7)),
                      reads=["hnT", wqkeys[j // 2]], writes=[pk(b)])
            if j % 2 == 0:
                P.add("act", CALL("activation", out=qT[:, j, r_], in_=PS(b, np_), func=AF.Copy), reads=[pk(b)], writes=[f"qT{j}"])
            else:
                P.add("dve", CALL("tensor_copy", out=qT[:, j, r_], in_=PS(b, np_)), reads=[pk(b)], writes=[f"qT{j}"])
        for q in range(4):
            b = P.ps()
            for jj in range(4):
                j = q * 4 + jj
                P.add("pe", CALL("matmul", PS(b, 128, jj * 128)[r_, :], lhsT=qT[:, j, r_], rhs=skT[:, j, :], start=True, stop=True),
                      reads=[f"qT{j}", f"skT{j // 8}"], writes=[pk(b)])
            P.add("act", CALL("activation", out=s1[r_, q * 512:(q + 1) * 512], in_=PS(b, 512)[r_, :], func=AF.Copy), reads=[pk(b)], writes=[f"s1_{q}"])
        s1v = s1[:, :].rearrange("p (j n) -> p j n", j=16)
        s2v = s2[:, :].rearrange("p (j n) -> p j n", j=16)
        for j in range(16):
            sk_ = f"s1_{j // 4}"
            P.add("dve", CALL("max", out=top[r_, j, 0:8], in_=s1v[r_, j, :]), reads=[sk_], writes=[f"top{j}a"])
            P.add("dve", CALL("max_index", out=topi[r_, j, 0:8], in_max=top[r_, j, 0:8], in_values=s1v[r_, j, :]), reads=[sk_, f"top{j}a"], writes=[f"topi{j}a"])
            P.add("dve", CALL("match_replace", out=s2v[r_, j, :], in_to_replace=top[r_, j, 0:8], in_values=s1v[r_, j, :], imm_value=NEG), reads=[sk_, f"top{j}a"], writes=[f"s2_{j}"])
            P.add("dve", CALL("max", out=top[r_, j, 8:16], in_=s2v[r_, j, :]), reads=[f"s2_{j}"], writes=[f"top{j}b"])
            P.add("dve", CALL("max_index", out=topi[r_, j, 8:16], in_max=top[r_, j, 8:16], in_values=s2v[r_, j, :]), reads=[f"s2_{j}", f"top{j}b"], writes=[f"topi{j}b"])
        allt = [f"top{j}{x}" for j in range(16) for x in "ab"]
        alli = [f"topi{j}{x}" for j in range(16) for x in "ab"]
        P.add("dve", CALL("tensor_copy", out=topif[r_, :, :, :].rearrange("p h t a -> p (h t a)"), in_=topi[r_, :, :].rearrange("p j a -> p (j a)")), reads=alli, writes=["topif"])
        topv = top[:, :, :].rearrange("p (h t) a -> p h t a", t=2)
        cand = s1[:, :].rearrange("p (h a b) -> p h a b", h=8, a=16)
        cand2 = s2[:, :].rearrange("p (h n) -> p h n", h=8)
        candf = s1[:, :].rearrange("p (h n) -> p h n", h=8)
        for h in range(8):
            P.add("dve", CALL("tensor_tensor", out=cand[r_, h, :, :], in0=topv[r_, h, 0, :].unsqueeze(2).to_broadcast([np_, 16, 16]),
                                                        in1=topv[r_, h, 1, :].unsqueeze(1).to_broadcast([np_, 16, 16]), op=ALU.add),
                  reads=allt, writes=[f"s1_{h // 2}"])
            P.add("dve", CALL("max", out=best[r_, h, 0:8], in_=candf[r_, h, :]), reads=[f"s1_{h // 2}"], writes=[f"best{h}a"])
            P.add("dve", CALL("max_index", out=pos[r_, h, 0:8], in_max=best[r_, h, 0:8], in_values=candf[r_, h, :]), reads=[f"s1_{h // 2}", f"best{h}a"], writes=[f"pos{h}a"])
            P.add("dve", CALL("match_replace", out=cand2[r_, h, :], in_to_replace=best[r_, h, 0:8], in_values=candf[r_, h, :], imm_value=NEG),
                  reads=[f"s1_{h // 2}", f"best{h}a"], writes=[f"s2_{2 * h}", f"s2_{2 * h + 1}"])
            P.add("dve", CALL("max", out=best[r_, h, 8:16], in_=cand2[r_, h, :]), reads=[f"s2_{2 * h}", f"s2_{2 * h + 1}"], writes=[f"best{h}b"])
            P.add("dve", CALL("max_index", out=pos[r_, h, 8:16], in_max=best[r_, h, 8:16], in_values=cand2[r_, h, :]), reads=[f"s2_{2 * h}", f"s2_{2 * h + 1}", f"best{h}b"], writes=[f"pos{h}b"])
        allb = [f"best{h}{x}" for h in range(8) for x in "ab"]
        allp = [f"pos{h}{x}" for h in range(8) for x in "ab"]
        P.add("dve", CALL("tensor_single_scalar", out=pint[r_, :, :], in_=pos[r_, :, :], scalar=4, op=ALU.logical_shift_right), reads=allp, writes=["pint"])
        P.add("dve", CALL("tensor_copy", out=paf[r_, :, :], in_=pint[r_, :, :]), reads=["pint"], writes=["paf"])
        P.add("dve", CALL("tensor_single_scalar", out=pint[r_, :, :], in_=pos[r_, :, :], scalar=15, op=ALU.bitwise_and), reads=allp + ["paf"], writes=["pint"])
        P.add("dve", CALL("tensor_copy", out=pbf[r_, :, :], in_=pint[r_, :, :]), reads=["pint"], writes=["pbf"])
        ohv = oh[:, :].rearrange("p (h k a) -> p h k a", h=8, k=16)
        for (pf, pfk, tsel, Iout, Ik) in ((paf, "paf", 0, I1, "I1"), (pbf, "pbf", 1, I2, "I2")):
            for h in range(8):
                P.add("dve", CALL("tensor_tensor", out=ohv[r_, h, :, :], in0=pf[r_, h, :].unsqueeze(2).to_broadcast([np_, 16, 16]),
                                                                   in1=G.iota16[r_, :].unsqueeze(1).to_broadcast([np_, 16, 16]), op=ALU.is_equal),
                      reads=[pfk, "cst"], writes=[f"oh{h}"])
                P.add("dve", CALL("tensor_tensor", out=ohv[r_, h, :, :], in0=ohv[r_, h, :, :], in1=topif[r_, h, tsel, :].unsqueeze(1).to_broadcast([np_, 16, 16]), op=ALU.mult),
                      reads=[f"oh{h}", "topif"], writes=[f"oh{h}"])
            P.add("dve", CALL("tensor_reduce", out=Iout[r_, :, :], in_=ohv[r_, :, :, :], axis=AX.X, op=ALU.add), reads=[f"oh{h}" for h in range(8)], writes=[Ik])
        P.add("dve", CALL("scalar_tensor_tensor", out=idxf[r_, :], in0=I1[r_, :, :].rearrange("p h k -> p (h k)"), scalar=128.0, in1=I2[r_, :, :].rearrange("p h k -> p (h k)"), op0=ALU.mult, op1=ALU.add),
              reads=["I1", "I2"], writes=["idxf"])
        if l > 0:
            P.add("dve", CALL("tensor_scalar", out=idxf[r_, :], in0=idxf[r_, :], scalar1=float(l * NEXP), scalar2=0.0, op0=ALU.add, op1=ALU.add), reads=["idxf"], writes=["idxf"])
        P.add("dve", CALL("tensor_copy", out=idx[r_, :], in_=idxf[r_, :]), reads=["idxf"], writes=[ik])
        P.add("dve", CALL("tensor_tensor", out=gate[r_, :, :], in0=best[r_, :, :], in1=best[r_, :, 0:1].to_broadcast([np_, 8, 16]), op=ALU.subtract), reads=allb, writes=["gate"])
        P.add("act", CALL("activation", out=gate[r_, :, :], in_=gate[r_, :, :], func=AF.Exp), reads=["gate"], writes=["gate"])
        P.add("dve", CALL("tensor_reduce", out=gsum[r_, 0:8], in_=gate[r_, :, :], axis=AX.X, op=ALU.add), reads=["gate"], writes=["gsum"])
        P.add("dve", CALL("reciprocal", out=gsum[r_, 0:8], in_=gsum[r_, 0:8]), reads=["gsum"], writes=["gsum"])
        P.add("dve", CALL("tensor_tensor", out=gate[r_, :, :], in0=gate[r_, :, :], in1=gsum[r_, 0:8].unsqueeze(2).to_broadcast([np_, 8, 16]), op=ALU.mult), reads=["gate", "gsum"], writes=["gate"])
    nsl = getattr(G, "peer_slots", 128)

    def part_U(x_ap, np_, xkey, par):
        r_ = slice(0, np_)
        idx = idx2[par]
        ik = f"idx{par}"
        for sl in range(nsl):
            i = 3 + gst["gi"] % 3
            gst["gi"] += 1
            P.add("pool", CALL("indirect_dma_start", out=gs[i][r_, :], out_offset=None, in_=G.p_u, in_offset=bass.IndirectOffsetOnAxis(ap=idx[r_, sl:sl + 1], axis=0)),
                  reads=[ik], writes=[f"gs{i}"], dma=1, key=f"gs{i}")
            P.add("dve", CALL("scalar_tensor_tensor", out=gs[i][r_, :], in0=gs[i][r_, :], scalar=1.0, in1=hn[r_, :], op0=ALU.mult, op1=ALU.mult, accum_out=av[r_, sl:sl + 1]),
                  reads=["hn"], writes=[f"gs{i}", f"av{sl}"])

    def part_gelu(x_ap, np_, xkey, par):
        r_ = slice(0, np_)
        wgt = wgt2[par]
        wk = f"wgt{par}"
        avk = [f"av{sl}" for sl in range(nsl)]
        if nsl < 128:
            P.add("pool", CALL("memset", av[r_, nsl:128], 0.0), writes=["avz"])
            avk.append("avz")
        P.add("dve", CALL("tensor_tensor", out=tg[r_, :], in0=av[r_, :], in1=av[r_, :], op=ALU.mult), reads=avk, writes=["tg"])
        P.add("dve", CALL("tensor_scalar", out=tg[r_, :], in0=tg[r_, :], scalar1=0.044715, scalar2=1.0, op0=ALU.mult, op1=ALU.add), reads=["tg"], writes=["tg"])
        P.add("dve", CALL("tensor_tensor", out=tg[r_, :], in0=tg[r_, :], in1=av[r_, :], op=ALU.mult), reads=["tg"] + avk, writes=["tg"])
        P.add("act", CALL("activation", out=tg[r_, :], in_=tg[r_, :], func=AF.Sigmoid, scale=1.5957691216057308), reads=["tg"], writes=["tg"])
        P.add("dve", CALL("tensor_tensor", out=tg[r_, :], in0=tg[r_, :], in1=av[r_, :], op=ALU.mult), reads=["tg"] + avk, writes=["tg"])
        P.add("dve", CALL("tensor_tensor", out=wgt[r_, :], in0=tg[r_, :], in1=gate[r_, :, :].rearrange("p h k -> p (h k)"), op=ALU.mult), reads=["tg", "gate"], writes=[wk])

    def part_V(x_ap, np_, xkey, par):
        r_ = slice(0, np_)
        idx = idx2[par]
        ik = f"idx{par}"
        wgt = wgt2[par]
        wk = f"wgt{par}"
        for sl in range(nsl):
            i = gst["gv"] % 3
            gst["gv"] += 1
            g8, j8 = sl // 8, sl % 8
            D_ = Dg[g8 % 2]
            dk = f"Dg{g8 % 2}"
            if j8 == 0:
                P.add("dve", CALL("tensor_tensor", out=D_[r_, :, r_], in0=G.identF[r_, r_].unsqueeze(1).to_broadcast([np_, 8, np_]),
                                  in1=wgt[r_, sl:sl + 8].unsqueeze(2).to_broadcast([np_, 8, np_]), op=ALU.mult), reads=[wk, "cst"], writes=[dk])
            vb_ = vbf[sl % 2]
            vk = f"vbf{sl % 2}"
            P.add("pool", CALL("indirect_dma_start", out=gs[i][r_, :], out_offset=None, in_=G.p_v, in_offset=bass.IndirectOffsetOnAxis(ap=idx[r_, sl:sl + 1], axis=0)),
                  reads=[ik], writes=[f"gs{i}"], dma=1, key=f"gs{i}")
            P.add("act", CALL("activation", out=vb_[r_, :], in_=gs[i][r_, :], func=AF.Copy), reads=[f"gs{i}"], writes=[vk])
            for hf in range(2):
                P.add("pe", CALL("matmul", PS(hf, 512)[r_, :], lhsT=D_[r_, j8, r_], rhs=vb_[r_, hf * 512:(hf + 1) * 512], start=(sl == 0), stop=(sl == nsl - 1)),
                      reads=[dk, vk], writes=[pk(hf)])
        for hf in range(2):
            xo = x_ap[:, hf * 512:(hf + 1) * 512]
            P.add("dve", CALL("tensor_tensor", out=xo, in0=xo, in1=PS(hf, 512)[r_, :], op=ALU.add), reads=[pk(hf), xkey], writes=[xkey])
    nt_ = len(tiles)
    part_topk(*tiles[0], 0)
    part_U(*tiles[0], 0)
    part_gelu(*tiles[0], 0)
    for t in range(1, nt_):
        P.begin_record((0, 1))
        part_V(*tiles[t - 1], (t - 1) % 2)
        ra = P.end_record()
        P.begin_record((2, 3, 4, 5, 6, 7))
        part_topk(*tiles[t], t % 2)
        part_U(*tiles[t], t % 2)
        rb = P.end_record()
        P.replay([ra, rb])
        part_gelu(*tiles[t], t % 2)
    part_V(*tiles[nt_ - 1], (nt_ - 1) % 2)
    P.barrier()
    P.ps_lo = 0
    AR.release(m0)


def final_phase(G):
    P, AR = G.P, G.AR
    m0 = AR.mark()
    ob = [AR.alloc(1024) for _ in range(2)]
    jb = AR.alloc(1024, BF16)
    ss = AR.alloc(8)
    gb_fin = AR.alloc(1024)
    P.add("sp", CALL("dma_start", out=gb_fin[:, :], in_=G.g_fin.partition_broadcast(128)), writes=["gb_fin"], dma=1, key="gb_fin")
    tiles = [(G.xp[:, t, :], 128, f"xp{t}", G.y_p[t * 128:(t + 1) * 128, :]) for t in range(16)] + [(G.xs[0:TS, :], TS, "xs", G.y_s)]
    for n, (x_ap, np_, xkey, o_ap) in enumerate(tiles):
        r_ = slice(0, np_)
        o = ob[n % 2]
        G.rms_rstd(x_ap, np_, xkey, jb, "jb", ss[:, 0:1], "fss")
        P.add("dve", CALL("scalar_tensor_tensor", out=o[r_, :], in0=x_ap, scalar=ss[r_, 0:1], in1=gb_fin[r_, :], op0=ALU.mult, op1=ALU.mult),
              reads=[xkey, "fss", "gb_fin"], writes=[f"ob{n % 2}"])
        P.add("sp", CALL("dma_start", out=o_ap, in_=o[r_, :]), reads=[f"ob{n % 2}"], dma=1, key=f"ob{n % 2}")
    AR.release(m0)

def build(dbg=(), stop=None, **opts):
    build.opts = opts
    nc = bass.Bass("TRN2", target_bir_lowering=False)
    es = ExitStack()
    with es:
        _build(nc, es, dbg, stop)
    return nc


def _build(nc, es, dbg, stop):
    def din(name, shape, dt=F32):
        return nc.dram_tensor(name, list(shape), dt, kind="ExternalInput").ap()

    def dout(name, shape, dt=F32):
        return nc.dram_tensor(name, list(shape), dt, kind="ExternalOutput").ap()

    x_p = din("x_p", [T, D])
    x_s = din("x_s", [TS, D])
    st_pool = din("st_pool", [DEPTH, NSQ * 15, BW])
    st_conv = din("st_conv", [DEPTH, NSQ * 3, 3 * BW])
    st_delta = din("st_delta", [DEPTH, NSQ, 4, 128, 128])
    c_k = din("c_k", [DEPTH, NSQ, 256, BW])
    c_v = din("c_v", [DEPTH, NSQ, 256, BW])
    memp = din("memp", [256, D])
    g_mix = din("g_mix", [DEPTH, D])
    w_in = din("w_in", [DEPTH, D, IN_COLS])
    w_conv = din("w_conv", [DEPTH, 4, 3 * BW])
    a_log = din("a_log", [DEPTH, 4])
    dt_bias = din("dt_bias", [DEPTH, 4])
    g_dn = din("g_dn", [DEPTH, 128])
    w_grp = din("w_grp", [DEPTH, 4, 128, 128])
    p_scale = din("p_scale", [DEPTH, BW])
    g_mem = din("g_mem", [DEPTH, D])
    w_mkv = din("w_mkv", [DEPTH, D, 2 * BW])
    w_br = din("w_br", [DEPTH, 3, BW, D])
    w_o = din("w_o", [DEPTH, D, D])
    g_ffn = din("g_ffn", [DEPTH, D])
    w_pq = din("w_pq", [DEPTH, D, 2048])
    subk = din("subk", [DEPTH, 16, 128, 128])
    p_u = din("p_u", [DEPTH * NEXP, D])
    p_v = din("p_v", [DEPTH * NEXP, D])
    g_fin = din("g_fin", [D])
    consts = din("consts", [128, 1024])

    y_p = dout("y_p", [T, D])
    y_s = dout("y_s", [TS, D])
    o_pool_p = dout("o_pool_p", [DEPTH, 15, BW])
    o_conv_p = dout("o_conv_p", [DEPTH, 3, 3 * BW])
    o_delta_p = dout("o_delta_p", [DEPTH, 4, 128, 128])
    o_mk = dout("o_mk", [DEPTH, 256, BW])
    o_mv = dout("o_mv", [DEPTH, 256, BW])
    o_pool_s = dout("o_pool_s", [DEPTH, NSQ * 15, BW])
    o_conv_s = dout("o_conv_s", [DEPTH, NSQ * 3, 3 * BW])
    o_delta_s = dout("o_delta_s", [DEPTH, NSQ, 4, 128, 128])

    P = Prog(nc)
    dbg_outs = {}

    def sb(name, shape, dt=F32):
        return es.enter_context(nc.sbuf_tensor(name, shape, dt))

    xp = sb("xp", [128, 16, D])
    xs = sb("xs", [128, D])
    cst = sb("cst", [128, 8, 128])
    identB_t = sb("identB", [128, 128], BF16)
    identB = identB_t[:, :]
    gmix_sb = sb("gmix_sb", [128, 8])
    gmem_sb = sb("gmem_sb", [128, 8])
    gb_ffn = sb("gb_ffn", [128, D])
    wconv_sb = sb("wconv_sb", [128, 12, 4])
    alog_b = sb("alog_b", [128, 4])
    dtb_b = sb("dtb_b", [128, 4])
    gdn_sb = sb("gdn_sb", [128, 1])
    psc_sb = sb("psc_sb", [128, 4])
    wgrp_b = sb("wgrp_b", [128, 4, 128], BF16)
    Sst = sb("Sst", [128, 4, 128])
    hist = sb("hist", [128, 12, 3])
    epsb = sb("epsb", [128, 1])
    ARENA_N = 32000
    arena_t = sb("arena", [128, ARENA_N])
    AR = Arena(arena_t, ARENA_N)
    psum = es.enter_context(nc.psum_tensor("psum", [128, 4096], F32))

    identF = cst[:, 0, :]
    onesF = cst[:, 1, :]
    Ltri = cst[:, 2, :]
    SLm = cst[:, 3, :]
    Ltri4 = cst[:, 4, :]
    SL4 = cst[:, 5, :]
    seqmask = cst[:, 6, 0:16]
    rcnt = cst[:, 7, 0:64]
    iota16 = cst[:, 7, 64:80]

    def PS(i, n=512, off=0):
        return psum[:, i * 512 + off:i * 512 + off + n]

    def PSB(i, n=1024, off=0):
        return psum[:, i * 512:(i + 1) * 512].bitcast(BF16)[:, off:off + n]

    def pk(i):
        return f"ps{i}"

    def dump(name, ap, reads, shape, view=None, **kw):
        if name not in dbg:
            return
        o = dout("dbg_" + name, shape, ap.dtype)
        dbg_outs[name] = o
        if view:
            o = o.rearrange(view, **kw)
        P.add("sp", CALL("dma_start", out=o, in_=ap), reads=reads, dma=1, key="dbg_" + name)

    P.add("sp", CALL("dma_start", out=cst[:].rearrange("p a b -> p (a b)"), in_=consts), writes=["cst"], dma=1, key="cst")
    P.add("pool", CALL("memset", epsb[:], EPS), writes=["epsb"])
    P.add("act", CALL("activation", out=identB, in_=identF, func=AF.Copy), reads=["cst"], writes=["identB"])
    for i in range(16):
        P.add("sp", CALL("dma_start", out=xp[:, i, :], in_=x_p[i * 128:(i + 1) * 128, :]), writes=[f"xp{i}"], dma=1, key=f"xp{i}")
    P.add("sp", CALL("dma_start", out=xs[0:TS, :], in_=x_s), writes=["xs"], dma=1, key="xs")

    def rms_rstd(x_ap, np_, xkey, junk, junk_key, ss, sskey):
        P.add("act", CALL("activation", out=junk[0:np_, :], in_=x_ap, func=AF.Square, accum_out=ss[0:np_, :]),
              reads=[xkey], writes=[junk_key, sskey])
        P.add("act", CALL("activation", out=ss[0:np_, :], in_=ss[0:np_, :], func=AF.Sqrt, scale=1.0 / D, bias=epsb[0:np_, :]),
              reads=[sskey, "epsb"], writes=[sskey])
        P.add("dve", CALL("reciprocal", out=ss[0:np_, :], in_=ss[0:np_, :]), reads=[sskey], writes=[sskey])

    def to_featmajor(src_bf, np_, srckey, dst, dstkey_fn, tsl, gsb=None, gkey=None):
        b = P.ps()
        for k in range(8):
            P.add("pe", CALL("transpose", out=PSB(b)[:, k * 128:k * 128 + np_], in_=src_bf[0:np_, k * 128:(k + 1) * 128], identity=identB[0:np_, 0:np_]),
                  reads=[srckey, "identB"], writes=[pk(b)])
        src = PSB(b).rearrange("p (k t) -> p k t", k=8)[:, :, 0:np_]
        if gsb is None:
            P.add("act", CALL("activation", out=dst[:, :, tsl], in_=src, func=AF.Copy), reads=[pk(b)], writes=[dstkey_fn])
        else:
            P.add("dve", CALL("tensor_tensor", out=dst[:, :, tsl], in0=src, in1=gsb[:, :].unsqueeze(2).to_broadcast([128, 8, np_]), op=ALU.mult),
                  reads=[pk(b), gkey], writes=[dstkey_fn])

    wst = {"i": 0, "s": 0}

    def load_w(dram2d, krows, c0, ncols, stg, wbf, caster=None):
        i = wst["i"] % len(wbf)
        wst["i"] += 1
        si = wst["s"] % len(stg)
        wst["s"] += 1
        s_ap, s_key = stg[si]
        b_ap, b_key = wbf[i]
        sv = s_ap[:, 0:krows * ncols].rearrange("p (k n) -> p k n", k=krows)
        bv = b_ap[:, 0:krows * ncols].rearrange("p (k n) -> p k n", k=krows)
        src = dram2d.rearrange("(k p) n -> p k n", p=128)[:, :, c0:c0 + ncols]
        P.add("sp", CALL("dma_start", out=sv, in_=src), writes=[s_key], dma=1, key=s_key)
        eng = caster or ("act" if (wst["i"] % 2 == 0) else "dve")
        if eng == "act":
            P.add("act", CALL("activation", out=bv, in_=sv, func=AF.Copy), reads=[s_key], writes=[b_key])
        else:
            P.add(eng, CALL("tensor_copy", out=bv, in_=sv), reads=[s_key], writes=[b_key])
        return bv, b_key

    def layer_params(l):
        P.add("sp", CALL("dma_start", out=gmix_sb[:], in_=g_mix[l].rearrange("(k p) -> p k", p=128), allow_slow_non_contiguous=True), writes=["gmix"], dma=1, key="gmix")
        P.add("sp", CALL("dma_start", out=gmem_sb[:], in_=g_mem[l].rearrange("(k p) -> p k", p=128), allow_slow_non_contiguous=True), writes=["gmem"], dma=1, key="gmem")
        P.add("sp", lambda e: [e.dma_start(out=wconv_sb[:, :, j], in_=w_conv[l, j].rearrange("(c p) -> p c", p=128), allow_slow_non_contiguous=True) for j in range(4)],
              writes=["wconv"], dma=4, key="wconv")
        P.add("sp", CALL("dma_start", out=gdn_sb[:], in_=g_dn[l].rearrange("(p o) -> p o", o=1), allow_slow_non_contiguous=True), writes=["gdn"], dma=1, key="gdn")
        P.add("sp", CALL("dma_start", out=psc_sb[:], in_=p_scale[l].rearrange("(g p) -> p g", p=128), allow_slow_non_contiguous=True), writes=["psc"], dma=1, key="psc")
        P.add("sp", CALL("dma_start", out=gb_ffn[:], in_=g_ffn[l].partition_broadcast(128)), writes=["gb_ffn"], dma=1, key="gb_ffn")
        P.add("sp", CALL("dma_start", out=alog_b[:], in_=a_log[l].partition_broadcast(128)), writes=["alog"], dma=1, key="alog")
        P.add("sp", CALL("dma_start", out=dtb_b[:], in_=dt_bias[l].partition_broadcast(128)), writes=["dtb"], dma=1, key="dtb")
        P.add("act", CALL("activation", out=alog_b[:], in_=alog_b[:], func=AF.Exp), reads=["alog"], writes=["alog"])
        P.add("dve", CALL("tensor_scalar", out=alog_b[:], in0=alog_b[:], scalar1=-1.0, scalar2=0.0, op0=ALU.mult, op1=ALU.add), reads=["alog"], writes=["alog"])

    G = type("G", (), {})()
    for k_, v_ in list(locals().items()):
        setattr(G, k_, v_)
    for k_, v_ in build.opts.items():
        setattr(G, k_, v_)

    for l in range(DEPTH):
        layer_params(l)
        mixer_phase(G, l)
        dump(f"xp_m{l}", xp[:, :, :], [f"xp{i}" for i in range(16)], [T, D], "(t p) d -> p t d", p=128)
        dump(f"xs_m{l}", xs[0:TS, :], ["xs"], [TS, D])
        if stop == ("mixer", l):
            break
        peer_phase(G, l)
        dump(f"xp_p{l}", xp[:, :, :], [f"xp{i}" for i in range(16)], [T, D], "(t p) d -> p t d", p=128)
        dump(f"xs_p{l}", xs[0:TS, :], ["xs"], [TS, D])
        if stop == ("peer", l):
            break
    else:
        final_phase(G)
    P.emit(es, maxops=build.opts.get('maxops'))
    G.P = P
    build.last = G


def _shard_inputs(inp):
    f = lambda a: np.ascontiguousarray(np.asarray(a, dtype=np.float32))
    shared = dict(
        g_mix=f(inp["g_mix"]), w_in=f(inp["w_in"]), w_conv=f(inp["w_conv"]), a_log=f(inp["a_log"]), dt_bias=f(inp["dt_bias"]),
        g_dn=f(inp["g_dn_out"]), w_grp=f(inp["w_pool_grp"]), p_scale=f(inp["pool_scale"]), g_mem=f(inp["g_mem"]),
        w_mkv=f(inp["w_mem_kv"]), w_br=f(inp["w_branch"]), w_o=f(inp["w_o"]), g_ffn=f(inp["g_ffn"]), w_pq=f(inp["w_peer_q"]),
        subk=f(inp["peer_subkeys"]).reshape(DEPTH, 16, 128, 128), p_u=f(inp["peer_u"]).reshape(DEPTH * NEXP, D),
        p_v=f(inp["peer_v"]).reshape(DEPTH * NEXP, D), g_fin=f(inp["g_final"]), consts=make_consts())
    maps = []
    for c in range(NCORES):
        sl = slice(c * NSQ, (c + 1) * NSQ)
        m = dict(shared)
        m["x_p"] = f(inp["x_prompt"][c])
        m["x_s"] = f(inp["x_sample"][sl]).reshape(TS, D)
        m["st_pool"] = f(inp["state_pool"][:, sl]).reshape(DEPTH, NSQ * 15, BW)
        m["st_conv"] = f(inp["state_conv"][:, sl]).reshape(DEPTH, NSQ * 3, 3 * BW)
        m["st_delta"] = f(inp["state_delta"][:, sl])
        m["c_k"] = f(inp["cache_mem_k"][:, sl]).reshape(DEPTH, NSQ, 256, BW)
        m["c_v"] = f(inp["cache_mem_v"][:, sl]).reshape(DEPTH, NSQ, 256, BW)
        m["memp"] = f(inp["mem_prompt"][c])
        maps.append(m)
    return maps


def _gather_outputs(res):
    R = res.results
    cat = lambda k, ax: np.concatenate([np.asarray(r[k]) for r in R], axis=ax)
    stk = lambda k, ax: np.stack([np.asarray(r[k]) for r in R], axis=ax)
    y_p = stk("y_p", 0)
    y_s = cat("y_s", 0).reshape(NCORES * NSQ, 4, D)
    pool_p = stk("o_pool_p", 1)
    conv_p = stk("o_conv_p", 1)
    delta_p = stk("o_delta_p", 1)
    mk = stk("o_mk", 1).reshape(DEPTH, NCORES, 256, 4, 128)
    mv = stk("o_mv", 1).reshape(DEPTH, NCORES, 256, 4, 128)
    pool_s = cat("o_pool_s", 1).reshape(DEPTH, NCORES * NSQ, 15, BW)
    conv_s = cat("o_conv_s", 1).reshape(DEPTH, NCORES * NSQ, 3, 3 * BW)
    delta_s = cat("o_delta_s", 1)
    outs = (y_p, y_s, pool_p, conv_p, delta_p, mk, mv, pool_s, conv_s, delta_s)
    return tuple(np.ascontiguousarray(o, dtype=np.float32) for o in outs)


def kernel(**inputs):
    maps = _shard_inputs(inputs)
    nc = build()
    res = run_bass_kernel_spmd(nc, maps, core_ids=list(range(NCORES)))
    return _gather_outputs(res)
```

```python
import numpy as np
from contextlib import ExitStack
import concourse.bass as bass
import concourse.mybir as mybir
from concourse.bass_utils import run_bass_kernel_spmd

F32 = mybir.dt.float32
BF16 = mybir.dt.bfloat16
I32 = mybir.dt.int32
U32 = mybir.dt.uint32
ALU = mybir.AluOpType
AF = mybir.ActivationFunctionType
AX = mybir.AxisListType

NCORES = 8
D = 1024
T = 2048
NSQ = 16
TS = 64
DEPTH = 2
BW = 512
IN_COLS = 6152
OFF_Q = 512
OFF_Z = 2048
OFF_BA = 2560
OFF_XQ = 2568
OFF_GATE = 3080
EPS = 1e-6
NKEY = 128
NEXP = 16384
SEM_CH = 30000
NEG = -1.0e30
NG = 6


def CALL(name, *a, **k):
    return lambda e: getattr(e, name)(*a, **k)


class Op:
    __slots__ = ("eng", "fn", "deps", "is_dma", "key", "nparts", "seq", "signaled", "dma_val")


class Prog:
    ENGS = ("pe", "act", "dve", "pool", "sp")

    def __init__(self, nc):
        self.nc = nc
        self.ops = []
        self.last_w = {}
        self.readers = {}
        self.dma_cnt = {}
        self.dma_gen = {}
        self.psi = 0
        self.inames = {}

    def begin_record(self, banks):
        self.rec = []
        self.ps_banks = list(banks)
        self.ps_bi = 0

    def end_record(self):
        r = self.rec
        self.rec = None
        self.ps_banks = None
        return r

    def replay(self, recs):
        recs = [r for r in recs if r]
        n = max(len(r) for r in recs)
        for i in range(n):
            for r in recs:
                if i < len(r):
                    self.add(*r[i][0], **r[i][1])

    def add(self, eng, fn, reads=(), writes=(), dma=0, key=None):
        if getattr(self, "rec", None) is not None:
            self.rec.append(((eng, fn), dict(reads=list(reads), writes=list(writes), dma=dma, key=key)))
            return None
        op = Op()
        op.eng = eng
        op.fn = fn
        op.is_dma = dma > 0
        op.nparts = dma
        op.signaled = False
        op.seq = 0
        deps = set()
        excl = [r for r in reads if isinstance(r, str) and r[:2] == "ps" and r[2:].isdigit()]
        if excl:
            reads = [r for r in reads if r not in excl]
            writes = list(writes) + excl
        for r in reads:
            w = self.last_w.get(r)
            if w is not None:
                deps.add(w)
        for w_ in writes:
            w = self.last_w.get(w_)
            if w is not None:
                deps.add(w)
            for rd in self.readers.get(w_, ()):
                deps.add(rd)
        op.deps = deps
        for r in reads:
            self.readers.setdefault(r, []).append(op)
        for w_ in writes:
            self.last_w[w_] = op
            self.readers[w_] = []
        if op.is_dma:
            g = self.dma_gen.get(key, 0)
            c = self.dma_cnt.get((key, g), 0) + dma * 16
            if c > SEM_CH:
                g += 1
                self.dma_gen[key] = g
                c = dma * 16
            self.dma_cnt[(key, g)] = c
            op.key = (key, g)
            op.dma_val = c
        self.ops.append(op)
        return op

    def barrier(self):
        last = {}
        dmas = {}
        for op in self.ops:
            if op.is_dma:
                dmas[op.key] = op
            else:
                last[op.eng] = op
        deps = set(last.values()) | set(dmas.values())
        bops = []
        for e in ("pe", "act", "dve", "pool", "sp"):
            op = self.add(e, lambda en: en.nop(), ())
            op.deps = set(deps)
            bops.append(op)
        self.last_w = {}
        self.readers = {}

    def ps(self):
        if getattr(self, "ps_banks", None):
            i = self.ps_banks[self.ps_bi % len(self.ps_banks)]
            self.ps_bi += 1
            return i
        lo = getattr(self, "ps_lo", 0)
        if self.psi < lo:
            self.psi = lo
        i = self.psi
        self.psi = self.psi + 1
        if self.psi >= 8:
            self.psi = lo
        return i

    def emit(self, es, maxops=None):
        nc = self.nc
        if maxops is not None:
            self.ops = self.ops[:maxops]
            cnt2 = {}
            for op in self.ops:
                if op.is_dma:
                    cnt2[op.key] = op.dma_val
            self.dma_cnt = cnt2
        for op in self.ops:
            nd = set()
            for d in op.deps:
                if (not d.is_dma) and (not op.is_dma) and d.eng == "pe" and op.eng == "pe":
                    continue
                nd.add(d)
                d.signaled = True
            op.deps = nd
        cnt = {e: 0 for e in self.ENGS}
        for op in self.ops:
            if not op.is_dma and op.signaled:
                cnt[op.eng] += 1
                op.seq = cnt[op.eng]
        eng_sems = {}
        for e in self.ENGS:
            n = (cnt[e] + SEM_CH - 1) // SEM_CH
            eng_sems[e] = [es.enter_context(nc.semaphore(f"s_{e}_{i}")) for i in range(n)]
        dma_sems = {}
        for i, k in enumerate(self.dma_cnt.keys()):
            dma_sems[k] = es.enter_context(nc.semaphore(f"d_{i}"))
        self.nsem = sum(len(v) for v in eng_sems.values()) + len(dma_sems)
        per_eng = {e: [o for o in self.ops if o.eng == e] for e in self.ENGS}
        block = es.enter_context(nc.Block())

        def run(engname, eobj):
            waited = {}

            def wait(sem, val):
                if waited.get(id(sem), 0) >= val:
                    return
                waited[id(sem)] = val
                eobj.wait_ge(sem, val)

            for op in per_eng[engname]:
                need = {}
                for d in op.deps:
                    if d.is_dma:
                        sem = dma_sems[d.key]
                        v = d.dma_val
                    else:
                        si = (d.seq - 1) // SEM_CH
                        sem = eng_sems[d.eng][si]
                        v = d.seq - si * SEM_CH
                    k = id(sem)
                    if k not in need or need[k][1] < v:
                        need[k] = (sem, v)
                for sem, v in need.values():
                    wait(sem, v)
                if op.is_dma:
                    sem = dma_sems[op.key]
                    insts = op.fn(eobj)
                    if not isinstance(insts, (list, tuple)):
                        insts = [insts]
                    assert len(insts) == op.nparts
                    for ins in insts:
                        ins.then_inc(sem, 16)
                else:
                    ins = op.fn(eobj)
                    try:
                        self.inames[ins.ins.name] = op
                    except Exception:
                        pass
                    if op.signaled:
                        si = (op.seq - 1) // SEM_CH
                        ins.then_inc(eng_sems[op.eng][si], 1)
            if engname == "sp":
                for k, c in self.dma_cnt.items():
                    wait(dma_sems[k], c)

        block.tensor(lambda e: run("pe", e))
        block.scalar(lambda e: run("act", e))
        block.vector(lambda e: run("dve", e))
        block.gpsimd(lambda e: run("pool", e))
        block.sync(lambda e: run("sp", e))


class Arena:
    def __init__(self, t, nf32):
        self.t = t
        self.n = nf32
        self.off = 0
        self.hw = 0

    def mark(self):
        return self.off

    def release(self, m):
        self.off = m

    def alloc(self, n, dt=F32):
        nf = n if dt in (F32, I32, U32) else (n + 1) // 2
        nf = (nf + 7) // 8 * 8
        a = self.off
        self.off += nf
        self.hw = max(self.hw, self.off)
        assert self.off <= self.n, f"arena overflow {self.off} > {self.n}"
        v = self.t[:, a:a + nf]
        if dt != F32:
            v = v.bitcast(dt)
        return v[:, 0:n]


def make_consts():
    c = np.zeros((128, 8, 128), np.float32)
    i = np.arange(128)
    c[:, 0, :] = np.eye(128)
    c[:, 1, :] = 1.0
    same64 = (i[:, None] // 64) == (i[None, :] // 64)
    same4 = ((i[:, None] // 4) == (i[None, :] // 4)) & (i[:, None] < 64) & (i[None, :] < 64)
    c[:, 2, :] = same64 & (i[:, None] <= i[None, :])
    c[:, 3, :] = same64 & (i[:, None] > i[None, :])
    c[:, 4, :] = same4 & (i[:, None] <= i[None, :])
    c[:, 5, :] = same4 & (i[:, None] > i[None, :])
    c[:, 6, 0:16] = (i[:, None] // 4) == np.arange(16)[None, :]
    for g, w in enumerate((2, 4, 8, 16)):
        c[:, 7, g * 16:(g + 1) * 16] = 1.0 / np.minimum(np.arange(16) + 1, w)
    c[:, 7, 64:80] = np.arange(16)[None, :]
    return c.reshape(128, 1024)


def _mixer_bufs(G, NT, sample):
    AR = G.AR
    B = type("B", (), {})()
    B.NT = NT
    B.hT = AR.alloc(8 * NT, BF16).rearrange("p (k t) -> p k t", k=8)
    B.xbf = [AR.alloc(1024, BF16)] * 2
    B.junk = B.xbf[0]
    B.ss = AR.alloc(8)
    B.stg = [(AR.alloc(2048), f"stg{i}") for i in range(2)]
    B.wbf = [(AR.alloc(2048, BF16), f"wbf{i}") for i in range(3)]
    B.pre = [AR.alloc(3 + NT) for _ in range(2)]
    B.cv = [AR.alloc(NT) for _ in range(2)]
    B.qkvc = AR.alloc(12 * NT).rearrange("p (c t) -> p c t", c=12)
    B.zs = AR.alloc(4 * NT).rearrange("p (c t) -> p c t", c=4)
    B.xqT = AR.alloc(4 * NT, BF16).rearrange("p (c t) -> p c t", c=4)
    B.yT = AR.alloc(12 * NT, BF16).rearrange("p (c t) -> p c t", c=12)
    B.macc = AR.alloc(4 * NT).rearrange("p (c t) -> p c t", c=4)
    B.mT = AR.alloc(8 * NT, BF16).rearrange("p (c t) -> p c t", c=8)
    B.sqb = [AR.alloc(NT) for _ in range(2)]
    B.rinv = [AR.alloc(NT) for _ in range(2)]
    B.sig = B.sqb
    B.prod = B.rinv
    B.dT = [AR.alloc(NT, BF16) for _ in range(2)]
    B.pA = AR.alloc(19 * 16 if sample else 15 + NT)
    B.pB = AR.alloc(19 * 16 if sample else 15 + NT)
    B.t16 = AR.alloc(16)
    B.ktok = AR.alloc(512).rearrange("p (h d) -> p h d", h=4)
    B.vtok = AR.alloc(512).rearrange("p (h d) -> p h d", h=4)
    B.ba = AR.alloc(32)
    B.gcc = AR.alloc(16)
    names = ["gL", "gcr", "egr", "dm", "t1", "dmT", "t2", "Pa", "Pb", "Qa", "Qb", "R", "u", "wT", "attnT", "vn", "qg", "kbg", "vb", "kd", "osq", "rr", "y1", "kdsc"]
    B.dw = [{}, {}]
    shared = ("gL", "dm", "dmT", "osq", "rr", "y1", "kdsc") if sample else ()
    for n in names:
        if n in shared:
            B.dw[0][n] = B.dw[1][n] = AR.alloc(128)
        else:
            B.dw[0][n] = AR.alloc(128)
            B.dw[1][n] = AR.alloc(128)
    B.dw_shared = shared
    B.pexp = [AR.alloc(256) for _ in range(2)]
    B.pn = [AR.alloc(256, BF16) for _ in range(2)]
    B.pT = [AR.alloc(256, BF16).rearrange("p (c t) -> p c t", c=2) for _ in range(2)]
    B.asm = AR.alloc(32)
    if not sample:
        B.uT = AR.alloc(4 * (15 + NT)).rearrange("p (c t) -> p c t", c=4)
        B.kTm = AR.alloc(4 * 256, BF16).rearrange("p (h m) -> p h m", h=4)
        B.vm = AR.alloc(2 * 512, BF16).rearrange("p (c n) -> p c n", c=2)
        B.hTm = B.hT
        qflat = B.qkvc.rearrange("p c t -> p (c t)")
        B.memx = qflat[:, 0:1024]
        B.kvrow = qflat[:, 1024:3072].rearrange("p (j n) -> p j n", j=2)
        B.rowbuf = qflat[:, 0:1536]
    else:
        B.uTs = AR.alloc(4 * 16 * 19).rearrange("p (c s e) -> p c s e", c=4, s=16)
        B.pre_s = [AR.alloc(16 * 7).rearrange("p (s e) -> p s e", s=16) for _ in range(2)]
        B.hist_s = AR.alloc(12 * 48).rearrange("p (c s e) -> p c s e", c=12, s=16)
        B.cvout = AR.alloc(12 * 48).rearrange("p (c s e) -> p c s e", c=12, s=16)
        B.ld1536 = AR.alloc(1536)
        B.ld512 = [AR.alloc(512) for _ in range(2)]
        mk_ = AR.mark()
        B.Sh = [AR.alloc(16 * 128).rearrange("p (s d) -> p s d", s=16) for _ in range(2)]
        B.rowbuf = B.Sh[0].rearrange("p s d -> p (s d)")[:, 0:1536]
        B.kdm = AR.alloc(16 * 128).rearrange("p (s d) -> p s d", s=16)
        B.wTm = AR.alloc(1088)
        B.o1 = AR.alloc(64)
        B.oTs = AR.alloc(64)
        hw_ = AR.mark()
        AR.release(mk_)
        B.xqm = AR.alloc(4 * 1088, BF16).rearrange("p (h r) -> p h r", h=4)
        B.kvs = [AR.alloc(1024) for _ in range(2)]
        B.kvb = [AR.alloc(1024, BF16).rearrange("p (c n) -> p c n", c=2) for _ in range(2)]
        B.kTs = [AR.alloc(1024, BF16).rearrange("p (h m) -> p h m", h=4) for _ in range(2)]
        B.pTall = [AR.alloc(256, BF16).rearrange("p (c t) -> p c t", c=2) for _ in range(4)]
        AR.release(max(hw_, AR.mark()))
    return B


def _mem_kv(G, B, l):
    P, PS, PSB, pk = G.P, G.PS, G.PSB, G.pk
    for j in range(2):
        P.add("sp", CALL("dma_start", out=B.memx[:, :], in_=G.memp[j * 128:(j + 1) * 128, :]), writes=["memx"], dma=1, key="memx")
        G.rms_rstd(B.memx[:, :], 128, "memx", B.junk, "xbf0", B.ss[:, 0:1], "ss0")
        xb = B.xbf[j % 2]
        P.add("act", CALL("activation", out=xb[:, :], in_=B.memx[:, :], func=AF.Copy, scale=B.ss[:, 0:1]),
              reads=["memx", "ss0"], writes=["xbf0"])
        G.to_featmajor(xb, 128, "xbf0", B.hTm, f"hTm{j}", slice(j * 128, (j + 1) * 128), gsb=G.gmem_sb, gkey="gmem")
    for g in range(4):
        wv, wk = G.load_w(G.w_mkv[l], 8, g * 256, 256, B.stg, B.wbf)
        for j in range(2):
            b = P.ps()
            for k in range(8):
                P.add("pe", CALL("matmul", PS(b, 256), lhsT=B.hTm[:, k, j * 128:(j + 1) * 128], rhs=wv[:, k, :], start=(k == 0), stop=(k == 7)),
                      reads=[f"hTm{j}", wk], writes=[pk(b)])
            P.add("act", CALL("activation", out=B.kvrow[:, j, g * 256:(g + 1) * 256], in_=PS(b, 256), func=AF.Copy),
                  reads=[pk(b)], writes=[f"kvrow{j}_{g}"])
            if g >= 2:
                P.add("dve", CALL("tensor_copy", out=B.vm[:, j, (g - 2) * 256:(g - 1) * 256], in_=PS(b, 256)),
                      reads=[pk(b)], writes=[f"vm{j}_{g}"])
        if g < 2:
            for cc in range(2):
                b = P.ps()
                for k in range(8):
                    P.add("pe", CALL("matmul", PS(b, 256), lhsT=wv[:, k, cc * 128:(cc + 1) * 128], rhs=B.hTm[:, k, :], start=(k == 0), stop=(k == 7)),
                          reads=["hTm0", "hTm1", wk], writes=[pk(b)])
                P.add("act", CALL("activation", out=B.kTm[:, g * 2 + cc, :], in_=PS(b, 256), func=AF.Copy),
                      reads=[pk(b)], writes=[f"kTm{g * 2 + cc}"])
    for j in range(2):
        P.add("pool", CALL("dma_start", out=G.o_mk[l, j * 128:(j + 1) * 128, :], in_=B.kvrow[:, j, 0:512]),
              reads=[f"kvrow{j}_0", f"kvrow{j}_1"], dma=1, key=f"o_mk{j}")
        P.add("pool", CALL("dma_start", out=G.o_mv[l, j * 128:(j + 1) * 128, :], in_=B.kvrow[:, j, 512:1024]),
              reads=[f"kvrow{j}_2", f"kvrow{j}_3"], dma=1, key=f"o_mv{j}")
    B.kTm_keys = [f"kTm{h}" for h in range(4)]
    B.vm_keys = [f"vm{j}_{g}" for j in range(2) for g in (2, 3)]


def _proj(G, B, l, wdram, krows, c0, nch, srcT, srckeys, NT, consume, chunk_w=128):
    P, PS, pk = G.P, G.PS, G.pk
    i = 0
    while i < nch:
        ng = min(2, nch - i)
        ncols = 128 * ng if chunk_w == 128 else chunk_w
        wv, wk = G.load_w(wdram, krows, c0 + i * 128, ncols, B.stg, B.wbf)
        for cc in range(ng):
            b = P.ps()
            for k in range(krows):
                P.add("pe", CALL("matmul", PS(b, NT)[0:chunk_w, :], lhsT=wv[:, k, cc * 128:cc * 128 + chunk_w], rhs=srcT[:, k, 0:NT],
                                                               start=(k == 0), stop=(k == krows - 1)),
                      reads=list(srckeys) + [wk], writes=[pk(b)])
            consume(i + cc, b)
        i += ng


def _delta_tile(G, B, l, j, np_, tsl, sample, S_ap=None):
    P, PS, PSB, pk = G.P, G.PS, G.PSB, G.pk
    LT = (G.Ltri4 if sample else G.Ltri)
    SLx = (G.SL4 if sample else G.SLm)
    I_ = G.identF
    ones = G.onesF
    r_ = slice(0, np_)
    for nm, c0, dst in (("ktok", 4, B.ktok), ("vtok", 8, B.vtok)):
        b = P.ps()
        for h in range(4):
            P.add("pe", CALL("transpose", out=PS(b, 128, h * 128)[r_, :], in_=B.qkvc[:, c0 + h, tsl], identity=I_),
                  reads=[f"qkvc{c0 + h}", "cst"], writes=[pk(b)])
        P.add("act", CALL("activation", out=dst[r_, :, :], in_=PS(b, 512)[r_, :].rearrange("p (h d) -> p h d", h=4), func=AF.Copy),
              reads=[pk(b)], writes=[nm])
    b = P.ps()
    P.add("pe", CALL("matmul", PS(b, 4)[r_, :], lhsT=LT[r_, r_], rhs=B.ba[r_, 8:12], start=True, stop=True), reads=["ba", "cst"], writes=[pk(b)])
    P.add("pe", CALL("matmul", PS(b, 4, 8)[r_, :], lhsT=LT[r_, r_], rhs=B.ba[r_, 8:12], start=True, stop=False), reads=["ba", "cst"], writes=[pk(b)])
    P.add("pe", CALL("matmul", PS(b, 4, 8)[r_, :], lhsT=SLx[r_, r_], rhs=B.ba[r_, 8:12], start=False, stop=True), reads=["ba", "cst"], writes=[pk(b)])
    P.add("dve", CALL("tensor_copy", out=B.gcc[r_, 0:4], in_=PS(b, 4)[r_, :]), reads=[pk(b)], writes=["gcc"])
    P.add("dve", CALL("tensor_copy", out=B.gcc[r_, 8:12], in_=PS(b, 4, 8)[r_, :]), reads=[pk(b)], writes=["gcc"])
    P.add("act", CALL("activation", out=B.gcc[r_, 4:8], in_=B.gcc[r_, 0:4], func=AF.Exp), reads=["gcc"], writes=["gcc"])
    def head_body(h):
        W = B.dw[h % 2]
        wn = lambda n, h=h: (f"dws_{n}" if n in B.dw_shared else f"dw{h % 2}_{n}")
        qT = B.qkvc[:, h, tsl]
        kT = B.qkvc[:, 4 + h, tsl]
        beta = B.ba[r_, h:h + 1]
        nbeta = B.ba[r_, 4 + h:5 + h]
        gcol = B.ba[r_, 8 + h:9 + h]
        gc_c = B.gcc[r_, h:h + 1]
        egc_c = B.gcc[r_, 4 + h:5 + h]
        gl_c = B.gcc[r_, 8 + h:9 + h]
        P.add("dve", CALL("tensor_scalar", out=W["gL"][r_, r_], in0=LT[r_, r_], scalar1=gcol, scalar2=0.0, op0=ALU.mult, op1=ALU.add),
              reads=["ba", "cst"], writes=[wn("gL")])
        b = P.ps()
        P.add("pe", CALL("matmul", PS(b, np_), lhsT=ones[r_, :], rhs=W["gL"][r_, r_], start=True, stop=True), reads=[wn("gL"), "cst"], writes=[pk(b)])
        P.add("act", CALL("activation", out=W["gcr"][:, r_], in_=PS(b, np_), func=AF.Copy), reads=[pk(b)], writes=[wn("gcr")])
        P.add("act", CALL("activation", out=W["egr"][:, r_], in_=PS(b, np_), func=AF.Exp), reads=[pk(b)], writes=[wn("egr")])
        P.add("dve", CALL("tensor_scalar", out=W["dm"][r_, r_], in0=W["gcr"][r_, r_], scalar1=gc_c, scalar2=0.0, op0=ALU.subtract, op1=ALU.max),
              reads=[wn("gcr"), "gcc"], writes=[wn("dm")])
        P.add("act", CALL("activation", out=W["dm"][r_, r_], in_=W["dm"][r_, r_], func=AF.Exp, scale=-1.0), reads=[wn("dm")], writes=[wn("dm")])
        P.add("pool", CALL("tensor_tensor", out=W["t1"][r_, r_], in0=W["dm"][r_, r_], in1=SLx[r_, r_], op=ALU.mult), reads=[wn("dm"), "cst"], writes=[wn("t1")])
        P.add("dve", CALL("tensor_scalar", out=W["dmT"][r_, r_], in0=W["gcr"][r_, r_], scalar1=gc_c, scalar2=0.0, op0=ALU.subtract, op1=ALU.min),
              reads=[wn("gcr"), "gcc"], writes=[wn("dmT")])
        P.add("act", CALL("activation", out=W["dmT"][r_, r_], in_=W["dmT"][r_, r_], func=AF.Exp), reads=[wn("dmT")], writes=[wn("dmT")])
        P.add("pool", CALL("tensor_tensor", out=W["t2"][r_, r_], in0=W["dmT"][r_, r_], in1=LT[r_, r_], op=ALU.mult), reads=[wn("dmT"), "cst"], writes=[wn("t2")])
        b = P.ps()
        P.add("pe", CALL("matmul", PS(b, np_)[r_, :], lhsT=kT, rhs=kT, start=True, stop=True), reads=[f"qkvc{4 + h}"], writes=[pk(b)])
        P.add("dve", CALL("scalar_tensor_tensor", out=W["Pa"][r_, r_], in0=PS(b, np_)[r_, :], scalar=nbeta, in1=W["t1"][r_, r_], op0=ALU.mult, op1=ALU.mult),
              reads=[pk(b), "ba", wn("t1")], writes=[wn("Pa")])
        b = P.ps()
        P.add("pe", CALL("transpose", out=PS(b, np_)[r_, :], in_=W["Pa"][r_, r_], identity=I_[r_, r_]), reads=[wn("Pa"), "cst"], writes=[pk(b)])
        P.add("act", CALL("activation", out=W["Qa"][r_, r_], in_=PS(b, np_)[r_, :], func=AF.Copy), reads=[pk(b)], writes=[wn("Qa")])
        P.add("dve", CALL("tensor_tensor", out=W["R"][r_, r_], in0=PS(b, np_)[r_, :], in1=I_[r_, r_], op=ALU.add), reads=[pk(b), "cst"], writes=[wn("R")])
        nst = 1 if sample else 5
        Pk, Qk, Pn, Qn = "Pa", "Qa", "Pb", "Qb"
        for k in range(nst):
            bP = P.ps()
            P.add("pe", CALL("matmul", PS(bP, np_)[r_, :], lhsT=W[Qk][r_, r_], rhs=W[Pk][r_, r_], start=True, stop=True),
                  reads=[wn(Pk), wn(Qk)], writes=[pk(bP)])
            P.add("act", CALL("activation", out=W[Pn][r_, r_], in_=PS(bP, np_)[r_, :], func=AF.Copy), reads=[pk(bP)], writes=[wn(Pn)])
            if k < nst - 1:
                bQ = P.ps()
                P.add("pe", CALL("matmul", PS(bQ, np_)[r_, :], lhsT=W[Pk][r_, r_], rhs=W[Qk][r_, r_], start=True, stop=True),
                      reads=[wn(Pk), wn(Qk)], writes=[pk(bQ)])
                P.add("dve", CALL("tensor_copy", out=W[Qn][r_, r_], in_=PS(bQ, np_)[r_, :]), reads=[pk(bQ)], writes=[wn(Qn)])
            bR = P.ps()
            P.add("pe", CALL("matmul", PS(bR, np_)[r_, :], lhsT=W[Pn][r_, r_], rhs=W["R"][r_, r_], start=True, stop=True),
                  reads=[wn(Pn), wn("R")], writes=[pk(bR)])
            P.add("dve", CALL("tensor_tensor", out=W["R"][r_, r_], in0=PS(bR, np_)[r_, :], in1=W["R"][r_, r_], op=ALU.add), reads=[pk(bR), wn("R")], writes=[wn("R")])
            Pk, Pn = Pn, Pk
            Qk, Qn = Qn, Qk
        P.add("dve", CALL("tensor_scalar", out=W["vb"][r_, :], in0=B.vtok[r_, h, :], scalar1=beta, scalar2=0.0, op0=ALU.mult, op1=ALU.add),
              reads=["vtok", "ba"], writes=[wn("vb")])
        P.add("dve", CALL("tensor_scalar", out=W["kbg"][r_, :], in0=B.ktok[r_, h, :], scalar1=beta, scalar2=egc_c, op0=ALU.mult, op1=ALU.mult),
              reads=["ktok", "ba", "gcc"], writes=[wn("kbg")])
        P.add("act", CALL("activation", out=W["kdsc"][r_, 0:1], in_=gc_c, func=AF.Exp, scale=-1.0, bias=gl_c), reads=["gcc"], writes=[wn("kdsc")])
        P.add("dve", CALL("tensor_scalar", out=W["kd"][r_, :], in0=B.ktok[r_, h, :], scalar1=W["kdsc"][r_, 0:1], scalar2=0.0, op0=ALU.mult, op1=ALU.add),
              reads=["ktok", wn("kdsc")], writes=[wn("kd")])
        P.add("pool", CALL("tensor_tensor", out=W["qg"][:, r_], in0=qT, in1=W["egr"][:, r_], op=ALU.mult), reads=[f"qkvc{h}", wn("egr")], writes=[wn("qg")])
        b = P.ps()
        P.add("pe", CALL("matmul", PS(b, 128)[r_, :], lhsT=W["R"][r_, r_], rhs=W["vb"][r_, :], start=True, stop=True), reads=[wn("R"), wn("vb")], writes=[pk(b)])
        P.add("act", CALL("activation", out=W["u"][r_, :], in_=PS(b, 128)[r_, :], func=AF.Copy), reads=[pk(b)], writes=[wn("u")])
        b = P.ps()
        P.add("pe", CALL("matmul", PS(b, np_), lhsT=W["kbg"][r_, :], rhs=W["R"][r_, r_], start=True, stop=True), reads=[wn("R"), wn("kbg")], writes=[pk(b)])
        P.add("act", CALL("activation", out=W["wT"][:, r_], in_=PS(b, np_), func=AF.Copy), reads=[pk(b)], writes=[wn("wT")])
        b = P.ps()
        P.add("pe", CALL("matmul", PS(b, np_)[r_, :], lhsT=kT, rhs=qT, start=True, stop=True), reads=[f"qkvc{4 + h}", f"qkvc{h}"], writes=[pk(b)])
        P.add("dve", CALL("tensor_tensor", out=W["attnT"][r_, r_], in0=PS(b, np_)[r_, :], in1=W["t2"][r_, r_], op=ALU.mult), reads=[pk(b), wn("t2")], writes=[wn("attnT")])
        if not sample:
            Sk = f"S{h}"
            Sh_ = G.Sst[:, h, :]
            bo = P.ps()
            for ci in range(2):
                rr_ = slice(ci * 64, ci * 64 + 64)
                bw = P.ps()
                P.add("pe", CALL("matmul", PS(bw, 128), lhsT=W["wT"][:, 0:128], rhs=Sh_, start=True, stop=True), reads=[wn("wT"), Sk], writes=[pk(bw)])
                P.add("dve", CALL("tensor_tensor", out=W["vn"][rr_, :], in0=W["u"][rr_, :], in1=PS(bw, 128)[rr_, :], op=ALU.subtract),
                      reads=[pk(bw), wn("u")], writes=[wn("vn") + str(ci)])
                P.add("pe", CALL("matmul", PS(bo, 64, 256 + ci * 64), lhsT=Sh_, rhs=W["qg"][:, rr_], start=True, stop=False), reads=[wn("qg"), Sk], writes=[pk(bo)])
                P.add("pe", CALL("matmul", PS(bo, 64, 256 + ci * 64), lhsT=W["vn"][rr_, :], rhs=W["attnT"][rr_, rr_], start=False, stop=True),
                      reads=[wn("vn") + str(ci), wn("attnT")], writes=[pk(bo)])
                bs = P.ps()
                P.add("pe", CALL("matmul", PS(bs, 128), lhsT=W["kd"][rr_, :], rhs=W["vn"][rr_, :], start=True, stop=True), reads=[wn("kd"), wn("vn") + str(ci)], writes=[pk(bs)])
                P.add("dve", CALL("scalar_tensor_tensor", out=Sh_, in0=Sh_, scalar=W["egr"][:, ci * 64 + 63:ci * 64 + 64], in1=PS(bs, 128), op0=ALU.mult, op1=ALU.add),
                      reads=[pk(bs), wn("egr"), Sk], writes=[Sk])
            o_ap = PS(bo, np_, 256)
            o_key = pk(bo)
        else:
            Shb = B.Sh[h % 2]
            Sk = f"Sh{h % 2}"
            P.add("sp", CALL("dma_start", out=Shb[:, :, :], in_=G.st_delta[l, :, h].rearrange("s k v -> k s v")), writes=[Sk], dma=1, key=Sk)
            P.add("dve", CALL("tensor_copy", out=B.wTm[:, 0:1088].rearrange("p (s r) -> p s r", r=68)[:, :, 0:4], in_=W["wT"][:, 0:64].rearrange("p (s i) -> p s i", i=4)),
                  reads=[wn("wT")], writes=["wTm"])
            bw = P.ps()
            for s in range(16):
                P.add("pe", CALL("matmul", PS(bw, 128)[0:64, :], lhsT=B.wTm[:, s * 64:(s + 1) * 64], rhs=Shb[:, s, :], start=(s == 0), stop=(s == 15)),
                      reads=["wTm", Sk], writes=[pk(bw)])
            P.add("dve", CALL("tensor_tensor", out=W["vn"][0:64, :], in0=W["u"][0:64, :], in1=PS(bw, 128)[0:64, :], op=ALU.subtract), reads=[pk(bw), wn("u")], writes=[wn("vn") + "0"])
            bo = P.ps()
            for s in range(16):
                P.add("pe", CALL("matmul", PS(bo, 4, 4 * s), lhsT=Shb[:, s, :], rhs=W["qg"][:, 4 * s:4 * s + 4], start=True, stop=True),
                      reads=[wn("qg"), Sk], writes=[pk(bo)])
            P.add("act", CALL("activation", out=B.o1[:, 0:64], in_=PS(bo, 64), func=AF.Copy), reads=[pk(bo)], writes=["o1"])
            b2 = P.ps()
            P.add("pe", CALL("matmul", PS(b2, 64), lhsT=W["vn"][0:64, :], rhs=W["attnT"][0:64, 0:64], start=True, stop=True), reads=[wn("vn") + "0", wn("attnT")], writes=[pk(b2)])
            P.add("dve", CALL("tensor_tensor", out=B.oTs[:, 0:64], in0=PS(b2, 64), in1=B.o1[:, 0:64], op=ALU.add), reads=[pk(b2), "o1"], writes=["oTs"])
            P.add("pool", CALL("tensor_tensor", out=B.kdm[0:64, :, :], in0=W["kd"][0:64, :].unsqueeze(1).to_broadcast([64, 16, 128]),
                                                         in1=G.seqmask[0:64, :].unsqueeze(2).to_broadcast([64, 16, 128]), op=ALU.mult),
                  reads=[wn("kd"), "cst"], writes=["kdm"])
            for s in range(16):
                bs = P.ps()
                P.add("pe", CALL("matmul", PS(bs, 128), lhsT=B.kdm[0:64, s, :], rhs=W["vn"][0:64, :], start=True, stop=True), reads=["kdm", wn("vn") + "0"], writes=[pk(bs)])
                P.add("dve", CALL("scalar_tensor_tensor", out=Shb[:, s, :], in0=Shb[:, s, :], scalar=W["egr"][:, 4 * s + 3:4 * s + 4], in1=PS(bs, 128), op0=ALU.mult, op1=ALU.add),
                      reads=[pk(bs), wn("egr"), Sk], writes=[Sk])
            P.add("pool", CALL("dma_start", out=G.o_delta_s[l, :, h].rearrange("s k v -> k s v"), in_=Shb[:, :, :]), reads=[Sk], dma=1, key="o_" + Sk)
            o_ap = B.oTs[:, 0:64]
            o_key = "oTs"
        P.add("act", CALL("activation", out=W["osq"][:, r_], in_=o_ap, func=AF.Square), reads=[o_key], writes=[wn("osq")])
        bq = P.ps()
        P.add("pe", CALL("matmul", PS(bq, np_), lhsT=ones, rhs=W["osq"][:, r_], start=True, stop=True), reads=[wn("osq"), "cst"], writes=[pk(bq)])
        P.add("act", CALL("activation", out=W["rr"][:, r_], in_=PS(bq, np_), func=AF.Sqrt, scale=1.0 / 128, bias=G.epsb[:, 0:1]), reads=[pk(bq), "epsb"], writes=[wn("rr")])
        P.add("dve", CALL("reciprocal", out=W["rr"][:, r_], in_=W["rr"][:, r_]), reads=[wn("rr")], writes=[wn("rr")])
        P.add("dve", CALL("scalar_tensor_tensor", out=W["y1"][:, r_], in0=o_ap, scalar=G.gdn_sb[:, 0:1], in1=W["rr"][:, r_], op0=ALU.mult, op1=ALU.mult),
              reads=[o_key, "gdn", wn("rr")], writes=[wn("y1")])
        P.add("pool", CALL("tensor_tensor", out=B.yT[:, 4 + h, tsl], in0=W["y1"][:, r_], in1=B.zs[:, h, tsl], op=ALU.mult), reads=[wn("y1"), f"zs{h}"], writes=[f"yT{4 + h}_{j}"])

    if sample:
        for h in range(4):
            head_body(h)
    else:
        for h0 in (0, 2):
            recs = []
            for hh, banks in ((h0, (0, 1, 2, 3)), (h0 + 1, (4, 5, 6, 7))):
                P.begin_record(banks)
                head_body(hh)
                recs.append(P.end_record())
            P.replay(recs)


def _attn_softmax(G, B, sc_ap, sc_key, np_, h, out_writes):
    P, PS, PSB, pk = G.P, G.PS, G.PSB, G.pk
    r_ = slice(0, np_)
    i = h % 2
    sc = 128.0 ** -0.5
    mx = B.asm[r_, 4 * i:4 * i + 1]
    nmx = B.asm[r_, 4 * i + 1:4 * i + 2]
    rs = B.asm[r_, 4 * i + 2:4 * i + 3]
    ak = f"asm{i}"
    P.add("dve", CALL("reduce_max", out=mx, in_=sc_ap, axis=AX.X), reads=[sc_key], writes=[ak])
    P.add("dve", CALL("tensor_scalar", out=nmx, in0=mx, scalar1=-sc, scalar2=0.0, op0=ALU.mult, op1=ALU.add), reads=[ak], writes=[ak])
    P.add("act", CALL("activation", out=B.pexp[i][r_, :], in_=sc_ap, func=AF.Exp, scale=sc, bias=nmx, accum_out=rs), reads=[sc_key, ak], writes=[f"pexp{i}", ak])
    P.add("dve", CALL("reciprocal", out=rs, in_=rs), reads=[ak], writes=[ak])
    P.add("dve", CALL("tensor_scalar", out=B.pn[i][r_, :], in0=B.pexp[i][r_, :], scalar1=rs, scalar2=0.0, op0=ALU.mult, op1=ALU.add), reads=[f"pexp{i}", ak], writes=[f"pn{i}"])
    bt = P.ps()
    for mc in range(2):
        P.add("pe", CALL("transpose", out=PSB(bt, np_, mc * 128), in_=B.pn[i][r_, mc * 128:(mc + 1) * 128], identity=G.identB[r_, r_]),
              reads=[f"pn{i}", "identB"], writes=[pk(bt)])
    P.add("act", CALL("activation", out=B.pT[i][:, :, r_], in_=PSB(bt, 256).rearrange("p (c t) -> p c t", c=2)[:, :, r_], func=AF.Copy), reads=[pk(bt)], writes=[f"pT{i}"])
    return B.pT[i], f"pT{i}"


def _mixer_st(G, B, l, st, tiles, sample):
    P, PS, PSB, pk = G.P, G.PS, G.PSB, G.pk
    NT = sum(t[1] for t in tiles)
    nt = len(tiles)
    hkeys = [f"hT{j}" for j in range(nt)]
    for j, (x_ap, np_, xkey) in enumerate(tiles):
        G.rms_rstd(x_ap, np_, xkey, B.junk, "xbf0", B.ss[:, 0:1], "ss0")
        xb = B.xbf[j % 2]
        P.add("act", CALL("activation", out=xb[0:np_, :], in_=x_ap, func=AF.Copy, scale=B.ss[0:np_, 0:1]),
              reads=[xkey, "ss0"], writes=["xbf0"])
        G.to_featmajor(xb, np_, "xbf0", B.hT, hkeys[j], slice(j * 128, j * 128 + np_), gsb=G.gmix_sb, gkey="gmix")

    win = (2, 4, 8, 16)
    if not sample:
        L = 15 + NT
        if st == 0:
            P.add("pool", CALL("memset", B.uT[:, :, 0:15], 0.0), writes=[f"uT{c}" for c in range(4)])

        def pool_consume(c, b):
            P.add("act", CALL("activation", out=B.uT[:, c, 15:L], in_=PS(b, NT), func=AF.Copy), reads=[pk(b)], writes=[f"uT{c}"])
            a = B.uT[:, c, :]
            bufs = [(B.pA, "pA"), (B.pB, "pB")]
            src, skey = a, f"uT{c}"
            sh = 1
            for s_ in range(c + 1):
                dst, dkey = bufs[s_ % 2]
                lo = 2 * sh - 1
                P.add("dve", CALL("tensor_tensor", out=dst[:, lo:L], in0=src[:, lo:L], in1=src[:, lo - sh:L - sh], op=ALU.add),
                      reads=[skey], writes=[dkey])
                src, skey = dst, dkey
                sh *= 2
            dT = B.dT[c % 2]
            dk = f"dT{c % 2}"
            P.add("dve", CALL("scalar_tensor_tensor", out=dT[:, 0:NT], in0=src[:, 15:L], scalar=1.0 / win[c], in1=a[:, 15:L], op0=ALU.mult, op1=ALU.subtract),
                  reads=[skey, f"uT{c}"], writes=[dk])
            if st == 0:
                P.add("dve", CALL("tensor_tensor", out=B.t16[:, 0:16], in0=src[:, 15:31], in1=G.rcnt[:, c * 16:(c + 1) * 16], op=ALU.mult), reads=[skey, "cst"], writes=["t16"])
                P.add("dve", CALL("tensor_tensor", out=dT[:, 0:16], in0=B.t16[:, 0:16], in1=a[:, 15:31], op=ALU.subtract), reads=["t16", f"uT{c}", dk], writes=[dk])
            b2 = P.ps()
            P.add("pe", CALL("matmul", PS(b2, NT), lhsT=G.wgrp_b[:, c, :], rhs=dT[:, 0:NT], start=True, stop=True), reads=[dk, "wgrp_b"], writes=[pk(b2)])
            P.add("act", CALL("activation", out=B.yT[:, c, 0:NT], in_=PS(b2, NT), func=AF.Copy, scale=G.psc_sb[:, c:c + 1]), reads=[pk(b2), "psc"], writes=[f"yT{c}_all"])
        _proj(G, B, l, G.w_in[l], 8, 0, 4, B.hT, hkeys, NT, pool_consume)
        P.add("pool", CALL("tensor_copy", out=B.uT[:, :, 0:15], in_=B.uT[:, :, NT:NT + 15]), writes=[f"uT{c}" for c in range(4)])
    else:
        def pool_consume(c, b):
            P.add("act", CALL("activation", out=B.uTs[:, c, :, 15:19], in_=PS(b, 64).rearrange("p (s i) -> p s i", i=4), func=AF.Copy), reads=[pk(b)], writes=[f"uTs{c}"])
            a = B.uTs[:, c, :, :]
            pA = B.pA[:, 0:304].rearrange("p (s e) -> p s e", e=19)
            pB = B.pB[:, 0:304].rearrange("p (s e) -> p s e", e=19)
            bufs = [(pA, "pA"), (pB, "pB")]
            src, skey = a, f"uTs{c}"
            sh = 1
            for s_ in range(c + 1):
                dst, dkey = bufs[s_ % 2]
                lo = 2 * sh - 1
                P.add("dve", CALL("tensor_tensor", out=dst[:, :, lo:19], in0=src[:, :, lo:19], in1=src[:, :, lo - sh:19 - sh], op=ALU.add),
                      reads=[skey], writes=[dkey])
                src, skey = dst, dkey
                sh *= 2
            dT = B.dT[c % 2]
            dk = f"dT{c % 2}"
            P.add("dve", CALL("scalar_tensor_tensor", out=dT[:, 0:64].rearrange("p (s i) -> p s i", i=4), in0=src[:, :, 15:19], scalar=1.0 / win[c], in1=a[:, :, 15:19], op0=ALU.mult, op1=ALU.subtract),
                  reads=[skey, f"uTs{c}"], writes=[dk])
            b2 = P.ps()
            P.add("pe", CALL("matmul", PS(b2, NT), lhsT=G.wgrp_b[:, c, :], rhs=dT[:, 0:NT], start=True, stop=True), reads=[dk, "wgrp_b"], writes=[pk(b2)])
            P.add("act", CALL("activation", out=B.yT[:, c, 0:NT], in_=PS(b2, NT), func=AF.Copy, scale=G.psc_sb[:, c:c + 1]), reads=[pk(b2), "psc"], writes=[f"yT{c}_all"])
        _proj(G, B, l, G.w_in[l], 8, 0, 4, B.hT, hkeys, NT, pool_consume)

    def qkv_consume(c, b):
        if not sample:
            pre = B.pre[c % 2]
            pkey = f"pre{c % 2}"
            cv = B.cv[c % 2]
            ckey = f"cv{c % 2}"
            P.add("pool", CALL("tensor_copy", out=pre[:, 0:3], in_=G.hist[:, c, :]), reads=[f"hist{c}"], writes=[pkey + "h"])
            P.add("act", CALL("activation", out=pre[:, 3:3 + NT], in_=PS(b, NT), func=AF.Copy), reads=[pk(b)], writes=[pkey])
            P.add("dve", CALL("tensor_scalar", out=cv[:, 0:NT], in0=pre[:, 0:NT], scalar1=G.wconv_sb[:, c, 0:1], scalar2=0.0, op0=ALU.mult, op1=ALU.add),
                  reads=[pkey, pkey + "h", "wconv"], writes=[ckey])
            for j in range(1, 4):
                P.add("dve", CALL("scalar_tensor_tensor", out=cv[:, 0:NT], in0=pre[:, j:j + NT], scalar=G.wconv_sb[:, c, j:j + 1], in1=cv[:, 0:NT], op0=ALU.mult, op1=ALU.add),
                      reads=[pkey, pkey + "h", "wconv", ckey], writes=[ckey])
            P.add("pool", CALL("tensor_copy", out=G.hist[:, c, :], in_=pre[:, NT:NT + 3]), reads=[pkey], writes=[f"hist{c}"])
            P.add("act", CALL("activation", out=B.qkvc[:, c, 0:NT], in_=cv[:, 0:NT], func=AF.Silu), reads=[ckey], writes=[f"qkvc{c}"])
        else:
            pre = B.pre_s[c % 2]
            pkey = f"pre{c % 2}"
            cv = B.cv[c % 2][:, 0:64].rearrange("p (s i) -> p s i", i=4)
            ckey = f"cv{c % 2}"
            P.add("pool", CALL("tensor_copy", out=pre[:, :, 0:3], in_=B.hist_s[:, c, :, :]), reads=[f"hist_s{c // 4}"], writes=[pkey + "h"])
            P.add("act", CALL("activation", out=pre[:, :, 3:7], in_=PS(b, 64).rearrange("p (s i) -> p s i", i=4), func=AF.Copy), reads=[pk(b)], writes=[pkey])
            P.add("dve", CALL("tensor_scalar", out=cv, in0=pre[:, :, 0:4], scalar1=G.wconv_sb[:, c, 0:1], scalar2=0.0, op0=ALU.mult, op1=ALU.add),
                  reads=[pkey, pkey + "h", "wconv"], writes=[ckey])
            for j in range(1, 4):
                P.add("dve", CALL("scalar_tensor_tensor", out=cv, in0=pre[:, :, j:j + 4], scalar=G.wconv_sb[:, c, j:j + 1], in1=cv, op0=ALU.mult, op1=ALU.add),
                      reads=[pkey, pkey + "h", "wconv", ckey], writes=[ckey])
            P.add("pool", CALL("tensor_copy", out=B.cvout[:, c, :, :], in_=pre[:, :, 4:7]), reads=[pkey], writes=[f"cvout{c // 4}"])
            P.add("act", CALL("activation", out=B.qkvc[:, c, 0:NT], in_=B.cv[c % 2][:, 0:64], func=AF.Silu), reads=[ckey], writes=[f"qkvc{c}"])
    _proj(G, B, l, G.w_in[l], 8, OFF_Q, 12, B.hT, hkeys, NT, qkv_consume)

    for c in range(8):
        i = c % 2
        P.add("act", CALL("activation", out=B.sqb[i][:, 0:NT], in_=B.qkvc[:, c, 0:NT], func=AF.Square), reads=[f"qkvc{c}"], writes=[f"sqb{i}"])
        b = P.ps()
        P.add("pe", CALL("matmul", PS(b, NT), lhsT=G.onesF, rhs=B.sqb[i][:, 0:NT], start=True, stop=True), reads=[f"sqb{i}", "cst"], writes=[pk(b)])
        P.add("act", CALL("activation", out=B.rinv[i][:, 0:NT], in_=PS(b, NT), func=AF.Sqrt, scale=1.0, bias=G.epsb[:, 0:1]), reads=[pk(b), "epsb"], writes=[f"rinv{i}"])
        P.add("dve", CALL("reciprocal", out=B.rinv[i][:, 0:NT], in_=B.rinv[i][:, 0:NT]), reads=[f"rinv{i}"], writes=[f"rinv{i}"])
        scl = (128.0 ** -0.5) if c < 4 else 1.0
        P.add("dve", CALL("scalar_tensor_tensor", out=B.qkvc[:, c, 0:NT], in0=B.rinv[i][:, 0:NT], scalar=scl, in1=B.qkvc[:, c, 0:NT], op0=ALU.mult, op1=ALU.mult),
              reads=[f"rinv{i}", f"qkvc{c}"], writes=[f"qkvc{c}"])

    def z_consume(c, b):
        P.add("act", CALL("activation", out=B.zs[:, c, 0:NT], in_=PS(b, NT), func=AF.Silu), reads=[pk(b)], writes=[f"zs{c}"])
    _proj(G, B, l, G.w_in[l], 8, OFF_Z, 4, B.hT, hkeys, NT, z_consume)

    def xq_consume(c, b):
        P.add("act", CALL("activation", out=B.xqT[:, c, 0:NT], in_=PS(b, NT), func=AF.Copy), reads=[pk(b)], writes=[f"xqT{c}"])
    _proj(G, B, l, G.w_in[l], 8, OFF_XQ, 4, B.hT, hkeys, NT, xq_consume)

    wba, wbak = G.load_w(G.w_in[l], 8, OFF_BA, 8, B.stg, B.wbf)

    if sample:
        _sample_attn(G, B, l)
        P.barrier()
        P.add("pool", CALL("memset", B.wTm[:, :], 0.0), writes=["wTm"])
    def per_tile(j, x_ap, np_, xkey):
        r_ = slice(0, np_)
        tsl = slice(j * 128, j * 128 + np_)
        b = P.ps()
        for k in range(8):
            P.add("pe", CALL("matmul", PS(b, 8)[r_, :], lhsT=B.hT[:, k, tsl], rhs=wba[:, k, :], start=(k == 0), stop=(k == 7)), reads=[hkeys[j], wbak], writes=[pk(b)])
        ba = B.ba
        P.add("act", CALL("activation", out=ba[r_, 0:4], in_=PS(b, 4)[r_, :], func=AF.Sigmoid), reads=[pk(b)], writes=["ba"])
        P.add("dve", CALL("tensor_scalar", out=ba[r_, 4:8], in0=ba[r_, 0:4], scalar1=-1.0, scalar2=0.0, op0=ALU.mult, op1=ALU.add), reads=["ba"], writes=["ba"])
        P.add("dve", CALL("tensor_tensor", out=ba[r_, 12:16], in0=PS(b, 4, 4)[r_, :], in1=G.dtb_b[r_, :], op=ALU.add), reads=[pk(b), "dtb"], writes=["ba"])
        P.add("dve", CALL("scalar_tensor_tensor", out=ba[r_, 16:20], in0=ba[r_, 12:16], scalar=-1.0, in1=ba[r_, 12:16], op0=ALU.mult, op1=ALU.max), reads=["ba"], writes=["ba"])
        P.add("act", CALL("activation", out=ba[r_, 16:20], in_=ba[r_, 16:20], func=AF.Exp, scale=-1.0), reads=["ba"], writes=["ba"])
        P.add("act", CALL("activation", out=ba[r_, 16:20], in_=ba[r_, 16:20], func=AF.Ln, scale=1.0, bias=G.onesF[r_, 0:1]), reads=["ba", "cst"], writes=["ba"])
        P.add("dve", CALL("scalar_tensor_tensor", out=ba[r_, 12:16], in0=ba[r_, 12:16], scalar=0.0, in1=ba[r_, 16:20], op0=ALU.max, op1=ALU.add), reads=["ba"], writes=["ba"])
        P.add("dve", CALL("tensor_tensor", out=ba[r_, 8:12], in0=ba[r_, 12:16], in1=G.alog_b[r_, :], op=ALU.mult), reads=["ba", "alog"], writes=["ba"])
        _delta_tile(G, B, l, j, np_, tsl, sample)
        if not sample:
            for h in range(4):
                bs_ = P.ps()
                P.add("pe", CALL("matmul", PS(bs_, 256)[r_, :], lhsT=B.xqT[:, h, tsl], rhs=B.kTm[:, h, :], start=True, stop=True),
                      reads=[f"xqT{h}", f"kTm{h}"], writes=[pk(bs_)])
                pT, pTk = _attn_softmax(G, B, PS(bs_, 256)[r_, :], pk(bs_), np_, h, None)
                bo = P.ps()
                for mc in range(2):
                    P.add("pe", CALL("matmul", PS(bo, np_), lhsT=B.vm[:, mc, h * 128:(h + 1) * 128], rhs=pT[:, mc, r_], start=(mc == 0), stop=(mc == 1)),
                          reads=[pTk] + B.vm_keys, writes=[pk(bo)])
                P.add("act", CALL("activation", out=B.yT[:, 8 + h, tsl], in_=PS(bo, np_), func=AF.Copy), reads=[pk(bo)], writes=[f"yT{8 + h}_{j}"])

    for j_, (x_ap_, np__, xkey_) in enumerate(tiles):
        per_tile(j_, x_ap_, np__, xkey_)

    ykeys = [[f"yT{c}_all" for c in range(4)], [f"yT{4 + h}_{j}" for h in range(4) for j in range(nt)], [f"yT{8 + h}_{j}" for h in range(4) for j in range(nt)]]
    if sample:
        ykeys[2] = [f"yT{8 + h}_0" for h in range(4)]

    for half in range(2):
        for n in range(3):
            wg0, wgk0 = G.load_w(G.w_in[l], 8, OFF_GATE + n * 1024 + half * 512, 256, B.stg, B.wbf)
            wb_, wbk = G.load_w(G.w_br[l, n], 4, half * 512, 512, B.stg, B.wbf)
            wg1, wgk1 = G.load_w(G.w_in[l], 8, OFF_GATE + n * 1024 + half * 512 + 256, 256, B.stg, B.wbf)
            for jj in range(4):
                wg, wgk = (wg0, wgk0) if jj < 2 else (wg1, wgk1)
                cc = jj % 2
                i = jj % 2
                bg = P.ps()
                for k in range(8):
                    P.add("pe", CALL("matmul", PS(bg, NT), lhsT=wg[:, k, cc * 128:(cc + 1) * 128], rhs=B.hT[:, k, 0:NT], start=(k == 0), stop=(k == 7)),
                          reads=hkeys + [wgk], writes=[pk(bg)])
                P.add("act", CALL("activation", out=B.sig[i][:, 0:NT], in_=PS(bg, NT), func=AF.Sigmoid), reads=[pk(bg)], writes=[f"sqb{i}"])
                bb = P.ps()
                for c in range(4):
                    P.add("pe", CALL("matmul", PS(bb, NT), lhsT=wb_[:, c, jj * 128:(jj + 1) * 128], rhs=B.yT[:, n * 4 + c, 0:NT], start=(c == 0), stop=(c == 3)),
                          reads=ykeys[n] + [wbk], writes=[pk(bb)])
                if n == 0:
                    P.add("dve", CALL("tensor_tensor", out=B.macc[:, jj, 0:NT], in0=PS(bb, NT), in1=B.sig[i][:, 0:NT], op=ALU.mult), reads=[pk(bb), f"sqb{i}"], writes=[f"macc{jj}"])
                else:
                    P.add("dve", CALL("tensor_tensor", out=B.prod[i][:, 0:NT], in0=PS(bb, NT), in1=B.sig[i][:, 0:NT], op=ALU.mult), reads=[pk(bb), f"sqb{i}"], writes=[f"rinv{i}"])
                    if n == 1:
                        P.add("pool", CALL("tensor_tensor", out=B.macc[:, jj, 0:NT], in0=B.macc[:, jj, 0:NT], in1=B.prod[i][:, 0:NT], op=ALU.add), reads=[f"rinv{i}", f"macc{jj}"], writes=[f"macc{jj}"])
                    else:
                        P.add("pool", CALL("tensor_tensor", out=B.mT[:, half * 4 + jj, 0:NT], in0=B.macc[:, jj, 0:NT], in1=B.prod[i][:, 0:NT], op=ALU.add),
                              reads=[f"rinv{i}", f"macc{jj}"], writes=[f"mT{half * 4 + jj}"])

    mkeys = [f"mT{c}" for c in range(8)]
    for q in range(4):
        wo, wok = G.load_w(G.w_o[l], 8, q * 256, 256, B.stg, B.wbf)
        for j, (x_ap, np_, xkey) in enumerate(tiles):
            tsl = slice(j * 128, j * 128 + np_)
            b = P.ps()
            for k in range(8):
                P.add("pe", CALL("matmul", PS(b, 256)[0:np_, :], lhsT=B.mT[:, k, tsl], rhs=wo[:, k, :], start=(k == 0), stop=(k == 7)),
                      reads=mkeys + [wok], writes=[pk(b)])
            xo = x_ap[:, q * 256:(q + 1) * 256]
            P.add("dve", CALL("tensor_tensor", out=xo, in0=xo, in1=PS(b, 256)[0:np_, :], op=ALU.add), reads=[pk(b), xkey], writes=[xkey])


def _sample_prep(G, B, l):
    P, PS, PSB, pk = G.P, G.PS, G.PSB, G.pk
    I_ = G.identF
    P.add("pool", CALL("memset", B.xqm[:, :, :], 0.0), writes=["xqm"])
    P.add("sp", CALL("dma_start", out=B.ld1536[0:48, :], in_=G.st_conv[l]), writes=["ld1536"], dma=1, key="ld1536")
    for g in range(3):
        b = P.ps()
        for cc in range(4):
            c = g * 4 + cc
            P.add("pe", CALL("transpose", out=PS(b, 48, cc * 48), in_=B.ld1536[0:48, c * 128:(c + 1) * 128], identity=I_[0:48, 0:48]), reads=["ld1536", "cst"], writes=[pk(b)])
        P.add("act", CALL("activation", out=B.hist_s[:, g * 4:(g + 1) * 4, :, :].rearrange("p c s e -> p (c s e)"), in_=PS(b, 192), func=AF.Copy), reads=[pk(b)], writes=[f"hist_s{g}"])
    for hf in range(2):
        ld = B.ld512[hf]
        P.add("sp", CALL("dma_start", out=ld[0:120, :], in_=G.st_pool[l, hf * 120:(hf + 1) * 120, :]), writes=[f"ld512{hf}"], dma=1, key=f"ld512{hf}")
        b = P.ps()
        for c in range(4):
            P.add("pe", CALL("transpose", out=PS(b, 120, c * 120), in_=ld[0:120, c * 128:(c + 1) * 128], identity=I_[0:120, 0:120]), reads=[f"ld512{hf}", "cst"], writes=[pk(b)])
        for c in range(4):
            P.add("act", CALL("activation", out=B.uTs[:, c, hf * 8:(hf + 1) * 8, 0:15], in_=PS(b, 120, c * 120).rearrange("p (s e) -> p s e", e=15), func=AF.Copy),
                  reads=[pk(b)], writes=[f"uTs{c}"])


def _sample_attn(G, B, l):
    P, PS, PSB, pk = G.P, G.PS, G.PSB, G.pk
    P.ps_lo = 4
    for h in range(4):
        P.add("dve", CALL("tensor_copy", out=B.xqm[:, h, 0:1088].rearrange("p (s r) -> p s r", r=68)[:, :, 0:4], in_=B.xqT[:, h, 0:64].rearrange("p (s i) -> p s i", i=4)),
              reads=[f"xqT{h}"], writes=["xqm"])
    for s in range(16):
        i = s % 2
        st_, stk = B.kvs[i], f"kvs{i}"
        P.add("sp", CALL("dma_start", out=st_[:, :].rearrange("p (c n) -> p c n", c=2), in_=G.c_k[l, s].rearrange("(c p) n -> p c n", p=128)), writes=[stk], dma=1, key=stk)
        P.add("act", CALL("activation", out=B.kvb[i][:, :, :].rearrange("p c n -> p (c n)"), in_=st_[:, :], func=AF.Copy), reads=[stk], writes=[f"kvb{i}"])
        bt = 4 + (s % 4)
        for h in range(4):
            for mc in range(2):
                P.add("pe", CALL("transpose", out=PSB(bt, 128, h * 256 + mc * 128), in_=B.kvb[i][:, mc, h * 128:(h + 1) * 128], identity=G.identB),
                      reads=[f"kvb{i}", "identB"], writes=[pk(bt)])
        P.add("dve", CALL("tensor_copy", out=B.kTs[i][:, :, :].rearrange("p h m -> p (h m)"), in_=PSB(bt, 1024)), reads=[pk(bt)], writes=[f"kTs{i}"])
        for h in range(4):
            P.add("pe", CALL("matmul", PS(h, 256)[0:64, :], lhsT=B.xqm[:, h, s * 64:(s + 1) * 64], rhs=B.kTs[i][:, h, :], start=(s == 0), stop=(s == 15)),
                  reads=["xqm", f"kTs{i}"], writes=[pk(h)])
    pTs = []
    for h in range(4):
        pT, pTk = _attn_softmax(G, B, PS(h, 256)[0:64, :], pk(h), 64, h, None)
        dst = B.pTall[h]
        P.add("pool", CALL("tensor_copy", out=dst[:, :, 0:64], in_=pT[:, :, 0:64]), reads=[pTk], writes=[f"pTall{h}"])
        pTs.append(dst)
    for s in range(16):
        i = s % 2
        st_, stk = B.kvs[i], f"kvs{i}"
        P.add("sp", CALL("dma_start", out=st_[:, :].rearrange("p (c n) -> p c n", c=2), in_=G.c_v[l, s].rearrange("(c p) n -> p c n", p=128)), writes=[stk], dma=1, key=stk)
        P.add("act", CALL("activation", out=B.kvb[i][:, :, :].rearrange("p c n -> p (c n)"), in_=st_[:, :], func=AF.Copy), reads=[stk], writes=[f"kvb{i}"])
        for h in range(4):
            for mc in range(2):
                P.add("pe", CALL("matmul", PS(h, 4, 4 * s), lhsT=B.kvb[i][:, mc, h * 128:(h + 1) * 128], rhs=pTs[h][:, mc, 4 * s:4 * s + 4], start=(mc == 0), stop=(mc == 1)),
                      reads=[f"kvb{i}", f"pTall{h}"], writes=[pk(h)])
    for h in range(4):
        P.add("act", CALL("activation", out=B.yT[:, 8 + h, 0:64], in_=PS(h, 64), func=AF.Copy), reads=[pk(h)], writes=[f"yT{8 + h}_0"])
    P.ps_lo = 0


def mixer_phase(G, l):
    P, AR, PS, pk = G.P, G.AR, G.PS, G.pk
    I_ = G.identF
    P.barrier()
    m0 = AR.mark()
    Bp = _mixer_bufs(G, 256, False)
    P.add("pool", CALL("memset", G.hist[:, :, :], 0.0), writes=[f"hist{c}" for c in range(12)])
    P.add("pool", CALL("memset", G.Sst[:, :, :], 0.0), writes=[f"S{h}" for h in range(4)])
    P.add("sp", CALL("dma_start", out=Bp.stg[0][0][:, 0:512].rearrange("p (g e) -> p g e", g=4), in_=G.w_grp[l].rearrange("g c e -> c g e")), writes=["stg0"], dma=1, key="stg0")
    P.add("act", CALL("activation", out=G.wgrp_b[:, :, :], in_=Bp.stg[0][0][:, 0:512].rearrange("p (g e) -> p g e", g=4), func=AF.Copy), reads=["stg0"], writes=["wgrp_b"])
    _mem_kv(G, Bp, l)
    P.barrier()
    nst = G.n_st if hasattr(G, "n_st") else 8
    for st in range(nst):
        tiles = [(G.xp[:, st * 2 + j, :], 128, f"xp{st * 2 + j}") for j in range(2)]
        _mixer_st(G, Bp, l, st, tiles, False)
    P.barrier()
    b = P.ps()
    for c in range(4):
        P.add("pe", CALL("transpose", out=PS(b, 128, c * 128)[0:15, :], in_=Bp.uT[:, c, 0:15], identity=I_), reads=[f"uT{c}", "cst"], writes=[pk(b)])
    P.add("act", CALL("activation", out=Bp.rowbuf[0:15, 0:512], in_=PS(b, 512)[0:15, :], func=AF.Copy), reads=[pk(b)], writes=["rowbuf"])
    P.add("pool", CALL("dma_start", out=G.o_pool_p[l], in_=Bp.rowbuf[0:15, 0:512]), reads=["rowbuf"], dma=1, key="o_pool_p")
    for g in range(3):
        b = P.ps()
        for cc in range(4):
            c = g * 4 + cc
            P.add("pe", CALL("transpose", out=PS(b, 128, cc * 128)[0:3, :], in_=G.hist[:, c, :], identity=I_), reads=[f"hist{c}", "cst"], writes=[pk(b)])
        P.add("act", CALL("activation", out=Bp.rowbuf[0:3, g * 512:(g + 1) * 512], in_=PS(b, 512)[0:3, :], func=AF.Copy), reads=[pk(b)], writes=["rowbuf"])
    P.add("pool", CALL("dma_start", out=G.o_conv_p[l], in_=Bp.rowbuf[0:3, 0:1536]), reads=["rowbuf"], dma=1, key="o_conv_p")
    P.add("pool", CALL("dma_start", out=G.o_delta_p[l].rearrange("h k v -> k h v"), in_=G.Sst[:, :, :]), reads=[f"S{h}" for h in range(4)], dma=1, key="o_delta_p")
    P.barrier()
    AR.release(m0)
    if getattr(G, "skip_sample", False):
        return
    Bs = _mixer_bufs(G, 64, True)
    _sample_prep(G, Bs, l)
    _mixer_st(G, Bs, l, 0, [(G.xs[0:TS, :], TS, "xs")], True)
    P.barrier()
    for hf in range(2):
        b = P.ps()
        for c in range(4):
            P.add("dve", CALL("tensor_copy", out=Bs.pA[:, 0:120].rearrange("p (s e) -> p s e", e=15), in_=Bs.uTs[:, c, hf * 8:(hf + 1) * 8, 4:19]), reads=[f"uTs{c}"], writes=["pA"])
            P.add("pe", CALL("transpose", out=PS(b, 128, c * 128)[0:120, :], in_=Bs.pA[:, 0:120], identity=I_), reads=["pA", "cst"], writes=[pk(b)])
        P.add("act", CALL("activation", out=Bs.rowbuf[0:120, 0:512], in_=PS(b, 512)[0:120, :], func=AF.Copy), reads=[pk(b)], writes=["rowbuf"])
        P.add("pool", CALL("dma_start", out=G.o_pool_s[l, hf * 120:(hf + 1) * 120, :], in_=Bs.rowbuf[0:120, 0:512]), reads=["rowbuf"], dma=1, key="o_pool_s")
    for g in range(3):
        b = P.ps()
        for cc in range(4):
            c = g * 4 + cc
            P.add("pe", CALL("transpose", out=PS(b, 128, cc * 128)[0:48, :], in_=Bs.cvout[:, c, :, :].rearrange("p s e -> p (s e)"), identity=I_), reads=[f"cvout{g}", "cst"], writes=[pk(b)])
        P.add("act", CALL("activation", out=Bs.ld1536[0:48, g * 512:(g + 1) * 512], in_=PS(b, 512)[0:48, :], func=AF.Copy), reads=[pk(b)], writes=["ld1536"])
    P.add("pool", CALL("dma_start", out=G.o_conv_s[l], in_=Bs.ld1536[0:48, :]), reads=["ld1536"], dma=1, key="o_conv_s")
    P.barrier()
    AR.release(m0)


def convert_tables(G):
    P, AR = G.P, G.AR
    if getattr(G, "skip_convert", False):
        return
    m0 = AR.mark()
    NB = 3
    R = 4
    stg = [AR.alloc(R * 1024) for _ in range(NB)]
    bfb = [AR.alloc(R * 1024, BF16) for _ in range(NB)]
    n = 0
    for src, dst in ((G.p_u, G.uv_bf[:, 0:D]), (G.p_v, G.uv_bf[:, D:2 * D])):
        sv = src.rearrange("(c j p) d -> c p j d", p=128, j=R)
        dv = dst.rearrange("(c j p) d -> c p j d", p=128, j=R)
        nchunk = (DEPTH * NEXP) // (128 * R)
        if hasattr(G, "conv_chunks"):
            nchunk = G.conv_chunks
        for c in range(nchunk):
            i = n % NB
            n += 1
            s3 = stg[i].rearrange("p (j d) -> p j d", j=R)
            b3 = bfb[i].rearrange("p (j d) -> p j d", j=R)
            P.add("sp", CALL("dma_start", out=s3, in_=sv[c]), writes=[f"cstg{i}"], dma=1, key=f"cstg{i}")
            if n % 2 == 0:
                P.add("act", CALL("activation", out=bfb[i][:, :], in_=stg[i][:, :], func=AF.Copy), reads=[f"cstg{i}"], writes=[f"cbf{i}"])
            else:
                P.add("dve", CALL("tensor_copy", out=bfb[i][:, :], in_=stg[i][:, :]), reads=[f"cstg{i}"], writes=[f"cbf{i}"])
            P.add("pool", CALL("dma_start", out=dv[c], in_=b3), reads=[f"cbf{i}"], dma=1, key=f"cbf{i}")
    P.barrier()
    AR.release(m0)


def peer_phase(G, l):
    P, AR, PS, PSB, pk = G.P, G.AR, G.PS, G.PSB, G.pk
    P.barrier()
    P.ps_lo = 2
    m0 = AR.mark()
    NS = 10
    csblk = AR.alloc(NS * 1024)
    csb = csblk.bitcast(BF16)
    cs = [csb[:, i * 2048:(i + 1) * 2048] for i in range(NS)]
    wq = AR.alloc(8 * 2048, BF16).rearrange("p (k n) -> p k n", k=8)
    skT = AR.alloc(16 * 128, BF16).rearrange("p (j n) -> p j n", j=16)
    skb = csb[:, 6 * 2048:7 * 2048].rearrange("p (j n) -> p j n", j=16)
    hnf = AR.alloc(1024)
    hnb2 = [AR.alloc(1024, BF16) for _ in range(2)]
    hnT = AR.alloc(8 * 128, BF16).rearrange("p (k t) -> p k t", k=8)
    qT = AR.alloc(16 * 128, BF16).rearrange("p (j t) -> p j t", j=16)
    s1 = AR.alloc(2048)
    s2 = AR.alloc(2048)
    oh = s2
    s2keys = [f"s2_{j}" for j in range(16)]
    top = AR.alloc(256).rearrange("p (j a) -> p j a", j=16)
    topi = AR.alloc(256, U32).rearrange("p (j a) -> p j a", j=16)
    topif = AR.alloc(256).rearrange("p (h t a) -> p h t a", h=8, t=2)
    best = AR.alloc(128).rearrange("p (h k) -> p h k", h=8)
    pos = AR.alloc(128, U32).rearrange("p (h k) -> p h k", h=8)
    pint = AR.alloc(128, U32).rearrange("p (h k) -> p h k", h=8)
    paf = AR.alloc(128).rearrange("p (h k) -> p h k", h=8)
    pbf = AR.alloc(128).rearrange("p (h k) -> p h k", h=8)
    I1 = AR.alloc(128).rearrange("p (h k) -> p h k", h=8)
    I2 = AR.alloc(128).rearrange("p (h k) -> p h k", h=8)
    idxf = AR.alloc(128)
    idx2 = [AR.alloc(128, I32) for _ in range(2)]
    gate2 = [AR.alloc(128).rearrange("p (h k) -> p h k", h=8) for _ in range(2)]
    gsum = AR.alloc(8)
    av = AR.alloc(128)
    tg = AR.alloc(128)
    ag = AR.alloc(128)
    sg = AR.alloc(128)
    wgt = AR.alloc(128)
    Dg = [AR.alloc(2 * 128, BF16).rearrange("p (j m) -> p j m", j=2) for _ in range(4)]
    jb = AR.alloc(1024, BF16)
    ss = AR.alloc(8)

    stgA = (csblk[:, 0:2048], ["cs0", "cs1"])
    stgB = (csblk[:, 2048:4096], ["cs2", "cs3"])
    for g in range(8):
        sv, skeys = (stgA, stgB)[g % 2]
        svv = sv.rearrange("p (k n) -> p k n", k=8)
        P.add("sp", CALL("dma_start", out=svv, in_=G.w_pq[l].rearrange("(k p) n -> p k n", p=128)[:, :, g * 256:(g + 1) * 256]), writes=skeys, dma=1, key="pq" + skeys[0])
        if g % 2 == 0:
            P.add("act", CALL("activation", out=wq[:, :, g * 256:(g + 1) * 256], in_=svv, func=AF.Copy), reads=skeys, writes=[f"wq{g}"])
        else:
            P.add("dve", CALL("tensor_copy", out=wq[:, :, g * 256:(g + 1) * 256], in_=svv), reads=skeys, writes=[f"wq{g}"])
    wqkeys = [f"wq{g}" for g in range(8)]
    sks = csblk[:, 4096:6144].rearrange("p (j c) -> p j c", j=16)
    P.add("sp", CALL("dma_start", out=sks, in_=G.subk[l].rearrange("j k c -> k j c")), writes=["cs4", "cs5"], dma=1, key="sks")
    P.add("act", CALL("activation", out=skb, in_=sks, func=AF.Copy), reads=["cs4", "cs5"], writes=["cs6"])
    for hf in range(2):
        b = P.ps()
        for jj in range(8):
            j = hf * 8 + jj
            P.add("pe", CALL("transpose", out=PSB(b, 128, jj * 128), in_=skb[:, j, :], identity=G.identB), reads=["cs6", "identB"], writes=[pk(b)])
        P.add("act", CALL("activation", out=skT[:, hf * 8:(hf + 1) * 8, :].rearrange("p j n -> p (j n)"), in_=PSB(b, 1024), func=AF.Copy), reads=[pk(b)], writes=[f"skT{hf}"])

    tiles = [(G.xp[:, t, :], 128, f"xp{t}") for t in range(16)] + [(G.xs[0:TS, :], TS, "xs")]
    if hasattr(G, "peer_tiles"):
        tiles = [tiles[i] for i in G.peer_tiles]
    gst = {"gi": 0, "d": 0}
    nsl = getattr(G, "peer_slots", 128)

    def part_topk(x_ap, np_, xkey, par):
        r_ = slice(0, np_)
        idx = idx2[par]
        ik = f"idx{par}"
        hnb = hnb2[par]
        hk = f"hnb{par}"
        gate = gate2[par]
        gk = f"gate{par}"
        G.rms_rstd(x_ap, np_, xkey, jb, "jb", ss[:, 0:1], "pss")
        P.add("dve", CALL("scalar_tensor_tensor", out=hnf[r_, :], in0=x_ap, scalar=ss[r_, 0:1], in1=G.gb_ffn[r_, :], op0=ALU.mult, op1=ALU.mult),
              reads=[xkey, "pss", "gb_ffn"], writes=["hnf"])
        P.add("act", CALL("activation", out=hnb[r_, :], in_=hnf[r_, :], func=AF.Copy), reads=["hnf"], writes=[hk])
        G.to_featmajor(hnb, np_, hk, hnT, "hnT", slice(0, np_))
        for j in range(16):
            b = P.ps()
            for k in range(8):
                P.add("pe", CALL("matmul", PS(b, np_), lhsT=wq[:, k, j * 128:(j + 1) * 128], rhs=hnT[:, k, r_], start=(k == 0), stop=(k == 7)),
                      reads=["hnT", wqkeys[j // 2]], writes=[pk(b)])
            if j % 2 == 0:
                P.add("act", CALL("activation", out=qT[:, j, r_], in_=PS(b, np_), func=AF.Copy), reads=[pk(b)], writes=[f"qT{j}"])
            else:
                P.add("dve", CALL("tensor_copy", out=qT[:, j, r_], in_=PS(b, np_)), reads=[pk(b)], writes=[f"qT{j}"])
        for q in range(4):
            b = P.ps()
            for jj in range(4):
                j = q * 4 + jj
                P.add("pe", CALL("matmul", PS(b, 128, jj * 128)[r_, :], lhsT=qT[:, j, r_], rhs=skT[:, j, :], start=True, stop=True),
                      reads=[f"qT{j}", f"skT{j // 8}"], writes=[pk(b)])
            P.add("act", CALL("activation", out=s1[r_, q * 512:(q + 1) * 512], in_=PS(b, 512)[r_, :], func=AF.Copy), reads=[pk(b)], writes=[f"s1_{q}"])
        s1v = s1[:, :].rearrange("p (j n) -> p j n", j=16)
        s2v = s2[:, :].rearrange("p (j n) -> p j n", j=16)
        for j in range(16):
            sk_ = f"s1_{j // 4}"
            P.add("dve", CALL("max", out=top[r_, j, 0:8], in_=s1v[r_, j, :]), reads=[sk_], writes=[f"top{j}a"])
            P.add("dve", CALL("max_index", out=topi[r_, j, 0:8], in_max=top[r_, j, 0:8], in_values=s1v[r_, j, :]), reads=[sk_, f"top{j}a"], writes=[f"topi{j}a"])
            P.add("dve", CALL("match_replace", out=s2v[r_, j, :], in_to_replace=top[r_, j, 0:8], in_values=s1v[r_, j, :], imm_value=NEG), reads=[sk_, f"top{j}a"], writes=[f"s2_{j}"])
            P.add("dve", CALL("max", out=top[r_, j, 8:16], in_=s2v[r_, j, :]), reads=[f"s2_{j}"], writes=[f"top{j}b"])
            P.add("dve", CALL("max_index", out=topi[r_, j, 8:16], in_max=top[r_, j, 8:16], in_values=s2v[r_, j, :]), reads=[f"s2_{j}", f"top{j}b"], writes=[f"topi{j}b"])
        allt = [f"top{j}{x}" for j in range(16) for x in "ab"]
        alli = [f"topi{j}{x}" for j in range(16) for x in "ab"]
        P.add("dve", CALL("tensor_copy", out=topif[r_, :, :, :].rearrange("p h t a -> p (h t a)"), in_=topi[r_, :, :].rearrange("p j a -> p (j a)")), reads=alli, writes=["topif"])
        topv = top[:, :, :].rearrange("p (h t) a -> p h t a", t=2)
        cand = s1[:, :].rearrange("p (h a b) -> p h a b", h=8, a=16)
        cand2 = s2[:, :].rearrange("p (h n) -> p h n", h=8)
        candf = s1[:, :].rearrange("p (h n) -> p h n", h=8)
        for h in range(8):
            P.add("dve", CALL("tensor_tensor", out=cand[r_, h, :, :], in0=topv[r_, h, 0, :].unsqueeze(2).to_broadcast([np_, 16, 16]),
                                                        in1=topv[r_, h, 1, :].unsqueeze(1).to_broadcast([np_, 16, 16]), op=ALU.add),
                  reads=allt, writes=[f"s1_{h // 2}"])
            P.add("dve", CALL("max", out=best[r_, h, 0:8], in_=candf[r_, h, :]), reads=[f"s1_{h // 2}"], writes=[f"best{h}a"])
            P.add("dve", CALL("max_index", out=pos[r_, h, 0:8], in_max=best[r_, h, 0:8], in_values=candf[r_, h, :]), reads=[f"s1_{h // 2}", f"best{h}a"], writes=[f"pos{h}a"])
            P.add("dve", CALL("match_replace", out=cand2[r_, h, :], in_to_replace=best[r_, h, 0:8], in_values=candf[r_, h, :], imm_value=NEG),
                  reads=[f"s1_{h // 2}", f"best{h}a"], writes=[f"s2_{2 * h}", f"s2_{2 * h + 1}"])
            P.add("dve", CALL("max", out=best[r_, h, 8:16], in_=cand2[r_, h, :]), reads=[f"s2_{2 * h}", f"s2_{2 * h + 1}"], writes=[f"best{h}b"])
            P.add("dve", CALL("max_index", out=pos[r_, h, 8:16], in_max=best[r_, h, 8:16], in_values=cand2[r_, h, :]), reads=[f"s2_{2 * h}", f"s2_{2 * h + 1}", f"best{h}b"], writes=[f"pos{h}b"])
        allb = [f"best{h}{x}" for h in range(8) for x in "ab"]
        allp = [f"pos{h}{x}" for h in range(8) for x in "ab"]
        P.add("dve", CALL("tensor_single_scalar", out=pint[r_, :, :], in_=pos[r_, :, :], scalar=4, op=ALU.logical_shift_right), reads=allp, writes=["pint"])
        P.add("dve", CALL("tensor_copy", out=paf[r_, :, :], in_=pint[r_, :, :]), reads=["pint"], writes=["paf"])
        P.add("dve", CALL("tensor_single_scalar", out=pint[r_, :, :], in_=pos[r_, :, :], scalar=15, op=ALU.bitwise_and), reads=allp + ["paf"], writes=["pint"])
        P.add("dve", CALL("tensor_copy", out=pbf[r_, :, :], in_=pint[r_, :, :]), reads=["pint"], writes=["pbf"])
        ohv = oh[:, :].rearrange("p (h k a) -> p h k a", h=8, k=16)
        for (pf, pfk, tsel, Iout, Ik) in ((paf, "paf", 0, I1, "I1"), (pbf, "pbf", 1, I2, "I2")):
            for h in range(8):
                P.add("dve", CALL("tensor_tensor", out=ohv[r_, h, :, :], in0=pf[r_, h, :].unsqueeze(2).to_broadcast([np_, 16, 16]),
                                                                   in1=G.iota16[r_, :].unsqueeze(1).to_broadcast([np_, 16, 16]), op=ALU.is_equal),
                      reads=[pfk, "cst"], writes=[f"s2_{2 * h}", f"s2_{2 * h + 1}"])
                P.add("dve", CALL("tensor_tensor", out=ohv[r_, h, :, :], in0=ohv[r_, h, :, :], in1=topif[r_, h, tsel, :].unsqueeze(1).to_broadcast([np_, 16, 16]), op=ALU.mult),
                      reads=["topif"], writes=[f"s2_{2 * h}", f"s2_{2 * h + 1}"])
            P.add("dve", CALL("tensor_reduce", out=Iout[r_, :, :], in_=ohv[r_, :, :, :], axis=AX.X, op=ALU.add), reads=s2keys, writes=[Ik])
        P.add("dve", CALL("scalar_tensor_tensor", out=idxf[r_, :], in0=I1[r_, :, :].rearrange("p h k -> p (h k)"), scalar=128.0, in1=I2[r_, :, :].rearrange("p h k -> p (h k)"), op0=ALU.mult, op1=ALU.add),
              reads=["I1", "I2"], writes=["idxf"])
        if l > 0:
            P.add("dve", CALL("tensor_scalar", out=idxf[r_, :], in0=idxf[r_, :], scalar1=float(l * NEXP), scalar2=0.0, op0=ALU.add, op1=ALU.add), reads=["idxf"], writes=["idxf"])
        P.add("dve", CALL("tensor_copy", out=idx[r_, :], in_=idxf[r_, :]), reads=["idxf"], writes=[ik])
        P.add("dve", CALL("tensor_tensor", out=gate[r_, :, :], in0=best[r_, :, :], in1=best[r_, :, 0:1].to_broadcast([np_, 8, 16]), op=ALU.subtract), reads=allb, writes=[gk])
        P.add("act", CALL("activation", out=gate[r_, :, :], in_=gate[r_, :, :], func=AF.Exp), reads=[gk], writes=[gk])
        P.add("dve", CALL("tensor_reduce", out=gsum[r_, 0:8], in_=gate[r_, :, :], axis=AX.X, op=ALU.add), reads=[gk], writes=["gsum"])
        P.add("dve", CALL("reciprocal", out=gsum[r_, 0:8], in_=gsum[r_, 0:8]), reads=["gsum"], writes=["gsum"])
        P.add("dve", CALL("tensor_tensor", out=gate[r_, :, :], in0=gate[r_, :, :], in1=gsum[r_, 0:8].unsqueeze(2).to_broadcast([np_, 8, 16]), op=ALU.mult), reads=[gk, "gsum"], writes=[gk])

    def part_pipe(x_ap, np_, xkey, par):
        r_ = slice(0, np_)
        idx = idx2[par]
        ik = f"idx{par}"
        hnb = hnb2[par]
        hk = f"hnb{par}"
        gatef = gate2[par][:, :, :].rearrange("p h k -> p (h k)")
        gk = f"gate{par}"
        ngrp = nsl // 2
        slot_of = {}

        def stage3(g):
            c2 = slice(2 * g, 2 * g + 2)
            P.add("dve", CALL("tensor_tensor", out=wgt[r_, c2], in0=sg[r_, c2], in1=ag[r_, c2], op=ALU.mult), reads=[f"sg{g}", f"ag{g}"], writes=[f"wgt{g}"])
            d_i = gst["d"] % 4
            gst["d"] += 1
            D_ = Dg[d_i]
            dk = f"Dg{d_i}"
            P.add("dve", CALL("tensor_tensor", out=D_[r_, :, r_], in0=G.identF[r_, r_].unsqueeze(1).to_broadcast([np_, 2, np_]),
                              in1=wgt[r_, c2].unsqueeze(2).to_broadcast([np_, 2, np_]), op=ALU.mult), reads=[f"wgt{g}", "cst"], writes=[dk])
            for j in range(2):
                sl = 2 * g + j
                i = slot_of[sl]
                for hf in range(2):
                    P.add("pe", CALL("matmul", PS(hf, 512)[r_, :], lhsT=D_[r_, j, r_], rhs=cs[i][r_, 1024 + hf * 512:1024 + (hf + 1) * 512], start=(sl == 0), stop=(sl == nsl - 1)),
                          reads=[dk, f"cs{i}"], writes=[pk(hf)])

        for g in range(ngrp + 1):
            if g < ngrp:
                c2 = slice(2 * g, 2 * g + 2)
                for j in range(2):
                    sl = 2 * g + j
                    i = gst["gi"] % NS
                    gst["gi"] += 1
                    slot_of[sl] = i
                    P.add("pool", CALL("indirect_dma_start", out=cs[i][r_, :], out_offset=None, in_=G.uv_bf, in_offset=bass.IndirectOffsetOnAxis(ap=idx[r_, sl:sl + 1], axis=0)),
                          reads=[ik], writes=[f"cs{i}"], dma=1, key=f"cs{i}")
                    P.add("dve", CALL("scalar_tensor_tensor", out=cs[i][r_, 0:1024], in0=cs[i][r_, 0:1024], scalar=1.0, in1=hnb[r_, :], op0=ALU.mult, op1=ALU.mult, accum_out=av[r_, sl:sl + 1]),
                          reads=[hk], writes=[f"cs{i}", f"av{sl}"])
                avk = [f"av{2 * g}", f"av{2 * g + 1}"]
                P.add("dve", CALL("scalar_tensor_tensor", out=tg[r_, c2], in0=av[r_, c2], scalar=0.044715, in1=av[r_, c2], op0=ALU.mult, op1=ALU.mult), reads=avk, writes=[f"tg{g}"])
                P.add("dve", CALL("scalar_tensor_tensor", out=tg[r_, c2], in0=tg[r_, c2], scalar=1.0, in1=av[r_, c2], op0=ALU.add, op1=ALU.mult), reads=avk + [f"tg{g}"], writes=[f"tg{g}"])
                P.add("act", CALL("activation", out=sg[r_, c2], in_=tg[r_, c2], func=AF.Sigmoid, scale=1.5957691216057308), reads=[f"tg{g}"], writes=[f"sg{g}"])
                P.add("dve", CALL("tensor_tensor", out=ag[r_, c2], in0=av[r_, c2], in1=gatef[r_, c2], op=ALU.mult), reads=avk + [gk], writes=[f"ag{g}"])
            if g >= 1:
                stage3(g - 1)
        for hf in range(2):
            xo = x_ap[:, hf * 512:(hf + 1) * 512]
            P.add("dve", CALL("tensor_tensor", out=xo, in0=xo, in1=PS(hf, 512)[r_, :], op=ALU.add), reads=[pk(hf), xkey], writes=[xkey])

    nt_ = len(tiles)
    part_topk(*tiles[0], 0)
    for t in range(nt_):
        P.begin_record((0, 1))
        part_pipe(*tiles[t], t % 2)
        ra = P.end_record()
        rb = []
        if t + 1 < nt_:
            P.begin_record((2, 3, 4, 5, 6, 7))
            part_topk(*tiles[t + 1], (t + 1) % 2)
            rb = P.end_record()
        P.replay([ra, rb])
    P.barrier()
    P.ps_lo = 0
    AR.release(m0)


def final_phase(G):
    P, AR = G.P, G.AR
    m0 = AR.mark()
    ob = [AR.alloc(1024) for _ in range(2)]
    jb = AR.alloc(1024, BF16)
    ss = AR.alloc(8)
    gb_fin = AR.alloc(1024)
    P.add("sp", CALL("dma_start", out=gb_fin[:, :], in_=G.g_fin.partition_broadcast(128)), writes=["gb_fin"], dma=1, key="gb_fin")
    tiles = [(G.xp[:, t, :], 128, f"xp{t}", G.y_p[t * 128:(t + 1) * 128, :]) for t in range(16)] + [(G.xs[0:TS, :], TS, "xs", G.y_s)]
    for n, (x_ap, np_, xkey, o_ap) in enumerate(tiles):
        r_ = slice(0, np_)
        o = ob[n % 2]
        G.rms_rstd(x_ap, np_, xkey, jb, "jb", ss[:, 0:1], "fss")
        P.add("dve", CALL("scalar_tensor_tensor", out=o[r_, :], in0=x_ap, scalar=ss[r_, 0:1], in1=gb_fin[r_, :], op0=ALU.mult, op1=ALU.mult),
              reads=[xkey, "fss", "gb_fin"], writes=[f"ob{n % 2}"])
        P.add("sp", CALL("dma_start", out=o_ap, in_=o[r_, :]), reads=[f"ob{n % 2}"], dma=1, key=f"ob{n % 2}")
    AR.release(m0)

def build(dbg=(), stop=None, **opts):
    build.opts = opts
    nc = bass.Bass("TRN2", target_bir_lowering=False)
    es = ExitStack()
    with es:
        _build(nc, es, dbg, stop)
    return nc


def _build(nc, es, dbg, stop):
    def din(name, shape, dt=F32):
        return nc.dram_tensor(name, list(shape), dt, kind="ExternalInput").ap()

    def dout(name, shape, dt=F32):
        return nc.dram_tensor(name, list(shape), dt, kind="ExternalOutput").ap()

    x_p = din("x_p", [T, D])
    x_s = din("x_s", [TS, D])
    st_pool = din("st_pool", [DEPTH, NSQ * 15, BW])
    st_conv = din("st_conv", [DEPTH, NSQ * 3, 3 * BW])
    st_delta = din("st_delta", [DEPTH, NSQ, 4, 128, 128])
    c_k = din("c_k", [DEPTH, NSQ, 256, BW])
    c_v = din("c_v", [DEPTH, NSQ, 256, BW])
    memp = din("memp", [256, D])
    g_mix = din("g_mix", [DEPTH, D])
    w_in = din("w_in", [DEPTH, D, IN_COLS])
    w_conv = din("w_conv", [DEPTH, 4, 3 * BW])
    a_log = din("a_log", [DEPTH, 4])
    dt_bias = din("dt_bias", [DEPTH, 4])
    g_dn = din("g_dn", [DEPTH, 128])
    w_grp = din("w_grp", [DEPTH, 4, 128, 128])
    p_scale = din("p_scale", [DEPTH, BW])
    g_mem = din("g_mem", [DEPTH, D])
    w_mkv = din("w_mkv", [DEPTH, D, 2 * BW])
    w_br = din("w_br", [DEPTH, 3, BW, D])
    w_o = din("w_o", [DEPTH, D, D])
    g_ffn = din("g_ffn", [DEPTH, D])
    w_pq = din("w_pq", [DEPTH, D, 2048])
    subk = din("subk", [DEPTH, 16, 128, 128])
    p_u = din("p_u", [DEPTH * NEXP, D])
    p_v = din("p_v", [DEPTH * NEXP, D])
    g_fin = din("g_fin", [D])
    consts = din("consts", [128, 1024])

    y_p = dout("y_p", [T, D])
    y_s = dout("y_s", [TS, D])
    o_pool_p = dout("o_pool_p", [DEPTH, 15, BW])
    o_conv_p = dout("o_conv_p", [DEPTH, 3, 3 * BW])
    o_delta_p = dout("o_delta_p", [DEPTH, 4, 128, 128])
    o_mk = dout("o_mk", [DEPTH, 256, BW])
    o_mv = dout("o_mv", [DEPTH, 256, BW])
    o_pool_s = dout("o_pool_s", [DEPTH, NSQ * 15, BW])
    o_conv_s = dout("o_conv_s", [DEPTH, NSQ * 3, 3 * BW])
    o_delta_s = dout("o_delta_s", [DEPTH, NSQ, 4, 128, 128])

    uv_bf = nc.dram_tensor("uv_bf", [DEPTH * NEXP, 2 * D], BF16, kind="Internal").ap()

    P = Prog(nc)
    dbg_outs = {}

    def sb(name, shape, dt=F32):
        return es.enter_context(nc.sbuf_tensor(name, shape, dt))

    xp = sb("xp", [128, 16, D])
    xs = sb("xs", [128, D])
    cst = sb("cst", [128, 8, 128])
    identB_t = sb("identB", [128, 128], BF16)
    identB = identB_t[:, :]
    gmix_sb = sb("gmix_sb", [128, 8])
    gmem_sb = sb("gmem_sb", [128, 8])
    gb_ffn = sb("gb_ffn", [128, D])
    wconv_sb = sb("wconv_sb", [128, 12, 4])
    alog_b = sb("alog_b", [128, 4])
    dtb_b = sb("dtb_b", [128, 4])
    gdn_sb = sb("gdn_sb", [128, 1])
    psc_sb = sb("psc_sb", [128, 4])
    wgrp_b = sb("wgrp_b", [128, 4, 128], BF16)
    Sst = sb("Sst", [128, 4, 128])
    hist = sb("hist", [128, 12, 3])
    epsb = sb("epsb", [128, 1])
    ARENA_N = 32000
    arena_t = sb("arena", [128, ARENA_N])
    AR = Arena(arena_t, ARENA_N)
    psum = es.enter_context(nc.psum_tensor("psum", [128, 4096], F32))

    identF = cst[:, 0, :]
    onesF = cst[:, 1, :]
    Ltri = cst[:, 2, :]
    SLm = cst[:, 3, :]
    Ltri4 = cst[:, 4, :]
    SL4 = cst[:, 5, :]
    seqmask = cst[:, 6, 0:16]
    rcnt = cst[:, 7, 0:64]
    iota16 = cst[:, 7, 64:80]

    def PS(i, n=512, off=0):
        return psum[:, i * 512 + off:i * 512 + off + n]

    def PSB(i, n=1024, off=0):
        return psum[:, i * 512:(i + 1) * 512].bitcast(BF16)[:, off:off + n]

    def pk(i):
        return f"ps{i}"

    def dump(name, ap, reads, shape, view=None, **kw):
        if name not in dbg:
            return
        o = dout("dbg_" + name, shape, ap.dtype)
        dbg_outs[name] = o
        if view:
            o = o.rearrange(view, **kw)
        P.add("sp", CALL("dma_start", out=o, in_=ap), reads=reads, dma=1, key="dbg_" + name)

    P.add("sp", CALL("dma_start", out=cst[:].rearrange("p a b -> p (a b)"), in_=consts), writes=["cst"], dma=1, key="cst")
    P.add("pool", CALL("memset", epsb[:], EPS), writes=["epsb"])
    P.add("act", CALL("activation", out=identB, in_=identF, func=AF.Copy), reads=["cst"], writes=["identB"])
    for i in range(16):
        P.add("sp", CALL("dma_start", out=xp[:, i, :], in_=x_p[i * 128:(i + 1) * 128, :]), writes=[f"xp{i}"], dma=1, key=f"xp{i}")
    P.add("sp", CALL("dma_start", out=xs[0:TS, :], in_=x_s), writes=["xs"], dma=1, key="xs")

    def rms_rstd(x_ap, np_, xkey, junk, junk_key, ss, sskey):
        P.add("act", CALL("activation", out=junk[0:np_, :], in_=x_ap, func=AF.Square, accum_out=ss[0:np_, :]),
              reads=[xkey], writes=[junk_key, sskey])
        P.add("act", CALL("activation", out=ss[0:np_, :], in_=ss[0:np_, :], func=AF.Sqrt, scale=1.0 / D, bias=epsb[0:np_, :]),
              reads=[sskey, "epsb"], writes=[sskey])
        P.add("dve", CALL("reciprocal", out=ss[0:np_, :], in_=ss[0:np_, :]), reads=[sskey], writes=[sskey])

    def to_featmajor(src_bf, np_, srckey, dst, dstkey_fn, tsl, gsb=None, gkey=None):
        b = P.ps()
        for k in range(8):
            P.add("pe", CALL("transpose", out=PSB(b)[:, k * 128:k * 128 + np_], in_=src_bf[0:np_, k * 128:(k + 1) * 128], identity=identB[0:np_, 0:np_]),
                  reads=[srckey, "identB"], writes=[pk(b)])
        src = PSB(b).rearrange("p (k t) -> p k t", k=8)[:, :, 0:np_]
        if gsb is None:
            P.add("act", CALL("activation", out=dst[:, :, tsl], in_=src, func=AF.Copy), reads=[pk(b)], writes=[dstkey_fn])
        else:
            P.add("dve", CALL("tensor_tensor", out=dst[:, :, tsl], in0=src, in1=gsb[:, :].unsqueeze(2).to_broadcast([128, 8, np_]), op=ALU.mult),
                  reads=[pk(b), gkey], writes=[dstkey_fn])

    wst = {"i": 0, "s": 0}

    def load_w(dram2d, krows, c0, ncols, stg, wbf, caster=None):
        i = wst["i"] % len(wbf)
        wst["i"] += 1
        si = wst["s"] % len(stg)
        wst["s"] += 1
        s_ap, s_key = stg[si]
        b_ap, b_key = wbf[i]
        sv = s_ap[:, 0:krows * ncols].rearrange("p (k n) -> p k n", k=krows)
        bv = b_ap[:, 0:krows * ncols].rearrange("p (k n) -> p k n", k=krows)
        src = dram2d.rearrange("(k p) n -> p k n", p=128)[:, :, c0:c0 + ncols]
        P.add("sp", CALL("dma_start", out=sv, in_=src), writes=[s_key], dma=1, key=s_key)
        eng = caster or ("act" if (wst["i"] % 2 == 0) else "dve")
        if eng == "act":
            P.add("act", CALL("activation", out=bv, in_=sv, func=AF.Copy), reads=[s_key], writes=[b_key])
        else:
            P.add(eng, CALL("tensor_copy", out=bv, in_=sv), reads=[s_key], writes=[b_key])
        return bv, b_key

    def layer_params(l):
        P.add("sp", CALL("dma_start", out=gmix_sb[:], in_=g_mix[l].rearrange("(k p) -> p k", p=128), allow_slow_non_contiguous=True), writes=["gmix"], dma=1, key="gmix")
        P.add("sp", CALL("dma_start", out=gmem_sb[:], in_=g_mem[l].rearrange("(k p) -> p k", p=128), allow_slow_non_contiguous=True), writes=["gmem"], dma=1, key="gmem")
        P.add("sp", lambda e: [e.dma_start(out=wconv_sb[:, :, j], in_=w_conv[l, j].rearrange("(c p) -> p c", p=128), allow_slow_non_contiguous=True) for j in range(4)],
              writes=["wconv"], dma=4, key="wconv")
        P.add("sp", CALL("dma_start", out=gdn_sb[:], in_=g_dn[l].rearrange("(p o) -> p o", o=1), allow_slow_non_contiguous=True), writes=["gdn"], dma=1, key="gdn")
        P.add("sp", CALL("dma_start", out=psc_sb[:], in_=p_scale[l].rearrange("(g p) -> p g", p=128), allow_slow_non_contiguous=True), writes=["psc"], dma=1, key="psc")
        P.add("sp", CALL("dma_start", out=gb_ffn[:], in_=g_ffn[l].partition_broadcast(128)), writes=["gb_ffn"], dma=1, key="gb_ffn")
        P.add("sp", CALL("dma_start", out=alog_b[:], in_=a_log[l].partition_broadcast(128)), writes=["alog"], dma=1, key="alog")
        P.add("sp", CALL("dma_start", out=dtb_b[:], in_=dt_bias[l].partition_broadcast(128)), writes=["dtb"], dma=1, key="dtb")
        P.add("act", CALL("activation", out=alog_b[:], in_=alog_b[:], func=AF.Exp), reads=["alog"], writes=["alog"])
        P.add("dve", CALL("tensor_scalar", out=alog_b[:], in0=alog_b[:], scalar1=-1.0, scalar2=0.0, op0=ALU.mult, op1=ALU.add), reads=["alog"], writes=["alog"])

    G = type("G", (), {})()
    for k_, v_ in list(locals().items()):
        setattr(G, k_, v_)
    for k_, v_ in build.opts.items():
        setattr(G, k_, v_)

    convert_tables(G)
    for l in range(DEPTH):
        layer_params(l)
        mixer_phase(G, l)
        dump(f"xp_m{l}", xp[:, :, :], [f"xp{i}" for i in range(16)], [T, D], "(t p) d -> p t d", p=128)
        dump(f"xs_m{l}", xs[0:TS, :], ["xs"], [TS, D])
        if stop == ("mixer", l):
            break
        peer_phase(G, l)
        dump(f"xp_p{l}", xp[:, :, :], [f"xp{i}" for i in range(16)], [T, D], "(t p) d -> p t d", p=128)
        dump(f"xs_p{l}", xs[0:TS, :], ["xs"], [TS, D])
        if stop == ("peer", l):
            break
    else:
        final_phase(G)
    P.emit(es, maxops=build.opts.get('maxops'))
    G.P = P
    build.last = G


def _shard_inputs(inp):
    f = lambda a: np.ascontiguousarray(np.asarray(a, dtype=np.float32))
    shared = dict(
        g_mix=f(inp["g_mix"]), w_in=f(inp["w_in"]), w_conv=f(inp["w_conv"]), a_log=f(inp["a_log"]), dt_bias=f(inp["dt_bias"]),
        g_dn=f(inp["g_dn_out"]), w_grp=f(inp["w_pool_grp"]), p_scale=f(inp["pool_scale"]), g_mem=f(inp["g_mem"]),
        w_mkv=f(inp["w_mem_kv"]), w_br=f(inp["w_branch"]), w_o=f(inp["w_o"]), g_ffn=f(inp["g_ffn"]), w_pq=f(inp["w_peer_q"]),
        subk=f(inp["peer_subkeys"]).reshape(DEPTH, 16, 128, 128), p_u=f(inp["peer_u"]).reshape(DEPTH * NEXP, D),
        p_v=f(inp["peer_v"]).reshape(DEPTH * NEXP, D), g_fin=f(inp["g_final"]), consts=make_consts())
    maps = []
    for c in range(NCORES):
        sl = slice(c * NSQ, (c + 1) * NSQ)
        m = dict(shared)
        m["x_p"] = f(inp["x_prompt"][c])
        m["x_s"] = f(inp["x_sample"][sl]).reshape(TS, D)
        m["st_pool"] = f(inp["state_pool"][:, sl]).reshape(DEPTH, NSQ * 15, BW)
        m["st_conv"] = f(inp["state_conv"][:, sl]).reshape(DEPTH, NSQ * 3, 3 * BW)
        m["st_delta"] = f(inp["state_delta"][:, sl])
        m["c_k"] = f(inp["cache_mem_k"][:, sl]).reshape(DEPTH, NSQ, 256, BW)
        m["c_v"] = f(inp["cache_mem_v"][:, sl]).reshape(DEPTH, NSQ, 256, BW)
        m["memp"] = f(inp["mem_prompt"][c])
        maps.append(m)
    return maps


def _gather_outputs(res):
    R = res.results
    cat = lambda k, ax: np.concatenate([np.asarray(r[k]) for r in R], axis=ax)
    stk = lambda k, ax: np.stack([np.asarray(r[k]) for r in R], axis=ax)
    y_p = stk("y_p", 0)
    y_s = cat("y_s", 0).reshape(NCORES * NSQ, 4, D)
    pool_p = stk("o_pool_p", 1)
    conv_p = stk("o_conv_p", 1)
    delta_p = stk("o_delta_p", 1)
    mk = stk("o_mk", 1).reshape(DEPTH, NCORES, 256, 4, 128)
    mv = stk("o_mv", 1).reshape(DEPTH, NCORES, 256, 4, 128)
    pool_s = cat("o_pool_s", 1).reshape(DEPTH, NCORES * NSQ, 15, BW)
    conv_s = cat("o_conv_s", 1).reshape(DEPTH, NCORES * NSQ, 3, 3 * BW)
    delta_s = cat("o_delta_s", 1)
    outs = (y_p, y_s, pool_p, conv_p, delta_p, mk, mv, pool_s, conv_s, delta_s)
    return tuple(np.ascontiguousarray(o, dtype=np.float32) for o in outs)


def kernel(**inputs):
    maps = _shard_inputs(inputs)
    nc = build()
    res = run_bass_kernel_spmd(nc, maps, core_ids=list(range(NCORES)))
    return _gather_outputs(res)
```

```python
import numpy as np
from contextlib import ExitStack
import concourse.bass as bass
import concourse.mybir as mybir
from concourse.bass_utils import run_bass_kernel_spmd

F32 = mybir.dt.float32
BF16 = mybir.dt.bfloat16
I32 = mybir.dt.int32
U32 = mybir.dt.uint32
ALU = mybir.AluOpType
AF = mybir.ActivationFunctionType
AX = mybir.AxisListType

NCORES = 8
D = 1024
T = 2048
NSQ = 16
TS = 64
DEPTH = 2
BW = 512
IN_COLS = 6152
OFF_Q = 512
OFF_Z = 2048
OFF_BA = 2560
OFF_XQ = 2568
OFF_GATE = 3080
EPS = 1e-6
NKEY = 128
NEXP = 16384
SEM_CH = 30000
NEG = -1.0e30
NG = 6


def CALL(name, *a, **k):
    return lambda e: getattr(e, name)(*a, **k)


class Op:
    __slots__ = ("eng", "fn", "deps", "is_dma", "key", "nparts", "seq", "signaled", "dma_val")


class Prog:
    ENGS = ("pe", "act", "dve", "pool", "sp")

    def __init__(self, nc):
        self.nc = nc
        self.ops = []
        self.last_w = {}
        self.readers = {}
        self.dma_cnt = {}
        self.dma_gen = {}
        self.psi = 0
        self.inames = {}

    def begin_record(self, banks):
        self.rec = []
        self.ps_banks = list(banks)
        self.ps_bi = 0

    def end_record(self):
        r = self.rec
        self.rec = None
        self.ps_banks = None
        return r

    def replay(self, recs):
        recs = [r for r in recs if r]
        n = max(len(r) for r in recs)
        for i in range(n):
            for r in recs:
                if i < len(r):
                    self.add(*r[i][0], **r[i][1])

    def add(self, eng, fn, reads=(), writes=(), dma=0, key=None):
        if getattr(self, "rec", None) is not None:
            self.rec.append(((eng, fn), dict(reads=list(reads), writes=list(writes), dma=dma, key=key)))
            return None
        op = Op()
        op.eng = eng
        op.fn = fn
        op.is_dma = dma > 0
        op.nparts = dma
        op.signaled = False
        op.seq = 0
        deps = set()
        excl = [r for r in reads if isinstance(r, str) and r[:2] == "ps" and r[2:].isdigit()]
        if excl:
            reads = [r for r in reads if r not in excl]
            writes = list(writes) + excl
        for r in reads:
            w = self.last_w.get(r)
            if w is not None:
                deps.add(w)
        for w_ in writes:
            w = self.last_w.get(w_)
            if w is not None:
                deps.add(w)
            for rd in self.readers.get(w_, ()):
                deps.add(rd)
        op.deps = deps
        for r in reads:
            self.readers.setdefault(r, []).append(op)
        for w_ in writes:
            self.last_w[w_] = op
            self.readers[w_] = []
        if op.is_dma:
            g = self.dma_gen.get(key, 0)
            c = self.dma_cnt.get((key, g), 0) + dma * 16
            if c > SEM_CH:
                g += 1
                self.dma_gen[key] = g
                c = dma * 16
            self.dma_cnt[(key, g)] = c
            op.key = (key, g)
            op.dma_val = c
        self.ops.append(op)
        return op

    def barrier(self):
        last = {}
        dmas = {}
        for op in self.ops:
            if op.is_dma:
                dmas[op.key] = op
            else:
                last[op.eng] = op
        deps = set(last.values()) | set(dmas.values())
        bops = []
        for e in ("pe", "act", "dve", "pool", "sp"):
            op = self.add(e, lambda en: en.nop(), ())
            op.deps = set(deps)
            bops.append(op)
        self.last_w = {}
        self.readers = {}

    def ps(self):
        if getattr(self, "ps_banks", None):
            i = self.ps_banks[self.ps_bi % len(self.ps_banks)]
            self.ps_bi += 1
            return i
        lo = getattr(self, "ps_lo", 0)
        if self.psi < lo:
            self.psi = lo
        i = self.psi
        self.psi = self.psi + 1
        if self.psi >= 8:
            self.psi = lo
        return i

    def emit(self, es, maxops=None):
        nc = self.nc
        if maxops is not None:
            self.ops = self.ops[:maxops]
            cnt2 = {}
            for op in self.ops:
                if op.is_dma:
                    cnt2[op.key] = op.dma_val
            self.dma_cnt = cnt2
        for op in self.ops:
            nd = set()
            for d in op.deps:
                if (not d.is_dma) and (not op.is_dma) and d.eng == "pe" and op.eng == "pe":
                    continue
                nd.add(d)
                d.signaled = True
            op.deps = nd
        cnt = {e: 0 for e in self.ENGS}
        for op in self.ops:
            if not op.is_dma and op.signaled:
                cnt[op.eng] += 1
                op.seq = cnt[op.eng]
        eng_sems = {}
        for e in self.ENGS:
            n = (cnt[e] + SEM_CH - 1) // SEM_CH
            eng_sems[e] = [es.enter_context(nc.semaphore(f"s_{e}_{i}")) for i in range(n)]
        dma_sems = {}
        for i, k in enumerate(self.dma_cnt.keys()):
            dma_sems[k] = es.enter_context(nc.semaphore(f"d_{i}"))
        self.nsem = sum(len(v) for v in eng_sems.values()) + len(dma_sems)
        per_eng = {e: [o for o in self.ops if o.eng == e] for e in self.ENGS}
        block = es.enter_context(nc.Block())

        def run(engname, eobj):
            waited = {}

            def wait(sem, val):
                if waited.get(id(sem), 0) >= val:
                    return
                waited[id(sem)] = val
                eobj.wait_ge(sem, val)

            for op in per_eng[engname]:
                need = {}
                for d in op.deps:
                    if d.is_dma:
                        sem = dma_sems[d.key]
                        v = d.dma_val
                    else:
                        si = (d.seq - 1) // SEM_CH
                        sem = eng_sems[d.eng][si]
                        v = d.seq - si * SEM_CH
                    k = id(sem)
                    if k not in need or need[k][1] < v:
                        need[k] = (sem, v)
                for sem, v in need.values():
                    wait(sem, v)
                if op.is_dma:
                    sem = dma_sems[op.key]
                    insts = op.fn(eobj)
                    if not isinstance(insts, (list, tuple)):
                        insts = [insts]
                    assert len(insts) == op.nparts
                    for ins in insts:
                        ins.then_inc(sem, 16)
                else:
                    ins = op.fn(eobj)
                    try:
                        self.inames[ins.ins.name] = op
                    except Exception:
                        pass
                    if op.signaled:
                        si = (op.seq - 1) // SEM_CH
                        ins.then_inc(eng_sems[op.eng][si], 1)
            if engname == "sp":
                for k, c in self.dma_cnt.items():
                    wait(dma_sems[k], c)

        block.tensor(lambda e: run("pe", e))
        block.scalar(lambda e: run("act", e))
        block.vector(lambda e: run("dve", e))
        block.gpsimd(lambda e: run("pool", e))
        block.sync(lambda e: run("sp", e))


class Arena:
    def __init__(self, t, nf32):
        self.t = t
        self.n = nf32
        self.off = 0
        self.hw = 0

    def mark(self):
        return self.off

    def release(self, m):
        self.off = m

    def alloc(self, n, dt=F32):
        nf = n if dt in (F32, I32, U32) else (n + 1) // 2
        nf = (nf + 7) // 8 * 8
        a = self.off
        self.off += nf
        self.hw = max(self.hw, self.off)
        assert self.off <= self.n, f"arena overflow {self.off} > {self.n}"
        v = self.t[:, a:a + nf]
        if dt != F32:
            v = v.bitcast(dt)
        return v[:, 0:n]


def make_consts():
    c = np.zeros((128, 8, 128), np.float32)
    i = np.arange(128)
    c[:, 0, :] = np.eye(128)
    c[:, 1, :] = 1.0
    same64 = (i[:, None] // 64) == (i[None, :] // 64)
    same4 = ((i[:, None] // 4) == (i[None, :] // 4)) & (i[:, None] < 64) & (i[None, :] < 64)
    c[:, 2, :] = same64 & (i[:, None] <= i[None, :])
    c[:, 3, :] = same64 & (i[:, None] > i[None, :])
    c[:, 4, :] = same4 & (i[:, None] <= i[None, :])
    c[:, 5, :] = same4 & (i[:, None] > i[None, :])
    c[:, 6, 0:16] = (i[:, None] // 4) == np.arange(16)[None, :]
    for g, w in enumerate((2, 4, 8, 16)):
        c[:, 7, g * 16:(g + 1) * 16] = 1.0 / np.minimum(np.arange(16) + 1, w)
    c[:, 7, 64:80] = np.arange(16)[None, :]
    return c.reshape(128, 1024)


def _mixer_bufs(G, NT, sample):
    AR = G.AR
    B = type("B", (), {})()
    B.NT = NT
    B.hT = AR.alloc(8 * NT, BF16).rearrange("p (k t) -> p k t", k=8)
    B.xbf = [AR.alloc(1024, BF16)] * 2
    B.junk = B.xbf[0]
    B.ss = AR.alloc(8)
    B.stg = [(AR.alloc(2048), f"stg{i}") for i in range(2)]
    B.wbf = [(AR.alloc(2048, BF16), f"wbf{i}") for i in range(3)]
    B.pre = [AR.alloc(3 + NT) for _ in range(2)]
    B.cv = [AR.alloc(NT) for _ in range(2)]
    B.qkvc = AR.alloc(12 * NT).rearrange("p (c t) -> p c t", c=12)
    B.zs = AR.alloc(4 * NT).rearrange("p (c t) -> p c t", c=4)
    B.xqT = AR.alloc(4 * NT, BF16).rearrange("p (c t) -> p c t", c=4)
    B.yT = AR.alloc(12 * NT, BF16).rearrange("p (c t) -> p c t", c=12)
    B.macc8 = AR.alloc(8 * NT).rearrange("p (c t) -> p c t", c=8)
    B.mT = AR.alloc(8 * NT, BF16).rearrange("p (c t) -> p c t", c=8)
    B.sqb = [AR.alloc(NT) for _ in range(2)]
    B.rinv = [AR.alloc(NT) for _ in range(2)]
    B.sig = B.sqb
    B.prod = B.rinv
    B.dT = [AR.alloc(NT, BF16) for _ in range(2)]
    B.pA = AR.alloc(19 * 16 if sample else 15 + NT)
    B.pB = AR.alloc(19 * 16 if sample else 15 + NT)
    B.t16 = AR.alloc(16)
    B.ktok = AR.alloc(512).rearrange("p (h d) -> p h d", h=4)
    B.vtok = AR.alloc(512).rearrange("p (h d) -> p h d", h=4)
    B.ba = AR.alloc(32)
    B.wba = AR.alloc(64, BF16)
    B.gcc = AR.alloc(16)
    names = ["gL", "gcr", "egr", "dm", "t1", "dmT", "t2", "Pa", "Pb", "Qa", "Qb", "R", "u", "wT", "attnT", "vn", "qg", "kbg", "vb", "kd", "osq", "rr", "y1", "kdsc"]
    B.dw = [{}, {}]
    shared = ("gL", "dm", "dmT", "osq", "rr", "y1", "kdsc") if sample else ()
    for n in names:
        if n in shared:
            B.dw[0][n] = B.dw[1][n] = AR.alloc(128)
        else:
            B.dw[0][n] = AR.alloc(128)
            B.dw[1][n] = AR.alloc(128)
    B.dw_shared = shared
    B.pexp = [AR.alloc(256) for _ in range(2)]
    B.pn = [AR.alloc(256, BF16) for _ in range(2)]
    B.pT = [AR.alloc(256, BF16).rearrange("p (c t) -> p c t", c=2) for _ in range(2)]
    B.asm = AR.alloc(32)
    if not sample:
        B.uT = AR.alloc(4 * (15 + NT)).rearrange("p (c t) -> p c t", c=4)
        B.kTm = AR.alloc(4 * 256, BF16).rearrange("p (h m) -> p h m", h=4)
        B.vm = AR.alloc(2 * 512, BF16).rearrange("p (c n) -> p c n", c=2)
        B.hTm = B.hT
        qflat = B.qkvc.rearrange("p c t -> p (c t)")
        B.memx = qflat[:, 0:1024]
        B.kvrow = qflat[:, 1024:3072].rearrange("p (j n) -> p j n", j=2)
        B.rowbuf = qflat[:, 0:1536]
    else:
        B.uTs = AR.alloc(4 * 16 * 19).rearrange("p (c s e) -> p c s e", c=4, s=16)
        B.pre_s = [AR.alloc(16 * 7).rearrange("p (s e) -> p s e", s=16) for _ in range(2)]
        B.hist_s = AR.alloc(12 * 48).rearrange("p (c s e) -> p c s e", c=12, s=16)
        B.cvout = AR.alloc(12 * 48).rearrange("p (c s e) -> p c s e", c=12, s=16)
        B.ld1536 = AR.alloc(1536)
        B.ld512 = [AR.alloc(512) for _ in range(2)]
        mk_ = AR.mark()
        B.Sh = [AR.alloc(16 * 128).rearrange("p (s d) -> p s d", s=16) for _ in range(2)]
        B.rowbuf = B.Sh[0].rearrange("p s d -> p (s d)")[:, 0:1536]
        B.kdm = AR.alloc(16 * 128).rearrange("p (s d) -> p s d", s=16)
        B.wTm = AR.alloc(1088)
        B.o1 = AR.alloc(64)
        B.oTs = AR.alloc(64)
        hw_ = AR.mark()
        AR.release(mk_)
        B.xqm = AR.alloc(4 * 1088, BF16).rearrange("p (h r) -> p h r", h=4)
        B.kvs = [AR.alloc(1024) for _ in range(2)]
        B.kvb = [AR.alloc(1024, BF16).rearrange("p (c n) -> p c n", c=2) for _ in range(2)]
        B.kTs = [AR.alloc(1024, BF16).rearrange("p (h m) -> p h m", h=4) for _ in range(2)]
        B.pTall = [AR.alloc(256, BF16).rearrange("p (c t) -> p c t", c=2) for _ in range(4)]
        AR.release(max(hw_, AR.mark()))
    return B


def _mem_kv(G, B, l):
    P, PS, PSB, pk = G.P, G.PS, G.PSB, G.pk
    for j in range(2):
        P.add("sp", CALL("dma_start", out=B.memx[:, :], in_=G.memp[j * 128:(j + 1) * 128, :]), writes=["memx"], dma=1, key="memx")
        G.rms_rstd(B.memx[:, :], 128, "memx", B.junk, "xbf0", B.ss[:, 0:1], "ss0")
        xb = B.xbf[j % 2]
        P.add("act", CALL("activation", out=xb[:, :], in_=B.memx[:, :], func=AF.Copy, scale=B.ss[:, 0:1]),
              reads=["memx", "ss0"], writes=["xbf0"])
        G.to_featmajor(xb, 128, "xbf0", B.hTm, f"hTm{j}", slice(j * 128, (j + 1) * 128), gsb=G.gmem_sb, gkey="gmem")
    for g in range(4):
        wv, wk = G.load_w(G.w_mkv[l], 8, g * 256, 256, B.stg, B.wbf)
        for j in range(2):
            b = P.ps()
            for k in range(8):
                P.add("pe", CALL("matmul", PS(b, 256), lhsT=B.hTm[:, k, j * 128:(j + 1) * 128], rhs=wv[:, k, :], start=(k == 0), stop=(k == 7)),
                      reads=[f"hTm{j}", wk], writes=[pk(b)])
            P.add("act", CALL("activation", out=B.kvrow[:, j, g * 256:(g + 1) * 256], in_=PS(b, 256), func=AF.Copy),
                  reads=[pk(b)], writes=[f"kvrow{j}_{g}"])
            if g >= 2:
                P.add("dve", CALL("tensor_copy", out=B.vm[:, j, (g - 2) * 256:(g - 1) * 256], in_=PS(b, 256)),
                      reads=[pk(b)], writes=[f"vm{j}_{g}"])
        if g < 2:
            for cc in range(2):
                b = P.ps()
                for k in range(8):
                    P.add("pe", CALL("matmul", PS(b, 256), lhsT=wv[:, k, cc * 128:(cc + 1) * 128], rhs=B.hTm[:, k, :], start=(k == 0), stop=(k == 7)),
                          reads=["hTm0", "hTm1", wk], writes=[pk(b)])
                P.add("act", CALL("activation", out=B.kTm[:, g * 2 + cc, :], in_=PS(b, 256), func=AF.Copy),
                      reads=[pk(b)], writes=[f"kTm{g * 2 + cc}"])
    for j in range(2):
        P.add("pool", CALL("dma_start", out=G.o_mk[l, j * 128:(j + 1) * 128, :], in_=B.kvrow[:, j, 0:512]),
              reads=[f"kvrow{j}_0", f"kvrow{j}_1"], dma=1, key=f"o_mk{j}")
        P.add("pool", CALL("dma_start", out=G.o_mv[l, j * 128:(j + 1) * 128, :], in_=B.kvrow[:, j, 512:1024]),
              reads=[f"kvrow{j}_2", f"kvrow{j}_3"], dma=1, key=f"o_mv{j}")
    B.kTm_keys = [f"kTm{h}" for h in range(4)]
    B.vm_keys = [f"vm{j}_{g}" for j in range(2) for g in (2, 3)]


def _proj(G, B, l, wdram, krows, c0, nch, srcT, srckeys, NT, consume, chunk_w=128):
    P, PS, pk = G.P, G.PS, G.pk
    i = 0
    while i < nch:
        ng = min(2, nch - i)
        ncols = 128 * ng if chunk_w == 128 else chunk_w
        wv, wk = G.load_w(wdram, krows, c0 + i * 128, ncols, B.stg, B.wbf)
        for cc in range(ng):
            b = P.ps()
            for k in range(krows):
                P.add("pe", CALL("matmul", PS(b, NT)[0:chunk_w, :], lhsT=wv[:, k, cc * 128:cc * 128 + chunk_w], rhs=srcT[:, k, 0:NT],
                                                               start=(k == 0), stop=(k == krows - 1)),
                      reads=list(srckeys) + [wk], writes=[pk(b)])
            consume(i + cc, b)
        i += ng


def _delta_tile(G, B, l, j, np_, tsl, sample, extra=None):
    P, PS, PSB, pk = G.P, G.PS, G.PSB, G.pk
    LT = (G.Ltri4 if sample else G.Ltri)
    SLx = (G.SL4 if sample else G.SLm)
    I_ = G.identF
    ones = G.onesF
    r_ = slice(0, np_)
    for nm, c0, dst in (("ktok", 4, B.ktok), ("vtok", 8, B.vtok)):
        b = P.ps()
        for h in range(4):
            P.add("pe", CALL("transpose", out=PS(b, 128, h * 128)[r_, :], in_=B.qkvc[:, c0 + h, tsl], identity=I_),
                  reads=[f"qkvc{c0 + h}", "cst"], writes=[pk(b)])
        P.add("act", CALL("activation", out=dst[r_, :, :], in_=PS(b, 512)[r_, :].rearrange("p (h d) -> p h d", h=4), func=AF.Copy),
              reads=[pk(b)], writes=[nm])
    b = P.ps()
    P.add("pe", CALL("matmul", PS(b, 4)[r_, :], lhsT=LT[r_, r_], rhs=B.ba[r_, 8:12], start=True, stop=True), reads=["ba", "cst"], writes=[pk(b)])
    P.add("pe", CALL("matmul", PS(b, 4, 8)[r_, :], lhsT=LT[r_, r_], rhs=B.ba[r_, 8:12], start=True, stop=False), reads=["ba", "cst"], writes=[pk(b)])
    P.add("pe", CALL("matmul", PS(b, 4, 8)[r_, :], lhsT=SLx[r_, r_], rhs=B.ba[r_, 8:12], start=False, stop=True), reads=["ba", "cst"], writes=[pk(b)])
    P.add("dve", CALL("tensor_copy", out=B.gcc[r_, 0:4], in_=PS(b, 4)[r_, :]), reads=[pk(b)], writes=["gcc"])
    P.add("dve", CALL("tensor_copy", out=B.gcc[r_, 8:12], in_=PS(b, 4, 8)[r_, :]), reads=[pk(b)], writes=["gcc"])
    P.add("act", CALL("activation", out=B.gcc[r_, 4:8], in_=B.gcc[r_, 0:4], func=AF.Exp), reads=["gcc"], writes=["gcc"])
    def head_body(h):
        W = B.dw[h % 2]
        wn = lambda n, h=h: (f"dws_{n}" if n in B.dw_shared else f"dw{h % 2}_{n}")
        qT = B.qkvc[:, h, tsl]
        kT = B.qkvc[:, 4 + h, tsl]
        beta = B.ba[r_, h:h + 1]
        nbeta = B.ba[r_, 4 + h:5 + h]
        gcol = B.ba[r_, 8 + h:9 + h]
        gc_c = B.gcc[r_, h:h + 1]
        egc_c = B.gcc[r_, 4 + h:5 + h]
        gl_c = B.gcc[r_, 8 + h:9 + h]
        P.add("dve", CALL("tensor_scalar", out=W["gL"][r_, r_], in0=LT[r_, r_], scalar1=gcol, scalar2=0.0, op0=ALU.mult, op1=ALU.add),
              reads=["ba", "cst"], writes=[wn("gL")])
        b = P.ps()
        P.add("pe", CALL("matmul", PS(b, np_), lhsT=ones[r_, :], rhs=W["gL"][r_, r_], start=True, stop=True), reads=[wn("gL"), "cst"], writes=[pk(b)])
        P.add("act", CALL("activation", out=W["gcr"][:, r_], in_=PS(b, np_), func=AF.Copy), reads=[pk(b)], writes=[wn("gcr")])
        P.add("act", CALL("activation", out=W["egr"][:, r_], in_=PS(b, np_), func=AF.Exp), reads=[pk(b)], writes=[wn("egr")])
        P.add("dve", CALL("tensor_scalar", out=W["dm"][r_, r_], in0=W["gcr"][r_, r_], scalar1=gc_c, scalar2=0.0, op0=ALU.subtract, op1=ALU.max),
              reads=[wn("gcr"), "gcc"], writes=[wn("dm")])
        P.add("act", CALL("activation", out=W["dm"][r_, r_], in_=W["dm"][r_, r_], func=AF.Exp, scale=-1.0), reads=[wn("dm")], writes=[wn("dm")])
        P.add("pool", CALL("tensor_tensor", out=W["t1"][r_, r_], in0=W["dm"][r_, r_], in1=SLx[r_, r_], op=ALU.mult), reads=[wn("dm"), "cst"], writes=[wn("t1")])
        P.add("dve", CALL("tensor_scalar", out=W["dmT"][r_, r_], in0=W["gcr"][r_, r_], scalar1=gc_c, scalar2=0.0, op0=ALU.subtract, op1=ALU.min),
              reads=[wn("gcr"), "gcc"], writes=[wn("dmT")])
        P.add("act", CALL("activation", out=W["dmT"][r_, r_], in_=W["dmT"][r_, r_], func=AF.Exp), reads=[wn("dmT")], writes=[wn("dmT")])
        P.add("pool", CALL("tensor_tensor", out=W["t2"][r_, r_], in0=W["dmT"][r_, r_], in1=LT[r_, r_], op=ALU.mult), reads=[wn("dmT"), "cst"], writes=[wn("t2")])
        b = P.ps()
        P.add("pe", CALL("matmul", PS(b, np_)[r_, :], lhsT=kT, rhs=kT, start=True, stop=True), reads=[f"qkvc{4 + h}"], writes=[pk(b)])
        P.add("dve", CALL("scalar_tensor_tensor", out=W["Pa"][r_, r_], in0=PS(b, np_)[r_, :], scalar=nbeta, in1=W["t1"][r_, r_], op0=ALU.mult, op1=ALU.mult),
              reads=[pk(b), "ba", wn("t1")], writes=[wn("Pa")])
        b = P.ps()
        P.add("pe", CALL("transpose", out=PS(b, np_)[r_, :], in_=W["Pa"][r_, r_], identity=I_[r_, r_]), reads=[wn("Pa"), "cst"], writes=[pk(b)])
        P.add("act", CALL("activation", out=W["Qa"][r_, r_], in_=PS(b, np_)[r_, :], func=AF.Copy), reads=[pk(b)], writes=[wn("Qa")])
        P.add("dve", CALL("tensor_tensor", out=W["R"][r_, r_], in0=PS(b, np_)[r_, :], in1=I_[r_, r_], op=ALU.add), reads=[pk(b), "cst"], writes=[wn("R")])
        nst = 1 if sample else 5
        Pk, Qk, Pn, Qn = "Pa", "Qa", "Pb", "Qb"
        for k in range(nst):
            bP = P.ps()
            P.add("pe", CALL("matmul", PS(bP, np_)[r_, :], lhsT=W[Qk][r_, r_], rhs=W[Pk][r_, r_], start=True, stop=True),
                  reads=[wn(Pk), wn(Qk)], writes=[pk(bP)])
            P.add("act", CALL("activation", out=W[Pn][r_, r_], in_=PS(bP, np_)[r_, :], func=AF.Copy), reads=[pk(bP)], writes=[wn(Pn)])
            if k < nst - 1:
                bQ = P.ps()
                P.add("pe", CALL("matmul", PS(bQ, np_)[r_, :], lhsT=W[Pk][r_, r_], rhs=W[Qk][r_, r_], start=True, stop=True),
                      reads=[wn(Pk), wn(Qk)], writes=[pk(bQ)])
                P.add("dve", CALL("tensor_copy", out=W[Qn][r_, r_], in_=PS(bQ, np_)[r_, :]), reads=[pk(bQ)], writes=[wn(Qn)])
            bR = P.ps()
            P.add("pe", CALL("matmul", PS(bR, np_)[r_, :], lhsT=W[Pn][r_, r_], rhs=W["R"][r_, r_], start=True, stop=True),
                  reads=[wn(Pn), wn("R")], writes=[pk(bR)])
            P.add("dve", CALL("tensor_tensor", out=W["R"][r_, r_], in0=PS(bR, np_)[r_, :], in1=W["R"][r_, r_], op=ALU.add), reads=[pk(bR), wn("R")], writes=[wn("R")])
            Pk, Pn = Pn, Pk
            Qk, Qn = Qn, Qk
        P.add("dve", CALL("tensor_scalar", out=W["vb"][r_, :], in0=B.vtok[r_, h, :], scalar1=beta, scalar2=0.0, op0=ALU.mult, op1=ALU.add),
              reads=["vtok", "ba"], writes=[wn("vb")])
        P.add("dve", CALL("tensor_scalar", out=W["kbg"][r_, :], in0=B.ktok[r_, h, :], scalar1=beta, scalar2=egc_c, op0=ALU.mult, op1=ALU.mult),
              reads=["ktok", "ba", "gcc"], writes=[wn("kbg")])
        P.add("act", CALL("activation", out=W["kdsc"][r_, 0:1], in_=gc_c, func=AF.Exp, scale=-1.0, bias=gl_c), reads=["gcc"], writes=[wn("kdsc")])
        P.add("dve", CALL("tensor_scalar", out=W["kd"][r_, :], in0=B.ktok[r_, h, :], scalar1=W["kdsc"][r_, 0:1], scalar2=0.0, op0=ALU.mult, op1=ALU.add),
              reads=["ktok", wn("kdsc")], writes=[wn("kd")])
        P.add("pool", CALL("tensor_tensor", out=W["qg"][:, r_], in0=qT, in1=W["egr"][:, r_], op=ALU.mult), reads=[f"qkvc{h}", wn("egr")], writes=[wn("qg")])
        b = P.ps()
        P.add("pe", CALL("matmul", PS(b, 128)[r_, :], lhsT=W["R"][r_, r_], rhs=W["vb"][r_, :], start=True, stop=True), reads=[wn("R"), wn("vb")], writes=[pk(b)])
        P.add("act", CALL("activation", out=W["u"][r_, :], in_=PS(b, 128)[r_, :], func=AF.Copy), reads=[pk(b)], writes=[wn("u")])
        b = P.ps()
        P.add("pe", CALL("matmul", PS(b, np_), lhsT=W["kbg"][r_, :], rhs=W["R"][r_, r_], start=True, stop=True), reads=[wn("R"), wn("kbg")], writes=[pk(b)])
        P.add("act", CALL("activation", out=W["wT"][:, r_], in_=PS(b, np_), func=AF.Copy), reads=[pk(b)], writes=[wn("wT")])
        b = P.ps()
        P.add("pe", CALL("matmul", PS(b, np_)[r_, :], lhsT=kT, rhs=qT, start=True, stop=True), reads=[f"qkvc{4 + h}", f"qkvc{h}"], writes=[pk(b)])
        P.add("dve", CALL("tensor_tensor", out=W["attnT"][r_, r_], in0=PS(b, np_)[r_, :], in1=W["t2"][r_, r_], op=ALU.mult), reads=[pk(b), wn("t2")], writes=[wn("attnT")])
        if not sample:
            Sk = f"S{h}"
            Sh_ = G.Sst[:, h, :]
            bo = P.ps()
            for ci in range(2):
                rr_ = slice(ci * 64, ci * 64 + 64)
                bw = P.ps()
                P.add("pe", CALL("matmul", PS(bw, 128), lhsT=W["wT"][:, 0:128], rhs=Sh_, start=True, stop=True), reads=[wn("wT"), Sk], writes=[pk(bw)])
                P.add("dve", CALL("tensor_tensor", out=W["vn"][rr_, :], in0=W["u"][rr_, :], in1=PS(bw, 128)[rr_, :], op=ALU.subtract),
                      reads=[pk(bw), wn("u")], writes=[wn("vn") + str(ci)])
                P.add("pe", CALL("matmul", PS(bo, 64, 256 + ci * 64), lhsT=Sh_, rhs=W["qg"][:, rr_], start=True, stop=False), reads=[wn("qg"), Sk], writes=[pk(bo)])
                P.add("pe", CALL("matmul", PS(bo, 64, 256 + ci * 64), lhsT=W["vn"][rr_, :], rhs=W["attnT"][rr_, rr_], start=False, stop=True),
                      reads=[wn("vn") + str(ci), wn("attnT")], writes=[pk(bo)])
                bs = P.ps()
                P.add("pe", CALL("matmul", PS(bs, 128), lhsT=W["kd"][rr_, :], rhs=W["vn"][rr_, :], start=True, stop=True), reads=[wn("kd"), wn("vn") + str(ci)], writes=[pk(bs)])
                P.add("dve", CALL("scalar_tensor_tensor", out=Sh_, in0=Sh_, scalar=W["egr"][:, ci * 64 + 63:ci * 64 + 64], in1=PS(bs, 128), op0=ALU.mult, op1=ALU.add),
                      reads=[pk(bs), wn("egr"), Sk], writes=[Sk])
            o_ap = PS(bo, np_, 256)
            o_key = pk(bo)
        else:
            Shb = B.Sh[h % 2]
            Sk = f"Sh{h % 2}"
            P.add("sp", CALL("dma_start", out=Shb[:, :, :], in_=G.st_delta[l, :, h].rearrange("s k v -> k s v")), writes=[Sk], dma=1, key=Sk)
            P.add("dve", CALL("tensor_copy", out=B.wTm[:, 0:1088].rearrange("p (s r) -> p s r", r=68)[:, :, 0:4], in_=W["wT"][:, 0:64].rearrange("p (s i) -> p s i", i=4)),
                  reads=[wn("wT")], writes=["wTm"])
            bw = P.ps()
            for s in range(16):
                P.add("pe", CALL("matmul", PS(bw, 128)[0:64, :], lhsT=B.wTm[:, s * 64:(s + 1) * 64], rhs=Shb[:, s, :], start=(s == 0), stop=(s == 15)),
                      reads=["wTm", Sk], writes=[pk(bw)])
            P.add("dve", CALL("tensor_tensor", out=W["vn"][0:64, :], in0=W["u"][0:64, :], in1=PS(bw, 128)[0:64, :], op=ALU.subtract), reads=[pk(bw), wn("u")], writes=[wn("vn") + "0"])
            bo = P.ps()
            for s in range(16):
                P.add("pe", CALL("matmul", PS(bo, 4, 4 * s), lhsT=Shb[:, s, :], rhs=W["qg"][:, 4 * s:4 * s + 4], start=True, stop=True),
                      reads=[wn("qg"), Sk], writes=[pk(bo)])
            P.add("act", CALL("activation", out=B.o1[:, 0:64], in_=PS(bo, 64), func=AF.Copy), reads=[pk(bo)], writes=["o1"])
            b2 = P.ps()
            P.add("pe", CALL("matmul", PS(b2, 64), lhsT=W["vn"][0:64, :], rhs=W["attnT"][0:64, 0:64], start=True, stop=True), reads=[wn("vn") + "0", wn("attnT")], writes=[pk(b2)])
            P.add("dve", CALL("tensor_tensor", out=B.oTs[:, 0:64], in0=PS(b2, 64), in1=B.o1[:, 0:64], op=ALU.add), reads=[pk(b2), "o1"], writes=["oTs"])
            P.add("pool", CALL("tensor_tensor", out=B.kdm[0:64, :, :], in0=W["kd"][0:64, :].unsqueeze(1).to_broadcast([64, 16, 128]),
                                                         in1=G.seqmask[0:64, :].unsqueeze(2).to_broadcast([64, 16, 128]), op=ALU.mult),
                  reads=[wn("kd"), "cst"], writes=["kdm"])
            for s in range(16):
                bs = P.ps()
                P.add("pe", CALL("matmul", PS(bs, 128), lhsT=B.kdm[0:64, s, :], rhs=W["vn"][0:64, :], start=True, stop=True), reads=["kdm", wn("vn") + "0"], writes=[pk(bs)])
                P.add("dve", CALL("scalar_tensor_tensor", out=Shb[:, s, :], in0=Shb[:, s, :], scalar=W["egr"][:, 4 * s + 3:4 * s + 4], in1=PS(bs, 128), op0=ALU.mult, op1=ALU.add),
                      reads=[pk(bs), wn("egr"), Sk], writes=[Sk])
            P.add("pool", CALL("dma_start", out=G.o_delta_s[l, :, h].rearrange("s k v -> k s v"), in_=Shb[:, :, :]), reads=[Sk], dma=1, key="o_" + Sk)
            o_ap = B.oTs[:, 0:64]
            o_key = "oTs"
        P.add("act", CALL("activation", out=W["osq"][:, r_], in_=o_ap, func=AF.Square), reads=[o_key], writes=[wn("osq")])
        bq = P.ps()
        P.add("pe", CALL("matmul", PS(bq, np_), lhsT=ones, rhs=W["osq"][:, r_], start=True, stop=True), reads=[wn("osq"), "cst"], writes=[pk(bq)])
        P.add("act", CALL("activation", out=W["rr"][:, r_], in_=PS(bq, np_), func=AF.Sqrt, scale=1.0 / 128, bias=G.epsb[:, 0:1]), reads=[pk(bq), "epsb"], writes=[wn("rr")])
        P.add("dve", CALL("reciprocal", out=W["rr"][:, r_], in_=W["rr"][:, r_]), reads=[wn("rr")], writes=[wn("rr")])
        P.add("dve", CALL("scalar_tensor_tensor", out=W["y1"][:, r_], in0=o_ap, scalar=G.gdn_sb[:, 0:1], in1=W["rr"][:, r_], op0=ALU.mult, op1=ALU.mult),
              reads=[o_key, "gdn", wn("rr")], writes=[wn("y1")])
        P.add("pool", CALL("tensor_tensor", out=B.yT[:, 4 + h, tsl], in0=W["y1"][:, r_], in1=B.zs[:, h, tsl], op=ALU.mult), reads=[wn("y1"), f"zs{h}"], writes=[f"yT{4 + h}_{j}"])

    if sample:
        for h in range(4):
            head_body(h)
    else:
        for h0 in (0, 2):
            recs = []
            for hh, banks in ((h0, (2, 3, 4)), (h0 + 1, (5, 6, 7))):
                P.begin_record(banks)
                head_body(hh)
                recs.append(P.end_record())
            if extra:
                recs.append(extra.pop(0))
            P.replay(recs)


def _attn_softmax(G, B, sc_ap, sc_key, np_, h, out_writes):
    P, PS, PSB, pk = G.P, G.PS, G.PSB, G.pk
    r_ = slice(0, np_)
    i = h % 2
    sc = 128.0 ** -0.5
    mx = B.asm[r_, 4 * i:4 * i + 1]
    nmx = B.asm[r_, 4 * i + 1:4 * i + 2]
    rs = B.asm[r_, 4 * i + 2:4 * i + 3]
    ak = f"asm{i}"
    P.add("dve", CALL("reduce_max", out=mx, in_=sc_ap, axis=AX.X), reads=[sc_key], writes=[ak])
    P.add("dve", CALL("tensor_scalar", out=nmx, in0=mx, scalar1=-sc, scalar2=0.0, op0=ALU.mult, op1=ALU.add), reads=[ak], writes=[ak])
    P.add("act", CALL("activation", out=B.pexp[i][r_, :], in_=sc_ap, func=AF.Exp, scale=sc, bias=nmx, accum_out=rs), reads=[sc_key, ak], writes=[f"pexp{i}", ak])
    P.add("dve", CALL("reciprocal", out=rs, in_=rs), reads=[ak], writes=[ak])
    P.add("dve", CALL("tensor_scalar", out=B.pn[i][r_, :], in0=B.pexp[i][r_, :], scalar1=rs, scalar2=0.0, op0=ALU.mult, op1=ALU.add), reads=[f"pexp{i}", ak], writes=[f"pn{i}"])
    bt = P.ps()
    for mc in range(2):
        P.add("pe", CALL("transpose", out=PSB(bt, np_, mc * 128), in_=B.pn[i][r_, mc * 128:(mc + 1) * 128], identity=G.identB[r_, r_]),
              reads=[f"pn{i}", "identB"], writes=[pk(bt)])
    P.add("act", CALL("activation", out=B.pT[i][:, :, r_], in_=PSB(bt, 256).rearrange("p (c t) -> p c t", c=2)[:, :, r_], func=AF.Copy), reads=[pk(bt)], writes=[f"pT{i}"])
    return B.pT[i], f"pT{i}"


def _mixer_st(G, B, l, st, tiles, sample):
    P, PS, PSB, pk = G.P, G.PS, G.PSB, G.pk
    NT = sum(t[1] for t in tiles)
    nt = len(tiles)
    hkeys = [f"hT{j}" for j in range(nt)]
    for j, (x_ap, np_, xkey) in enumerate(tiles):
        G.rms_rstd(x_ap, np_, xkey, B.junk, "xbf0", B.ss[:, 0:1], "ss0")
        xb = B.xbf[j % 2]
        P.add("act", CALL("activation", out=xb[0:np_, :], in_=x_ap, func=AF.Copy, scale=B.ss[0:np_, 0:1]),
              reads=[xkey, "ss0"], writes=["xbf0"])
        G.to_featmajor(xb, np_, "xbf0", B.hT, hkeys[j], slice(j * 128, j * 128 + np_), gsb=G.gmix_sb, gkey="gmix")

    win = (2, 4, 8, 16)
    if not sample:
        L = 15 + NT
        if st == 0:
            P.add("pool", CALL("memset", B.uT[:, :, 0:15], 0.0), writes=[f"uT{c}" for c in range(4)])

        def pool_consume(c, b):
            P.add("act", CALL("activation", out=B.uT[:, c, 15:L], in_=PS(b, NT), func=AF.Copy), reads=[pk(b)], writes=[f"uT{c}"])
            a = B.uT[:, c, :]
            bufs = [(B.pA, "pA"), (B.pB, "pB")]
            src, skey = a, f"uT{c}"
            sh = 1
            for s_ in range(c + 1):
                dst, dkey = bufs[s_ % 2]
                lo = 2 * sh - 1
                P.add("dve", CALL("tensor_tensor", out=dst[:, lo:L], in0=src[:, lo:L], in1=src[:, lo - sh:L - sh], op=ALU.add),
                      reads=[skey], writes=[dkey])
                src, skey = dst, dkey
                sh *= 2
            dT = B.dT[c % 2]
            dk = f"dT{c % 2}"
            P.add("dve", CALL("scalar_tensor_tensor", out=dT[:, 0:NT], in0=src[:, 15:L], scalar=1.0 / win[c], in1=a[:, 15:L], op0=ALU.mult, op1=ALU.subtract),
                  reads=[skey, f"uT{c}"], writes=[dk])
            if st == 0:
                P.add("dve", CALL("tensor_tensor", out=B.t16[:, 0:16], in0=src[:, 15:31], in1=G.rcnt[:, c * 16:(c + 1) * 16], op=ALU.mult), reads=[skey, "cst"], writes=["t16"])
                P.add("dve", CALL("tensor_tensor", out=dT[:, 0:16], in0=B.t16[:, 0:16], in1=a[:, 15:31], op=ALU.subtract), reads=["t16", f"uT{c}", dk], writes=[dk])
            b2 = P.ps()
            P.add("pe", CALL("matmul", PS(b2, NT), lhsT=G.wgrp_b[:, c, :], rhs=dT[:, 0:NT], start=True, stop=True), reads=[dk, "wgrp_b"], writes=[pk(b2)])
            P.add("act", CALL("activation", out=B.yT[:, c, 0:NT], in_=PS(b2, NT), func=AF.Copy, scale=G.psc_sb[:, c:c + 1]), reads=[pk(b2), "psc"], writes=[f"yT{c}_all"])
        _proj(G, B, l, G.w_in[l], 8, 0, 4, B.hT, hkeys, NT, pool_consume)
        P.add("pool", CALL("tensor_copy", out=B.uT[:, :, 0:15], in_=B.uT[:, :, NT:NT + 15]), writes=[f"uT{c}" for c in range(4)])
    else:
        def pool_consume(c, b):
            P.add("act", CALL("activation", out=B.uTs[:, c, :, 15:19], in_=PS(b, 64).rearrange("p (s i) -> p s i", i=4), func=AF.Copy), reads=[pk(b)], writes=[f"uTs{c}"])
            a = B.uTs[:, c, :, :]
            pA = B.pA[:, 0:304].rearrange("p (s e) -> p s e", e=19)
            pB = B.pB[:, 0:304].rearrange("p (s e) -> p s e", e=19)
            bufs = [(pA, "pA"), (pB, "pB")]
            src, skey = a, f"uTs{c}"
            sh = 1
            for s_ in range(c + 1):
                dst, dkey = bufs[s_ % 2]
                lo = 2 * sh - 1
                P.add("dve", CALL("tensor_tensor", out=dst[:, :, lo:19], in0=src[:, :, lo:19], in1=src[:, :, lo - sh:19 - sh], op=ALU.add),
                      reads=[skey], writes=[dkey])
                src, skey = dst, dkey
                sh *= 2
            dT = B.dT[c % 2]
            dk = f"dT{c % 2}"
            P.add("dve", CALL("scalar_tensor_tensor", out=dT[:, 0:64].rearrange("p (s i) -> p s i", i=4), in0=src[:, :, 15:19], scalar=1.0 / win[c], in1=a[:, :, 15:19], op0=ALU.mult, op1=ALU.subtract),
                  reads=[skey, f"uTs{c}"], writes=[dk])
            b2 = P.ps()
            P.add("pe", CALL("matmul", PS(b2, NT), lhsT=G.wgrp_b[:, c, :], rhs=dT[:, 0:NT], start=True, stop=True), reads=[dk, "wgrp_b"], writes=[pk(b2)])
            P.add("act", CALL("activation", out=B.yT[:, c, 0:NT], in_=PS(b2, NT), func=AF.Copy, scale=G.psc_sb[:, c:c + 1]), reads=[pk(b2), "psc"], writes=[f"yT{c}_all"])
        _proj(G, B, l, G.w_in[l], 8, 0, 4, B.hT, hkeys, NT, pool_consume)

    def qkv_consume(c, b):
        if not sample:
            pre = B.pre[c % 2]
            pkey = f"pre{c % 2}"
            cv = B.cv[c % 2]
            ckey = f"cv{c % 2}"
            P.add("pool", CALL("tensor_copy", out=pre[:, 0:3], in_=G.hist[:, c, :]), reads=[f"hist{c}"], writes=[pkey + "h"])
            P.add("act", CALL("activation", out=pre[:, 3:3 + NT], in_=PS(b, NT), func=AF.Copy), reads=[pk(b)], writes=[pkey])
            P.add("dve", CALL("tensor_scalar", out=cv[:, 0:NT], in0=pre[:, 0:NT], scalar1=G.wconv_sb[:, c, 0:1], scalar2=0.0, op0=ALU.mult, op1=ALU.add),
                  reads=[pkey, pkey + "h", "wconv"], writes=[ckey])
            for j in range(1, 4):
                P.add("dve", CALL("scalar_tensor_tensor", out=cv[:, 0:NT], in0=pre[:, j:j + NT], scalar=G.wconv_sb[:, c, j:j + 1], in1=cv[:, 0:NT], op0=ALU.mult, op1=ALU.add),
                      reads=[pkey, pkey + "h", "wconv", ckey], writes=[ckey])
            P.add("pool", CALL("tensor_copy", out=G.hist[:, c, :], in_=pre[:, NT:NT + 3]), reads=[pkey], writes=[f"hist{c}"])
            P.add("act", CALL("activation", out=B.qkvc[:, c, 0:NT], in_=cv[:, 0:NT], func=AF.Silu), reads=[ckey], writes=[f"qkvc{c}"])
        else:
            pre = B.pre_s[c % 2]
            pkey = f"pre{c % 2}"
            cv = B.cv[c % 2][:, 0:64].rearrange("p (s i) -> p s i", i=4)
            ckey = f"cv{c % 2}"
            P.add("pool", CALL("tensor_copy", out=pre[:, :, 0:3], in_=B.hist_s[:, c, :, :]), reads=[f"hist_s{c // 4}"], writes=[pkey + "h"])
            P.add("act", CALL("activation", out=pre[:, :, 3:7], in_=PS(b, 64).rearrange("p (s i) -> p s i", i=4), func=AF.Copy), reads=[pk(b)], writes=[pkey])
            P.add("dve", CALL("tensor_scalar", out=cv, in0=pre[:, :, 0:4], scalar1=G.wconv_sb[:, c, 0:1], scalar2=0.0, op0=ALU.mult, op1=ALU.add),
                  reads=[pkey, pkey + "h", "wconv"], writes=[ckey])
            for j in range(1, 4):
                P.add("dve", CALL("scalar_tensor_tensor", out=cv, in0=pre[:, :, j:j + 4], scalar=G.wconv_sb[:, c, j:j + 1], in1=cv, op0=ALU.mult, op1=ALU.add),
                      reads=[pkey, pkey + "h", "wconv", ckey], writes=[ckey])
            P.add("pool", CALL("tensor_copy", out=B.cvout[:, c, :, :], in_=pre[:, :, 4:7]), reads=[pkey], writes=[f"cvout{c // 4}"])
            P.add("act", CALL("activation", out=B.qkvc[:, c, 0:NT], in_=B.cv[c % 2][:, 0:64], func=AF.Silu), reads=[ckey], writes=[f"qkvc{c}"])
    _proj(G, B, l, G.w_in[l], 8, OFF_Q, 12, B.hT, hkeys, NT, qkv_consume)

    for c in range(8):
        i = c % 2
        P.add("act", CALL("activation", out=B.sqb[i][:, 0:NT], in_=B.qkvc[:, c, 0:NT], func=AF.Square), reads=[f"qkvc{c}"], writes=[f"sqb{i}"])
        b = P.ps()
        P.add("pe", CALL("matmul", PS(b, NT), lhsT=G.onesF, rhs=B.sqb[i][:, 0:NT], start=True, stop=True), reads=[f"sqb{i}", "cst"], writes=[pk(b)])
        P.add("act", CALL("activation", out=B.rinv[i][:, 0:NT], in_=PS(b, NT), func=AF.Sqrt, scale=1.0, bias=G.epsb[:, 0:1]), reads=[pk(b), "epsb"], writes=[f"rinv{i}"])
        P.add("dve", CALL("reciprocal", out=B.rinv[i][:, 0:NT], in_=B.rinv[i][:, 0:NT]), reads=[f"rinv{i}"], writes=[f"rinv{i}"])
        scl = (128.0 ** -0.5) if c < 4 else 1.0
        P.add("dve", CALL("scalar_tensor_tensor", out=B.qkvc[:, c, 0:NT], in0=B.rinv[i][:, 0:NT], scalar=scl, in1=B.qkvc[:, c, 0:NT], op0=ALU.mult, op1=ALU.mult),
              reads=[f"rinv{i}", f"qkvc{c}"], writes=[f"qkvc{c}"])

    def z_consume(c, b):
        P.add("act", CALL("activation", out=B.zs[:, c, 0:NT], in_=PS(b, NT), func=AF.Silu), reads=[pk(b)], writes=[f"zs{c}"])
    _proj(G, B, l, G.w_in[l], 8, OFF_Z, 4, B.hT, hkeys, NT, z_consume)

    def xq_consume(c, b):
        P.add("act", CALL("activation", out=B.xqT[:, c, 0:NT], in_=PS(b, NT), func=AF.Copy), reads=[pk(b)], writes=[f"xqT{c}"])
    _proj(G, B, l, G.w_in[l], 8, OFF_XQ, 4, B.hT, hkeys, NT, xq_consume)

    wba_t, wbak_t = G.load_w(G.w_in[l], 8, OFF_BA, 8, B.stg, B.wbf)
    wba = B.wba.rearrange("p (k n) -> p k n", k=8)
    wbak = "wba"
    P.add("act", CALL("activation", out=wba, in_=wba_t, func=AF.Copy), reads=[wbak_t], writes=[wbak])

    if sample:
        _sample_attn(G, B, l)
        P.barrier()
        P.add("pool", CALL("memset", B.wTm[:, :], 0.0), writes=["wTm"])
    def attn_tile(j, np_):
        r_ = slice(0, np_)
        tsl = slice(j * 128, j * 128 + np_)
        for h in range(4):
            bs_ = P.ps()
            P.add("pe", CALL("matmul", PS(bs_, 256)[r_, :], lhsT=B.xqT[:, h, tsl], rhs=B.kTm[:, h, :], start=True, stop=True),
                  reads=[f"xqT{h}", f"kTm{h}"], writes=[pk(bs_)])
            pT, pTk = _attn_softmax(G, B, PS(bs_, 256)[r_, :], pk(bs_), np_, h, None)
            bo = P.ps()
            for mc in range(2):
                P.add("pe", CALL("matmul", PS(bo, np_), lhsT=B.vm[:, mc, h * 128:(h + 1) * 128], rhs=pT[:, mc, r_], start=(mc == 0), stop=(mc == 1)),
                      reads=[pTk] + B.vm_keys, writes=[pk(bo)])
            P.add("act", CALL("activation", out=B.yT[:, 8 + h, tsl], in_=PS(bo, np_), func=AF.Copy), reads=[pk(bo)], writes=[f"yT{8 + h}_{j}"])

    ykeys = [[f"yT{c}_all" for c in range(4)], [f"yT{4 + h}_{j}" for h in range(4) for j in range(nt)], [f"yT{8 + h}_{j}" for h in range(4) for j in range(nt)]]
    if sample:
        ykeys[2] = [f"yT{8 + h}_0" for h in range(4)]
    norder = (0, 2, 1)

    def merge_branch(n):
        first, last = (n == norder[0]), (n == norder[-1])
        for half in range(2):
            wg0, wgk0 = G.load_w(G.w_in[l], 8, OFF_GATE + n * 1024 + half * 512, 256, B.stg, B.wbf)
            wb_, wbk = G.load_w(G.w_br[l, n], 4, half * 512, 512, B.stg, B.wbf)
            wg1, wgk1 = G.load_w(G.w_in[l], 8, OFF_GATE + n * 1024 + half * 512 + 256, 256, B.stg, B.wbf)
            for jj in range(4):
                wg, wgk = (wg0, wgk0) if jj < 2 else (wg1, wgk1)
                cc = jj % 2
                i = jj % 2
                mj = half * 4 + jj
                bg = P.ps()
                for k in range(8):
                    P.add("pe", CALL("matmul", PS(bg, NT), lhsT=wg[:, k, cc * 128:(cc + 1) * 128], rhs=B.hT[:, k, 0:NT], start=(k == 0), stop=(k == 7)),
                          reads=hkeys + [wgk], writes=[pk(bg)])
                P.add("act", CALL("activation", out=B.sig[i][:, 0:NT], in_=PS(bg, NT), func=AF.Sigmoid), reads=[pk(bg)], writes=[f"sqb{i}"])
                bb = P.ps()
                for c in range(4):
                    P.add("pe", CALL("matmul", PS(bb, NT), lhsT=wb_[:, c, jj * 128:(jj + 1) * 128], rhs=B.yT[:, n * 4 + c, 0:NT], start=(c == 0), stop=(c == 3)),
                          reads=ykeys[n] + [wbk], writes=[pk(bb)])
                if first:
                    P.add("dve", CALL("tensor_tensor", out=B.macc8[:, mj, 0:NT], in0=PS(bb, NT), in1=B.sig[i][:, 0:NT], op=ALU.mult), reads=[pk(bb), f"sqb{i}"], writes=[f"macc{mj}"])
                else:
                    P.add("dve", CALL("tensor_tensor", out=B.prod[i][:, 0:NT], in0=PS(bb, NT), in1=B.sig[i][:, 0:NT], op=ALU.mult), reads=[pk(bb), f"sqb{i}"], writes=[f"rinv{i}"])
                    if not last:
                        P.add("pool", CALL("tensor_tensor", out=B.macc8[:, mj, 0:NT], in0=B.macc8[:, mj, 0:NT], in1=B.prod[i][:, 0:NT], op=ALU.add), reads=[f"rinv{i}", f"macc{mj}"], writes=[f"macc{mj}"])
                    else:
                        P.add("pool", CALL("tensor_tensor", out=B.mT[:, mj, 0:NT], in0=B.macc8[:, mj, 0:NT], in1=B.prod[i][:, 0:NT], op=ALU.add),
                              reads=[f"rinv{i}", f"macc{mj}"], writes=[f"mT{mj}"])

    extra = None
    if not sample:
        P.begin_record((0, 1))
        for j_, (x_ap_, np__, xkey_) in enumerate(tiles):
            attn_tile(j_, np__)
        merge_branch(0)
        merge_branch(2)
        rc = P.end_record()
        nparts = 2 * len(tiles)
        step = (len(rc) + nparts - 1) // nparts
        extra = [rc[i * step:(i + 1) * step] for i in range(nparts)]

    def per_tile(j, x_ap, np_, xkey):
        r_ = slice(0, np_)
        tsl = slice(j * 128, j * 128 + np_)
        b = P.ps()
        for k in range(8):
            P.add("pe", CALL("matmul", PS(b, 8)[r_, :], lhsT=B.hT[:, k, tsl], rhs=wba[:, k, :], start=(k == 0), stop=(k == 7)), reads=[hkeys[j], wbak], writes=[pk(b)])
        ba = B.ba
        P.add("act", CALL("activation", out=ba[r_, 0:4], in_=PS(b, 4)[r_, :], func=AF.Sigmoid), reads=[pk(b)], writes=["ba"])
        P.add("dve", CALL("tensor_scalar", out=ba[r_, 4:8], in0=ba[r_, 0:4], scalar1=-1.0, scalar2=0.0, op0=ALU.mult, op1=ALU.add), reads=["ba"], writes=["ba"])
        P.add("dve", CALL("tensor_tensor", out=ba[r_, 12:16], in0=PS(b, 4, 4)[r_, :], in1=G.dtb_b[r_, :], op=ALU.add), reads=[pk(b), "dtb"], writes=["ba"])
        P.add("dve", CALL("scalar_tensor_tensor", out=ba[r_, 16:20], in0=ba[r_, 12:16], scalar=-1.0, in1=ba[r_, 12:16], op0=ALU.mult, op1=ALU.max), reads=["ba"], writes=["ba"])
        P.add("act", CALL("activation", out=ba[r_, 16:20], in_=ba[r_, 16:20], func=AF.Exp, scale=-1.0), reads=["ba"], writes=["ba"])
        P.add("act", CALL("activation", out=ba[r_, 16:20], in_=ba[r_, 16:20], func=AF.Ln, scale=1.0, bias=G.onesF[r_, 0:1]), reads=["ba", "cst"], writes=["ba"])
        P.add("dve", CALL("scalar_tensor_tensor", out=ba[r_, 12:16], in0=ba[r_, 12:16], scalar=0.0, in1=ba[r_, 16:20], op0=ALU.max, op1=ALU.add), reads=["ba"], writes=["ba"])
        P.add("dve", CALL("tensor_tensor", out=ba[r_, 8:12], in0=ba[r_, 12:16], in1=G.alog_b[r_, :], op=ALU.mult), reads=["ba", "alog"], writes=["ba"])
        _delta_tile(G, B, l, j, np_, tsl, sample, extra)

    if extra:
        P.ps_lo = 2
    for j_, (x_ap_, np__, xkey_) in enumerate(tiles):
        per_tile(j_, x_ap_, np__, xkey_)
    if not sample:
        P.ps_lo = 0

    if sample:
        merge_branch(0)
        merge_branch(2)
    merge_branch(1)

    mkeys = [f"mT{c}" for c in range(8)]
    for q in range(4):
        wo, wok = G.load_w(G.w_o[l], 8, q * 256, 256, B.stg, B.wbf)
        for j, (x_ap, np_, xkey) in enumerate(tiles):
            tsl = slice(j * 128, j * 128 + np_)
            b = P.ps()
            for k in range(8):
                P.add("pe", CALL("matmul", PS(b, 256)[0:np_, :], lhsT=B.mT[:, k, tsl], rhs=wo[:, k, :], start=(k == 0), stop=(k == 7)),
                      reads=mkeys + [wok], writes=[pk(b)])
            xo = x_ap[:, q * 256:(q + 1) * 256]
            P.add("dve", CALL("tensor_tensor", out=xo, in0=xo, in1=PS(b, 256)[0:np_, :], op=ALU.add), reads=[pk(b), xkey], writes=[xkey])


def _sample_prep(G, B, l):
    P, PS, PSB, pk = G.P, G.PS, G.PSB, G.pk
    I_ = G.identF
    P.add("pool", CALL("memset", B.xqm[:, :, :], 0.0), writes=["xqm"])
    P.add("sp", CALL("dma_start", out=B.ld1536[0:48, :], in_=G.st_conv[l]), writes=["ld1536"], dma=1, key="ld1536")
    for g in range(3):
        b = P.ps()
        for cc in range(4):
            c = g * 4 + cc
            P.add("pe", CALL("transpose", out=PS(b, 48, cc * 48), in_=B.ld1536[0:48, c * 128:(c + 1) * 128], identity=I_[0:48, 0:48]), reads=["ld1536", "cst"], writes=[pk(b)])
        P.add("act", CALL("activation", out=B.hist_s[:, g * 4:(g + 1) * 4, :, :].rearrange("p c s e -> p (c s e)"), in_=PS(b, 192), func=AF.Copy), reads=[pk(b)], writes=[f"hist_s{g}"])
    for hf in range(2):
        ld = B.ld512[hf]
        P.add("sp", CALL("dma_start", out=ld[0:120, :], in_=G.st_pool[l, hf * 120:(hf + 1) * 120, :]), writes=[f"ld512{hf}"], dma=1, key=f"ld512{hf}")
        b = P.ps()
        for c in range(4):
            P.add("pe", CALL("transpose", out=PS(b, 120, c * 120), in_=ld[0:120, c * 128:(c + 1) * 128], identity=I_[0:120, 0:120]), reads=[f"ld512{hf}", "cst"], writes=[pk(b)])
        for c in range(4):
            P.add("act", CALL("activation", out=B.uTs[:, c, hf * 8:(hf + 1) * 8, 0:15], in_=PS(b, 120, c * 120).rearrange("p (s e) -> p s e", e=15), func=AF.Copy),
                  reads=[pk(b)], writes=[f"uTs{c}"])


def _sample_attn(G, B, l):
    P, PS, PSB, pk = G.P, G.PS, G.PSB, G.pk
    P.ps_lo = 4
    for h in range(4):
        P.add("dve", CALL("tensor_copy", out=B.xqm[:, h, 0:1088].rearrange("p (s r) -> p s r", r=68)[:, :, 0:4], in_=B.xqT[:, h, 0:64].rearrange("p (s i) -> p s i", i=4)),
              reads=[f"xqT{h}"], writes=["xqm"])
    for s in range(16):
        i = s % 2
        st_, stk = B.kvs[i], f"kvs{i}"
        P.add("sp", CALL("dma_start", out=st_[:, :].rearrange("p (c n) -> p c n", c=2), in_=G.c_k[l, s].rearrange("(c p) n -> p c n", p=128)), writes=[stk], dma=1, key=stk)
        P.add("act", CALL("activation", out=B.kvb[i][:, :, :].rearrange("p c n -> p (c n)"), in_=st_[:, :], func=AF.Copy), reads=[stk], writes=[f"kvb{i}"])
        bt = 4 + (s % 4)
        for h in range(4):
            for mc in range(2):
                P.add("pe", CALL("transpose", out=PSB(bt, 128, h * 256 + mc * 128), in_=B.kvb[i][:, mc, h * 128:(h + 1) * 128], identity=G.identB),
                      reads=[f"kvb{i}", "identB"], writes=[pk(bt)])
        P.add("dve", CALL("tensor_copy", out=B.kTs[i][:, :, :].rearrange("p h m -> p (h m)"), in_=PSB(bt, 1024)), reads=[pk(bt)], writes=[f"kTs{i}"])
        for h in range(4):
            P.add("pe", CALL("matmul", PS(h, 256)[0:64, :], lhsT=B.xqm[:, h, s * 64:(s + 1) * 64], rhs=B.kTs[i][:, h, :], start=(s == 0), stop=(s == 15)),
                  reads=["xqm", f"kTs{i}"], writes=[pk(h)])
    pTs = []
    for h in range(4):
        pT, pTk = _attn_softmax(G, B, PS(h, 256)[0:64, :], pk(h), 64, h, None)
        dst = B.pTall[h]
        P.add("pool", CALL("tensor_copy", out=dst[:, :, 0:64], in_=pT[:, :, 0:64]), reads=[pTk], writes=[f"pTall{h}"])
        pTs.append(dst)
    for s in range(16):
        i = s % 2
        st_, stk = B.kvs[i], f"kvs{i}"
        P.add("sp", CALL("dma_start", out=st_[:, :].rearrange("p (c n) -> p c n", c=2), in_=G.c_v[l, s].rearrange("(c p) n -> p c n", p=128)), writes=[stk], dma=1, key=stk)
        P.add("act", CALL("activation", out=B.kvb[i][:, :, :].rearrange("p c n -> p (c n)"), in_=st_[:, :], func=AF.Copy), reads=[stk], writes=[f"kvb{i}"])
        for h in range(4):
            for mc in range(2):
                P.add("pe", CALL("matmul", PS(h, 4, 4 * s), lhsT=B.kvb[i][:, mc, h * 128:(h + 1) * 128], rhs=pTs[h][:, mc, 4 * s:4 * s + 4], start=(mc == 0), stop=(mc == 1)),
                      reads=[f"kvb{i}", f"pTall{h}"], writes=[pk(h)])
    for h in range(4):
        P.add("act", CALL("activation", out=B.yT[:, 8 + h, 0:64], in_=PS(h, 64), func=AF.Copy), reads=[pk(h)], writes=[f"yT{8 + h}_0"])
    P.ps_lo = 0


def mixer_phase(G, l):
    P, AR, PS, pk = G.P, G.AR, G.PS, G.pk
    I_ = G.identF
    P.barrier()
    m0 = AR.mark()
    Bp = _mixer_bufs(G, 256, False)
    P.add("pool", CALL("memset", G.hist[:, :, :], 0.0), writes=[f"hist{c}" for c in range(12)])
    P.add("pool", CALL("memset", G.Sst[:, :, :], 0.0), writes=[f"S{h}" for h in range(4)])
    P.add("sp", CALL("dma_start", out=Bp.stg[0][0][:, 0:512].rearrange("p (g e) -> p g e", g=4), in_=G.w_grp[l].rearrange("g c e -> c g e")), writes=["stg0"], dma=1, key="stg0")
    P.add("act", CALL("activation", out=G.wgrp_b[:, :, :], in_=Bp.stg[0][0][:, 0:512].rearrange("p (g e) -> p g e", g=4), func=AF.Copy), reads=["stg0"], writes=["wgrp_b"])
    _mem_kv(G, Bp, l)
    P.barrier()
    nst = G.n_st if hasattr(G, "n_st") else 8
    for st in range(nst):
        tiles = [(G.xp[:, st * 2 + j, :], 128, f"xp{st * 2 + j}") for j in range(2)]
        _mixer_st(G, Bp, l, st, tiles, False)
    P.barrier()
    b = P.ps()
    for c in range(4):
        P.add("pe", CALL("transpose", out=PS(b, 128, c * 128)[0:15, :], in_=Bp.uT[:, c, 0:15], identity=I_), reads=[f"uT{c}", "cst"], writes=[pk(b)])
    P.add("act", CALL("activation", out=Bp.rowbuf[0:15, 0:512], in_=PS(b, 512)[0:15, :], func=AF.Copy), reads=[pk(b)], writes=["rowbuf"])
    P.add("pool", CALL("dma_start", out=G.o_pool_p[l], in_=Bp.rowbuf[0:15, 0:512]), reads=["rowbuf"], dma=1, key="o_pool_p")
    for g in range(3):
        b = P.ps()
        for cc in range(4):
            c = g * 4 + cc
            P.add("pe", CALL("transpose", out=PS(b, 128, cc * 128)[0:3, :], in_=G.hist[:, c, :], identity=I_), reads=[f"hist{c}", "cst"], writes=[pk(b)])
        P.add("act", CALL("activation", out=Bp.rowbuf[0:3, g * 512:(g + 1) * 512], in_=PS(b, 512)[0:3, :], func=AF.Copy), reads=[pk(b)], writes=["rowbuf"])
    P.add("pool", CALL("dma_start", out=G.o_conv_p[l], in_=Bp.rowbuf[0:3, 0:1536]), reads=["rowbuf"], dma=1, key="o_conv_p")
    P.add("pool", CALL("dma_start", out=G.o_delta_p[l].rearrange("h k v -> k h v"), in_=G.Sst[:, :, :]), reads=[f"S{h}" for h in range(4)], dma=1, key="o_delta_p")
    P.barrier()
    AR.release(m0)
    if getattr(G, "skip_sample", False):
        return
    Bs = _mixer_bufs(G, 64, True)
    _sample_prep(G, Bs, l)
    _mixer_st(G, Bs, l, 0, [(G.xs[0:TS, :], TS, "xs")], True)
    P.barrier()
    for hf in range(2):
        b = P.ps()
        for c in range(4):
            P.add("dve", CALL("tensor_copy", out=Bs.pA[:, 0:120].rearrange("p (s e) -> p s e", e=15), in_=Bs.uTs[:, c, hf * 8:(hf + 1) * 8, 4:19]), reads=[f"uTs{c}"], writes=["pA"])
            P.add("pe", CALL("transpose", out=PS(b, 128, c * 128)[0:120, :], in_=Bs.pA[:, 0:120], identity=I_), reads=["pA", "cst"], writes=[pk(b)])
        P.add("act", CALL("activation", out=Bs.rowbuf[0:120, 0:512], in_=PS(b, 512)[0:120, :], func=AF.Copy), reads=[pk(b)], writes=["rowbuf"])
        P.add("pool", CALL("dma_start", out=G.o_pool_s[l, hf * 120:(hf + 1) * 120, :], in_=Bs.rowbuf[0:120, 0:512]), reads=["rowbuf"], dma=1, key="o_pool_s")
    for g in range(3):
        b = P.ps()
        for cc in range(4):
            c = g * 4 + cc
            P.add("pe", CALL("transpose", out=PS(b, 128, cc * 128)[0:48, :], in_=Bs.cvout[:, c, :, :].rearrange("p s e -> p (s e)"), identity=I_), reads=[f"cvout{g}", "cst"], writes=[pk(b)])
        P.add("act", CALL("activation", out=Bs.ld1536[0:48, g * 512:(g + 1) * 512], in_=PS(b, 512)[0:48, :], func=AF.Copy), reads=[pk(b)], writes=["ld1536"])
    P.add("pool", CALL("dma_start", out=G.o_conv_s[l], in_=Bs.ld1536[0:48, :]), reads=["ld1536"], dma=1, key="o_conv_s")
    P.barrier()
    AR.release(m0)


def convert_tables(G):
    P, AR = G.P, G.AR
    if getattr(G, "skip_convert", False):
        return
    m0 = AR.mark()
    NB = 3
    R = 4
    stg = [AR.alloc(R * 1024) for _ in range(NB)]
    bfb = [AR.alloc(R * 1024, BF16) for _ in range(NB)]
    n = 0
    for src, dst in ((G.p_u, G.uv_bf[:, 0:D]), (G.p_v, G.uv_bf[:, D:2 * D])):
        sv = src.rearrange("(c j p) d -> c p j d", p=128, j=R)
        dv = dst.rearrange("(c j p) d -> c p j d", p=128, j=R)
        nchunk = (DEPTH * NEXP) // (128 * R)
        if hasattr(G, "conv_chunks"):
            nchunk = G.conv_chunks
        for c in range(nchunk):
            i = n % NB
            n += 1
            s3 = stg[i].rearrange("p (j d) -> p j d", j=R)
            b3 = bfb[i].rearrange("p (j d) -> p j d", j=R)
            P.add("sp", CALL("dma_start", out=s3, in_=sv[c]), writes=[f"cstg{i}"], dma=1, key=f"cstg{i}")
            if n % 2 == 0:
                P.add("act", CALL("activation", out=bfb[i][:, :], in_=stg[i][:, :], func=AF.Copy), reads=[f"cstg{i}"], writes=[f"cbf{i}"])
            else:
                P.add("dve", CALL("tensor_copy", out=bfb[i][:, :], in_=stg[i][:, :]), reads=[f"cstg{i}"], writes=[f"cbf{i}"])
            P.add("pool", CALL("dma_start", out=dv[c], in_=b3), reads=[f"cbf{i}"], dma=1, key=f"cbf{i}")
    P.barrier()
    AR.release(m0)


def peer_phase(G, l):
    P, AR, PS, PSB, pk = G.P, G.AR, G.PS, G.PSB, G.pk
    P.barrier()
    P.ps_lo = 2
    m0 = AR.mark()
    NS = 10
    csblk = AR.alloc(NS * 1024)
    csb = csblk.bitcast(BF16)
    cs = [csb[:, i * 2048:(i + 1) * 2048] for i in range(NS)]
    wq = AR.alloc(8 * 2048, BF16).rearrange("p (k n) -> p k n", k=8)
    skT = AR.alloc(16 * 128, BF16).rearrange("p (j n) -> p j n", j=16)
    skb = csb[:, 6 * 2048:7 * 2048].rearrange("p (j n) -> p j n", j=16)
    hnf = AR.alloc(1024)
    hnb2 = [AR.alloc(1024, BF16) for _ in range(2)]
    hnT = AR.alloc(8 * 128, BF16).rearrange("p (k t) -> p k t", k=8)
    qT = AR.alloc(16 * 128, BF16).rearrange("p (j t) -> p j t", j=16)
    s1 = AR.alloc(2048)
    s2 = AR.alloc(2048)
    oh = s2
    s2keys = [f"s2_{j}" for j in range(16)]
    top = AR.alloc(256).rearrange("p (j a) -> p j a", j=16)
    topi = AR.alloc(256, U32).rearrange("p (j a) -> p j a", j=16)
    topif = AR.alloc(256).rearrange("p (h t a) -> p h t a", h=8, t=2)
    best = AR.alloc(128).rearrange("p (h k) -> p h k", h=8)
    pos = AR.alloc(128, U32).rearrange("p (h k) -> p h k", h=8)
    pint = AR.alloc(128, U32).rearrange("p (h k) -> p h k", h=8)
    paf = AR.alloc(128).rearrange("p (h k) -> p h k", h=8)
    pbf = AR.alloc(128).rearrange("p (h k) -> p h k", h=8)
    I1 = AR.alloc(128).rearrange("p (h k) -> p h k", h=8)
    I2 = AR.alloc(128).rearrange("p (h k) -> p h k", h=8)
    idxf = AR.alloc(128)
    idx2 = [AR.alloc(128, I32) for _ in range(2)]
    gate2 = [AR.alloc(128).rearrange("p (h k) -> p h k", h=8) for _ in range(2)]
    gsum = AR.alloc(8)
    av = AR.alloc(128)
    tg = AR.alloc(128)
    ag = AR.alloc(128)
    sg = AR.alloc(128)
    wgt = AR.alloc(128)
    Dg = [AR.alloc(2 * 128, BF16).rearrange("p (j m) -> p j m", j=2) for _ in range(4)]
    jb = AR.alloc(1024, BF16)
    ss = AR.alloc(8)

    stgA = (csblk[:, 0:2048], ["cs0", "cs1"])
    stgB = (csblk[:, 2048:4096], ["cs2", "cs3"])
    for g in range(8):
        sv, skeys = (stgA, stgB)[g % 2]
        svv = sv.rearrange("p (k n) -> p k n", k=8)
        P.add("sp", CALL("dma_start", out=svv, in_=G.w_pq[l].rearrange("(k p) n -> p k n", p=128)[:, :, g * 256:(g + 1) * 256]), writes=skeys, dma=1, key="pq" + skeys[0])
        if g % 2 == 0:
            P.add("act", CALL("activation", out=wq[:, :, g * 256:(g + 1) * 256], in_=svv, func=AF.Copy), reads=skeys, writes=[f"wq{g}"])
        else:
            P.add("dve", CALL("tensor_copy", out=wq[:, :, g * 256:(g + 1) * 256], in_=svv), reads=skeys, writes=[f"wq{g}"])
    wqkeys = [f"wq{g}" for g in range(8)]
    sks = csblk[:, 4096:6144].rearrange("p (j c) -> p j c", j=16)
    P.add("sp", CALL("dma_start", out=sks, in_=G.subk[l].rearrange("j k c -> k j c")), writes=["cs4", "cs5"], dma=1, key="sks")
    P.add("act", CALL("activation", out=skb, in_=sks, func=AF.Copy), reads=["cs4", "cs5"], writes=["cs6"])
    for hf in range(2):
        b = P.ps()
        for jj in range(8):
            j = hf * 8 + jj
            P.add("pe", CALL("transpose", out=PSB(b, 128, jj * 128), in_=skb[:, j, :], identity=G.identB), reads=["cs6", "identB"], writes=[pk(b)])
        P.add("act", CALL("activation", out=skT[:, hf * 8:(hf + 1) * 8, :].rearrange("p j n -> p (j n)"), in_=PSB(b, 1024), func=AF.Copy), reads=[pk(b)], writes=[f"skT{hf}"])

    tiles = [(G.xp[:, t, :], 128, f"xp{t}") for t in range(16)] + [(G.xs[0:TS, :], TS, "xs")]
    if hasattr(G, "peer_tiles"):
        tiles = [tiles[i] for i in G.peer_tiles]
    gst = {"gi": 0, "d": 0}
    nsl = getattr(G, "peer_slots", 128)

    def part_topk(x_ap, np_, xkey, par):
        r_ = slice(0, np_)
        idx = idx2[par]
        ik = f"idx{par}"
        hnb = hnb2[par]
        hk = f"hnb{par}"
        gate = gate2[par]
        gk = f"gate{par}"
        G.rms_rstd(x_ap, np_, xkey, jb, "jb", ss[:, 0:1], "pss")
        P.add("dve", CALL("scalar_tensor_tensor", out=hnf[r_, :], in0=x_ap, scalar=ss[r_, 0:1], in1=G.gb_ffn[r_, :], op0=ALU.mult, op1=ALU.mult),
              reads=[xkey, "pss", "gb_ffn"], writes=["hnf"])
        P.add("act", CALL("activation", out=hnb[r_, :], in_=hnf[r_, :], func=AF.Copy), reads=["hnf"], writes=[hk])
        G.to_featmajor(hnb, np_, hk, hnT, "hnT", slice(0, np_))
        for j in range(16):
            b = P.ps()
            for k in range(8):
                P.add("pe", CALL("matmul", PS(b, np_), lhsT=wq[:, k, j * 128:(j + 1) * 128], rhs=hnT[:, k, r_], start=(k == 0), stop=(k == 7)),
                      reads=["hnT", wqkeys[j // 2]], writes=[pk(b)])
            if j % 2 == 0:
                P.add("act", CALL("activation", out=qT[:, j, r_], in_=PS(b, np_), func=AF.Copy), reads=[pk(b)], writes=[f"qT{j}"])
            else:
                P.add("dve", CALL("tensor_copy", out=qT[:, j, r_], in_=PS(b, np_)), reads=[pk(b)], writes=[f"qT{j}"])
        for q in range(4):
            b = P.ps()
            for jj in range(4):
                j = q * 4 + jj
                P.add("pe", CALL("matmul", PS(b, 128, jj * 128)[r_, :], lhsT=qT[:, j, r_], rhs=skT[:, j, :], start=True, stop=True),
                      reads=[f"qT{j}", f"skT{j // 8}"], writes=[pk(b)])
            P.add("act", CALL("activation", out=s1[r_, q * 512:(q + 1) * 512], in_=PS(b, 512)[r_, :], func=AF.Copy), reads=[pk(b)], writes=[f"s1_{q}"])
        s1v = s1[:, :].rearrange("p (j n) -> p j n", j=16)
        s2v = s2[:, :].rearrange("p (j n) -> p j n", j=16)
        for j in range(16):
            sk_ = f"s1_{j // 4}"
            P.add("dve", CALL("max", out=top[r_, j, 0:8], in_=s1v[r_, j, :]), reads=[sk_], writes=[f"top{j}a"])
            P.add("dve", CALL("max_index", out=topi[r_, j, 0:8], in_max=top[r_, j, 0:8], in_values=s1v[r_, j, :]), reads=[sk_, f"top{j}a"], writes=[f"topi{j}a"])
            P.add("dve", CALL("match_replace", out=s2v[r_, j, :], in_to_replace=top[r_, j, 0:8], in_values=s1v[r_, j, :], imm_value=NEG), reads=[sk_, f"top{j}a"], writes=[f"s2_{j}"])
            P.add("dve", CALL("max", out=top[r_, j, 8:16], in_=s2v[r_, j, :]), reads=[f"s2_{j}"], writes=[f"top{j}b"])
            P.add("dve", CALL("max_index", out=topi[r_, j, 8:16], in_max=top[r_, j, 8:16], in_values=s2v[r_, j, :]), reads=[f"s2_{j}", f"top{j}b"], writes=[f"topi{j}b"])
        allt = [f"top{j}{x}" for j in range(16) for x in "ab"]
        alli = [f"topi{j}{x}" for j in range(16) for x in "ab"]
        P.add("dve", CALL("tensor_copy", out=topif[r_, :, :, :].rearrange("p h t a -> p (h t a)"), in_=topi[r_, :, :].rearrange("p j a -> p (j a)")), reads=alli, writes=["topif"])
        topv = top[:, :, :].rearrange("p (h t) a -> p h t a", t=2)
        cand = s1[:, :].rearrange("p (h a b) -> p h a b", h=8, a=16)
        cand2 = s2[:, :].rearrange("p (h n) -> p h n", h=8)
        candf = s1[:, :].rearrange("p (h n) -> p h n", h=8)
        for h in range(8):
            P.add("dve", CALL("tensor_tensor", out=cand[r_, h, :, :], in0=topv[r_, h, 0, :].unsqueeze(2).to_broadcast([np_, 16, 16]),
                                                        in1=topv[r_, h, 1, :].unsqueeze(1).to_broadcast([np_, 16, 16]), op=ALU.add),
                  reads=allt, writes=[f"s1_{h // 2}"])
            P.add("dve", CALL("max", out=best[r_, h, 0:8], in_=candf[r_, h, :]), reads=[f"s1_{h // 2}"], writes=[f"best{h}a"])
            P.add("dve", CALL("max_index", out=pos[r_, h, 0:8], in_max=best[r_, h, 0:8], in_values=candf[r_, h, :]), reads=[f"s1_{h // 2}", f"best{h}a"], writes=[f"pos{h}a"])
            P.add("dve", CALL("match_replace", out=cand2[r_, h, :], in_to_replace=best[r_, h, 0:8], in_values=candf[r_, h, :], imm_value=NEG),
                  reads=[f"s1_{h // 2}", f"best{h}a"], writes=[f"s2_{2 * h}", f"s2_{2 * h + 1}"])
            P.add("dve", CALL("max", out=best[r_, h, 8:16], in_=cand2[r_, h, :]), reads=[f"s2_{2 * h}", f"s2_{2 * h + 1}"], writes=[f"best{h}b"])
            P.add("dve", CALL("max_index", out=pos[r_, h, 8:16], in_max=best[r_, h, 8:16], in_values=cand2[r_, h, :]), reads=[f"s2_{2 * h}", f"s2_{2 * h + 1}", f"best{h}b"], writes=[f"pos{h}b"])
        allb = [f"best{h}{x}" for h in range(8) for x in "ab"]
        allp = [f"pos{h}{x}" for h in range(8) for x in "ab"]
        P.add("dve", CALL("tensor_single_scalar", out=pint[r_, :, :], in_=pos[r_, :, :], scalar=4, op=ALU.logical_shift_right), reads=allp, writes=["pint"])
        P.add("dve", CALL("tensor_copy", out=paf[r_, :, :], in_=pint[r_, :, :]), reads=["pint"], writes=["paf"])
        P.add("dve", CALL("tensor_single_scalar", out=pint[r_, :, :], in_=pos[r_, :, :], scalar=15, op=ALU.bitwise_and), reads=allp + ["paf"], writes=["pint"])
        P.add("dve", CALL("tensor_copy", out=pbf[r_, :, :], in_=pint[r_, :, :]), reads=["pint"], writes=["pbf"])
        ohv = oh[:, :].rearrange("p (h k a) -> p h k a", h=8, k=16)
        for (pf, pfk, tsel, Iout, Ik) in ((paf, "paf", 0, I1, "I1"), (pbf, "pbf", 1, I2, "I2")):
            for h in range(8):
                P.add("dve", CALL("tensor_tensor", out=ohv[r_, h, :, :], in0=pf[r_, h, :].unsqueeze(2).to_broadcast([np_, 16, 16]),
                                                                   in1=G.iota16[r_, :].unsqueeze(1).to_broadcast([np_, 16, 16]), op=ALU.is_equal),
                      reads=[pfk, "cst"], writes=[f"s2_{2 * h}", f"s2_{2 * h + 1}"])
                P.add("dve", CALL("tensor_tensor", out=ohv[r_, h, :, :], in0=ohv[r_, h, :, :], in1=topif[r_, h, tsel, :].unsqueeze(1).to_broadcast([np_, 16, 16]), op=ALU.mult),
                      reads=["topif"], writes=[f"s2_{2 * h}", f"s2_{2 * h + 1}"])
            P.add("dve", CALL("tensor_reduce", out=Iout[r_, :, :], in_=ohv[r_, :, :, :], axis=AX.X, op=ALU.add), reads=s2keys, writes=[Ik])
        P.add("dve", CALL("scalar_tensor_tensor", out=idxf[r_, :], in0=I1[r_, :, :].rearrange("p h k -> p (h k)"), scalar=128.0, in1=I2[r_, :, :].rearrange("p h k -> p (h k)"), op0=ALU.mult, op1=ALU.add),
              reads=["I1", "I2"], writes=["idxf"])
        if l > 0:
            P.add("dve", CALL("tensor_scalar", out=idxf[r_, :], in0=idxf[r_, :], scalar1=float(l * NEXP), scalar2=0.0, op0=ALU.add, op1=ALU.add), reads=["idxf"], writes=["idxf"])
        P.add("dve", CALL("tensor_copy", out=idx[r_, :], in_=idxf[r_, :]), reads=["idxf"], writes=[ik])
        P.add("dve", CALL("tensor_tensor", out=gate[r_, :, :], in0=best[r_, :, :], in1=best[r_, :, 0:1].to_broadcast([np_, 8, 16]), op=ALU.subtract), reads=allb, writes=[gk])
        P.add("act", CALL("activation", out=gate[r_, :, :], in_=gate[r_, :, :], func=AF.Exp), reads=[gk], writes=[gk])
        P.add("dve", CALL("tensor_reduce", out=gsum[r_, 0:8], in_=gate[r_, :, :], axis=AX.X, op=ALU.add), reads=[gk], writes=["gsum"])
        P.add("dve", CALL("reciprocal", out=gsum[r_, 0:8], in_=gsum[r_, 0:8]), reads=["gsum"], writes=["gsum"])
        P.add("dve", CALL("tensor_tensor", out=gate[r_, :, :], in0=gate[r_, :, :], in1=gsum[r_, 0:8].unsqueeze(2).to_broadcast([np_, 8, 16]), op=ALU.mult), reads=[gk, "gsum"], writes=[gk])

    def part_pipe(x_ap, np_, xkey, par):
        r_ = slice(0, np_)
        idx = idx2[par]
        ik = f"idx{par}"
        hnb = hnb2[par]
        hk = f"hnb{par}"
        gatef = gate2[par][:, :, :].rearrange("p h k -> p (h k)")
        gk = f"gate{par}"
        ngrp = nsl // 2
        slot_of = {}

        def stage3(g):
            c2 = slice(2 * g, 2 * g + 2)
            P.add("dve", CALL("tensor_tensor", out=wgt[r_, c2], in0=sg[r_, c2], in1=ag[r_, c2], op=ALU.mult), reads=[f"sg{g}", f"ag{g}"], writes=[f"wgt{g}"])
            d_i = gst["d"] % 4
            gst["d"] += 1
            D_ = Dg[d_i]
            dk = f"Dg{d_i}"
            P.add("dve", CALL("tensor_tensor", out=D_[r_, :, r_], in0=G.identF[r_, r_].unsqueeze(1).to_broadcast([np_, 2, np_]),
                              in1=wgt[r_, c2].unsqueeze(2).to_broadcast([np_, 2, np_]), op=ALU.mult), reads=[f"wgt{g}", "cst"], writes=[dk])
            for j in range(2):
                sl = 2 * g + j
                i = slot_of[sl]
                for hf in range(2):
                    P.add("pe", CALL("matmul", PS(hf, 512)[r_, :], lhsT=D_[r_, j, r_], rhs=cs[i][r_, 1024 + hf * 512:1024 + (hf + 1) * 512], start=(sl == 0), stop=(sl == nsl - 1)),
                          reads=[dk, f"cs{i}"], writes=[pk(hf)])

        for g in range(ngrp + 1):
            if g < ngrp:
                c2 = slice(2 * g, 2 * g + 2)
                for j in range(2):
                    sl = 2 * g + j
                    i = gst["gi"] % NS
                    gst["gi"] += 1
                    slot_of[sl] = i
                    P.add("pool", CALL("indirect_dma_start", out=cs[i][r_, :], out_offset=None, in_=G.uv_bf, in_offset=bass.IndirectOffsetOnAxis(ap=idx[r_, sl:sl + 1], axis=0)),
                          reads=[ik], writes=[f"cs{i}"], dma=1, key=f"cs{i}")
                    P.add("dve", CALL("scalar_tensor_tensor", out=cs[i][r_, 0:1024], in0=cs[i][r_, 0:1024], scalar=1.0, in1=hnb[r_, :], op0=ALU.mult, op1=ALU.mult, accum_out=av[r_, sl:sl + 1]),
                          reads=[hk], writes=[f"cs{i}", f"av{sl}"])
                avk = [f"av{2 * g}", f"av{2 * g + 1}"]
                P.add("dve", CALL("scalar_tensor_tensor", out=tg[r_, c2], in0=av[r_, c2], scalar=0.044715, in1=av[r_, c2], op0=ALU.mult, op1=ALU.mult), reads=avk, writes=[f"tg{g}"])
                P.add("dve", CALL("scalar_tensor_tensor", out=tg[r_, c2], in0=tg[r_, c2], scalar=1.0, in1=av[r_, c2], op0=ALU.add, op1=ALU.mult), reads=avk + [f"tg{g}"], writes=[f"tg{g}"])
                P.add("act", CALL("activation", out=sg[r_, c2], in_=tg[r_, c2], func=AF.Sigmoid, scale=1.5957691216057308), reads=[f"tg{g}"], writes=[f"sg{g}"])
                P.add("dve", CALL("tensor_tensor", out=ag[r_, c2], in0=av[r_, c2], in1=gatef[r_, c2], op=ALU.mult), reads=avk + [gk], writes=[f"ag{g}"])
            if g >= 1:
                stage3(g - 1)
        for hf in range(2):
            xo = x_ap[:, hf * 512:(hf + 1) * 512]
            P.add("dve", CALL("tensor_tensor", out=xo, in0=xo, in1=PS(hf, 512)[r_, :], op=ALU.add), reads=[pk(hf), xkey], writes=[xkey])

    nt_ = len(tiles)
    part_topk(*tiles[0], 0)
    for t in range(nt_):
        P.begin_record((0, 1))
        part_pipe(*tiles[t], t % 2)
        ra = P.end_record()
        rb = []
        if t + 1 < nt_:
            P.begin_record((2, 3, 4, 5, 6, 7))
            part_topk(*tiles[t + 1], (t + 1) % 2)
            rb = P.end_record()
        P.replay([ra, rb])
    P.barrier()
    P.ps_lo = 0
    AR.release(m0)


def final_phase(G):
    P, AR = G.P, G.AR
    m0 = AR.mark()
    ob = [AR.alloc(1024) for _ in range(2)]
    jb = AR.alloc(1024, BF16)
    ss = AR.alloc(8)
    gb_fin = AR.alloc(1024)
    P.add("sp", CALL("dma_start", out=gb_fin[:, :], in_=G.g_fin.partition_broadcast(128)), writes=["gb_fin"], dma=1, key="gb_fin")
    tiles = [(G.xp[:, t, :], 128, f"xp{t}", G.y_p[t * 128:(t + 1) * 128, :]) for t in range(16)] + [(G.xs[0:TS, :], TS, "xs", G.y_s)]
    for n, (x_ap, np_, xkey, o_ap) in enumerate(tiles):
        r_ = slice(0, np_)
        o = ob[n % 2]
        G.rms_rstd(x_ap, np_, xkey, jb, "jb", ss[:, 0:1], "fss")
        P.add("dve", CALL("scalar_tensor_tensor", out=o[r_, :], in0=x_ap, scalar=ss[r_, 0:1], in1=gb_fin[r_, :], op0=ALU.mult, op1=ALU.mult),
              reads=[xkey, "fss", "gb_fin"], writes=[f"ob{n % 2}"])
        P.add("sp", CALL("dma_start", out=o_ap, in_=o[r_, :]), reads=[f"ob{n % 2}"], dma=1, key=f"ob{n % 2}")
    AR.release(m0)


def build(dbg=(), stop=None, **opts):
    build.opts = opts
    nc = bass.Bass("TRN2", target_bir_lowering=False)
    es = ExitStack()
    with es:
        _build(nc, es, dbg, stop)
    return nc


def _build(nc, es, dbg, stop):
    def din(name, shape, dt=F32):
        return nc.dram_tensor(name, list(shape), dt, kind="ExternalInput").ap()

    def dout(name, shape, dt=F32):
        return nc.dram_tensor(name, list(shape), dt, kind="ExternalOutput").ap()

    x_p = din("x_p", [T, D])
    x_s = din("x_s", [TS, D])
    st_pool = din("st_pool", [DEPTH, NSQ * 15, BW])
    st_conv = din("st_conv", [DEPTH, NSQ * 3, 3 * BW])
    st_delta = din("st_delta", [DEPTH, NSQ, 4, 128, 128])
    c_k = din("c_k", [DEPTH, NSQ, 256, BW])
    c_v = din("c_v", [DEPTH, NSQ, 256, BW])
    memp = din("memp", [256, D])
    g_mix = din("g_mix", [DEPTH, D])
    w_in = din("w_in", [DEPTH, D, IN_COLS])
    w_conv = din("w_conv", [DEPTH, 4, 3 * BW])
    a_log = din("a_log", [DEPTH, 4])
    dt_bias = din("dt_bias", [DEPTH, 4])
    g_dn = din("g_dn", [DEPTH, 128])
    w_grp = din("w_grp", [DEPTH, 4, 128, 128])
    p_scale = din("p_scale", [DEPTH, BW])
    g_mem = din("g_mem", [DEPTH, D])
    w_mkv = din("w_mkv", [DEPTH, D, 2 * BW])
    w_br = din("w_br", [DEPTH, 3, BW, D])
    w_o = din("w_o", [DEPTH, D, D])
    g_ffn = din("g_ffn", [DEPTH, D])
    w_pq = din("w_pq", [DEPTH, D, 2048])
    subk = din("subk", [DEPTH, 16, 128, 128])
    p_u = din("p_u", [DEPTH * NEXP, D])
    p_v = din("p_v", [DEPTH * NEXP, D])
    g_fin = din("g_fin", [D])
    consts = din("consts", [128, 1024])

    y_p = dout("y_p", [T, D])
    y_s = dout("y_s", [TS, D])
    o_pool_p = dout("o_pool_p", [DEPTH, 15, BW])
    o_conv_p = dout("o_conv_p", [DEPTH, 3, 3 * BW])
    o_delta_p = dout("o_delta_p", [DEPTH, 4, 128, 128])
    o_mk = dout("o_mk", [DEPTH, 256, BW])
    o_mv = dout("o_mv", [DEPTH, 256, BW])
    o_pool_s = dout("o_pool_s", [DEPTH, NSQ * 15, BW])
    o_conv_s = dout("o_conv_s", [DEPTH, NSQ * 3, 3 * BW])
    o_delta_s = dout("o_delta_s", [DEPTH, NSQ, 4, 128, 128])

    uv_bf = nc.dram_tensor("uv_bf", [DEPTH * NEXP, 2 * D], BF16, kind="Internal").ap()

    P = Prog(nc)
    dbg_outs = {}

    def sb(name, shape, dt=F32):
        return es.enter_context(nc.sbuf_tensor(name, shape, dt))

    xp = sb("xp", [128, 16, D])
    xs = sb("xs", [128, D])
    cst = sb("cst", [128, 8, 128])
    identB_t = sb("identB", [128, 128], BF16)
    identB = identB_t[:, :]
    gmix_sb = sb("gmix_sb", [128, 8])
    gmem_sb = sb("gmem_sb", [128, 8])
    gb_ffn = sb("gb_ffn", [128, D])
    wconv_sb = sb("wconv_sb", [128, 12, 4])
    alog_b = sb("alog_b", [128, 4])
    dtb_b = sb("dtb_b", [128, 4])
    gdn_sb = sb("gdn_sb", [128, 1])
    psc_sb = sb("psc_sb", [128, 4])
    wgrp_b = sb("wgrp_b", [128, 4, 128], BF16)
    Sst = sb("Sst", [128, 4, 128])
    hist = sb("hist", [128, 12, 3])
    epsb = sb("epsb", [128, 1])
    ARENA_N = 32000
    arena_t = sb("arena", [128, ARENA_N])
    AR = Arena(arena_t, ARENA_N)
    psum = es.enter_context(nc.psum_tensor("psum", [128, 4096], F32))

    identF = cst[:, 0, :]
    onesF = cst[:, 1, :]
    Ltri = cst[:, 2, :]
    SLm = cst[:, 3, :]
    Ltri4 = cst[:, 4, :]
    SL4 = cst[:, 5, :]
    seqmask = cst[:, 6, 0:16]
    rcnt = cst[:, 7, 0:64]
    iota16 = cst[:, 7, 64:80]

    def PS(i, n=512, off=0):
        return psum[:, i * 512 + off:i * 512 + off + n]

    def PSB(i, n=1024, off=0):
        return psum[:, i * 512:(i + 1) * 512].bitcast(BF16)[:, off:off + n]

    def pk(i):
        return f"ps{i}"

    def dump(name, ap, reads, shape, view=None, **kw):
        if name not in dbg:
            return
        o = dout("dbg_" + name, shape, ap.dtype)
        dbg_outs[name] = o
        if view:
            o = o.rearrange(view, **kw)
        P.add("sp", CALL("dma_start", out=o, in_=ap), reads=reads, dma=1, key="dbg_" + name)

    P.add("sp", CALL("dma_start", out=cst[:].rearrange("p a b -> p (a b)"), in_=consts), writes=["cst"], dma=1, key="cst")
    P.add("pool", CALL("memset", epsb[:], EPS), writes=["epsb"])
    P.add("act", CALL("activation", out=identB, in_=identF, func=AF.Copy), reads=["cst"], writes=["identB"])
    for i in range(16):
        P.add("sp", CALL("dma_start", out=xp[:, i, :], in_=x_p[i * 128:(i + 1) * 128, :]), writes=[f"xp{i}"], dma=1, key=f"xp{i}")
    P.add("sp", CALL("dma_start", out=xs[0:TS, :], in_=x_s), writes=["xs"], dma=1, key="xs")

    def rms_rstd(x_ap, np_, xkey, junk, junk_key, ss, sskey):
        P.add("act", CALL("activation", out=junk[0:np_, :], in_=x_ap, func=AF.Square, accum_out=ss[0:np_, :]),
              reads=[xkey], writes=[junk_key, sskey])
        P.add("act", CALL("activation", out=ss[0:np_, :], in_=ss[0:np_, :], func=AF.Sqrt, scale=1.0 / D, bias=epsb[0:np_, :]),
              reads=[sskey, "epsb"], writes=[sskey])
        P.add("dve", CALL("reciprocal", out=ss[0:np_, :], in_=ss[0:np_, :]), reads=[sskey], writes=[sskey])

    def to_featmajor(src_bf, np_, srckey, dst, dstkey_fn, tsl, gsb=None, gkey=None):
        b = P.ps()
        for k in range(8):
            P.add("pe", CALL("transpose", out=PSB(b)[:, k * 128:k * 128 + np_], in_=src_bf[0:np_, k * 128:(k + 1) * 128], identity=identB[0:np_, 0:np_]),
                  reads=[srckey, "identB"], writes=[pk(b)])
        src = PSB(b).rearrange("p (k t) -> p k t", k=8)[:, :, 0:np_]
        if gsb is None:
            P.add("act", CALL("activation", out=dst[:, :, tsl], in_=src, func=AF.Copy), reads=[pk(b)], writes=[dstkey_fn])
        else:
            P.add("dve", CALL("tensor_tensor", out=dst[:, :, tsl], in0=src, in1=gsb[:, :].unsqueeze(2).to_broadcast([128, 8, np_]), op=ALU.mult),
                  reads=[pk(b), gkey], writes=[dstkey_fn])

    wst = {"i": 0, "s": 0}

    def load_w(dram2d, krows, c0, ncols, stg, wbf, caster=None):
        i = wst["i"] % len(wbf)
        wst["i"] += 1
        si = wst["s"] % len(stg)
        wst["s"] += 1
        s_ap, s_key = stg[si]
        b_ap, b_key = wbf[i]
        sv = s_ap[:, 0:krows * ncols].rearrange("p (k n) -> p k n", k=krows)
        bv = b_ap[:, 0:krows * ncols].rearrange("p (k n) -> p k n", k=krows)
        src = dram2d.rearrange("(k p) n -> p k n", p=128)[:, :, c0:c0 + ncols]
        P.add("sp", CALL("dma_start", out=sv, in_=src), writes=[s_key], dma=1, key=s_key)
        eng = caster or ("act" if (wst["i"] % 2 == 0) else "dve")
        if eng == "act":
            P.add("act", CALL("activation", out=bv, in_=sv, func=AF.Copy), reads=[s_key], writes=[b_key])
        else:
            P.add(eng, CALL("tensor_copy", out=bv, in_=sv), reads=[s_key], writes=[b_key])
        return bv, b_key

    def layer_params(l):
        P.add("sp", CALL("dma_start", out=gmix_sb[:], in_=g_mix[l].rearrange("(k p) -> p k", p=128), allow_slow_non_contiguous=True), writes=["gmix"], dma=1, key="gmix")
        P.add("sp", CALL("dma_start", out=gmem_sb[:], in_=g_mem[l].rearrange("(k p) -> p k", p=128), allow_slow_non_contiguous=True), writes=["gmem"], dma=1, key="gmem")
        P.add("sp", lambda e: [e.dma_start(out=wconv_sb[:, :, j], in_=w_conv[l, j].rearrange("(c p) -> p c", p=128), allow_slow_non_contiguous=True) for j in range(4)],
              writes=["wconv"], dma=4, key="wconv")
        P.add("sp", CALL("dma_start", out=gdn_sb[:], in_=g_dn[l].rearrange("(p o) -> p o", o=1), allow_slow_non_contiguous=True), writes=["gdn"], dma=1, key="gdn")
        P.add("sp", CALL("dma_start", out=psc_sb[:], in_=p_scale[l].rearrange("(g p) -> p g", p=128), allow_slow_non_contiguous=True), writes=["psc"], dma=1, key="psc")
        P.add("sp", CALL("dma_start", out=gb_ffn[:], in_=g_ffn[l].partition_broadcast(128)), writes=["gb_ffn"], dma=1, key="gb_ffn")
        P.add("sp", CALL("dma_start", out=alog_b[:], in_=a_log[l].partition_broadcast(128)), writes=["alog"], dma=1, key="alog")
        P.add("sp", CALL("dma_start", out=dtb_b[:], in_=dt_bias[l].partition_broadcast(128)), writes=["dtb"], dma=1, key="dtb")
        P.add("act", CALL("activation", out=alog_b[:], in_=alog_b[:], func=AF.Exp), reads=["alog"], writes=["alog"])
        P.add("dve", CALL("tensor_scalar", out=alog_b[:], in0=alog_b[:], scalar1=-1.0, scalar2=0.0, op0=ALU.mult, op1=ALU.add), reads=["alog"], writes=["alog"])

    G = type("G", (), {})()
    for k_, v_ in list(locals().items()):
        setattr(G, k_, v_)
    for k_, v_ in build.opts.items():
        setattr(G, k_, v_)

    convert_tables(G)
    for l in range(DEPTH):
        layer_params(l)
        mixer_phase(G, l)
        dump(f"xp_m{l}", xp[:, :, :], [f"xp{i}" for i in range(16)], [T, D], "(t p) d -> p t d", p=128)
        dump(f"xs_m{l}", xs[0:TS, :], ["xs"], [TS, D])
        if stop == ("mixer", l):
            break
        peer_phase(G, l)
        dump(f"xp_p{l}", xp[:, :, :], [f"xp{i}" for i in range(16)], [T, D], "(t p) d -> p t d", p=128)
        dump(f"xs_p{l}", xs[0:TS, :], ["xs"], [TS, D])
        if stop == ("peer", l):
            break
    else:
        final_phase(G)
    P.emit(es, maxops=build.opts.get('maxops'))
    G.P = P
    build.last = G


def _shard_inputs(inp):
    f = lambda a: np.ascontiguousarray(np.asarray(a, dtype=np.float32))
    shared = dict(
        g_mix=f(inp["g_mix"]), w_in=f(inp["w_in"]), w_conv=f(inp["w_conv"]), a_log=f(inp["a_log"]), dt_bias=f(inp["dt_bias"]),
        g_dn=f(inp["g_dn_out"]), w_grp=f(inp["w_pool_grp"]), p_scale=f(inp["pool_scale"]), g_mem=f(inp["g_mem"]),
        w_mkv=f(inp["w_mem_kv"]), w_br=f(inp["w_branch"]), w_o=f(inp["w_o"]), g_ffn=f(inp["g_ffn"]), w_pq=f(inp["w_peer_q"]),
        subk=f(inp["peer_subkeys"]).reshape(DEPTH, 16, 128, 128), p_u=f(inp["peer_u"]).reshape(DEPTH * NEXP, D),
        p_v=f(inp["peer_v"]).reshape(DEPTH * NEXP, D), g_fin=f(inp["g_final"]), consts=make_consts())
    maps = []
    for c in range(NCORES):
        sl = slice(c * NSQ, (c + 1) * NSQ)
        m = dict(shared)
        m["x_p"] = f(inp["x_prompt"][c])
        m["x_s"] = f(inp["x_sample"][sl]).reshape(TS, D)
        m["st_pool"] = f(inp["state_pool"][:, sl]).reshape(DEPTH, NSQ * 15, BW)
        m["st_conv"] = f(inp["state_conv"][:, sl]).reshape(DEPTH, NSQ * 3, 3 * BW)
        m["st_delta"] = f(inp["state_delta"][:, sl])
        m["c_k"] = f(inp["cache_mem_k"][:, sl]).reshape(DEPTH, NSQ, 256, BW)
        m["c_v"] = f(inp["cache_mem_v"][:, sl]).reshape(DEPTH, NSQ, 256, BW)
        m["memp"] = f(inp["mem_prompt"][c])
        maps.append(m)
    return maps


def _gather_outputs(res):
    R = res.results
    cat = lambda k, ax: np.concatenate([np.asarray(r[k]) for r in R], axis=ax)
    stk = lambda k, ax: np.stack([np.asarray(r[k]) for r in R], axis=ax)
    y_p = stk("y_p", 0)
    y_s = cat("y_s", 0).reshape(NCORES * NSQ, 4, D)
    pool_p = stk("o_pool_p", 1)
    conv_p = stk("o_conv_p", 1)
    delta_p = stk("o_delta_p", 1)
    mk = stk("o_mk", 1).reshape(DEPTH, NCORES, 256, 4, 128)
    mv = stk("o_mv", 1).reshape(DEPTH, NCORES, 256, 4, 128)
    pool_s = cat("o_pool_s", 1).reshape(DEPTH, NCORES * NSQ, 15, BW)
    conv_s = cat("o_conv_s", 1).reshape(DEPTH, NCORES * NSQ, 3, 3 * BW)
    delta_s = cat("o_delta_s", 1)
    outs = (y_p, y_s, pool_p, conv_p, delta_p, mk, mv, pool_s, conv_s, delta_s)
    return tuple(np.ascontiguousarray(o, dtype=np.float32) for o in outs)


def kernel(**inputs):
    maps = _shard_inputs(inputs)
    nc = build()
    res = run_bass_kernel_spmd(nc, maps, core_ids=list(range(NCORES)))
    return _gather_outputs(res)
```

```python
import numpy as np
from contextlib import ExitStack
import concourse.bass as bass
import concourse.mybir as mybir
from concourse.bass_utils import run_bass_kernel_spmd

F32 = mybir.dt.float32
BF16 = mybir.dt.bfloat16
I32 = mybir.dt.int32
U32 = mybir.dt.uint32
ALU = mybir.AluOpType
AF = mybir.ActivationFunctionType
AX = mybir.AxisListType

NCORES = 8
D = 1024
T = 2048
NSQ = 16
TS = 64
DEPTH = 2
BW = 512
IN_COLS = 6152
OFF_Q = 512
OFF_Z = 2048
OFF_BA = 2560
OFF_XQ = 2568
OFF_GATE = 3080
EPS = 1e-6
NKEY = 128
NEXP = 16384
SEM_CH = 30000
NEG = -1.0e30
NG = 6


def CALL(name, *a, **k):
    return lambda e: getattr(e, name)(*a, **k)


class Op:
    __slots__ = ("eng", "fn", "deps", "is_dma", "key", "nparts", "seq", "signaled", "dma_val")


class Prog:
    ENGS = ("pe", "act", "dve", "pool", "sp")

    def __init__(self, nc):
        self.nc = nc
        self.ops = []
        self.last_w = {}
        self.readers = {}
        self.dma_cnt = {}
        self.dma_gen = {}
        self.psi = 0
        self.inames = {}

    def begin_record(self, banks):
        self.rec = []
        self.ps_banks = list(banks)
        self.ps_bi = 0

    def end_record(self):
        r = self.rec
        self.rec = None
        self.ps_banks = None
        return r

    def replay(self, recs):
        recs = [r for r in recs if r]
        n = max(len(r) for r in recs)
        for i in range(n):
            for r in recs:
                if i < len(r):
                    self.add(*r[i][0], **r[i][1])

    def add(self, eng, fn, reads=(), writes=(), dma=0, key=None):
        if getattr(self, "rec", None) is not None:
            self.rec.append(((eng, fn), dict(reads=list(reads), writes=list(writes), dma=dma, key=key)))
            return None
        op = Op()
        op.eng = eng
        op.fn = fn
        op.is_dma = dma > 0
        op.nparts = dma
        op.signaled = False
        op.seq = 0
        deps = set()
        excl = [r for r in reads if isinstance(r, str) and r[:2] == "ps" and r[2:].isdigit()]
        if excl:
            reads = [r for r in reads if r not in excl]
            writes = list(writes) + excl
        for r in reads:
            w = self.last_w.get(r)
            if w is not None:
                deps.add(w)
        for w_ in writes:
            w = self.last_w.get(w_)
            if w is not None:
                deps.add(w)
            for rd in self.readers.get(w_, ()):
                deps.add(rd)
        op.deps = deps
        for r in reads:
            self.readers.setdefault(r, []).append(op)
        for w_ in writes:
            self.last_w[w_] = op
            self.readers[w_] = []
        if op.is_dma:
            g = self.dma_gen.get(key, 0)
            c = self.dma_cnt.get((key, g), 0) + dma * 16
            if c > SEM_CH:
                g += 1
                self.dma_gen[key] = g
                c = dma * 16
            self.dma_cnt[(key, g)] = c
            op.key = (key, g)
            op.dma_val = c
        self.ops.append(op)
        return op

    def barrier(self):
        last = {}
        dmas = {}
        for op in self.ops:
            if op.is_dma:
                dmas[op.key] = op
            else:
                last[op.eng] = op
        deps = set(last.values()) | set(dmas.values())
        bops = []
        for e in ("pe", "act", "dve", "pool", "sp"):
            op = self.add(e, lambda en: en.nop(), ())
            op.deps = set(deps)
            bops.append(op)
        self.last_w = {}
        self.readers = {}

    def ps(self):
        if getattr(self, "ps_banks", None):
            i = self.ps_banks[self.ps_bi % len(self.ps_banks)]
            self.ps_bi += 1
            return i
        lo = getattr(self, "ps_lo", 0)
        if self.psi < lo:
            self.psi = lo
        i = self.psi
        self.psi = self.psi + 1
        if self.psi >= 8:
            self.psi = lo
        return i

    def emit(self, es, maxops=None):
        nc = self.nc
        if maxops is not None:
            self.ops = self.ops[:maxops]
            cnt2 = {}
            for op in self.ops:
                if op.is_dma:
                    cnt2[op.key] = op.dma_val
            self.dma_cnt = cnt2
        for op in self.ops:
            nd = set()
            for d in op.deps:
                if (not d.is_dma) and (not op.is_dma) and d.eng == "pe" and op.eng == "pe":
                    continue
                nd.add(d)
                d.signaled = True
            op.deps = nd
        cnt = {e: 0 for e in self.ENGS}
        for op in self.ops:
            if not op.is_dma and op.signaled:
                cnt[op.eng] += 1
                op.seq = cnt[op.eng]
        eng_sems = {}
        for e in self.ENGS:
            n = (cnt[e] + SEM_CH - 1) // SEM_CH
            eng_sems[e] = [es.enter_context(nc.semaphore(f"s_{e}_{i}")) for i in range(n)]
        dma_sems = {}
        for i, k in enumerate(self.dma_cnt.keys()):
            dma_sems[k] = es.enter_context(nc.semaphore(f"d_{i}"))
        self.nsem = sum(len(v) for v in eng_sems.values()) + len(dma_sems)
        per_eng = {e: [o for o in self.ops if o.eng == e] for e in self.ENGS}
        block = es.enter_context(nc.Block())

        def run(engname, eobj):
            waited = {}

            def wait(sem, val):
                if waited.get(id(sem), 0) >= val:
                    return
                waited[id(sem)] = val
                eobj.wait_ge(sem, val)

            for op in per_eng[engname]:
                need = {}
                for d in op.deps:
                    if d.is_dma:
                        sem = dma_sems[d.key]
                        v = d.dma_val
                    else:
                        si = (d.seq - 1) // SEM_CH
                        sem = eng_sems[d.eng][si]
                        v = d.seq - si * SEM_CH
                    k = id(sem)
                    if k not in need or need[k][1] < v:
                        need[k] = (sem, v)
                for sem, v in need.values():
                    wait(sem, v)
                if op.is_dma:
                    sem = dma_sems[op.key]
                    insts = op.fn(eobj)
                    if not isinstance(insts, (list, tuple)):
                        insts = [insts]
                    assert len(insts) == op.nparts
                    for ins in insts:
                        ins.then_inc(sem, 16)
                else:
                    ins = op.fn(eobj)
                    try:
                        self.inames[ins.ins.name] = op
                    except Exception:
                        pass
                    if op.signaled:
                        si = (op.seq - 1) // SEM_CH
                        ins.then_inc(eng_sems[op.eng][si], 1)
            if engname == "sp":
                for k, c in self.dma_cnt.items():
                    wait(dma_sems[k], c)

        block.tensor(lambda e: run("pe", e))
        block.scalar(lambda e: run("act", e))
        block.vector(lambda e: run("dve", e))
        block.gpsimd(lambda e: run("pool", e))
        block.sync(lambda e: run("sp", e))


class Arena:
    def __init__(self, t, nf32):
        self.t = t
        self.n = nf32
        self.off = 0
        self.hw = 0

    def mark(self):
        return self.off

    def release(self, m):
        self.off = m

    def alloc(self, n, dt=F32):
        nf = n if dt in (F32, I32, U32) else (n + 1) // 2
        nf = (nf + 7) // 8 * 8
        a = self.off
        self.off += nf
        self.hw = max(self.hw, self.off)
        assert self.off <= self.n, f"arena overflow {self.off} > {self.n}"
        v = self.t[:, a:a + nf]
        if dt != F32:
            v = v.bitcast(dt)
        return v[:, 0:n]


def make_consts():
    c = np.zeros((128, 8, 128), np.float32)
    i = np.arange(128)
    c[:, 0, :] = np.eye(128)
    c[:, 1, :] = 1.0
    same64 = (i[:, None] // 64) == (i[None, :] // 64)
    same4 = ((i[:, None] // 4) == (i[None, :] // 4)) & (i[:, None] < 64) & (i[None, :] < 64)
    c[:, 2, :] = same64 & (i[:, None] <= i[None, :])
    c[:, 3, :] = same64 & (i[:, None] > i[None, :])
    c[:, 4, :] = same4 & (i[:, None] <= i[None, :])
    c[:, 5, :] = same4 & (i[:, None] > i[None, :])
    c[:, 6, 0:16] = (i[:, None] // 4) == np.arange(16)[None, :]
    for g, w in enumerate((2, 4, 8, 16)):
        c[:, 7, g * 16:(g + 1) * 16] = 1.0 / np.minimum(np.arange(16) + 1, w)
    c[:, 7, 64:80] = np.arange(16)[None, :]
    return c.reshape(128, 1024)


def _mixer_bufs(G, NT, sample):
    AR = G.AR
    B = type("B", (), {})()
    B.NT = NT
    B.hT = AR.alloc(8 * NT, BF16).rearrange("p (k t) -> p k t", k=8)
    B.xbf = [AR.alloc(1024, BF16)] * 2
    B.junk = B.xbf[0]
    B.ss = AR.alloc(8)
    B.stg = [(AR.alloc(2048), f"stg{i}") for i in range(2)]
    B.wbf = [(AR.alloc(2048, BF16), f"wbf{i}") for i in range(3)]
    B.pre = [AR.alloc(3 + NT) for _ in range(2)]
    B.cv = [AR.alloc(NT) for _ in range(2)]
    B.qkvc = AR.alloc(12 * NT).rearrange("p (c t) -> p c t", c=12)
    B.zs = AR.alloc(4 * NT).rearrange("p (c t) -> p c t", c=4)
    B.xqT = AR.alloc(4 * NT, BF16).rearrange("p (c t) -> p c t", c=4)
    B.yT = AR.alloc(12 * NT, BF16).rearrange("p (c t) -> p c t", c=12)
    B.macc8 = AR.alloc(8 * NT).rearrange("p (c t) -> p c t", c=8)
    B.mT = AR.alloc(8 * NT, BF16).rearrange("p (c t) -> p c t", c=8)
    B.sqb = [AR.alloc(NT) for _ in range(2)]
    B.rinv = [AR.alloc(NT) for _ in range(2)]
    B.sig = B.sqb
    B.prod = B.rinv
    B.dT = [AR.alloc(NT, BF16) for _ in range(2)]
    B.pA = AR.alloc(19 * 16 if sample else 15 + NT)
    B.pB = AR.alloc(19 * 16 if sample else 15 + NT)
    B.t16 = AR.alloc(16)
    B.ktok = AR.alloc(512).rearrange("p (h d) -> p h d", h=4)
    B.vtok = AR.alloc(512).rearrange("p (h d) -> p h d", h=4)
    B.ba = AR.alloc(32)
    B.wba = AR.alloc(64, BF16)
    B.gcc = AR.alloc(16)
    names = ["gL", "gcr", "egr", "dm", "t1", "dmT", "t2", "Pa", "Pb", "Qa", "Qb", "R", "u", "wT", "attnT", "vn", "qg", "kbg", "vb", "kd", "osq", "rr", "y1", "kdsc"]
    B.dw = [{}, {}]
    shared = ("gL", "dm", "dmT", "osq", "rr", "y1", "kdsc") if sample else ()
    for n in names:
        if n in shared:
            B.dw[0][n] = B.dw[1][n] = AR.alloc(128)
        else:
            B.dw[0][n] = AR.alloc(128)
            B.dw[1][n] = AR.alloc(128)
    B.dw_shared = shared
    B.pexp = [AR.alloc(256) for _ in range(2)]
    B.pn = [AR.alloc(256, BF16) for _ in range(2)]
    B.pT = [AR.alloc(256, BF16).rearrange("p (c t) -> p c t", c=2) for _ in range(2)]
    B.asm = AR.alloc(32)
    if not sample:
        B.uT = AR.alloc(4 * (15 + NT)).rearrange("p (c t) -> p c t", c=4)
        B.kTm = AR.alloc(4 * 256, BF16).rearrange("p (h m) -> p h m", h=4)
        B.vm = AR.alloc(2 * 512, BF16).rearrange("p (c n) -> p c n", c=2)
        B.hTm = B.hT
        qflat = B.qkvc.rearrange("p c t -> p (c t)")
        B.memx = qflat[:, 0:1024]
        B.kvrow = qflat[:, 1024:3072].rearrange("p (j n) -> p j n", j=2)
        B.rowbuf = qflat[:, 0:1536]
    else:
        B.uTs = AR.alloc(4 * 16 * 19).rearrange("p (c s e) -> p c s e", c=4, s=16)
        B.pre_s = [AR.alloc(16 * 7).rearrange("p (s e) -> p s e", s=16) for _ in range(2)]
        B.hist_s = AR.alloc(12 * 48).rearrange("p (c s e) -> p c s e", c=12, s=16)
        B.cvout = AR.alloc(12 * 48).rearrange("p (c s e) -> p c s e", c=12, s=16)
        B.ld1536 = AR.alloc(1536)
        B.ld512 = [AR.alloc(512) for _ in range(2)]
        mk_ = AR.mark()
        B.Sh = [AR.alloc(16 * 128).rearrange("p (s d) -> p s d", s=16) for _ in range(2)]
        B.rowbuf = B.Sh[0].rearrange("p s d -> p (s d)")[:, 0:1536]
        B.kdm = AR.alloc(16 * 128).rearrange("p (s d) -> p s d", s=16)
        B.wTm = AR.alloc(1088)
        B.o1 = AR.alloc(64)
        B.oTs = AR.alloc(64)
        hw_ = AR.mark()
        AR.release(mk_)
        B.xqm = AR.alloc(4 * 1088, BF16).rearrange("p (h r) -> p h r", h=4)
        B.kvs = [AR.alloc(1024) for _ in range(2)]
        B.kvb = [AR.alloc(1024, BF16).rearrange("p (c n) -> p c n", c=2) for _ in range(2)]
        B.kTs = [AR.alloc(1024, BF16).rearrange("p (h m) -> p h m", h=4) for _ in range(2)]
        B.pTall = [AR.alloc(256, BF16).rearrange("p (c t) -> p c t", c=2) for _ in range(4)]
        AR.release(max(hw_, AR.mark()))
    return B


def _mem_kv(G, B, l):
    P, PS, PSB, pk = G.P, G.PS, G.PSB, G.pk
    for j in range(2):
        P.add("sp", CALL("dma_start", out=B.memx[:, :], in_=G.memp[j * 128:(j + 1) * 128, :]), writes=["memx"], dma=1, key="memx")
        G.rms_rstd(B.memx[:, :], 128, "memx", B.junk, "xbf0", B.ss[:, 0:1], "ss0")
        xb = B.xbf[j % 2]
        P.add("act", CALL("activation", out=xb[:, :], in_=B.memx[:, :], func=AF.Copy, scale=B.ss[:, 0:1]),
              reads=["memx", "ss0"], writes=["xbf0"])
        G.to_featmajor(xb, 128, "xbf0", B.hTm, f"hTm{j}", slice(j * 128, (j + 1) * 128), gsb=G.gmem_sb, gkey="gmem")
    for g in range(4):
        wv, wk = G.load_w(G.w_mkv[l], 8, g * 256, 256, B.stg, B.wbf)
        for j in range(2):
            b = P.ps()
            for k in range(8):
                P.add("pe", CALL("matmul", PS(b, 256), lhsT=B.hTm[:, k, j * 128:(j + 1) * 128], rhs=wv[:, k, :], start=(k == 0), stop=(k == 7)),
                      reads=[f"hTm{j}", wk], writes=[pk(b)])
            P.add("act", CALL("activation", out=B.kvrow[:, j, g * 256:(g + 1) * 256], in_=PS(b, 256), func=AF.Copy),
                  reads=[pk(b)], writes=[f"kvrow{j}_{g}"])
            if g >= 2:
                P.add("dve", CALL("tensor_copy", out=B.vm[:, j, (g - 2) * 256:(g - 1) * 256], in_=PS(b, 256)),
                      reads=[pk(b)], writes=[f"vm{j}_{g}"])
        if g < 2:
            for cc in range(2):
                b = P.ps()
                for k in range(8):
                    P.add("pe", CALL("matmul", PS(b, 256), lhsT=wv[:, k, cc * 128:(cc + 1) * 128], rhs=B.hTm[:, k, :], start=(k == 0), stop=(k == 7)),
                          reads=["hTm0", "hTm1", wk], writes=[pk(b)])
                P.add("act", CALL("activation", out=B.kTm[:, g * 2 + cc, :], in_=PS(b, 256), func=AF.Copy),
                      reads=[pk(b)], writes=[f"kTm{g * 2 + cc}"])
    for j in range(2):
        P.add("pool", CALL("dma_start", out=G.o_mk[l, j * 128:(j + 1) * 128, :], in_=B.kvrow[:, j, 0:512]),
              reads=[f"kvrow{j}_0", f"kvrow{j}_1"], dma=1, key=f"o_mk{j}")
        P.add("pool", CALL("dma_start", out=G.o_mv[l, j * 128:(j + 1) * 128, :], in_=B.kvrow[:, j, 512:1024]),
              reads=[f"kvrow{j}_2", f"kvrow{j}_3"], dma=1, key=f"o_mv{j}")
    B.kTm_keys = [f"kTm{h}" for h in range(4)]
    B.vm_keys = [f"vm{j}_{g}" for j in range(2) for g in (2, 3)]


def _proj(G, B, l, wdram, krows, c0, nch, srcT, srckeys, NT, consume, chunk_w=128):
    P, PS, pk = G.P, G.PS, G.pk
    i = 0
    while i < nch:
        ng = min(2, nch - i)
        ncols = 128 * ng if chunk_w == 128 else chunk_w
        wv, wk = G.load_w(wdram, krows, c0 + i * 128, ncols, B.stg, B.wbf)
        for cc in range(ng):
            b = P.ps()
            for k in range(krows):
                P.add("pe", CALL("matmul", PS(b, NT)[0:chunk_w, :], lhsT=wv[:, k, cc * 128:cc * 128 + chunk_w], rhs=srcT[:, k, 0:NT],
                                                               start=(k == 0), stop=(k == krows - 1)),
                      reads=list(srckeys) + [wk], writes=[pk(b)])
            consume(i + cc, b)
        i += ng


def _delta_tile(G, B, l, j, np_, tsl, sample, extra=None):
    P, PS, PSB, pk = G.P, G.PS, G.PSB, G.pk
    LT = (G.Ltri4 if sample else G.Ltri)
    SLx = (G.SL4 if sample else G.SLm)
    I_ = G.identF
    ones = G.onesF
    r_ = slice(0, np_)
    for nm, c0, dst in (("ktok", 4, B.ktok), ("vtok", 8, B.vtok)):
        b = P.ps()
        for h in range(4):
            P.add("pe", CALL("transpose", out=PS(b, 128, h * 128)[r_, :], in_=B.qkvc[:, c0 + h, tsl], identity=I_),
                  reads=[f"qkvc{c0 + h}", "cst"], writes=[pk(b)])
        P.add("act", CALL("activation", out=dst[r_, :, :], in_=PS(b, 512)[r_, :].rearrange("p (h d) -> p h d", h=4), func=AF.Copy),
              reads=[pk(b)], writes=[nm])
    b = P.ps()
    P.add("pe", CALL("matmul", PS(b, 4)[r_, :], lhsT=LT[r_, r_], rhs=B.ba[r_, 8:12], start=True, stop=True), reads=["ba", "cst"], writes=[pk(b)])
    P.add("pe", CALL("matmul", PS(b, 4, 8)[r_, :], lhsT=LT[r_, r_], rhs=B.ba[r_, 8:12], start=True, stop=False), reads=["ba", "cst"], writes=[pk(b)])
    P.add("pe", CALL("matmul", PS(b, 4, 8)[r_, :], lhsT=SLx[r_, r_], rhs=B.ba[r_, 8:12], start=False, stop=True), reads=["ba", "cst"], writes=[pk(b)])
    P.add("dve", CALL("tensor_copy", out=B.gcc[r_, 0:4], in_=PS(b, 4)[r_, :]), reads=[pk(b)], writes=["gcc"])
    P.add("dve", CALL("tensor_copy", out=B.gcc[r_, 8:12], in_=PS(b, 4, 8)[r_, :]), reads=[pk(b)], writes=["gcc"])
    P.add("act", CALL("activation", out=B.gcc[r_, 4:8], in_=B.gcc[r_, 0:4], func=AF.Exp), reads=["gcc"], writes=["gcc"])
    def head_body(h):
        W = B.dw[h % 2]
        wn = lambda n, h=h: (f"dws_{n}" if n in B.dw_shared else f"dw{h % 2}_{n}")
        qT = B.qkvc[:, h, tsl]
        kT = B.qkvc[:, 4 + h, tsl]
        beta = B.ba[r_, h:h + 1]
        nbeta = B.ba[r_, 4 + h:5 + h]
        gcol = B.ba[r_, 8 + h:9 + h]
        gc_c = B.gcc[r_, h:h + 1]
        egc_c = B.gcc[r_, 4 + h:5 + h]
        gl_c = B.gcc[r_, 8 + h:9 + h]
        P.add("dve", CALL("tensor_scalar", out=W["gL"][r_, r_], in0=LT[r_, r_], scalar1=gcol, scalar2=0.0, op0=ALU.mult, op1=ALU.add),
              reads=["ba", "cst"], writes=[wn("gL")])
        b = P.ps()
        P.add("pe", CALL("matmul", PS(b, np_), lhsT=ones[r_, :], rhs=W["gL"][r_, r_], start=True, stop=True), reads=[wn("gL"), "cst"], writes=[pk(b)])
        P.add("act", CALL("activation", out=W["gcr"][:, r_], in_=PS(b, np_), func=AF.Copy), reads=[pk(b)], writes=[wn("gcr")])
        P.add("act", CALL("activation", out=W["egr"][:, r_], in_=PS(b, np_), func=AF.Exp), reads=[pk(b)], writes=[wn("egr")])
        P.add("dve", CALL("tensor_scalar", out=W["dm"][r_, r_], in0=W["gcr"][r_, r_], scalar1=gc_c, scalar2=0.0, op0=ALU.subtract, op1=ALU.max),
              reads=[wn("gcr"), "gcc"], writes=[wn("dm")])
        P.add("act", CALL("activation", out=W["dm"][r_, r_], in_=W["dm"][r_, r_], func=AF.Exp, scale=-1.0), reads=[wn("dm")], writes=[wn("dm")])
        P.add("pool", CALL("tensor_tensor", out=W["t1"][r_, r_], in0=W["dm"][r_, r_], in1=SLx[r_, r_], op=ALU.mult), reads=[wn("dm"), "cst"], writes=[wn("t1")])
        P.add("dve", CALL("tensor_scalar", out=W["dmT"][r_, r_], in0=W["gcr"][r_, r_], scalar1=gc_c, scalar2=0.0, op0=ALU.subtract, op1=ALU.min),
              reads=[wn("gcr"), "gcc"], writes=[wn("dmT")])
        P.add("act", CALL("activation", out=W["dmT"][r_, r_], in_=W["dmT"][r_, r_], func=AF.Exp), reads=[wn("dmT")], writes=[wn("dmT")])
        P.add("pool", CALL("tensor_tensor", out=W["t2"][r_, r_], in0=W["dmT"][r_, r_], in1=LT[r_, r_], op=ALU.mult), reads=[wn("dmT"), "cst"], writes=[wn("t2")])
        b = P.ps()
        P.add("pe", CALL("matmul", PS(b, np_)[r_, :], lhsT=kT, rhs=kT, start=True, stop=True), reads=[f"qkvc{4 + h}"], writes=[pk(b)])
        P.add("dve", CALL("scalar_tensor_tensor", out=W["Pa"][r_, r_], in0=PS(b, np_)[r_, :], scalar=nbeta, in1=W["t1"][r_, r_], op0=ALU.mult, op1=ALU.mult),
              reads=[pk(b), "ba", wn("t1")], writes=[wn("Pa")])
        b = P.ps()
        P.add("pe", CALL("transpose", out=PS(b, np_)[r_, :], in_=W["Pa"][r_, r_], identity=I_[r_, r_]), reads=[wn("Pa"), "cst"], writes=[pk(b)])
        P.add("act", CALL("activation", out=W["Qa"][r_, r_], in_=PS(b, np_)[r_, :], func=AF.Copy), reads=[pk(b)], writes=[wn("Qa")])
        P.add("dve", CALL("tensor_tensor", out=W["R"][r_, r_], in0=PS(b, np_)[r_, :], in1=I_[r_, r_], op=ALU.add), reads=[pk(b), "cst"], writes=[wn("R")])
        nst = 1 if sample else 5
        Pk, Qk, Pn, Qn = "Pa", "Qa", "Pb", "Qb"
        for k in range(nst):
            bP = P.ps()
            P.add("pe", CALL("matmul", PS(bP, np_)[r_, :], lhsT=W[Qk][r_, r_], rhs=W[Pk][r_, r_], start=True, stop=True),
                  reads=[wn(Pk), wn(Qk)], writes=[pk(bP)])
            P.add("act", CALL("activation", out=W[Pn][r_, r_], in_=PS(bP, np_)[r_, :], func=AF.Copy), reads=[pk(bP)], writes=[wn(Pn)])
            if k < nst - 1:
                bQ = P.ps()
                P.add("pe", CALL("matmul", PS(bQ, np_)[r_, :], lhsT=W[Pk][r_, r_], rhs=W[Qk][r_, r_], start=True, stop=True),
                      reads=[wn(Pk), wn(Qk)], writes=[pk(bQ)])
                P.add("dve", CALL("tensor_copy", out=W[Qn][r_, r_], in_=PS(bQ, np_)[r_, :]), reads=[pk(bQ)], writes=[wn(Qn)])
            bR = P.ps()
            P.add("pe", CALL("matmul", PS(bR, np_)[r_, :], lhsT=W[Pn][r_, r_], rhs=W["R"][r_, r_], start=True, stop=True),
                  reads=[wn(Pn), wn("R")], writes=[pk(bR)])
            P.add("dve", CALL("tensor_tensor", out=W["R"][r_, r_], in0=PS(bR, np_)[r_, :], in1=W["R"][r_, r_], op=ALU.add), reads=[pk(bR), wn("R")], writes=[wn("R")])
            Pk, Pn = Pn, Pk
            Qk, Qn = Qn, Qk
        P.add("dve", CALL("tensor_scalar", out=W["vb"][r_, :], in0=B.vtok[r_, h, :], scalar1=beta, scalar2=0.0, op0=ALU.mult, op1=ALU.add),
              reads=["vtok", "ba"], writes=[wn("vb")])
        P.add("dve", CALL("tensor_scalar", out=W["kbg"][r_, :], in0=B.ktok[r_, h, :], scalar1=beta, scalar2=egc_c, op0=ALU.mult, op1=ALU.mult),
              reads=["ktok", "ba", "gcc"], writes=[wn("kbg")])
        P.add("act", CALL("activation", out=W["kdsc"][r_, 0:1], in_=gc_c, func=AF.Exp, scale=-1.0, bias=gl_c), reads=["gcc"], writes=[wn("kdsc")])
        P.add("dve", CALL("tensor_scalar", out=W["kd"][r_, :], in0=B.ktok[r_, h, :], scalar1=W["kdsc"][r_, 0:1], scalar2=0.0, op0=ALU.mult, op1=ALU.add),
              reads=["ktok", wn("kdsc")], writes=[wn("kd")])
        P.add("pool", CALL("tensor_tensor", out=W["qg"][:, r_], in0=qT, in1=W["egr"][:, r_], op=ALU.mult), reads=[f"qkvc{h}", wn("egr")], writes=[wn("qg")])
        b = P.ps()
        P.add("pe", CALL("matmul", PS(b, 128)[r_, :], lhsT=W["R"][r_, r_], rhs=W["vb"][r_, :], start=True, stop=True), reads=[wn("R"), wn("vb")], writes=[pk(b)])
        P.add("act", CALL("activation", out=W["u"][r_, :], in_=PS(b, 128)[r_, :], func=AF.Copy), reads=[pk(b)], writes=[wn("u")])
        b = P.ps()
        P.add("pe", CALL("matmul", PS(b, np_), lhsT=W["kbg"][r_, :], rhs=W["R"][r_, r_], start=True, stop=True), reads=[wn("R"), wn("kbg")], writes=[pk(b)])
        P.add("act", CALL("activation", out=W["wT"][:, r_], in_=PS(b, np_), func=AF.Copy), reads=[pk(b)], writes=[wn("wT")])
        b = P.ps()
        P.add("pe", CALL("matmul", PS(b, np_)[r_, :], lhsT=kT, rhs=qT, start=True, stop=True), reads=[f"qkvc{4 + h}", f"qkvc{h}"], writes=[pk(b)])
        P.add("dve", CALL("tensor_tensor", out=W["attnT"][r_, r_], in0=PS(b, np_)[r_, :], in1=W["t2"][r_, r_], op=ALU.mult), reads=[pk(b), wn("t2")], writes=[wn("attnT")])
        if not sample:
            Sk = f"S{h}"
            Sh_ = G.Sst[:, h, :]
            bo = P.ps()
            for ci in range(2):
                rr_ = slice(ci * 64, ci * 64 + 64)
                bw = P.ps()
                P.add("pe", CALL("matmul", PS(bw, 128), lhsT=W["wT"][:, 0:128], rhs=Sh_, start=True, stop=True), reads=[wn("wT"), Sk], writes=[pk(bw)])
                P.add("dve", CALL("tensor_tensor", out=W["vn"][rr_, :], in0=W["u"][rr_, :], in1=PS(bw, 128)[rr_, :], op=ALU.subtract),
                      reads=[pk(bw), wn("u")], writes=[wn("vn") + str(ci)])
                P.add("pe", CALL("matmul", PS(bo, 64, 256 + ci * 64), lhsT=Sh_, rhs=W["qg"][:, rr_], start=True, stop=False), reads=[wn("qg"), Sk], writes=[pk(bo)])
                P.add("pe", CALL("matmul", PS(bo, 64, 256 + ci * 64), lhsT=W["vn"][rr_, :], rhs=W["attnT"][rr_, rr_], start=False, stop=True),
                      reads=[wn("vn") + str(ci), wn("attnT")], writes=[pk(bo)])
                bs = P.ps()
                P.add("pe", CALL("matmul", PS(bs, 128), lhsT=W["kd"][rr_, :], rhs=W["vn"][rr_, :], start=True, stop=True), reads=[wn("kd"), wn("vn") + str(ci)], writes=[pk(bs)])
                P.add("dve", CALL("scalar_tensor_tensor", out=Sh_, in0=Sh_, scalar=W["egr"][:, ci * 64 + 63:ci * 64 + 64], in1=PS(bs, 128), op0=ALU.mult, op1=ALU.add),
                      reads=[pk(bs), wn("egr"), Sk], writes=[Sk])
            o_ap = PS(bo, np_, 256)
            o_key = pk(bo)
        else:
            Shb = B.Sh[h % 2]
            Sk = f"Sh{h % 2}"
            P.add("sp", CALL("dma_start", out=Shb[:, :, :], in_=G.st_delta[l, :, h].rearrange("s k v -> k s v")), writes=[Sk], dma=1, key=Sk)
            P.add("dve", CALL("tensor_copy", out=B.wTm[:, 0:1088].rearrange("p (s r) -> p s r", r=68)[:, :, 0:4], in_=W["wT"][:, 0:64].rearrange("p (s i) -> p s i", i=4)),
                  reads=[wn("wT")], writes=["wTm"])
            bw = P.ps()
            for s in range(16):
                P.add("pe", CALL("matmul", PS(bw, 128)[0:64, :], lhsT=B.wTm[:, s * 64:(s + 1) * 64], rhs=Shb[:, s, :], start=(s == 0), stop=(s == 15)),
                      reads=["wTm", Sk], writes=[pk(bw)])
            P.add("dve", CALL("tensor_tensor", out=W["vn"][0:64, :], in0=W["u"][0:64, :], in1=PS(bw, 128)[0:64, :], op=ALU.subtract), reads=[pk(bw), wn("u")], writes=[wn("vn") + "0"])
            bo = P.ps()
            for s in range(16):
                P.add("pe", CALL("matmul", PS(bo, 4, 4 * s), lhsT=Shb[:, s, :], rhs=W["qg"][:, 4 * s:4 * s + 4], start=True, stop=True),
                      reads=[wn("qg"), Sk], writes=[pk(bo)])
            P.add("act", CALL("activation", out=B.o1[:, 0:64], in_=PS(bo, 64), func=AF.Copy), reads=[pk(bo)], writes=["o1"])
            b2 = P.ps()
            P.add("pe", CALL("matmul", PS(b2, 64), lhsT=W["vn"][0:64, :], rhs=W["attnT"][0:64, 0:64], start=True, stop=True), reads=[wn("vn") + "0", wn("attnT")], writes=[pk(b2)])
            P.add("dve", CALL("tensor_tensor", out=B.oTs[:, 0:64], in0=PS(b2, 64), in1=B.o1[:, 0:64], op=ALU.add), reads=[pk(b2), "o1"], writes=["oTs"])
            P.add("pool", CALL("tensor_tensor", out=B.kdm[0:64, :, :], in0=W["kd"][0:64, :].unsqueeze(1).to_broadcast([64, 16, 128]),
                                                         in1=G.seqmask[0:64, :].unsqueeze(2).to_broadcast([64, 16, 128]), op=ALU.mult),
                  reads=[wn("kd"), "cst"], writes=["kdm"])
            for s in range(16):
                bs = P.ps()
                P.add("pe", CALL("matmul", PS(bs, 128), lhsT=B.kdm[0:64, s, :], rhs=W["vn"][0:64, :], start=True, stop=True), reads=["kdm", wn("vn") + "0"], writes=[pk(bs)])
                P.add("dve", CALL("scalar_tensor_tensor", out=Shb[:, s, :], in0=Shb[:, s, :], scalar=W["egr"][:, 4 * s + 3:4 * s + 4], in1=PS(bs, 128), op0=ALU.mult, op1=ALU.add),
                      reads=[pk(bs), wn("egr"), Sk], writes=[Sk])
            P.add("pool", CALL("dma_start", out=G.o_delta_s[l, :, h].rearrange("s k v -> k s v"), in_=Shb[:, :, :]), reads=[Sk], dma=1, key="o_" + Sk)
            o_ap = B.oTs[:, 0:64]
            o_key = "oTs"
        P.add("act", CALL("activation", out=W["osq"][:, r_], in_=o_ap, func=AF.Square), reads=[o_key], writes=[wn("osq")])
        bq = P.ps()
        P.add("pe", CALL("matmul", PS(bq, np_), lhsT=ones, rhs=W["osq"][:, r_], start=True, stop=True), reads=[wn("osq"), "cst"], writes=[pk(bq)])
        P.add("act", CALL("activation", out=W["rr"][:, r_], in_=PS(bq, np_), func=AF.Sqrt, scale=1.0 / 128, bias=G.epsb[:, 0:1]), reads=[pk(bq), "epsb"], writes=[wn("rr")])
        P.add("dve", CALL("reciprocal", out=W["rr"][:, r_], in_=W["rr"][:, r_]), reads=[wn("rr")], writes=[wn("rr")])
        P.add("dve", CALL("scalar_tensor_tensor", out=W["y1"][:, r_], in0=o_ap, scalar=G.gdn_sb[:, 0:1], in1=W["rr"][:, r_], op0=ALU.mult, op1=ALU.mult),
              reads=[o_key, "gdn", wn("rr")], writes=[wn("y1")])
        P.add("pool", CALL("tensor_tensor", out=B.yT[:, 4 + h, tsl], in0=W["y1"][:, r_], in1=B.zs[:, h, tsl], op=ALU.mult), reads=[wn("y1"), f"zs{h}"], writes=[f"yT{4 + h}_{j}"])

    if sample:
        for h in range(4):
            head_body(h)
    else:
        for h0 in (0, 2):
            recs = []
            for hh, banks in ((h0, (2, 3, 4)), (h0 + 1, (5, 6, 7))):
                P.begin_record(banks)
                head_body(hh)
                recs.append(P.end_record())
            if extra:
                recs.append(extra.pop(0))
            P.replay(recs)


def _attn_softmax(G, B, sc_ap, sc_key, np_, h, out_writes):
    P, PS, PSB, pk = G.P, G.PS, G.PSB, G.pk
    r_ = slice(0, np_)
    i = h % 2
    sc = 128.0 ** -0.5
    mx = B.asm[r_, 4 * i:4 * i + 1]
    nmx = B.asm[r_, 4 * i + 1:4 * i + 2]
    rs = B.asm[r_, 4 * i + 2:4 * i + 3]
    ak = f"asm{i}"
    P.add("dve", CALL("reduce_max", out=mx, in_=sc_ap, axis=AX.X), reads=[sc_key], writes=[ak])
    P.add("dve", CALL("tensor_scalar", out=nmx, in0=mx, scalar1=-sc, scalar2=0.0, op0=ALU.mult, op1=ALU.add), reads=[ak], writes=[ak])
    P.add("act", CALL("activation", out=B.pexp[i][r_, :], in_=sc_ap, func=AF.Exp, scale=sc, bias=nmx, accum_out=rs), reads=[sc_key, ak], writes=[f"pexp{i}", ak])
    P.add("dve", CALL("reciprocal", out=rs, in_=rs), reads=[ak], writes=[ak])
    P.add("dve", CALL("tensor_scalar", out=B.pn[i][r_, :], in0=B.pexp[i][r_, :], scalar1=rs, scalar2=0.0, op0=ALU.mult, op1=ALU.add), reads=[f"pexp{i}", ak], writes=[f"pn{i}"])
    bt = P.ps()
    for mc in range(2):
        P.add("pe", CALL("transpose", out=PSB(bt, np_, mc * 128), in_=B.pn[i][r_, mc * 128:(mc + 1) * 128], identity=G.identB[r_, r_]),
              reads=[f"pn{i}", "identB"], writes=[pk(bt)])
    P.add("act", CALL("activation", out=B.pT[i][:, :, r_], in_=PSB(bt, 256).rearrange("p (c t) -> p c t", c=2)[:, :, r_], func=AF.Copy), reads=[pk(bt)], writes=[f"pT{i}"])
    return B.pT[i], f"pT{i}"


def _mixer_st(G, B, l, st, tiles, sample):
    P, PS, PSB, pk = G.P, G.PS, G.PSB, G.pk
    NT = sum(t[1] for t in tiles)
    nt = len(tiles)
    hkeys = [f"hT{j}" for j in range(nt)]
    for j, (x_ap, np_, xkey) in enumerate(tiles):
        G.rms_rstd(x_ap, np_, xkey, B.junk, "xbf0", B.ss[:, 0:1], "ss0")
        xb = B.xbf[j % 2]
        P.add("act", CALL("activation", out=xb[0:np_, :], in_=x_ap, func=AF.Copy, scale=B.ss[0:np_, 0:1]),
              reads=[xkey, "ss0"], writes=["xbf0"])
        G.to_featmajor(xb, np_, "xbf0", B.hT, hkeys[j], slice(j * 128, j * 128 + np_), gsb=G.gmix_sb, gkey="gmix")

    win = (2, 4, 8, 16)
    if not sample:
        L = 15 + NT
        if st == 0:
            P.add("pool", CALL("memset", B.uT[:, :, 0:15], 0.0), writes=[f"uT{c}" for c in range(4)])

        def pool_consume(c, b):
            P.add("act", CALL("activation", out=B.uT[:, c, 15:L], in_=PS(b, NT), func=AF.Copy), reads=[pk(b)], writes=[f"uT{c}"])
            a = B.uT[:, c, :]
            bufs = [(B.pA, "pA"), (B.pB, "pB")]
            src, skey = a, f"uT{c}"
            sh = 1
            for s_ in range(c + 1):
                dst, dkey = bufs[s_ % 2]
                lo = 2 * sh - 1
                P.add("dve", CALL("tensor_tensor", out=dst[:, lo:L], in0=src[:, lo:L], in1=src[:, lo - sh:L - sh], op=ALU.add),
                      reads=[skey], writes=[dkey])
                src, skey = dst, dkey
                sh *= 2
            dT = B.dT[c % 2]
            dk = f"dT{c % 2}"
            P.add("dve", CALL("scalar_tensor_tensor", out=dT[:, 0:NT], in0=src[:, 15:L], scalar=1.0 / win[c], in1=a[:, 15:L], op0=ALU.mult, op1=ALU.subtract),
                  reads=[skey, f"uT{c}"], writes=[dk])
            if st == 0:
                P.add("dve", CALL("tensor_tensor", out=B.t16[:, 0:16], in0=src[:, 15:31], in1=G.rcnt[:, c * 16:(c + 1) * 16], op=ALU.mult), reads=[skey, "cst"], writes=["t16"])
                P.add("dve", CALL("tensor_tensor", out=dT[:, 0:16], in0=B.t16[:, 0:16], in1=a[:, 15:31], op=ALU.subtract), reads=["t16", f"uT{c}", dk], writes=[dk])
            b2 = P.ps()
            P.add("pe", CALL("matmul", PS(b2, NT), lhsT=G.wgrp_b[:, c, :], rhs=dT[:, 0:NT], start=True, stop=True), reads=[dk, "wgrp_b"], writes=[pk(b2)])
            P.add("act", CALL("activation", out=B.yT[:, c, 0:NT], in_=PS(b2, NT), func=AF.Copy, scale=G.psc_sb[:, c:c + 1]), reads=[pk(b2), "psc"], writes=[f"yT{c}_all"])
        _proj(G, B, l, G.w_in[l], 8, 0, 4, B.hT, hkeys, NT, pool_consume)
        P.add("pool", CALL("tensor_copy", out=B.uT[:, :, 0:15], in_=B.uT[:, :, NT:NT + 15]), writes=[f"uT{c}" for c in range(4)])
    else:
        def pool_consume(c, b):
            P.add("act", CALL("activation", out=B.uTs[:, c, :, 15:19], in_=PS(b, 64).rearrange("p (s i) -> p s i", i=4), func=AF.Copy), reads=[pk(b)], writes=[f"uTs{c}"])
            a = B.uTs[:, c, :, :]
            pA = B.pA[:, 0:304].rearrange("p (s e) -> p s e", e=19)
            pB = B.pB[:, 0:304].rearrange("p (s e) -> p s e", e=19)
            bufs = [(pA, "pA"), (pB, "pB")]
            src, skey = a, f"uTs{c}"
            sh = 1
            for s_ in range(c + 1):
                dst, dkey = bufs[s_ % 2]
                lo = 2 * sh - 1
                P.add("dve", CALL("tensor_tensor", out=dst[:, :, lo:19], in0=src[:, :, lo:19], in1=src[:, :, lo - sh:19 - sh], op=ALU.add),
                      reads=[skey], writes=[dkey])
                src, skey = dst, dkey
                sh *= 2
            dT = B.dT[c % 2]
            dk = f"dT{c % 2}"
            P.add("dve", CALL("scalar_tensor_tensor", out=dT[:, 0:64].rearrange("p (s i) -> p s i", i=4), in0=src[:, :, 15:19], scalar=1.0 / win[c], in1=a[:, :, 15:19], op0=ALU.mult, op1=ALU.subtract),
                  reads=[skey, f"uTs{c}"], writes=[dk])
            b2 = P.ps()
            P.add("pe", CALL("matmul", PS(b2, NT), lhsT=G.wgrp_b[:, c, :], rhs=dT[:, 0:NT], start=True, stop=True), reads=[dk, "wgrp_b"], writes=[pk(b2)])
            P.add("act", CALL("activation", out=B.yT[:, c, 0:NT], in_=PS(b2, NT), func=AF.Copy, scale=G.psc_sb[:, c:c + 1]), reads=[pk(b2), "psc"], writes=[f"yT{c}_all"])
        _proj(G, B, l, G.w_in[l], 8, 0, 4, B.hT, hkeys, NT, pool_consume)

    def qkv_consume(c, b):
        if not sample:
            pre = B.pre[c % 2]
            pkey = f"pre{c % 2}"
            cv = B.cv[c % 2]
            ckey = f"cv{c % 2}"
            P.add("pool", CALL("tensor_copy", out=pre[:, 0:3], in_=G.hist[:, c, :]), reads=[f"hist{c}"], writes=[pkey + "h"])
            P.add("act", CALL("activation", out=pre[:, 3:3 + NT], in_=PS(b, NT), func=AF.Copy), reads=[pk(b)], writes=[pkey])
            P.add("dve", CALL("tensor_scalar", out=cv[:, 0:NT], in0=pre[:, 0:NT], scalar1=G.wconv_sb[:, c, 0:1], scalar2=0.0, op0=ALU.mult, op1=ALU.add),
                  reads=[pkey, pkey + "h", "wconv"], writes=[ckey])
            for j in range(1, 4):
                P.add("dve", CALL("scalar_tensor_tensor", out=cv[:, 0:NT], in0=pre[:, j:j + NT], scalar=G.wconv_sb[:, c, j:j + 1], in1=cv[:, 0:NT], op0=ALU.mult, op1=ALU.add),
                      reads=[pkey, pkey + "h", "wconv", ckey], writes=[ckey])
            P.add("pool", CALL("tensor_copy", out=G.hist[:, c, :], in_=pre[:, NT:NT + 3]), reads=[pkey], writes=[f"hist{c}"])
            P.add("act", CALL("activation", out=B.qkvc[:, c, 0:NT], in_=cv[:, 0:NT], func=AF.Silu), reads=[ckey], writes=[f"qkvc{c}"])
        else:
            pre = B.pre_s[c % 2]
            pkey = f"pre{c % 2}"
            cv = B.cv[c % 2][:, 0:64].rearrange("p (s i) -> p s i", i=4)
            ckey = f"cv{c % 2}"
            P.add("pool", CALL("tensor_copy", out=pre[:, :, 0:3], in_=B.hist_s[:, c, :, :]), reads=[f"hist_s{c // 4}"], writes=[pkey + "h"])
            P.add("act", CALL("activation", out=pre[:, :, 3:7], in_=PS(b, 64).rearrange("p (s i) -> p s i", i=4), func=AF.Copy), reads=[pk(b)], writes=[pkey])
            P.add("dve", CALL("tensor_scalar", out=cv, in0=pre[:, :, 0:4], scalar1=G.wconv_sb[:, c, 0:1], scalar2=0.0, op0=ALU.mult, op1=ALU.add),
                  reads=[pkey, pkey + "h", "wconv"], writes=[ckey])
            for j in range(1, 4):
                P.add("dve", CALL("scalar_tensor_tensor", out=cv, in0=pre[:, :, j:j + 4], scalar=G.wconv_sb[:, c, j:j + 1], in1=cv, op0=ALU.mult, op1=ALU.add),
                      reads=[pkey, pkey + "h", "wconv", ckey], writes=[ckey])
            P.add("pool", CALL("tensor_copy", out=B.cvout[:, c, :, :], in_=pre[:, :, 4:7]), reads=[pkey], writes=[f"cvout{c // 4}"])
            P.add("act", CALL("activation", out=B.qkvc[:, c, 0:NT], in_=B.cv[c % 2][:, 0:64], func=AF.Silu), reads=[ckey], writes=[f"qkvc{c}"])
    _proj(G, B, l, G.w_in[l], 8, OFF_Q, 12, B.hT, hkeys, NT, qkv_consume)

    for c in range(8):
        i = c % 2
        P.add("act", CALL("activation", out=B.sqb[i][:, 0:NT], in_=B.qkvc[:, c, 0:NT], func=AF.Square), reads=[f"qkvc{c}"], writes=[f"sqb{i}"])
        b = P.ps()
        P.add("pe", CALL("matmul", PS(b, NT), lhsT=G.onesF, rhs=B.sqb[i][:, 0:NT], start=True, stop=True), reads=[f"sqb{i}", "cst"], writes=[pk(b)])
        P.add("act", CALL("activation", out=B.rinv[i][:, 0:NT], in_=PS(b, NT), func=AF.Sqrt, scale=1.0, bias=G.epsb[:, 0:1]), reads=[pk(b), "epsb"], writes=[f"rinv{i}"])
        P.add("dve", CALL("reciprocal", out=B.rinv[i][:, 0:NT], in_=B.rinv[i][:, 0:NT]), reads=[f"rinv{i}"], writes=[f"rinv{i}"])
        scl = (128.0 ** -0.5) if c < 4 else 1.0
        P.add("dve", CALL("scalar_tensor_tensor", out=B.qkvc[:, c, 0:NT], in0=B.rinv[i][:, 0:NT], scalar=scl, in1=B.qkvc[:, c, 0:NT], op0=ALU.mult, op1=ALU.mult),
              reads=[f"rinv{i}", f"qkvc{c}"], writes=[f"qkvc{c}"])

    def z_consume(c, b):
        P.add("act", CALL("activation", out=B.zs[:, c, 0:NT], in_=PS(b, NT), func=AF.Silu), reads=[pk(b)], writes=[f"zs{c}"])
    _proj(G, B, l, G.w_in[l], 8, OFF_Z, 4, B.hT, hkeys, NT, z_consume)

    def xq_consume(c, b):
        P.add("act", CALL("activation", out=B.xqT[:, c, 0:NT], in_=PS(b, NT), func=AF.Copy), reads=[pk(b)], writes=[f"xqT{c}"])
    _proj(G, B, l, G.w_in[l], 8, OFF_XQ, 4, B.hT, hkeys, NT, xq_consume)

    wba_t, wbak_t = G.load_w(G.w_in[l], 8, OFF_BA, 8, B.stg, B.wbf)
    wba = B.wba.rearrange("p (k n) -> p k n", k=8)
    wbak = "wba"
    P.add("act", CALL("activation", out=wba, in_=wba_t, func=AF.Copy), reads=[wbak_t], writes=[wbak])

    if sample:
        _sample_attn(G, B, l)
        P.barrier()
        P.add("pool", CALL("memset", B.wTm[:, :], 0.0), writes=["wTm"])
    def attn_tile(j, np_):
        r_ = slice(0, np_)
        tsl = slice(j * 128, j * 128 + np_)
        for h in range(4):
            bs_ = P.ps()
            P.add("pe", CALL("matmul", PS(bs_, 256)[r_, :], lhsT=B.xqT[:, h, tsl], rhs=B.kTm[:, h, :], start=True, stop=True),
                  reads=[f"xqT{h}", f"kTm{h}"], writes=[pk(bs_)])
            pT, pTk = _attn_softmax(G, B, PS(bs_, 256)[r_, :], pk(bs_), np_, h, None)
            bo = P.ps()
            for mc in range(2):
                P.add("pe", CALL("matmul", PS(bo, np_), lhsT=B.vm[:, mc, h * 128:(h + 1) * 128], rhs=pT[:, mc, r_], start=(mc == 0), stop=(mc == 1)),
                      reads=[pTk] + B.vm_keys, writes=[pk(bo)])
            P.add("act", CALL("activation", out=B.yT[:, 8 + h, tsl], in_=PS(bo, np_), func=AF.Copy), reads=[pk(bo)], writes=[f"yT{8 + h}_{j}"])

    ykeys = [[f"yT{c}_all" for c in range(4)], [f"yT{4 + h}_{j}" for h in range(4) for j in range(nt)], [f"yT{8 + h}_{j}" for h in range(4) for j in range(nt)]]
    if sample:
        ykeys[2] = [f"yT{8 + h}_0" for h in range(4)]
    norder = (0, 2, 1)

    def merge_branch(n):
        first, last = (n == norder[0]), (n == norder[-1])
        for half in range(2):
            wg0, wgk0 = G.load_w(G.w_in[l], 8, OFF_GATE + n * 1024 + half * 512, 256, B.stg, B.wbf)
            wb_, wbk = G.load_w(G.w_br[l, n], 4, half * 512, 512, B.stg, B.wbf)
            wg1, wgk1 = G.load_w(G.w_in[l], 8, OFF_GATE + n * 1024 + half * 512 + 256, 256, B.stg, B.wbf)
            for jj in range(4):
                wg, wgk = (wg0, wgk0) if jj < 2 else (wg1, wgk1)
                cc = jj % 2
                i = jj % 2
                mj = half * 4 + jj
                bg = P.ps()
                for k in range(8):
                    P.add("pe", CALL("matmul", PS(bg, NT), lhsT=wg[:, k, cc * 128:(cc + 1) * 128], rhs=B.hT[:, k, 0:NT], start=(k == 0), stop=(k == 7)),
                          reads=hkeys + [wgk], writes=[pk(bg)])
                P.add("act", CALL("activation", out=B.sig[i][:, 0:NT], in_=PS(bg, NT), func=AF.Sigmoid), reads=[pk(bg)], writes=[f"sqb{i}"])
                bb = P.ps()
                for c in range(4):
                    P.add("pe", CALL("matmul", PS(bb, NT), lhsT=wb_[:, c, jj * 128:(jj + 1) * 128], rhs=B.yT[:, n * 4 + c, 0:NT], start=(c == 0), stop=(c == 3)),
                          reads=ykeys[n] + [wbk], writes=[pk(bb)])
                if first:
                    P.add("dve", CALL("tensor_tensor", out=B.macc8[:, mj, 0:NT], in0=PS(bb, NT), in1=B.sig[i][:, 0:NT], op=ALU.mult), reads=[pk(bb), f"sqb{i}"], writes=[f"macc{mj}"])
                else:
                    P.add("dve", CALL("tensor_tensor", out=B.prod[i][:, 0:NT], in0=PS(bb, NT), in1=B.sig[i][:, 0:NT], op=ALU.mult), reads=[pk(bb), f"sqb{i}"], writes=[f"rinv{i}"])
                    if not last:
                        P.add("pool", CALL("tensor_tensor", out=B.macc8[:, mj, 0:NT], in0=B.macc8[:, mj, 0:NT], in1=B.prod[i][:, 0:NT], op=ALU.add), reads=[f"rinv{i}", f"macc{mj}"], writes=[f"macc{mj}"])
                    else:
                        P.add("pool", CALL("tensor_tensor", out=B.mT[:, mj, 0:NT], in0=B.macc8[:, mj, 0:NT], in1=B.prod[i][:, 0:NT], op=ALU.add),
                              reads=[f"rinv{i}", f"macc{mj}"], writes=[f"mT{mj}"])

    extra = None
    if not sample:
        P.begin_record((0, 1))
        for j_, (x_ap_, np__, xkey_) in enumerate(tiles):
            attn_tile(j_, np__)
        merge_branch(0)
        merge_branch(2)
        rc = P.end_record()
        nparts = 2 * len(tiles)
        step = (len(rc) + nparts - 1) // nparts
        extra = [rc[i * step:(i + 1) * step] for i in range(nparts)]

    def per_tile(j, x_ap, np_, xkey):
        r_ = slice(0, np_)
        tsl = slice(j * 128, j * 128 + np_)
        b = P.ps()
        for k in range(8):
            P.add("pe", CALL("matmul", PS(b, 8)[r_, :], lhsT=B.hT[:, k, tsl], rhs=wba[:, k, :], start=(k == 0), stop=(k == 7)), reads=[hkeys[j], wbak], writes=[pk(b)])
        ba = B.ba
        P.add("act", CALL("activation", out=ba[r_, 0:4], in_=PS(b, 4)[r_, :], func=AF.Sigmoid), reads=[pk(b)], writes=["ba"])
        P.add("dve", CALL("tensor_scalar", out=ba[r_, 4:8], in0=ba[r_, 0:4], scalar1=-1.0, scalar2=0.0, op0=ALU.mult, op1=ALU.add), reads=["ba"], writes=["ba"])
        P.add("dve", CALL("tensor_tensor", out=ba[r_, 12:16], in0=PS(b, 4, 4)[r_, :], in1=G.dtb_b[r_, :], op=ALU.add), reads=[pk(b), "dtb"], writes=["ba"])
        P.add("dve", CALL("scalar_tensor_tensor", out=ba[r_, 16:20], in0=ba[r_, 12:16], scalar=-1.0, in1=ba[r_, 12:16], op0=ALU.mult, op1=ALU.max), reads=["ba"], writes=["ba"])
        P.add("act", CALL("activation", out=ba[r_, 16:20], in_=ba[r_, 16:20], func=AF.Exp, scale=-1.0), reads=["ba"], writes=["ba"])
        P.add("act", CALL("activation", out=ba[r_, 16:20], in_=ba[r_, 16:20], func=AF.Ln, scale=1.0, bias=G.onesF[r_, 0:1]), reads=["ba", "cst"], writes=["ba"])
        P.add("dve", CALL("scalar_tensor_tensor", out=ba[r_, 12:16], in0=ba[r_, 12:16], scalar=0.0, in1=ba[r_, 16:20], op0=ALU.max, op1=ALU.add), reads=["ba"], writes=["ba"])
        P.add("dve", CALL("tensor_tensor", out=ba[r_, 8:12], in0=ba[r_, 12:16], in1=G.alog_b[r_, :], op=ALU.mult), reads=["ba", "alog"], writes=["ba"])
        _delta_tile(G, B, l, j, np_, tsl, sample, extra)

    if extra:
        P.ps_lo = 2
    for j_, (x_ap_, np__, xkey_) in enumerate(tiles):
        per_tile(j_, x_ap_, np__, xkey_)
    if not sample:
        P.ps_lo = 0

    if sample:
        merge_branch(0)
        merge_branch(2)
    merge_branch(1)

    mkeys = [f"mT{c}" for c in range(8)]
    for q in range(4):
        wo, wok = G.load_w(G.w_o[l], 8, q * 256, 256, B.stg, B.wbf)
        for j, (x_ap, np_, xkey) in enumerate(tiles):
            tsl = slice(j * 128, j * 128 + np_)
            b = P.ps()
            for k in range(8):
                P.add("pe", CALL("matmul", PS(b, 256)[0:np_, :], lhsT=B.mT[:, k, tsl], rhs=wo[:, k, :], start=(k == 0), stop=(k == 7)),
                      reads=mkeys + [wok], writes=[pk(b)])
            xo = x_ap[:, q * 256:(q + 1) * 256]
            P.add("dve", CALL("tensor_tensor", out=xo, in0=xo, in1=PS(b, 256)[0:np_, :], op=ALU.add), reads=[pk(b), xkey], writes=[xkey])


def _sample_prep(G, B, l):
    P, PS, PSB, pk = G.P, G.PS, G.PSB, G.pk
    I_ = G.identF
    P.add("pool", CALL("memset", B.xqm[:, :, :], 0.0), writes=["xqm"])
    P.add("sp", CALL("dma_start", out=B.ld1536[0:48, :], in_=G.st_conv[l]), writes=["ld1536"], dma=1, key="ld1536")
    for g in range(3):
        b = P.ps()
        for cc in range(4):
            c = g * 4 + cc
            P.add("pe", CALL("transpose", out=PS(b, 48, cc * 48), in_=B.ld1536[0:48, c * 128:(c + 1) * 128], identity=I_[0:48, 0:48]), reads=["ld1536", "cst"], writes=[pk(b)])
        P.add("act", CALL("activation", out=B.hist_s[:, g * 4:(g + 1) * 4, :, :].rearrange("p c s e -> p (c s e)"), in_=PS(b, 192), func=AF.Copy), reads=[pk(b)], writes=[f"hist_s{g}"])
    for hf in range(2):
        ld = B.ld512[hf]
        P.add("sp", CALL("dma_start", out=ld[0:120, :], in_=G.st_pool[l, hf * 120:(hf + 1) * 120, :]), writes=[f"ld512{hf}"], dma=1, key=f"ld512{hf}")
        b = P.ps()
        for c in range(4):
            P.add("pe", CALL("transpose", out=PS(b, 120, c * 120), in_=ld[0:120, c * 128:(c + 1) * 128], identity=I_[0:120, 0:120]), reads=[f"ld512{hf}", "cst"], writes=[pk(b)])
        for c in range(4):
            P.add("act", CALL("activation", out=B.uTs[:, c, hf * 8:(hf + 1) * 8, 0:15], in_=PS(b, 120, c * 120).rearrange("p (s e) -> p s e", e=15), func=AF.Copy),
                  reads=[pk(b)], writes=[f"uTs{c}"])


def _sample_attn(G, B, l):
    P, PS, PSB, pk = G.P, G.PS, G.PSB, G.pk
    P.ps_lo = 4
    for h in range(4):
        P.add("dve", CALL("tensor_copy", out=B.xqm[:, h, 0:1088].rearrange("p (s r) -> p s r", r=68)[:, :, 0:4], in_=B.xqT[:, h, 0:64].rearrange("p (s i) -> p s i", i=4)),
              reads=[f"xqT{h}"], writes=["xqm"])
    for s in range(16):
        i = s % 2
        st_, stk = B.kvs[i], f"kvs{i}"
        P.add("sp", CALL("dma_start", out=st_[:, :].rearrange("p (c n) -> p c n", c=2), in_=G.c_k[l, s].rearrange("(c p) n -> p c n", p=128)), writes=[stk], dma=1, key=stk)
        P.add("act", CALL("activation", out=B.kvb[i][:, :, :].rearrange("p c n -> p (c n)"), in_=st_[:, :], func=AF.Copy), reads=[stk], writes=[f"kvb{i}"])
        bt = 4 + (s % 4)
        for h in range(4):
            for mc in range(2):
                P.add("pe", CALL("transpose", out=PSB(bt, 128, h * 256 + mc * 128), in_=B.kvb[i][:, mc, h * 128:(h + 1) * 128], identity=G.identB),
                      reads=[f"kvb{i}", "identB"], writes=[pk(bt)])
        P.add("dve", CALL("tensor_copy", out=B.kTs[i][:, :, :].rearrange("p h m -> p (h m)"), in_=PSB(bt, 1024)), reads=[pk(bt)], writes=[f"kTs{i}"])
        for h in range(4):
            P.add("pe", CALL("matmul", PS(h, 256)[0:64, :], lhsT=B.xqm[:, h, s * 64:(s + 1) * 64], rhs=B.kTs[i][:, h, :], start=(s == 0), stop=(s == 15)),
                  reads=["xqm", f"kTs{i}"], writes=[pk(h)])
    pTs = []
    for h in range(4):
        pT, pTk = _attn_softmax(G, B, PS(h, 256)[0:64, :], pk(h), 64, h, None)
        dst = B.pTall[h]
        P.add("pool", CALL("tensor_copy", out=dst[:, :, 0:64], in_=pT[:, :, 0:64]), reads=[pTk], writes=[f"pTall{h}"])
        pTs.append(dst)
    for s in range(16):
        i = s % 2
        st_, stk = B.kvs[i], f"kvs{i}"
        P.add("sp", CALL("dma_start", out=st_[:, :].rearrange("p (c n) -> p c n", c=2), in_=G.c_v[l, s].rearrange("(c p) n -> p c n", p=128)), writes=[stk], dma=1, key=stk)
        P.add("act", CALL("activation", out=B.kvb[i][:, :, :].rearrange("p c n -> p (c n)"), in_=st_[:, :], func=AF.Copy), reads=[stk], writes=[f"kvb{i}"])
        for h in range(4):
            for mc in range(2):
                P.add("pe", CALL("matmul", PS(h, 4, 4 * s), lhsT=B.kvb[i][:, mc, h * 128:(h + 1) * 128], rhs=pTs[h][:, mc, 4 * s:4 * s + 4], start=(mc == 0), stop=(mc == 1)),
                      reads=[f"kvb{i}", f"pTall{h}"], writes=[pk(h)])
    for h in range(4):
        P.add("act", CALL("activation", out=B.yT[:, 8 + h, 0:64], in_=PS(h, 64), func=AF.Copy), reads=[pk(h)], writes=[f"yT{8 + h}_0"])
    P.ps_lo = 0


def mixer_phase(G, l):
    P, AR, PS, pk = G.P, G.AR, G.PS, G.pk
    I_ = G.identF
    P.barrier()
    m0 = AR.mark()
    Bp = _mixer_bufs(G, 256, False)
    P.add("pool", CALL("memset", G.hist[:, :, :], 0.0), writes=[f"hist{c}" for c in range(12)])
    P.add("pool", CALL("memset", G.Sst[:, :, :], 0.0), writes=[f"S{h}" for h in range(4)])
    P.add("sp", CALL("dma_start", out=Bp.stg[0][0][:, 0:512].rearrange("p (g e) -> p g e", g=4), in_=G.w_grp[l].rearrange("g c e -> c g e")), writes=["stg0"], dma=1, key="stg0")
    P.add("act", CALL("activation", out=G.wgrp_b[:, :, :], in_=Bp.stg[0][0][:, 0:512].rearrange("p (g e) -> p g e", g=4), func=AF.Copy), reads=["stg0"], writes=["wgrp_b"])
    _mem_kv(G, Bp, l)
    P.barrier()
    nst = G.n_st if hasattr(G, "n_st") else 8
    for st in range(nst):
        tiles = [(G.xp[:, st * 2 + j, :], 128, f"xp{st * 2 + j}") for j in range(2)]
        _mixer_st(G, Bp, l, st, tiles, False)
    P.barrier()
    b = P.ps()
    for c in range(4):
        P.add("pe", CALL("transpose", out=PS(b, 128, c * 128)[0:15, :], in_=Bp.uT[:, c, 0:15], identity=I_), reads=[f"uT{c}", "cst"], writes=[pk(b)])
    P.add("act", CALL("activation", out=Bp.rowbuf[0:15, 0:512], in_=PS(b, 512)[0:15, :], func=AF.Copy), reads=[pk(b)], writes=["rowbuf"])
    P.add("pool", CALL("dma_start", out=G.o_pool_p[l], in_=Bp.rowbuf[0:15, 0:512]), reads=["rowbuf"], dma=1, key="o_pool_p")
    for g in range(3):
        b = P.ps()
        for cc in range(4):
            c = g * 4 + cc
            P.add("pe", CALL("transpose", out=PS(b, 128, cc * 128)[0:3, :], in_=G.hist[:, c, :], identity=I_), reads=[f"hist{c}", "cst"], writes=[pk(b)])
        P.add("act", CALL("activation", out=Bp.rowbuf[0:3, g * 512:(g + 1) * 512], in_=PS(b, 512)[0:3, :], func=AF.Copy), reads=[pk(b)], writes=["rowbuf"])
    P.add("pool", CALL("dma_start", out=G.o_conv_p[l], in_=Bp.rowbuf[0:3, 0:1536]), reads=["rowbuf"], dma=1, key="o_conv_p")
    P.add("pool", CALL("dma_start", out=G.o_delta_p[l].rearrange("h k v -> k h v"), in_=G.Sst[:, :, :]), reads=[f"S{h}" for h in range(4)], dma=1, key="o_delta_p")
    P.barrier()
    AR.release(m0)
    if getattr(G, "skip_sample", False):
        return
    Bs = _mixer_bufs(G, 64, True)
    _sample_prep(G, Bs, l)
    _mixer_st(G, Bs, l, 0, [(G.xs[0:TS, :], TS, "xs")], True)
    P.barrier()
    for hf in range(2):
        b = P.ps()
        for c in range(4):
            P.add("dve", CALL("tensor_copy", out=Bs.pA[:, 0:120].rearrange("p (s e) -> p s e", e=15), in_=Bs.uTs[:, c, hf * 8:(hf + 1) * 8, 4:19]), reads=[f"uTs{c}"], writes=["pA"])
            P.add("pe", CALL("transpose", out=PS(b, 128, c * 128)[0:120, :], in_=Bs.pA[:, 0:120], identity=I_), reads=["pA", "cst"], writes=[pk(b)])
        P.add("act", CALL("activation", out=Bs.rowbuf[0:120, 0:512], in_=PS(b, 512)[0:120, :], func=AF.Copy), reads=[pk(b)], writes=["rowbuf"])
        P.add("pool", CALL("dma_start", out=G.o_pool_s[l, hf * 120:(hf + 1) * 120, :], in_=Bs.rowbuf[0:120, 0:512]), reads=["rowbuf"], dma=1, key="o_pool_s")
    for g in range(3):
        b = P.ps()
        for cc in range(4):
            c = g * 4 + cc
            P.add("pe", CALL("transpose", out=PS(b, 128, cc * 128)[0:48, :], in_=Bs.cvout[:, c, :, :].rearrange("p s e -> p (s e)"), identity=I_), reads=[f"cvout{g}", "cst"], writes=[pk(b)])
        P.add("act", CALL("activation", out=Bs.ld1536[0:48, g * 512:(g + 1) * 512], in_=PS(b, 512)[0:48, :], func=AF.Copy), reads=[pk(b)], writes=["ld1536"])
    P.add("pool", CALL("dma_start", out=G.o_conv_s[l], in_=Bs.ld1536[0:48, :]), reads=["ld1536"], dma=1, key="o_conv_s")
    P.barrier()
    AR.release(m0)


def convert_tables(G):
    P, AR = G.P, G.AR
    if getattr(G, "skip_convert", False):
        return
    m0 = AR.mark()
    NB = 3
    R = 4
    stg = [AR.alloc(R * 1024) for _ in range(NB)]
    bfb = [AR.alloc(R * 1024, BF16) for _ in range(NB)]
    n = 0
    for src, dst in ((G.p_u, G.uv_bf[:, 0:D]), (G.p_v, G.uv_bf[:, D:2 * D])):
        sv = src.rearrange("(c j p) d -> c p j d", p=128, j=R)
        dv = dst.rearrange("(c j p) d -> c p j d", p=128, j=R)
        nchunk = (DEPTH * NEXP) // (128 * R)
        if hasattr(G, "conv_chunks"):
            nchunk = G.conv_chunks
        for c in range(nchunk):
            i = n % NB
            n += 1
            s3 = stg[i].rearrange("p (j d) -> p j d", j=R)
            b3 = bfb[i].rearrange("p (j d) -> p j d", j=R)
            P.add("sp", CALL("dma_start", out=s3, in_=sv[c]), writes=[f"cstg{i}"], dma=1, key=f"cstg{i}")
            if n % 2 == 0:
                P.add("act", CALL("activation", out=bfb[i][:, :], in_=stg[i][:, :], func=AF.Copy), reads=[f"cstg{i}"], writes=[f"cbf{i}"])
            else:
                P.add("dve", CALL("tensor_copy", out=bfb[i][:, :], in_=stg[i][:, :]), reads=[f"cstg{i}"], writes=[f"cbf{i}"])
            P.add("pool", CALL("dma_start", out=dv[c], in_=b3), reads=[f"cbf{i}"], dma=1, key=f"cbf{i}")
    P.barrier()
    AR.release(m0)


def peer_phase(G, l):
    P, AR, PS, PSB, pk = G.P, G.AR, G.PS, G.PSB, G.pk
    P.barrier()
    P.ps_lo = 2
    m0 = AR.mark()
    NS = 10
    csblk = AR.alloc(NS * 1024)
    csb = csblk.bitcast(BF16)
    cs = [csb[:, i * 2048:(i + 1) * 2048] for i in range(NS)]
    wq = AR.alloc(8 * 2048, BF16).rearrange("p (k n) -> p k n", k=8)
    skT = AR.alloc(16 * 128, BF16).rearrange("p (j n) -> p j n", j=16)
    skb = csb[:, 6 * 2048:7 * 2048].rearrange("p (j n) -> p j n", j=16)
    hnf = AR.alloc(1024)
    hnb2 = [AR.alloc(1024, BF16) for _ in range(2)]
    hnT = AR.alloc(8 * 128, BF16).rearrange("p (k t) -> p k t", k=8)
    qT = AR.alloc(16 * 128, BF16).rearrange("p (j t) -> p j t", j=16)
    s1 = AR.alloc(2048)
    s2 = AR.alloc(2048)
    oh = s2
    s2keys = [f"s2_{j}" for j in range(16)]
    top = AR.alloc(256).rearrange("p (j a) -> p j a", j=16)
    topi = AR.alloc(256, U32).rearrange("p (j a) -> p j a", j=16)
    topif = AR.alloc(256).rearrange("p (h t a) -> p h t a", h=8, t=2)
    best = AR.alloc(128).rearrange("p (h k) -> p h k", h=8)
    pos = AR.alloc(128, U32).rearrange("p (h k) -> p h k", h=8)
    pint = AR.alloc(128, U32).rearrange("p (h k) -> p h k", h=8)
    paf = AR.alloc(128).rearrange("p (h k) -> p h k", h=8)
    pbf = AR.alloc(128).rearrange("p (h k) -> p h k", h=8)
    I1 = AR.alloc(128).rearrange("p (h k) -> p h k", h=8)
    I2 = AR.alloc(128).rearrange("p (h k) -> p h k", h=8)
    idxf = AR.alloc(128)
    idx2 = [AR.alloc(128, I32) for _ in range(2)]
    gate2 = [AR.alloc(128).rearrange("p (h k) -> p h k", h=8) for _ in range(2)]
    gsum = AR.alloc(8)
    av = AR.alloc(128)
    tg = AR.alloc(128)
    ag = AR.alloc(128)
    sg = AR.alloc(128)
    wgt = AR.alloc(128)
    Dg = [AR.alloc(2 * 128, BF16).rearrange("p (j m) -> p j m", j=2) for _ in range(4)]
    jb = AR.alloc(1024, BF16)
    ss = AR.alloc(8)

    stgA = (csblk[:, 0:2048], ["cs0", "cs1"])
    stgB = (csblk[:, 2048:4096], ["cs2", "cs3"])
    for g in range(8):
        sv, skeys = (stgA, stgB)[g % 2]
        svv = sv.rearrange("p (k n) -> p k n", k=8)
        P.add("sp", CALL("dma_start", out=svv, in_=G.w_pq[l].rearrange("(k p) n -> p k n", p=128)[:, :, g * 256:(g + 1) * 256]), writes=skeys, dma=1, key="pq" + skeys[0])
        if g % 2 == 0:
            P.add("act", CALL("activation", out=wq[:, :, g * 256:(g + 1) * 256], in_=svv, func=AF.Copy), reads=skeys, writes=[f"wq{g}"])
        else:
            P.add("dve", CALL("tensor_copy", out=wq[:, :, g * 256:(g + 1) * 256], in_=svv), reads=skeys, writes=[f"wq{g}"])
    wqkeys = [f"wq{g}" for g in range(8)]
    sks = csblk[:, 4096:6144].rearrange("p (j c) -> p j c", j=16)
    P.add("sp", CALL("dma_start", out=sks, in_=G.subk[l].rearrange("j k c -> k j c")), writes=["cs4", "cs5"], dma=1, key="sks")
    P.add("act", CALL("activation", out=skb, in_=sks, func=AF.Copy), reads=["cs4", "cs5"], writes=["cs6"])
    for hf in range(2):
        b = P.ps()
        for jj in range(8):
            j = hf * 8 + jj
            P.add("pe", CALL("transpose", out=PSB(b, 128, jj * 128), in_=skb[:, j, :], identity=G.identB), reads=["cs6", "identB"], writes=[pk(b)])
        P.add("act", CALL("activation", out=skT[:, hf * 8:(hf + 1) * 8, :].rearrange("p j n -> p (j n)"), in_=PSB(b, 1024), func=AF.Copy), reads=[pk(b)], writes=[f"skT{hf}"])

    tiles = [(G.xp[:, t, :], 128, f"xp{t}") for t in range(16)] + [(G.xs[0:TS, :], TS, "xs")]
    if hasattr(G, "peer_tiles"):
        tiles = [tiles[i] for i in G.peer_tiles]
    gst = {"gi": 0, "d": 0}
    nsl = getattr(G, "peer_slots", 128)

    def part_topk(x_ap, np_, xkey, par):
        r_ = slice(0, np_)
        idx = idx2[par]
        ik = f"idx{par}"
        hnb = hnb2[par]
        hk = f"hnb{par}"
        gate = gate2[par]
        gk = f"gate{par}"
        G.rms_rstd(x_ap, np_, xkey, jb, "jb", ss[:, 0:1], "pss")
        P.add("dve", CALL("scalar_tensor_tensor", out=hnf[r_, :], in0=x_ap, scalar=ss[r_, 0:1], in1=G.gb_ffn[r_, :], op0=ALU.mult, op1=ALU.mult),
              reads=[xkey, "pss", "gb_ffn"], writes=["hnf"])
        P.add("act", CALL("activation", out=hnb[r_, :], in_=hnf[r_, :], func=AF.Copy), reads=["hnf"], writes=[hk])
        G.to_featmajor(hnb, np_, hk, hnT, "hnT", slice(0, np_))
        for j in range(16):
            b = P.ps()
            for k in range(8):
                P.add("pe", CALL("matmul", PS(b, np_), lhsT=wq[:, k, j * 128:(j + 1) * 128], rhs=hnT[:, k, r_], start=(k == 0), stop=(k == 7)),
                      reads=["hnT", wqkeys[j // 2]], writes=[pk(b)])
            if j % 2 == 0:
                P.add("act", CALL("activation", out=qT[:, j, r_], in_=PS(b, np_), func=AF.Copy), reads=[pk(b)], writes=[f"qT{j}"])
            else:
                P.add("dve", CALL("tensor_copy", out=qT[:, j, r_], in_=PS(b, np_)), reads=[pk(b)], writes=[f"qT{j}"])
        for q in range(4):
            b = P.ps()
            for jj in range(4):
                j = q * 4 + jj
                P.add("pe", CALL("matmul", PS(b, 128, jj * 128)[r_, :], lhsT=qT[:, j, r_], rhs=skT[:, j, :], start=True, stop=True),
                      reads=[f"qT{j}", f"skT{j // 8}"], writes=[pk(b)])
            P.add("act", CALL("activation", out=s1[r_, q * 512:(q + 1) * 512], in_=PS(b, 512)[r_, :], func=AF.Copy), reads=[pk(b)], writes=[f"s1_{q}"])
        s1v = s1[:, :].rearrange("p (j n) -> p j n", j=16)
        s2v = s2[:, :].rearrange("p (j n) -> p j n", j=16)
        for j in range(16):
            sk_ = f"s1_{j // 4}"
            P.add("dve", CALL("max", out=top[r_, j, 0:8], in_=s1v[r_, j, :]), reads=[sk_], writes=[f"top{j}a"])
            P.add("dve", CALL("max_index", out=topi[r_, j, 0:8], in_max=top[r_, j, 0:8], in_values=s1v[r_, j, :]), reads=[sk_, f"top{j}a"], writes=[f"topi{j}a"])
            P.add("dve", CALL("match_replace", out=s2v[r_, j, :], in_to_replace=top[r_, j, 0:8], in_values=s1v[r_, j, :], imm_value=NEG), reads=[sk_, f"top{j}a"], writes=[f"s2_{j}"])
            P.add("dve", CALL("max", out=top[r_, j, 8:16], in_=s2v[r_, j, :]), reads=[f"s2_{j}"], writes=[f"top{j}b"])
            P.add("dve", CALL("max_index", out=topi[r_, j, 8:16], in_max=top[r_, j, 8:16], in_values=s2v[r_, j, :]), reads=[f"s2_{j}", f"top{j}b"], writes=[f"topi{j}b"])
        allt = [f"top{j}{x}" for j in range(16) for x in "ab"]
        alli = [f"topi{j}{x}" for j in range(16) for x in "ab"]
        P.add("dve", CALL("tensor_copy", out=topif[r_, :, :, :].rearrange("p h t a -> p (h t a)"), in_=topi[r_, :, :].rearrange("p j a -> p (j a)")), reads=alli, writes=["topif"])
        topv = top[:, :, :].rearrange("p (h t) a -> p h t a", t=2)
        cand = s1[:, :].rearrange("p (h a b) -> p h a b", h=8, a=16)
        cand2 = s2[:, :].rearrange("p (h n) -> p h n", h=8)
        candf = s1[:, :].rearrange("p (h n) -> p h n", h=8)
        for h in range(8):
            P.add("dve", CALL("tensor_tensor", out=cand[r_, h, :, :], in0=topv[r_, h, 0, :].unsqueeze(2).to_broadcast([np_, 16, 16]),
                                                        in1=topv[r_, h, 1, :].unsqueeze(1).to_broadcast([np_, 16, 16]), op=ALU.add),
                  reads=allt, writes=[f"s1_{h // 2}"])
            P.add("dve", CALL("max", out=best[r_, h, 0:8], in_=candf[r_, h, :]), reads=[f"s1_{h // 2}"], writes=[f"best{h}a"])
            P.add("dve", CALL("max_index", out=pos[r_, h, 0:8], in_max=best[r_, h, 0:8], in_values=candf[r_, h, :]), reads=[f"s1_{h // 2}", f"best{h}a"], writes=[f"pos{h}a"])
            P.add("dve", CALL("match_replace", out=cand2[r_, h, :], in_to_replace=best[r_, h, 0:8], in_values=candf[r_, h, :], imm_value=NEG),
                  reads=[f"s1_{h // 2}", f"best{h}a"], writes=[f"s2_{2 * h}", f"s2_{2 * h + 1}"])
            P.add("dve", CALL("max", out=best[r_, h, 8:16], in_=cand2[r_, h, :]), reads=[f"s2_{2 * h}", f"s2_{2 * h + 1}"], writes=[f"best{h}b"])
            P.add("dve", CALL("max_index", out=pos[r_, h, 8:16], in_max=best[r_, h, 8:16], in_values=cand2[r_, h, :]), reads=[f"s2_{2 * h}", f"s2_{2 * h + 1}", f"best{h}b"], writes=[f"pos{h}b"])
        allb = [f"best{h}{x}" for h in range(8) for x in "ab"]
        allp = [f"pos{h}{x}" for h in range(8) for x in "ab"]
        P.add("dve", CALL("tensor_single_scalar", out=pint[r_, :, :], in_=pos[r_, :, :], scalar=4, op=ALU.logical_shift_right), reads=allp, writes=["pint"])
        P.add("dve", CALL("tensor_copy", out=paf[r_, :, :], in_=pint[r_, :, :]), reads=["pint"], writes=["paf"])
        P.add("dve", CALL("tensor_single_scalar", out=pint[r_, :, :], in_=pos[r_, :, :], scalar=15, op=ALU.bitwise_and), reads=allp + ["paf"], writes=["pint"])
        P.add("dve", CALL("tensor_copy", out=pbf[r_, :, :], in_=pint[r_, :, :]), reads=["pint"], writes=["pbf"])
        ohv = oh[:, :].rearrange("p (h k a) -> p h k a", h=8, k=16)
        for (pf, pfk, tsel, Iout, Ik) in ((paf, "paf", 0, I1, "I1"), (pbf, "pbf", 1, I2, "I2")):
            for h in range(8):
                P.add("dve", CALL("tensor_tensor", out=ohv[r_, h, :, :], in0=pf[r_, h, :].unsqueeze(2).to_broadcast([np_, 16, 16]),
                                                                   in1=G.iota16[r_, :].unsqueeze(1).to_broadcast([np_, 16, 16]), op=ALU.is_equal),
                      reads=[pfk, "cst"], writes=[f"s2_{2 * h}", f"s2_{2 * h + 1}"])
                P.add("dve", CALL("tensor_tensor", out=ohv[r_, h, :, :], in0=ohv[r_, h, :, :], in1=topif[r_, h, tsel, :].unsqueeze(1).to_broadcast([np_, 16, 16]), op=ALU.mult),
                      reads=["topif"], writes=[f"s2_{2 * h}", f"s2_{2 * h + 1}"])
            P.add("dve", CALL("tensor_reduce", out=Iout[r_, :, :], in_=ohv[r_, :, :, :], axis=AX.X, op=ALU.add), reads=s2keys, writes=[Ik])
        P.add("dve", CALL("scalar_tensor_tensor", out=idxf[r_, :], in0=I1[r_, :, :].rearrange("p h k -> p (h k)"), scalar=128.0, in1=I2[r_, :, :].rearrange("p h k -> p (h k)"), op0=ALU.mult, op1=ALU.add),
              reads=["I1", "I2"], writes=["idxf"])
        if l > 0:
            P.add("dve", CALL("tensor_scalar", out=idxf[r_, :], in0=idxf[r_, :], scalar1=float(l * NEXP), scalar2=0.0, op0=ALU.add, op1=ALU.add), reads=["idxf"], writes=["idxf"])
        P.add("dve", CALL("tensor_copy", out=idx[r_, :], in_=idxf[r_, :]), reads=["idxf"], writes=[ik])
        P.add("dve", CALL("tensor_tensor", out=gate[r_, :, :], in0=best[r_, :, :], in1=best[r_, :, 0:1].to_broadcast([np_, 8, 16]), op=ALU.subtract), reads=allb, writes=[gk])
        P.add("act", CALL("activation", out=gate[r_, :, :], in_=gate[r_, :, :], func=AF.Exp), reads=[gk], writes=[gk])
        P.add("dve", CALL("tensor_reduce", out=gsum[r_, 0:8], in_=gate[r_, :, :], axis=AX.X, op=ALU.add), reads=[gk], writes=["gsum"])
        P.add("dve", CALL("reciprocal", out=gsum[r_, 0:8], in_=gsum[r_, 0:8]), reads=["gsum"], writes=["gsum"])
        P.add("dve", CALL("tensor_tensor", out=gate[r_, :, :], in0=gate[r_, :, :], in1=gsum[r_, 0:8].unsqueeze(2).to_broadcast([np_, 8, 16]), op=ALU.mult), reads=[gk, "gsum"], writes=[gk])

    def part_pipe(x_ap, np_, xkey, par):
        r_ = slice(0, np_)
        idx = idx2[par]
        ik = f"idx{par}"
        hnb = hnb2[par]
        hk = f"hnb{par}"
        gatef = gate2[par][:, :, :].rearrange("p h k -> p (h k)")
        gk = f"gate{par}"
        ngrp = nsl // 2
        slot_of = {}

        def stage3(g):
            c2 = slice(2 * g, 2 * g + 2)
            P.add("dve", CALL("tensor_tensor", out=wgt[r_, c2], in0=sg[r_, c2], in1=gatef[r_, c2], op=ALU.mult), reads=[f"sg{g}", gk], writes=[f"wgt{g}"])
            d_i = gst["d"] % 4
            gst["d"] += 1
            D_ = Dg[d_i]
            dk = f"Dg{d_i}"
            P.add("dve", CALL("tensor_tensor", out=D_[r_, :, r_], in0=G.identF[r_, r_].unsqueeze(1).to_broadcast([np_, 2, np_]),
                              in1=wgt[r_, c2].unsqueeze(2).to_broadcast([np_, 2, np_]), op=ALU.mult), reads=[f"wgt{g}", "cst"], writes=[dk])
            for j in range(2):
                sl = 2 * g + j
                i = slot_of[sl]
                for hf in range(2):
                    P.add("pe", CALL("matmul", PS(hf, 512)[r_, :], lhsT=D_[r_, j, r_], rhs=cs[i][r_, 1024 + hf * 512:1024 + (hf + 1) * 512], start=(sl == 0), stop=(sl == nsl - 1)),
                          reads=[dk, f"cs{i}"], writes=[pk(hf)])

        for g in range(ngrp + 1):
            if g < ngrp:
                c2 = slice(2 * g, 2 * g + 2)
                for j in range(2):
                    sl = 2 * g + j
                    i = gst["gi"] % NS
                    gst["gi"] += 1
                    slot_of[sl] = i
                    P.add("pool", CALL("indirect_dma_start", out=cs[i][r_, :], out_offset=None, in_=G.uv_bf, in_offset=bass.IndirectOffsetOnAxis(ap=idx[r_, sl:sl + 1], axis=0)),
                          reads=[ik], writes=[f"cs{i}"], dma=1, key=f"cs{i}")
                    P.add("dve", CALL("scalar_tensor_tensor", out=cs[i][r_, 0:1024], in0=cs[i][r_, 0:1024], scalar=1.0, in1=hnb[r_, :], op0=ALU.mult, op1=ALU.mult, accum_out=av[r_, sl:sl + 1]),
                          reads=[hk], writes=[f"cs{i}", f"av{sl}"])
                avk = [f"av{2 * g}", f"av{2 * g + 1}"]
                P.add("act", CALL("activation", out=sg[r_, c2], in_=av[r_, c2], func=AF.Gelu_apprx_tanh), reads=avk, writes=[f"sg{g}"])
            if g >= 1:
                stage3(g - 1)
        for hf in range(2):
            xo = x_ap[:, hf * 512:(hf + 1) * 512]
            P.add("dve", CALL("tensor_tensor", out=xo, in0=xo, in1=PS(hf, 512)[r_, :], op=ALU.add), reads=[pk(hf), xkey], writes=[xkey])

    nt_ = len(tiles)
    part_topk(*tiles[0], 0)
    for t in range(nt_):
        P.begin_record((0, 1))
        part_pipe(*tiles[t], t % 2)
        ra = P.end_record()
        rb = []
        if t + 1 < nt_:
            P.begin_record((2, 3, 4, 5, 6, 7))
            part_topk(*tiles[t + 1], (t + 1) % 2)
            rb = P.end_record()
        P.replay([ra, rb])
    P.barrier()
    P.ps_lo = 0
    AR.release(m0)


def final_phase(G):
    P, AR = G.P, G.AR
    m0 = AR.mark()
    ob = [AR.alloc(1024) for _ in range(2)]
    jb = AR.alloc(1024, BF16)
    ss = AR.alloc(8)
    gb_fin = AR.alloc(1024)
    P.add("sp", CALL("dma_start", out=gb_fin[:, :], in_=G.g_fin.partition_broadcast(128)), writes=["gb_fin"], dma=1, key="gb_fin")
    tiles = [(G.xp[:, t, :], 128, f"xp{t}", G.y_p[t * 128:(t + 1) * 128, :]) for t in range(16)] + [(G.xs[0:TS, :], TS, "xs", G.y_s)]
    for n, (x_ap, np_, xkey, o_ap) in enumerate(tiles):
        r_ = slice(0, np_)
        o = ob[n % 2]
        G.rms_rstd(x_ap, np_, xkey, jb, "jb", ss[:, 0:1], "fss")
        P.add("dve", CALL("scalar_tensor_tensor", out=o[r_, :], in0=x_ap, scalar=ss[r_, 0:1], in1=gb_fin[r_, :], op0=ALU.mult, op1=ALU.mult),
              reads=[xkey, "fss", "gb_fin"], writes=[f"ob{n % 2}"])
        P.add("sp", CALL("dma_start", out=o_ap, in_=o[r_, :]), reads=[f"ob{n % 2}"], dma=1, key=f"ob{n % 2}")
    AR.release(m0)


def build(dbg=(), stop=None, **opts):
    build.opts = opts
    nc = bass.Bass("TRN2", target_bir_lowering=False)
    es = ExitStack()
    with es:
        _build(nc, es, dbg, stop)
    return nc


def _build(nc, es, dbg, stop):
    def din(name, shape, dt=F32):
        return nc.dram_tensor(name, list(shape), dt, kind="ExternalInput").ap()

    def dout(name, shape, dt=F32):
        return nc.dram_tensor(name, list(shape), dt, kind="ExternalOutput").ap()

    x_p = din("x_p", [T, D])
    x_s = din("x_s", [TS, D])
    st_pool = din("st_pool", [DEPTH, NSQ * 15, BW])
    st_conv = din("st_conv", [DEPTH, NSQ * 3, 3 * BW])
    st_delta = din("st_delta", [DEPTH, NSQ, 4, 128, 128])
    c_k = din("c_k", [DEPTH, NSQ, 256, BW])
    c_v = din("c_v", [DEPTH, NSQ, 256, BW])
    memp = din("memp", [256, D])
    g_mix = din("g_mix", [DEPTH, D])
    w_in = din("w_in", [DEPTH, D, IN_COLS])
    w_conv = din("w_conv", [DEPTH, 4, 3 * BW])
    a_log = din("a_log", [DEPTH, 4])
    dt_bias = din("dt_bias", [DEPTH, 4])
    g_dn = din("g_dn", [DEPTH, 128])
    w_grp = din("w_grp", [DEPTH, 4, 128, 128])
    p_scale = din("p_scale", [DEPTH, BW])
    g_mem = din("g_mem", [DEPTH, D])
    w_mkv = din("w_mkv", [DEPTH, D, 2 * BW])
    w_br = din("w_br", [DEPTH, 3, BW, D])
    w_o = din("w_o", [DEPTH, D, D])
    g_ffn = din("g_ffn", [DEPTH, D])
    w_pq = din("w_pq", [DEPTH, D, 2048])
    subk = din("subk", [DEPTH, 16, 128, 128])
    p_u = din("p_u", [DEPTH * NEXP, D])
    p_v = din("p_v", [DEPTH * NEXP, D])
    g_fin = din("g_fin", [D])
    consts = din("consts", [128, 1024])

    y_p = dout("y_p", [T, D])
    y_s = dout("y_s", [TS, D])
    o_pool_p = dout("o_pool_p", [DEPTH, 15, BW])
    o_conv_p = dout("o_conv_p", [DEPTH, 3, 3 * BW])
    o_delta_p = dout("o_delta_p", [DEPTH, 4, 128, 128])
    o_mk = dout("o_mk", [DEPTH, 256, BW])
    o_mv = dout("o_mv", [DEPTH, 256, BW])
    o_pool_s = dout("o_pool_s", [DEPTH, NSQ * 15, BW])
    o_conv_s = dout("o_conv_s", [DEPTH, NSQ * 3, 3 * BW])
    o_delta_s = dout("o_delta_s", [DEPTH, NSQ, 4, 128, 128])

    uv_bf = nc.dram_tensor("uv_bf", [DEPTH * NEXP, 2 * D], BF16, kind="Internal").ap()

    P = Prog(nc)
    dbg_outs = {}

    def sb(name, shape, dt=F32):
        return es.enter_context(nc.sbuf_tensor(name, shape, dt))

    xp = sb("xp", [128, 16, D])
    xs = sb("xs", [128, D])
    cst = sb("cst", [128, 8, 128])
    identB_t = sb("identB", [128, 128], BF16)
    identB = identB_t[:, :]
    gmix_sb = sb("gmix_sb", [128, 8])
    gmem_sb = sb("gmem_sb", [128, 8])
    gb_ffn = sb("gb_ffn", [128, D])
    wconv_sb = sb("wconv_sb", [128, 12, 4])
    alog_b = sb("alog_b", [128, 4])
    dtb_b = sb("dtb_b", [128, 4])
    gdn_sb = sb("gdn_sb", [128, 1])
    psc_sb = sb("psc_sb", [128, 4])
    wgrp_b = sb("wgrp_b", [128, 4, 128], BF16)
    Sst = sb("Sst", [128, 4, 128])
    hist = sb("hist", [128, 12, 3])
    epsb = sb("epsb", [128, 1])
    ARENA_N = 32000
    arena_t = sb("arena", [128, ARENA_N])
    AR = Arena(arena_t, ARENA_N)
    psum = es.enter_context(nc.psum_tensor("psum", [128, 4096], F32))

    identF = cst[:, 0, :]
    onesF = cst[:, 1, :]
    Ltri = cst[:, 2, :]
    SLm = cst[:, 3, :]
    Ltri4 = cst[:, 4, :]
    SL4 = cst[:, 5, :]
    seqmask = cst[:, 6, 0:16]
    rcnt = cst[:, 7, 0:64]
    iota16 = cst[:, 7, 64:80]

    def PS(i, n=512, off=0):
        return psum[:, i * 512 + off:i * 512 + off + n]

    def PSB(i, n=1024, off=0):
        return psum[:, i * 512:(i + 1) * 512].bitcast(BF16)[:, off:off + n]

    def pk(i):
        return f"ps{i}"

    def dump(name, ap, reads, shape, view=None, **kw):
        if name not in dbg:
            return
        o = dout("dbg_" + name, shape, ap.dtype)
        dbg_outs[name] = o
        if view:
            o = o.rearrange(view, **kw)
        P.add("sp", CALL("dma_start", out=o, in_=ap), reads=reads, dma=1, key="dbg_" + name)

    P.add("sp", CALL("dma_start", out=cst[:].rearrange("p a b -> p (a b)"), in_=consts), writes=["cst"], dma=1, key="cst")
    P.add("pool", CALL("memset", epsb[:], EPS), writes=["epsb"])
    P.add("act", CALL("activation", out=identB, in_=identF, func=AF.Copy), reads=["cst"], writes=["identB"])
    for i in range(16):
        P.add("sp", CALL("dma_start", out=xp[:, i, :], in_=x_p[i * 128:(i + 1) * 128, :]), writes=[f"xp{i}"], dma=1, key=f"xp{i}")
    P.add("sp", CALL("dma_start", out=xs[0:TS, :], in_=x_s), writes=["xs"], dma=1, key="xs")

    def rms_rstd(x_ap, np_, xkey, junk, junk_key, ss, sskey):
        P.add("act", CALL("activation", out=junk[0:np_, :], in_=x_ap, func=AF.Square, accum_out=ss[0:np_, :]),
              reads=[xkey], writes=[junk_key, sskey])
        P.add("act", CALL("activation", out=ss[0:np_, :], in_=ss[0:np_, :], func=AF.Sqrt, scale=1.0 / D, bias=epsb[0:np_, :]),
              reads=[sskey, "epsb"], writes=[sskey])
        P.add("dve", CALL("reciprocal", out=ss[0:np_, :], in_=ss[0:np_, :]), reads=[sskey], writes=[sskey])

    def to_featmajor(src_bf, np_, srckey, dst, dstkey_fn, tsl, gsb=None, gkey=None):
        b = P.ps()
        for k in range(8):
            P.add("pe", CALL("transpose", out=PSB(b)[:, k * 128:k * 128 + np_], in_=src_bf[0:np_, k * 128:(k + 1) * 128], identity=identB[0:np_, 0:np_]),
                  reads=[srckey, "identB"], writes=[pk(b)])
        src = PSB(b).rearrange("p (k t) -> p k t", k=8)[:, :, 0:np_]
        if gsb is None:
            P.add("act", CALL("activation", out=dst[:, :, tsl], in_=src, func=AF.Copy), reads=[pk(b)], writes=[dstkey_fn])
        else:
            P.add("dve", CALL("tensor_tensor", out=dst[:, :, tsl], in0=src, in1=gsb[:, :].unsqueeze(2).to_broadcast([128, 8, np_]), op=ALU.mult),
                  reads=[pk(b), gkey], writes=[dstkey_fn])

    wst = {"i": 0, "s": 0}

    def load_w(dram2d, krows, c0, ncols, stg, wbf, caster=None):
        i = wst["i"] % len(wbf)
        wst["i"] += 1
        si = wst["s"] % len(stg)
        wst["s"] += 1
        s_ap, s_key = stg[si]
        b_ap, b_key = wbf[i]
        sv = s_ap[:, 0:krows * ncols].rearrange("p (k n) -> p k n", k=krows)
        bv = b_ap[:, 0:krows * ncols].rearrange("p (k n) -> p k n", k=krows)
        src = dram2d.rearrange("(k p) n -> p k n", p=128)[:, :, c0:c0 + ncols]
        P.add("sp", CALL("dma_start", out=sv, in_=src), writes=[s_key], dma=1, key=s_key)
        eng = caster or ("act" if (wst["i"] % 2 == 0) else "dve")
        if eng == "act":
            P.add("act", CALL("activation", out=bv, in_=sv, func=AF.Copy), reads=[s_key], writes=[b_key])
        else:
            P.add(eng, CALL("tensor_copy", out=bv, in_=sv), reads=[s_key], writes=[b_key])
        return bv, b_key

    def layer_params(l):
        P.add("sp", CALL("dma_start", out=gmix_sb[:], in_=g_mix[l].rearrange("(k p) -> p k", p=128), allow_slow_non_contiguous=True), writes=["gmix"], dma=1, key="gmix")
        P.add("sp", CALL("dma_start", out=gmem_sb[:], in_=g_mem[l].rearrange("(k p) -> p k", p=128), allow_slow_non_contiguous=True), writes=["gmem"], dma=1, key="gmem")
        P.add("sp", lambda e: [e.dma_start(out=wconv_sb[:, :, j], in_=w_conv[l, j].rearrange("(c p) -> p c", p=128), allow_slow_non_contiguous=True) for j in range(4)],
              writes=["wconv"], dma=4, key="wconv")
        P.add("sp", CALL("dma_start", out=gdn_sb[:], in_=g_dn[l].rearrange("(p o) -> p o", o=1), allow_slow_non_contiguous=True), writes=["gdn"], dma=1, key="gdn")
        P.add("sp", CALL("dma_start", out=psc_sb[:], in_=p_scale[l].rearrange("(g p) -> p g", p=128), allow_slow_non_contiguous=True), writes=["psc"], dma=1, key="psc")
        P.add("sp", CALL("dma_start", out=gb_ffn[:], in_=g_ffn[l].partition_broadcast(128)), writes=["gb_ffn"], dma=1, key="gb_ffn")
        P.add("sp", CALL("dma_start", out=alog_b[:], in_=a_log[l].partition_broadcast(128)), writes=["alog"], dma=1, key="alog")
        P.add("sp", CALL("dma_start", out=dtb_b[:], in_=dt_bias[l].partition_broadcast(128)), writes=["dtb"], dma=1, key="dtb")
        P.add("act", CALL("activation", out=alog_b[:], in_=alog_b[:], func=AF.Exp), reads=["alog"], writes=["alog"])
        P.add("dve", CALL("tensor_scalar", out=alog_b[:], in0=alog_b[:], scalar1=-1.0, scalar2=0.0, op0=ALU.mult, op1=ALU.add), reads=["alog"], writes=["alog"])

    G = type("G", (), {})()
    for k_, v_ in list(locals().items()):
        setattr(G, k_, v_)
    for k_, v_ in build.opts.items():
        setattr(G, k_, v_)

    convert_tables(G)
    for l in range(DEPTH):
        layer_params(l)
        mixer_phase(G, l)
        dump(f"xp_m{l}", xp[:, :, :], [f"xp{i}" for i in range(16)], [T, D], "(t p) d -> p t d", p=128)
        dump(f"xs_m{l}", xs[0:TS, :], ["xs"], [TS, D])
        if stop == ("mixer", l):
            break
        peer_phase(G, l)
        dump(f"xp_p{l}", xp[:, :, :], [f"xp{i}" for i in range(16)], [T, D], "(t p) d -> p t d", p=128)
        dump(f"xs_p{l}", xs[0:TS, :], ["xs"], [TS, D])
        if stop == ("peer", l):
            break
    else:
        final_phase(G)
    P.emit(es, maxops=build.opts.get('maxops'))
    G.P = P
    build.last = G


def _shard_inputs(inp):
    f = lambda a: np.ascontiguousarray(np.asarray(a, dtype=np.float32))
    shared = dict(
        g_mix=f(inp["g_mix"]), w_in=f(inp["w_in"]), w_conv=f(inp["w_conv"]), a_log=f(inp["a_log"]), dt_bias=f(inp["dt_bias"]),
        g_dn=f(inp["g_dn_out"]), w_grp=f(inp["w_pool_grp"]), p_scale=f(inp["pool_scale"]), g_mem=f(inp["g_mem"]),
        w_mkv=f(inp["w_mem_kv"]), w_br=f(inp["w_branch"]), w_o=f(inp["w_o"]), g_ffn=f(inp["g_ffn"]), w_pq=f(inp["w_peer_q"]),
        subk=f(inp["peer_subkeys"]).reshape(DEPTH, 16, 128, 128), p_u=f(inp["peer_u"]).reshape(DEPTH * NEXP, D),
        p_v=f(inp["peer_v"]).reshape(DEPTH * NEXP, D), g_fin=f(inp["g_final"]), consts=make_consts())
    maps = []
    for c in range(NCORES):
        sl = slice(c * NSQ, (c + 1) * NSQ)
        m = dict(shared)
        m["x_p"] = f(inp["x_prompt"][c])
        m["x_s"] = f(inp["x_sample"][sl]).reshape(TS, D)
        m["st_pool"] = f(inp["state_pool"][:, sl]).reshape(DEPTH, NSQ * 15, BW)
        m["st_conv"] = f(inp["state_conv"][:, sl]).reshape(DEPTH, NSQ * 3, 3 * BW)
        m["st_delta"] = f(inp["state_delta"][:, sl])
        m["c_k"] = f(inp["cache_mem_k"][:, sl]).reshape(DEPTH, NSQ, 256, BW)
        m["c_v"] = f(inp["cache_mem_v"][:, sl]).reshape(DEPTH, NSQ, 256, BW)
        m["memp"] = f(inp["mem_prompt"][c])
        maps.append(m)
    return maps


def _gather_outputs(res):
    R = res.results
    cat = lambda k, ax: np.concatenate([np.asarray(r[k]) for r in R], axis=ax)
    stk = lambda k, ax: np.stack([np.asarray(r[k]) for r in R], axis=ax)
    y_p = stk("y_p", 0)
    y_s = cat("y_s", 0).reshape(NCORES * NSQ, 4, D)
    pool_p = stk("o_pool_p", 1)
    conv_p = stk("o_conv_p", 1)
    delta_p = stk("o_delta_p", 1)
    mk = stk("o_mk", 1).reshape(DEPTH, NCORES, 256, 4, 128)
    mv = stk("o_mv", 1).reshape(DEPTH, NCORES, 256, 4, 128)
    pool_s = cat("o_pool_s", 1).reshape(DEPTH, NCORES * NSQ, 15, BW)
    conv_s = cat("o_conv_s", 1).reshape(DEPTH, NCORES * NSQ, 3, 3 * BW)
    delta_s = cat("o_delta_s", 1)
    outs = (y_p, y_s, pool_p, conv_p, delta_p, mk, mv, pool_s, conv_s, delta_s)
    return tuple(np.ascontiguousarray(o, dtype=np.float32) for o in outs)


def kernel(**inputs):
    maps = _shard_inputs(inputs)
    nc = build()
    res = run_bass_kernel_spmd(nc, maps, core_ids=list(range(NCORES)))
    return _gather_outputs(res)
```

```python
import numpy as np
from contextlib import ExitStack
import concourse.bass as bass
import concourse.mybir as mybir
from concourse.bass_utils import run_bass_kernel_spmd

F32 = mybir.dt.float32
BF16 = mybir.dt.bfloat16
I32 = mybir.dt.int32
U32 = mybir.dt.uint32
ALU = mybir.AluOpType
AF = mybir.ActivationFunctionType
AX = mybir.AxisListType

NCORES = 8
D = 1024
T = 2048
NSQ = 16
TS = 64
DEPTH = 2
BW = 512
IN_COLS = 6152
OFF_Q = 512
OFF_Z = 2048
OFF_BA = 2560
OFF_XQ = 2568
OFF_GATE = 3080
EPS = 1e-6
NKEY = 128
NEXP = 16384
SEM_CH = 30000
NEG = -1.0e30
NG = 6


def CALL(name, *a, **k):
    return lambda e: getattr(e, name)(*a, **k)


class Op:
    __slots__ = ("eng", "fn", "deps", "is_dma", "key", "nparts", "seq", "signaled", "dma_val")


class Prog:
    ENGS = ("pe", "act", "dve", "pool", "sp")

    def __init__(self, nc):
        self.nc = nc
        self.ops = []
        self.last_w = {}
        self.readers = {}
        self.dma_cnt = {}
        self.dma_gen = {}
        self.psi = 0
        self.inames = {}

    def begin_record(self, banks):
        self.rec = []
        self.ps_banks = list(banks)
        self.ps_bi = 0

    def end_record(self):
        r = self.rec
        self.rec = None
        self.ps_banks = None
        return r

    def replay(self, recs):
        recs = [r for r in recs if r]
        n = max(len(r) for r in recs)
        for i in range(n):
            for r in recs:
                if i < len(r):
                    self.add(*r[i][0], **r[i][1])

    def add(self, eng, fn, reads=(), writes=(), dma=0, key=None):
        if getattr(self, "rec", None) is not None:
            self.rec.append(((eng, fn), dict(reads=list(reads), writes=list(writes), dma=dma, key=key)))
            return None
        op = Op()
        op.eng = eng
        op.fn = fn
        op.is_dma = dma > 0
        op.nparts = dma
        op.signaled = False
        op.seq = 0
        deps = set()
        excl = [r for r in reads if isinstance(r, str) and r[:2] == "ps" and r[2:].isdigit()]
        if excl:
            reads = [r for r in reads if r not in excl]
            writes = list(writes) + excl
        for r in reads:
            w = self.last_w.get(r)
            if w is not None:
                deps.add(w)
        for w_ in writes:
            w = self.last_w.get(w_)
            if w is not None:
                deps.add(w)
            for rd in self.readers.get(w_, ()):
                deps.add(rd)
        op.deps = deps
        for r in reads:
            self.readers.setdefault(r, []).append(op)
        for w_ in writes:
            self.last_w[w_] = op
            self.readers[w_] = []
        if op.is_dma:
            g = self.dma_gen.get(key, 0)
            c = self.dma_cnt.get((key, g), 0) + dma * 16
            if c > SEM_CH:
                g += 1
                self.dma_gen[key] = g
                c = dma * 16
            self.dma_cnt[(key, g)] = c
            op.key = (key, g)
            op.dma_val = c
        self.ops.append(op)
        return op

    def barrier(self):
        last = {}
        dmas = {}
        for op in self.ops:
            if op.is_dma:
                dmas[op.key] = op
            else:
                last[op.eng] = op
        deps = set(last.values()) | set(dmas.values())
        bops = []
        for e in ("pe", "act", "dve", "pool", "sp"):
            op = self.add(e, lambda en: en.nop(), ())
            op.deps = set(deps)
            bops.append(op)
        self.last_w = {}
        self.readers = {}

    def ps(self):
        if getattr(self, "ps_banks", None):
            i = self.ps_banks[self.ps_bi % len(self.ps_banks)]
            self.ps_bi += 1
            return i
        lo = getattr(self, "ps_lo", 0)
        if self.psi < lo:
            self.psi = lo
        i = self.psi
        self.psi = self.psi + 1
        if self.psi >= 8:
            self.psi = lo
        return i

    def emit(self, es, maxops=None):
        nc = self.nc
        if maxops is not None:
            self.ops = self.ops[:maxops]
            cnt2 = {}
            for op in self.ops:
                if op.is_dma:
                    cnt2[op.key] = op.dma_val
            self.dma_cnt = cnt2
        for op in self.ops:
            nd = set()
            for d in op.deps:
                if (not d.is_dma) and (not op.is_dma) and d.eng == "pe" and op.eng == "pe":
                    continue
                nd.add(d)
                d.signaled = True
            op.deps = nd
        cnt = {e: 0 for e in self.ENGS}
        for op in self.ops:
            if not op.is_dma and op.signaled:
                cnt[op.eng] += 1
                op.seq = cnt[op.eng]
        eng_sems = {}
        for e in self.ENGS:
            n = (cnt[e] + SEM_CH - 1) // SEM_CH
            eng_sems[e] = [es.enter_context(nc.semaphore(f"s_{e}_{i}")) for i in range(n)]
        dma_sems = {}
        for i, k in enumerate(self.dma_cnt.keys()):
            dma_sems[k] = es.enter_context(nc.semaphore(f"d_{i}"))
        self.nsem = sum(len(v) for v in eng_sems.values()) + len(dma_sems)
        per_eng = {e: [o for o in self.ops if o.eng == e] for e in self.ENGS}
        block = es.enter_context(nc.Block())

        def run(engname, eobj):
            waited = {}

            def wait(sem, val):
                if waited.get(id(sem), 0) >= val:
                    return
                waited[id(sem)] = val
                eobj.wait_ge(sem, val)

            for op in per_eng[engname]:
                need = {}
                for d in op.deps:
                    if d.is_dma:
                        sem = dma_sems[d.key]
                        v = d.dma_val
                    else:
                        si = (d.seq - 1) // SEM_CH
                        sem = eng_sems[d.eng][si]
                        v = d.seq - si * SEM_CH
                    k = id(sem)
                    if k not in need or need[k][1] < v:
                        need[k] = (sem, v)
                for sem, v in need.values():
                    wait(sem, v)
                if op.is_dma:
                    sem = dma_sems[op.key]
                    insts = op.fn(eobj)
                    if not isinstance(insts, (list, tuple)):
                        insts = [insts]
                    assert len(insts) == op.nparts
                    for ins in insts:
                        ins.then_inc(sem, 16)
                else:
                    ins = op.fn(eobj)
                    try:
                        self.inames[ins.ins.name] = op
                    except Exception:
                        pass
                    if op.signaled:
                        si = (op.seq - 1) // SEM_CH
                        ins.then_inc(eng_sems[op.eng][si], 1)
            if engname == "sp":
                for k, c in self.dma_cnt.items():
                    wait(dma_sems[k], c)

        block.tensor(lambda e: run("pe", e))
        block.scalar(lambda e: run("act", e))
        block.vector(lambda e: run("dve", e))
        block.gpsimd(lambda e: run("pool", e))
        block.sync(lambda e: run("sp", e))


class Arena:
    def __init__(self, t, nf32):
        self.t = t
        self.n = nf32
        self.off = 0
        self.hw = 0

    def mark(self):
        return self.off

    def release(self, m):
        self.off = m

    def alloc(self, n, dt=F32):
        nf = n if dt in (F32, I32, U32) else (n + 1) // 2
        nf = (nf + 7) // 8 * 8
        a = self.off
        self.off += nf
        self.hw = max(self.hw, self.off)
        assert self.off <= self.n, f"arena overflow {self.off} > {self.n}"
        v = self.t[:, a:a + nf]
        if dt != F32:
            v = v.bitcast(dt)
        return v[:, 0:n]


def make_consts():
    c = np.zeros((128, 8, 128), np.float32)
    i = np.arange(128)
    c[:, 0, :] = np.eye(128)
    c[:, 1, :] = 1.0
    same64 = (i[:, None] // 64) == (i[None, :] // 64)
    same4 = ((i[:, None] // 4) == (i[None, :] // 4)) & (i[:, None] < 64) & (i[None, :] < 64)
    c[:, 2, :] = same64 & (i[:, None] <= i[None, :])
    c[:, 3, :] = same64 & (i[:, None] > i[None, :])
    c[:, 4, :] = same4 & (i[:, None] <= i[None, :])
    c[:, 5, :] = same4 & (i[:, None] > i[None, :])
    c[:, 6, 0:16] = (i[:, None] // 4) == np.arange(16)[None, :]
    for g, w in enumerate((2, 4, 8, 16)):
        c[:, 7, g * 16:(g + 1) * 16] = 1.0 / np.minimum(np.arange(16) + 1, w)
    c[:, 7, 64:80] = np.arange(16)[None, :]
    return c.reshape(128, 1024)


def _mixer_bufs(G, NT, sample):
    AR = G.AR
    B = type("B", (), {})()
    B.NT = NT
    B.hT = AR.alloc(8 * NT, BF16).rearrange("p (k t) -> p k t", k=8)
    B.xbf = [AR.alloc(1024, BF16)] * 2
    B.junk = B.xbf[0]
    B.ss = AR.alloc(8)
    B.stg = [(AR.alloc(2048), f"stg{i}") for i in range(2)]
    B.wbf = [(AR.alloc(2048, BF16), f"wbf{i}") for i in range(3)]
    B.pre = [AR.alloc(3 + NT) for _ in range(2)]
    B.cv = [AR.alloc(NT) for _ in range(2)]
    B.qkvc = AR.alloc(12 * NT).rearrange("p (c t) -> p c t", c=12)
    B.zs = AR.alloc(4 * NT).rearrange("p (c t) -> p c t", c=4)
    B.xqT = AR.alloc(4 * NT, BF16).rearrange("p (c t) -> p c t", c=4)
    B.yT = AR.alloc(12 * NT, BF16).rearrange("p (c t) -> p c t", c=12)
    B.macc8 = AR.alloc(8 * NT).rearrange("p (c t) -> p c t", c=8)
    B.mT = AR.alloc(8 * NT, BF16).rearrange("p (c t) -> p c t", c=8)
    B.sqb = [AR.alloc(NT) for _ in range(2)]
    B.rinv = [AR.alloc(NT) for _ in range(2)]
    B.sig = B.sqb
    B.prod = B.rinv
    B.dT = [AR.alloc(NT, BF16) for _ in range(2)]
    B.pA = AR.alloc(19 * 16 if sample else 15 + NT)
    B.pB = AR.alloc(19 * 16 if sample else 15 + NT)
    B.t16 = AR.alloc(16)
    B.ktok = AR.alloc(512).rearrange("p (h d) -> p h d", h=4)
    B.vtok = AR.alloc(512).rearrange("p (h d) -> p h d", h=4)
    B.ba = AR.alloc(32)
    B.wba = AR.alloc(64, BF16)
    B.gcc = AR.alloc(16)
    names = ["gL", "gcr", "egr", "dm", "t1", "dmT", "t2", "Pa", "Pb", "Qa", "Qb", "R", "u", "wT", "attnT", "vn", "qg", "kbg", "vb", "kd", "osq", "rr", "y1", "kdsc"]
    B.dw = [{}, {}]
    shared = ("gL", "dm", "dmT", "osq", "rr", "y1", "kdsc") if sample else ()
    for n in names:
        if n in shared:
            B.dw[0][n] = B.dw[1][n] = AR.alloc(128)
        else:
            B.dw[0][n] = AR.alloc(128)
            B.dw[1][n] = AR.alloc(128)
    B.dw_shared = shared
    B.pexp = [AR.alloc(256) for _ in range(2)]
    B.pn = [AR.alloc(256, BF16) for _ in range(2)]
    B.pT = [AR.alloc(256, BF16).rearrange("p (c t) -> p c t", c=2) for _ in range(2)]
    B.asm = AR.alloc(32)
    if not sample:
        B.uT = AR.alloc(4 * (15 + NT)).rearrange("p (c t) -> p c t", c=4)
        B.kTm = AR.alloc(4 * 256, BF16).rearrange("p (h m) -> p h m", h=4)
        B.vm = AR.alloc(2 * 512, BF16).rearrange("p (c n) -> p c n", c=2)
        B.hTm = B.hT
        qflat = B.qkvc.rearrange("p c t -> p (c t)")
        B.memx = qflat[:, 0:1024]
        B.kvrow = qflat[:, 1024:3072].rearrange("p (j n) -> p j n", j=2)
        B.rowbuf = qflat[:, 0:1536]
    else:
        B.uTs = AR.alloc(4 * 16 * 19).rearrange("p (c s e) -> p c s e", c=4, s=16)
        B.pre_s = [AR.alloc(16 * 7).rearrange("p (s e) -> p s e", s=16) for _ in range(2)]
        B.hist_s = AR.alloc(12 * 48).rearrange("p (c s e) -> p c s e", c=12, s=16)
        B.cvout = AR.alloc(12 * 48).rearrange("p (c s e) -> p c s e", c=12, s=16)
        B.ld1536 = AR.alloc(1536)
        B.ld512 = [AR.alloc(512) for _ in range(2)]
        mk_ = AR.mark()
        B.Sh = [AR.alloc(16 * 128).rearrange("p (s d) -> p s d", s=16) for _ in range(2)]
        B.rowbuf = B.Sh[0].rearrange("p s d -> p (s d)")[:, 0:1536]
        B.kdm = AR.alloc(16 * 128).rearrange("p (s d) -> p s d", s=16)
        B.wTm = AR.alloc(1088)
        B.o1 = AR.alloc(64)
        B.oTs = AR.alloc(64)
        hw_ = AR.mark()
        AR.release(mk_)
        B.xqm = AR.alloc(4 * 1088, BF16).rearrange("p (h r) -> p h r", h=4)
        B.kvs = [AR.alloc(1024) for _ in range(2)]
        B.kvb = [AR.alloc(1024, BF16).rearrange("p (c n) -> p c n", c=2) for _ in range(2)]
        B.kTs = [AR.alloc(1024, BF16).rearrange("p (h m) -> p h m", h=4) for _ in range(2)]
        B.pTall = [AR.alloc(256, BF16).rearrange("p (c t) -> p c t", c=2) for _ in range(4)]
        AR.release(max(hw_, AR.mark()))
    return B


def _mem_kv(G, B, l):
    P, PS, PSB, pk = G.P, G.PS, G.PSB, G.pk
    for j in range(2):
        P.add("sp", CALL("dma_start", out=B.memx[:, :], in_=G.memp[j * 128:(j + 1) * 128, :]), writes=["memx"], dma=1, key="memx")
        G.rms_rstd(B.memx[:, :], 128, "memx", B.junk, "xbf0", B.ss[:, 0:1], "ss0")
        xb = B.xbf[j % 2]
        P.add("act", CALL("activation", out=xb[:, :], in_=B.memx[:, :], func=AF.Copy, scale=B.ss[:, 0:1]),
              reads=["memx", "ss0"], writes=["xbf0"])
        G.to_featmajor(xb, 128, "xbf0", B.hTm, f"hTm{j}", slice(j * 128, (j + 1) * 128), gsb=G.gmem_sb, gkey="gmem")
    for g in range(4):
        wv, wk = G.load_w(G.w_mkv[l], 8, g * 256, 256, B.stg, B.wbf)
        for j in range(2):
            b = P.ps()
            for k in range(8):
                P.add("pe", CALL("matmul", PS(b, 256), lhsT=B.hTm[:, k, j * 128:(j + 1) * 128], rhs=wv[:, k, :], start=(k == 0), stop=(k == 7)),
                      reads=[f"hTm{j}", wk], writes=[pk(b)])
            P.add("act", CALL("activation", out=B.kvrow[:, j, g * 256:(g + 1) * 256], in_=PS(b, 256), func=AF.Copy),
                  reads=[pk(b)], writes=[f"kvrow{j}_{g}"])
            if g >= 2:
                P.add("dve", CALL("tensor_copy", out=B.vm[:, j, (g - 2) * 256:(g - 1) * 256], in_=PS(b, 256)),
                      reads=[pk(b)], writes=[f"vm{j}_{g}"])
        if g < 2:
            for cc in range(2):
                b = P.ps()
                for k in range(8):
                    P.add("pe", CALL("matmul", PS(b, 256), lhsT=wv[:, k, cc * 128:(cc + 1) * 128], rhs=B.hTm[:, k, :], start=(k == 0), stop=(k == 7)),
                          reads=["hTm0", "hTm1", wk], writes=[pk(b)])
                P.add("act", CALL("activation", out=B.kTm[:, g * 2 + cc, :], in_=PS(b, 256), func=AF.Copy),
                      reads=[pk(b)], writes=[f"kTm{g * 2 + cc}"])
    for j in range(2):
        P.add("pool", CALL("dma_start", out=G.o_mk[l, j * 128:(j + 1) * 128, :], in_=B.kvrow[:, j, 0:512]),
              reads=[f"kvrow{j}_0", f"kvrow{j}_1"], dma=1, key=f"o_mk{j}")
        P.add("pool", CALL("dma_start", out=G.o_mv[l, j * 128:(j + 1) * 128, :], in_=B.kvrow[:, j, 512:1024]),
              reads=[f"kvrow{j}_2", f"kvrow{j}_3"], dma=1, key=f"o_mv{j}")
    B.kTm_keys = [f"kTm{h}" for h in range(4)]
    B.vm_keys = [f"vm{j}_{g}" for j in range(2) for g in (2, 3)]


def _proj(G, B, l, wdram, krows, c0, nch, srcT, srckeys, NT, consume, chunk_w=128):
    P, PS, pk = G.P, G.PS, G.pk
    i = 0
    while i < nch:
        ng = min(2, nch - i)
        ncols = 128 * ng if chunk_w == 128 else chunk_w
        wv, wk = G.load_w(wdram, krows, c0 + i * 128, ncols, B.stg, B.wbf)
        for cc in range(ng):
            b = P.ps()
            for k in range(krows):
                P.add("pe", CALL("matmul", PS(b, NT)[0:chunk_w, :], lhsT=wv[:, k, cc * 128:cc * 128 + chunk_w], rhs=srcT[:, k, 0:NT],
                                                               start=(k == 0), stop=(k == krows - 1)),
                      reads=list(srckeys) + [wk], writes=[pk(b)])
            consume(i + cc, b)
        i += ng


def _delta_tile(G, B, l, j, np_, tsl, sample, extra=None):
    P, PS, PSB, pk = G.P, G.PS, G.PSB, G.pk
    LT = (G.Ltri4 if sample else G.Ltri)
    SLx = (G.SL4 if sample else G.SLm)
    I_ = G.identF
    ones = G.onesF
    r_ = slice(0, np_)
    for nm, c0, dst in (("ktok", 4, B.ktok), ("vtok", 8, B.vtok)):
        b = P.ps()
        for h in range(4):
            P.add("pe", CALL("transpose", out=PS(b, 128, h * 128)[r_, :], in_=B.qkvc[:, c0 + h, tsl], identity=I_),
                  reads=[f"qkvc{c0 + h}", "cst"], writes=[pk(b)])
        P.add("act", CALL("activation", out=dst[r_, :, :], in_=PS(b, 512)[r_, :].rearrange("p (h d) -> p h d", h=4), func=AF.Copy),
              reads=[pk(b)], writes=[nm])
    b = P.ps()
    P.add("pe", CALL("matmul", PS(b, 4)[r_, :], lhsT=LT[r_, r_], rhs=B.ba[r_, 8:12], start=True, stop=True), reads=["ba", "cst"], writes=[pk(b)])
    P.add("pe", CALL("matmul", PS(b, 4, 8)[r_, :], lhsT=LT[r_, r_], rhs=B.ba[r_, 8:12], start=True, stop=False), reads=["ba", "cst"], writes=[pk(b)])
    P.add("pe", CALL("matmul", PS(b, 4, 8)[r_, :], lhsT=SLx[r_, r_], rhs=B.ba[r_, 8:12], start=False, stop=True), reads=["ba", "cst"], writes=[pk(b)])
    P.add("dve", CALL("tensor_copy", out=B.gcc[r_, 0:4], in_=PS(b, 4)[r_, :]), reads=[pk(b)], writes=["gcc"])
    P.add("dve", CALL("tensor_copy", out=B.gcc[r_, 8:12], in_=PS(b, 4, 8)[r_, :]), reads=[pk(b)], writes=["gcc"])
    P.add("act", CALL("activation", out=B.gcc[r_, 4:8], in_=B.gcc[r_, 0:4], func=AF.Exp), reads=["gcc"], writes=["gcc"])
    def head_body(h):
        W = B.dw[h % 2]
        wn = lambda n, h=h: (f"dws_{n}" if n in B.dw_shared else f"dw{h % 2}_{n}")
        qT = B.qkvc[:, h, tsl]
        kT = B.qkvc[:, 4 + h, tsl]
        beta = B.ba[r_, h:h + 1]
        nbeta = B.ba[r_, 4 + h:5 + h]
        gcol = B.ba[r_, 8 + h:9 + h]
        gc_c = B.gcc[r_, h:h + 1]
        egc_c = B.gcc[r_, 4 + h:5 + h]
        gl_c = B.gcc[r_, 8 + h:9 + h]
        P.add("dve", CALL("tensor_scalar", out=W["gL"][r_, r_], in0=LT[r_, r_], scalar1=gcol, scalar2=0.0, op0=ALU.mult, op1=ALU.add),
              reads=["ba", "cst"], writes=[wn("gL")])
        b = P.ps()
        P.add("pe", CALL("matmul", PS(b, np_), lhsT=ones[r_, :], rhs=W["gL"][r_, r_], start=True, stop=True), reads=[wn("gL"), "cst"], writes=[pk(b)])
        P.add("act", CALL("activation", out=W["gcr"][:, r_], in_=PS(b, np_), func=AF.Copy), reads=[pk(b)], writes=[wn("gcr")])
        P.add("act", CALL("activation", out=W["egr"][:, r_], in_=PS(b, np_), func=AF.Exp), reads=[pk(b)], writes=[wn("egr")])
        P.add("dve", CALL("tensor_scalar", out=W["dm"][r_, r_], in0=W["gcr"][r_, r_], scalar1=gc_c, scalar2=0.0, op0=ALU.subtract, op1=ALU.max),
              reads=[wn("gcr"), "gcc"], writes=[wn("dm")])
        P.add("act", CALL("activation", out=W["dm"][r_, r_], in_=W["dm"][r_, r_], func=AF.Exp, scale=-1.0), reads=[wn("dm")], writes=[wn("dm")])
        P.add("pool", CALL("tensor_tensor", out=W["t1"][r_, r_], in0=W["dm"][r_, r_], in1=SLx[r_, r_], op=ALU.mult), reads=[wn("dm"), "cst"], writes=[wn("t1")])
        P.add("dve", CALL("tensor_scalar", out=W["dmT"][r_, r_], in0=W["gcr"][r_, r_], scalar1=gc_c, scalar2=0.0, op0=ALU.subtract, op1=ALU.min),
              reads=[wn("gcr"), "gcc"], writes=[wn("dmT")])
        P.add("act", CALL("activation", out=W["dmT"][r_, r_], in_=W["dmT"][r_, r_], func=AF.Exp), reads=[wn("dmT")], writes=[wn("dmT")])
        P.add("pool", CALL("tensor_tensor", out=W["t2"][r_, r_], in0=W["dmT"][r_, r_], in1=LT[r_, r_], op=ALU.mult), reads=[wn("dmT"), "cst"], writes=[wn("t2")])
        b = P.ps()
        P.add("pe", CALL("matmul", PS(b, np_)[r_, :], lhsT=kT, rhs=kT, start=True, stop=True), reads=[f"qkvc{4 + h}"], writes=[pk(b)])
        P.add("dve", CALL("scalar_tensor_tensor", out=W["Pa"][r_, r_], in0=PS(b, np_)[r_, :], scalar=nbeta, in1=W["t1"][r_, r_], op0=ALU.mult, op1=ALU.mult),
              reads=[pk(b), "ba", wn("t1")], writes=[wn("Pa")])
        b = P.ps()
        P.add("pe", CALL("transpose", out=PS(b, np_)[r_, :], in_=W["Pa"][r_, r_], identity=I_[r_, r_]), reads=[wn("Pa"), "cst"], writes=[pk(b)])
        P.add("act", CALL("activation", out=W["Qa"][r_, r_], in_=PS(b, np_)[r_, :], func=AF.Copy), reads=[pk(b)], writes=[wn("Qa")])
        P.add("dve", CALL("tensor_tensor", out=W["R"][r_, r_], in0=PS(b, np_)[r_, :], in1=I_[r_, r_], op=ALU.add), reads=[pk(b), "cst"], writes=[wn("R")])
        nst = 1 if sample else 5
        Pk, Qk, Pn, Qn = "Pa", "Qa", "Pb", "Qb"
        for k in range(nst):
            bP = P.ps()
            P.add("pe", CALL("matmul", PS(bP, np_)[r_, :], lhsT=W[Qk][r_, r_], rhs=W[Pk][r_, r_], start=True, stop=True),
                  reads=[wn(Pk), wn(Qk)], writes=[pk(bP)])
            P.add("act", CALL("activation", out=W[Pn][r_, r_], in_=PS(bP, np_)[r_, :], func=AF.Copy), reads=[pk(bP)], writes=[wn(Pn)])
            if k < nst - 1:
                bQ = P.ps()
                P.add("pe", CALL("matmul", PS(bQ, np_)[r_, :], lhsT=W[Pk][r_, r_], rhs=W[Qk][r_, r_], start=True, stop=True),
                      reads=[wn(Pk), wn(Qk)], writes=[pk(bQ)])
                P.add("dve", CALL("tensor_copy", out=W[Qn][r_, r_], in_=PS(bQ, np_)[r_, :]), reads=[pk(bQ)], writes=[wn(Qn)])
            bR = P.ps()
            P.add("pe", CALL("matmul", PS(bR, np_)[r_, :], lhsT=W[Pn][r_, r_], rhs=W["R"][r_, r_], start=True, stop=True),
                  reads=[wn(Pn), wn("R")], writes=[pk(bR)])
            P.add("dve", CALL("tensor_tensor", out=W["R"][r_, r_], in0=PS(bR, np_)[r_, :], in1=W["R"][r_, r_], op=ALU.add), reads=[pk(bR), wn("R")], writes=[wn("R")])
            Pk, Pn = Pn, Pk
            Qk, Qn = Qn, Qk
        P.add("dve", CALL("tensor_scalar", out=W["vb"][r_, :], in0=B.vtok[r_, h, :], scalar1=beta, scalar2=0.0, op0=ALU.mult, op1=ALU.add),
              reads=["vtok", "ba"], writes=[wn("vb")])
        P.add("dve", CALL("tensor_scalar", out=W["kbg"][r_, :], in0=B.ktok[r_, h, :], scalar1=beta, scalar2=egc_c, op0=ALU.mult, op1=ALU.mult),
              reads=["ktok", "ba", "gcc"], writes=[wn("kbg")])
        P.add("act", CALL("activation", out=W["kdsc"][r_, 0:1], in_=gc_c, func=AF.Exp, scale=-1.0, bias=gl_c), reads=["gcc"], writes=[wn("kdsc")])
        P.add("dve", CALL("tensor_scalar", out=W["kd"][r_, :], in0=B.ktok[r_, h, :], scalar1=W["kdsc"][r_, 0:1], scalar2=0.0, op0=ALU.mult, op1=ALU.add),
              reads=["ktok", wn("kdsc")], writes=[wn("kd")])
        P.add("pool", CALL("tensor_tensor", out=W["qg"][:, r_], in0=qT, in1=W["egr"][:, r_], op=ALU.mult), reads=[f"qkvc{h}", wn("egr")], writes=[wn("qg")])
        b = P.ps()
        P.add("pe", CALL("matmul", PS(b, 128)[r_, :], lhsT=W["R"][r_, r_], rhs=W["vb"][r_, :], start=True, stop=True), reads=[wn("R"), wn("vb")], writes=[pk(b)])
        P.add("act", CALL("activation", out=W["u"][r_, :], in_=PS(b, 128)[r_, :], func=AF.Copy), reads=[pk(b)], writes=[wn("u")])
        b = P.ps()
        P.add("pe", CALL("matmul", PS(b, np_), lhsT=W["kbg"][r_, :], rhs=W["R"][r_, r_], start=True, stop=True), reads=[wn("R"), wn("kbg")], writes=[pk(b)])
        P.add("act", CALL("activation", out=W["wT"][:, r_], in_=PS(b, np_), func=AF.Copy), reads=[pk(b)], writes=[wn("wT")])
        b = P.ps()
        P.add("pe", CALL("matmul", PS(b, np_)[r_, :], lhsT=kT, rhs=qT, start=True, stop=True), reads=[f"qkvc{4 + h}", f"qkvc{h}"], writes=[pk(b)])
        P.add("dve", CALL("tensor_tensor", out=W["attnT"][r_, r_], in0=PS(b, np_)[r_, :], in1=W["t2"][r_, r_], op=ALU.mult), reads=[pk(b), wn("t2")], writes=[wn("attnT")])
        if not sample:
            Sk = f"S{h}"
            Sh_ = G.Sst[:, h, :]
            bo = P.ps()
            for ci in range(2):
                rr_ = slice(ci * 64, ci * 64 + 64)
                bw = P.ps()
                P.add("pe", CALL("matmul", PS(bw, 128), lhsT=W["wT"][:, 0:128], rhs=Sh_, start=True, stop=True), reads=[wn("wT"), Sk], writes=[pk(bw)])
                P.add("dve", CALL("tensor_tensor", out=W["vn"][rr_, :], in0=W["u"][rr_, :], in1=PS(bw, 128)[rr_, :], op=ALU.subtract),
                      reads=[pk(bw), wn("u")], writes=[wn("vn") + str(ci)])
                P.add("pe", CALL("matmul", PS(bo, 64, 256 + ci * 64), lhsT=Sh_, rhs=W["qg"][:, rr_], start=True, stop=False), reads=[wn("qg"), Sk], writes=[pk(bo)])
                P.add("pe", CALL("matmul", PS(bo, 64, 256 + ci * 64), lhsT=W["vn"][rr_, :], rhs=W["attnT"][rr_, rr_], start=False, stop=True),
                      reads=[wn("vn") + str(ci), wn("attnT")], writes=[pk(bo)])
                bs = P.ps()
                P.add("pe", CALL("matmul", PS(bs, 128), lhsT=W["kd"][rr_, :], rhs=W["vn"][rr_, :], start=True, stop=True), reads=[wn("kd"), wn("vn") + str(ci)], writes=[pk(bs)])
                P.add("dve", CALL("scalar_tensor_tensor", out=Sh_, in0=Sh_, scalar=W["egr"][:, ci * 64 + 63:ci * 64 + 64], in1=PS(bs, 128), op0=ALU.mult, op1=ALU.add),
                      reads=[pk(bs), wn("egr"), Sk], writes=[Sk])
            o_ap = PS(bo, np_, 256)
            o_key = pk(bo)
        else:
            Shb = B.Sh[h % 2]
            Sk = f"Sh{h % 2}"
            P.add("sp", CALL("dma_start", out=Shb[:, :, :], in_=G.st_delta[l, :, h].rearrange("s k v -> k s v")), writes=[Sk], dma=1, key=Sk)
            P.add("dve", CALL("tensor_copy", out=B.wTm[:, 0:1088].rearrange("p (s r) -> p s r", r=68)[:, :, 0:4], in_=W["wT"][:, 0:64].rearrange("p (s i) -> p s i", i=4)),
                  reads=[wn("wT")], writes=["wTm"])
            bw = P.ps()
            for s in range(16):
                P.add("pe", CALL("matmul", PS(bw, 128)[0:64, :], lhsT=B.wTm[:, s * 64:(s + 1) * 64], rhs=Shb[:, s, :], start=(s == 0), stop=(s == 15)),
                      reads=["wTm", Sk], writes=[pk(bw)])
            P.add("dve", CALL("tensor_tensor", out=W["vn"][0:64, :], in0=W["u"][0:64, :], in1=PS(bw, 128)[0:64, :], op=ALU.subtract), reads=[pk(bw), wn("u")], writes=[wn("vn") + "0"])
            bo = P.ps()
            for s in range(16):
                P.add("pe", CALL("matmul", PS(bo, 4, 4 * s), lhsT=Shb[:, s, :], rhs=W["qg"][:, 4 * s:4 * s + 4], start=True, stop=True),
                      reads=[wn("qg"), Sk], writes=[pk(bo)])
            P.add("act", CALL("activation", out=B.o1[:, 0:64], in_=PS(bo, 64), func=AF.Copy), reads=[pk(bo)], writes=["o1"])
            b2 = P.ps()
            P.add("pe", CALL("matmul", PS(b2, 64), lhsT=W["vn"][0:64, :], rhs=W["attnT"][0:64, 0:64], start=True, stop=True), reads=[wn("vn") + "0", wn("attnT")], writes=[pk(b2)])
            P.add("dve", CALL("tensor_tensor", out=B.oTs[:, 0:64], in0=PS(b2, 64), in1=B.o1[:, 0:64], op=ALU.add), reads=[pk(b2), "o1"], writes=["oTs"])
            P.add("pool", CALL("tensor_tensor", out=B.kdm[0:64, :, :], in0=W["kd"][0:64, :].unsqueeze(1).to_broadcast([64, 16, 128]),
                                                         in1=G.seqmask[0:64, :].unsqueeze(2).to_broadcast([64, 16, 128]), op=ALU.mult),
                  reads=[wn("kd"), "cst"], writes=["kdm"])
            for s in range(16):
                bs = P.ps()
                P.add("pe", CALL("matmul", PS(bs, 128), lhsT=B.kdm[0:64, s, :], rhs=W["vn"][0:64, :], start=True, stop=True), reads=["kdm", wn("vn") + "0"], writes=[pk(bs)])
                P.add("dve", CALL("scalar_tensor_tensor", out=Shb[:, s, :], in0=Shb[:, s, :], scalar=W["egr"][:, 4 * s + 3:4 * s + 4], in1=PS(bs, 128), op0=ALU.mult, op1=ALU.add),
                      reads=[pk(bs), wn("egr"), Sk], writes=[Sk])
            P.add("pool", CALL("dma_start", out=G.o_delta_s[l, :, h].rearrange("s k v -> k s v"), in_=Shb[:, :, :]), reads=[Sk], dma=1, key="o_" + Sk)
            o_ap = B.oTs[:, 0:64]
            o_key = "oTs"
        P.add("act", CALL("activation", out=W["osq"][:, r_], in_=o_ap, func=AF.Square), reads=[o_key], writes=[wn("osq")])
        bq = P.ps()
        P.add("pe", CALL("matmul", PS(bq, np_), lhsT=ones, rhs=W["osq"][:, r_], start=True, stop=True), reads=[wn("osq"), "cst"], writes=[pk(bq)])
        P.add("act", CALL("activation", out=W["rr"][:, r_], in_=PS(bq, np_), func=AF.Sqrt, scale=1.0 / 128, bias=G.epsb[:, 0:1]), reads=[pk(bq), "epsb"], writes=[wn("rr")])
        P.add("dve", CALL("reciprocal", out=W["rr"][:, r_], in_=W["rr"][:, r_]), reads=[wn("rr")], writes=[wn("rr")])
        P.add("dve", CALL("scalar_tensor_tensor", out=W["y1"][:, r_], in0=o_ap, scalar=G.gdn_sb[:, 0:1], in1=W["rr"][:, r_], op0=ALU.mult, op1=ALU.mult),
              reads=[o_key, "gdn", wn("rr")], writes=[wn("y1")])
        P.add("pool", CALL("tensor_tensor", out=B.yT[:, 4 + h, tsl], in0=W["y1"][:, r_], in1=B.zs[:, h, tsl], op=ALU.mult), reads=[wn("y1"), f"zs{h}"], writes=[f"yT{4 + h}_{j}"])

    if sample:
        for h in range(4):
            head_body(h)
    else:
        for h0 in (0, 2):
            recs = []
            for hh, banks in ((h0, (2, 3, 4)), (h0 + 1, (5, 6, 7))):
                P.begin_record(banks)
                head_body(hh)
                recs.append(P.end_record())
            if extra:
                recs.append(extra.pop(0))
            P.replay(recs)


def _attn_softmax(G, B, sc_ap, sc_key, np_, h, out_writes):
    P, PS, PSB, pk = G.P, G.PS, G.PSB, G.pk
    r_ = slice(0, np_)
    i = h % 2
    sc = 128.0 ** -0.5
    mx = B.asm[r_, 4 * i:4 * i + 1]
    nmx = B.asm[r_, 4 * i + 1:4 * i + 2]
    rs = B.asm[r_, 4 * i + 2:4 * i + 3]
    ak = f"asm{i}"
    P.add("dve", CALL("reduce_max", out=mx, in_=sc_ap, axis=AX.X), reads=[sc_key], writes=[ak])
    P.add("dve", CALL("tensor_scalar", out=nmx, in0=mx, scalar1=-sc, scalar2=0.0, op0=ALU.mult, op1=ALU.add), reads=[ak], writes=[ak])
    P.add("act", CALL("activation", out=B.pexp[i][r_, :], in_=sc_ap, func=AF.Exp, scale=sc, bias=nmx, accum_out=rs), reads=[sc_key, ak], writes=[f"pexp{i}", ak])
    P.add("dve", CALL("reciprocal", out=rs, in_=rs), reads=[ak], writes=[ak])
    P.add("dve", CALL("tensor_scalar", out=B.pn[i][r_, :], in0=B.pexp[i][r_, :], scalar1=rs, scalar2=0.0, op0=ALU.mult, op1=ALU.add), reads=[f"pexp{i}", ak], writes=[f"pn{i}"])
    bt = P.ps()
    for mc in range(2):
        P.add("pe", CALL("transpose", out=PSB(bt, np_, mc * 128), in_=B.pn[i][r_, mc * 128:(mc + 1) * 128], identity=G.identB[r_, r_]),
              reads=[f"pn{i}", "identB"], writes=[pk(bt)])
    P.add("act", CALL("activation", out=B.pT[i][:, :, r_], in_=PSB(bt, 256).rearrange("p (c t) -> p c t", c=2)[:, :, r_], func=AF.Copy), reads=[pk(bt)], writes=[f"pT{i}"])
    return B.pT[i], f"pT{i}"


def _mixer_st(G, B, l, st, tiles, sample):
    P, PS, PSB, pk = G.P, G.PS, G.PSB, G.pk
    NT = sum(t[1] for t in tiles)
    nt = len(tiles)
    hkeys = [f"hT{j}" for j in range(nt)]
    for j, (x_ap, np_, xkey) in enumerate(tiles):
        G.rms_rstd(x_ap, np_, xkey, B.junk, "xbf0", B.ss[:, 0:1], "ss0")
        xb = B.xbf[j % 2]
        P.add("act", CALL("activation", out=xb[0:np_, :], in_=x_ap, func=AF.Copy, scale=B.ss[0:np_, 0:1]),
              reads=[xkey, "ss0"], writes=["xbf0"])
        G.to_featmajor(xb, np_, "xbf0", B.hT, hkeys[j], slice(j * 128, j * 128 + np_), gsb=G.gmix_sb, gkey="gmix")

    win = (2, 4, 8, 16)
    if not sample:
        L = 15 + NT
        if st == 0:
            P.add("pool", CALL("memset", B.uT[:, :, 0:15], 0.0), writes=[f"uT{c}" for c in range(4)])

        def pool_consume(c, b):
            P.add("act", CALL("activation", out=B.uT[:, c, 15:L], in_=PS(b, NT), func=AF.Copy), reads=[pk(b)], writes=[f"uT{c}"])
            a = B.uT[:, c, :]
            bufs = [(B.pA, "pA"), (B.pB, "pB")]
            src, skey = a, f"uT{c}"
            sh = 1
            for s_ in range(c + 1):
                dst, dkey = bufs[s_ % 2]
                lo = 2 * sh - 1
                P.add("dve", CALL("tensor_tensor", out=dst[:, lo:L], in0=src[:, lo:L], in1=src[:, lo - sh:L - sh], op=ALU.add),
                      reads=[skey], writes=[dkey])
                src, skey = dst, dkey
                sh *= 2
            dT = B.dT[c % 2]
            dk = f"dT{c % 2}"
            P.add("dve", CALL("scalar_tensor_tensor", out=dT[:, 0:NT], in0=src[:, 15:L], scalar=1.0 / win[c], in1=a[:, 15:L], op0=ALU.mult, op1=ALU.subtract),
                  reads=[skey, f"uT{c}"], writes=[dk])
            if st == 0:
                P.add("dve", CALL("tensor_tensor", out=B.t16[:, 0:16], in0=src[:, 15:31], in1=G.rcnt[:, c * 16:(c + 1) * 16], op=ALU.mult), reads=[skey, "cst"], writes=["t16"])
                P.add("dve", CALL("tensor_tensor", out=dT[:, 0:16], in0=B.t16[:, 0:16], in1=a[:, 15:31], op=ALU.subtract), reads=["t16", f"uT{c}", dk], writes=[dk])
            b2 = P.ps()
            P.add("pe", CALL("matmul", PS(b2, NT), lhsT=G.wgrp_b[:, c, :], rhs=dT[:, 0:NT], start=True, stop=True), reads=[dk, "wgrp_b"], writes=[pk(b2)])
            P.add("act", CALL("activation", out=B.yT[:, c, 0:NT], in_=PS(b2, NT), func=AF.Copy, scale=G.psc_sb[:, c:c + 1]), reads=[pk(b2), "psc"], writes=[f"yT{c}_all"])
        _proj(G, B, l, G.w_in[l], 8, 0, 4, B.hT, hkeys, NT, pool_consume)
        P.add("pool", CALL("tensor_copy", out=B.uT[:, :, 0:15], in_=B.uT[:, :, NT:NT + 15]), writes=[f"uT{c}" for c in range(4)])
    else:
        def pool_consume(c, b):
            P.add("act", CALL("activation", out=B.uTs[:, c, :, 15:19], in_=PS(b, 64).rearrange("p (s i) -> p s i", i=4), func=AF.Copy), reads=[pk(b)], writes=[f"uTs{c}"])
            a = B.uTs[:, c, :, :]
            pA = B.pA[:, 0:304].rearrange("p (s e) -> p s e", e=19)
            pB = B.pB[:, 0:304].rearrange("p (s e) -> p s e", e=19)
            bufs = [(pA, "pA"), (pB, "pB")]
            src, skey = a, f"uTs{c}"
            sh = 1
            for s_ in range(c + 1):
                dst, dkey = bufs[s_ % 2]
                lo = 2 * sh - 1
                P.add("dve", CALL("tensor_tensor", out=dst[:, :, lo:19], in0=src[:, :, lo:19], in1=src[:, :, lo - sh:19 - sh], op=ALU.add),
                      reads=[skey], writes=[dkey])
                src, skey = dst, dkey
                sh *= 2
            dT = B.dT[c % 2]
            dk = f"dT{c % 2}"
            P.add("dve", CALL("scalar_tensor_tensor", out=dT[:, 0:64].rearrange("p (s i) -> p s i", i=4), in0=src[:, :, 15:19], scalar=1.0 / win[c], in1=a[:, :, 15:19], op0=ALU.mult, op1=ALU.subtract),
                  reads=[skey, f"uTs{c}"], writes=[dk])
            b2 = P.ps()
            P.add("pe", CALL("matmul", PS(b2, NT), lhsT=G.wgrp_b[:, c, :], rhs=dT[:, 0:NT], start=True, stop=True), reads=[dk, "wgrp_b"], writes=[pk(b2)])
            P.add("act", CALL("activation", out=B.yT[:, c, 0:NT], in_=PS(b2, NT), func=AF.Copy, scale=G.psc_sb[:, c:c + 1]), reads=[pk(b2), "psc"], writes=[f"yT{c}_all"])
        _proj(G, B, l, G.w_in[l], 8, 0, 4, B.hT, hkeys, NT, pool_consume)

    def qkv_consume(c, b):
        if not sample:
            pre = B.pre[c % 2]
            pkey = f"pre{c % 2}"
            cv = B.cv[c % 2]
            ckey = f"cv{c % 2}"
            P.add("pool", CALL("tensor_copy", out=pre[:, 0:3], in_=G.hist[:, c, :]), reads=[f"hist{c}"], writes=[pkey + "h"])
            P.add("act", CALL("activation", out=pre[:, 3:3 + NT], in_=PS(b, NT), func=AF.Copy), reads=[pk(b)], writes=[pkey])
            P.add("dve", CALL("tensor_scalar", out=cv[:, 0:NT], in0=pre[:, 0:NT], scalar1=G.wconv_sb[:, c, 0:1], scalar2=0.0, op0=ALU.mult, op1=ALU.add),
                  reads=[pkey, pkey + "h", "wconv"], writes=[ckey])
            for j in range(1, 4):
                P.add("dve", CALL("scalar_tensor_tensor", out=cv[:, 0:NT], in0=pre[:, j:j + NT], scalar=G.wconv_sb[:, c, j:j + 1], in1=cv[:, 0:NT], op0=ALU.mult, op1=ALU.add),
                      reads=[pkey, pkey + "h", "wconv", ckey], writes=[ckey])
            P.add("pool", CALL("tensor_copy", out=G.hist[:, c, :], in_=pre[:, NT:NT + 3]), reads=[pkey], writes=[f"hist{c}"])
            P.add("act", CALL("activation", out=B.qkvc[:, c, 0:NT], in_=cv[:, 0:NT], func=AF.Silu), reads=[ckey], writes=[f"qkvc{c}"])
        else:
            pre = B.pre_s[c % 2]
            pkey = f"pre{c % 2}"
            cv = B.cv[c % 2][:, 0:64].rearrange("p (s i) -> p s i", i=4)
            ckey = f"cv{c % 2}"
            P.add("pool", CALL("tensor_copy", out=pre[:, :, 0:3], in_=B.hist_s[:, c, :, :]), reads=[f"hist_s{c // 4}"], writes=[pkey + "h"])
            P.add("act", CALL("activation", out=pre[:, :, 3:7], in_=PS(b, 64).rearrange("p (s i) -> p s i", i=4), func=AF.Copy), reads=[pk(b)], writes=[pkey])
            P.add("dve", CALL("tensor_scalar", out=cv, in0=pre[:, :, 0:4], scalar1=G.wconv_sb[:, c, 0:1], scalar2=0.0, op0=ALU.mult, op1=ALU.add),
                  reads=[pkey, pkey + "h", "wconv"], writes=[ckey])
            for j in range(1, 4):
                P.add("dve", CALL("scalar_tensor_tensor", out=cv, in0=pre[:, :, j:j + 4], scalar=G.wconv_sb[:, c, j:j + 1], in1=cv, op0=ALU.mult, op1=ALU.add),
                      reads=[pkey, pkey + "h", "wconv", ckey], writes=[ckey])
            P.add("pool", CALL("tensor_copy", out=B.cvout[:, c, :, :], in_=pre[:, :, 4:7]), reads=[pkey], writes=[f"cvout{c // 4}"])
            P.add("act", CALL("activation", out=B.qkvc[:, c, 0:NT], in_=B.cv[c % 2][:, 0:64], func=AF.Silu), reads=[ckey], writes=[f"qkvc{c}"])
    _proj(G, B, l, G.w_in[l], 8, OFF_Q, 12, B.hT, hkeys, NT, qkv_consume)

    for c in range(8):
        i = c % 2
        P.add("act", CALL("activation", out=B.sqb[i][:, 0:NT], in_=B.qkvc[:, c, 0:NT], func=AF.Square), reads=[f"qkvc{c}"], writes=[f"sqb{i}"])
        b = P.ps()
        P.add("pe", CALL("matmul", PS(b, NT), lhsT=G.onesF, rhs=B.sqb[i][:, 0:NT], start=True, stop=True), reads=[f"sqb{i}", "cst"], writes=[pk(b)])
        P.add("act", CALL("activation", out=B.rinv[i][:, 0:NT], in_=PS(b, NT), func=AF.Sqrt, scale=1.0, bias=G.epsb[:, 0:1]), reads=[pk(b), "epsb"], writes=[f"rinv{i}"])
        P.add("dve", CALL("reciprocal", out=B.rinv[i][:, 0:NT], in_=B.rinv[i][:, 0:NT]), reads=[f"rinv{i}"], writes=[f"rinv{i}"])
        scl = (128.0 ** -0.5) if c < 4 else 1.0
        P.add("dve", CALL("scalar_tensor_tensor", out=B.qkvc[:, c, 0:NT], in0=B.rinv[i][:, 0:NT], scalar=scl, in1=B.qkvc[:, c, 0:NT], op0=ALU.mult, op1=ALU.mult),
              reads=[f"rinv{i}", f"qkvc{c}"], writes=[f"qkvc{c}"])

    def z_consume(c, b):
        P.add("act", CALL("activation", out=B.zs[:, c, 0:NT], in_=PS(b, NT), func=AF.Silu), reads=[pk(b)], writes=[f"zs{c}"])
    _proj(G, B, l, G.w_in[l], 8, OFF_Z, 4, B.hT, hkeys, NT, z_consume)

    def xq_consume(c, b):
        P.add("act", CALL("activation", out=B.xqT[:, c, 0:NT], in_=PS(b, NT), func=AF.Copy), reads=[pk(b)], writes=[f"xqT{c}"])
    _proj(G, B, l, G.w_in[l], 8, OFF_XQ, 4, B.hT, hkeys, NT, xq_consume)

    wba_t, wbak_t = G.load_w(G.w_in[l], 8, OFF_BA, 8, B.stg, B.wbf)
    wba = B.wba.rearrange("p (k n) -> p k n", k=8)
    wbak = "wba"
    P.add("act", CALL("activation", out=wba, in_=wba_t, func=AF.Copy), reads=[wbak_t], writes=[wbak])

    if sample:
        _sample_attn(G, B, l)
        P.barrier()
        P.add("pool", CALL("memset", B.wTm[:, :], 0.0), writes=["wTm"])
    def attn_tile(j, np_):
        r_ = slice(0, np_)
        tsl = slice(j * 128, j * 128 + np_)
        for h in range(4):
            bs_ = P.ps()
            P.add("pe", CALL("matmul", PS(bs_, 256)[r_, :], lhsT=B.xqT[:, h, tsl], rhs=B.kTm[:, h, :], start=True, stop=True),
                  reads=[f"xqT{h}", f"kTm{h}"], writes=[pk(bs_)])
            pT, pTk = _attn_softmax(G, B, PS(bs_, 256)[r_, :], pk(bs_), np_, h, None)
            bo = P.ps()
            for mc in range(2):
                P.add("pe", CALL("matmul", PS(bo, np_), lhsT=B.vm[:, mc, h * 128:(h + 1) * 128], rhs=pT[:, mc, r_], start=(mc == 0), stop=(mc == 1)),
                      reads=[pTk] + B.vm_keys, writes=[pk(bo)])
            P.add("act", CALL("activation", out=B.yT[:, 8 + h, tsl], in_=PS(bo, np_), func=AF.Copy), reads=[pk(bo)], writes=[f"yT{8 + h}_{j}"])

    ykeys = [[f"yT{c}_all" for c in range(4)], [f"yT{4 + h}_{j}" for h in range(4) for j in range(nt)], [f"yT{8 + h}_{j}" for h in range(4) for j in range(nt)]]
    if sample:
        ykeys[2] = [f"yT{8 + h}_0" for h in range(4)]
    norder = (0, 2, 1)

    def merge_branch(n):
        first, last = (n == norder[0]), (n == norder[-1])
        for half in range(2):
            wg0, wgk0 = G.load_w(G.w_in[l], 8, OFF_GATE + n * 1024 + half * 512, 256, B.stg, B.wbf)
            wb_, wbk = G.load_w(G.w_br[l, n], 4, half * 512, 512, B.stg, B.wbf)
            wg1, wgk1 = G.load_w(G.w_in[l], 8, OFF_GATE + n * 1024 + half * 512 + 256, 256, B.stg, B.wbf)
            for jj in range(4):
                wg, wgk = (wg0, wgk0) if jj < 2 else (wg1, wgk1)
                cc = jj % 2
                i = jj % 2
                mj = half * 4 + jj
                bg = P.ps()
                for k in range(8):
                    P.add("pe", CALL("matmul", PS(bg, NT), lhsT=wg[:, k, cc * 128:(cc + 1) * 128], rhs=B.hT[:, k, 0:NT], start=(k == 0), stop=(k == 7)),
                          reads=hkeys + [wgk], writes=[pk(bg)])
                P.add("act", CALL("activation", out=B.sig[i][:, 0:NT], in_=PS(bg, NT), func=AF.Sigmoid), reads=[pk(bg)], writes=[f"sqb{i}"])
                bb = P.ps()
                for c in range(4):
                    P.add("pe", CALL("matmul", PS(bb, NT), lhsT=wb_[:, c, jj * 128:(jj + 1) * 128], rhs=B.yT[:, n * 4 + c, 0:NT], start=(c == 0), stop=(c == 3)),
                          reads=ykeys[n] + [wbk], writes=[pk(bb)])
                if first:
                    P.add("dve", CALL("tensor_tensor", out=B.macc8[:, mj, 0:NT], in0=PS(bb, NT), in1=B.sig[i][:, 0:NT], op=ALU.mult), reads=[pk(bb), f"sqb{i}"], writes=[f"macc{mj}"])
                else:
                    P.add("dve", CALL("tensor_tensor", out=B.prod[i][:, 0:NT], in0=PS(bb, NT), in1=B.sig[i][:, 0:NT], op=ALU.mult), reads=[pk(bb), f"sqb{i}"], writes=[f"rinv{i}"])
                    if not last:
                        P.add("pool", CALL("tensor_tensor", out=B.macc8[:, mj, 0:NT], in0=B.macc8[:, mj, 0:NT], in1=B.prod[i][:, 0:NT], op=ALU.add), reads=[f"rinv{i}", f"macc{mj}"], writes=[f"macc{mj}"])
                    else:
                        P.add("pool", CALL("tensor_tensor", out=B.mT[:, mj, 0:NT], in0=B.macc8[:, mj, 0:NT], in1=B.prod[i][:, 0:NT], op=ALU.add),
                              reads=[f"rinv{i}", f"macc{mj}"], writes=[f"mT{mj}"])

    extra = None
    if not sample:
        P.begin_record((0, 1))
        for j_, (x_ap_, np__, xkey_) in enumerate(tiles):
            attn_tile(j_, np__)
        merge_branch(0)
        merge_branch(2)
        rc = P.end_record()
        nparts = 2 * len(tiles)
        step = (len(rc) + nparts - 1) // nparts
        extra = [rc[i * step:(i + 1) * step] for i in range(nparts)]

    def per_tile(j, x_ap, np_, xkey):
        r_ = slice(0, np_)
        tsl = slice(j * 128, j * 128 + np_)
        b = P.ps()
        for k in range(8):
            P.add("pe", CALL("matmul", PS(b, 8)[r_, :], lhsT=B.hT[:, k, tsl], rhs=wba[:, k, :], start=(k == 0), stop=(k == 7)), reads=[hkeys[j], wbak], writes=[pk(b)])
        ba = B.ba
        P.add("act", CALL("activation", out=ba[r_, 0:4], in_=PS(b, 4)[r_, :], func=AF.Sigmoid), reads=[pk(b)], writes=["ba"])
        P.add("dve", CALL("tensor_scalar", out=ba[r_, 4:8], in0=ba[r_, 0:4], scalar1=-1.0, scalar2=0.0, op0=ALU.mult, op1=ALU.add), reads=["ba"], writes=["ba"])
        P.add("dve", CALL("tensor_tensor", out=ba[r_, 12:16], in0=PS(b, 4, 4)[r_, :], in1=G.dtb_b[r_, :], op=ALU.add), reads=[pk(b), "dtb"], writes=["ba"])
        P.add("dve", CALL("scalar_tensor_tensor", out=ba[r_, 16:20], in0=ba[r_, 12:16], scalar=-1.0, in1=ba[r_, 12:16], op0=ALU.mult, op1=ALU.max), reads=["ba"], writes=["ba"])
        P.add("act", CALL("activation", out=ba[r_, 16:20], in_=ba[r_, 16:20], func=AF.Exp, scale=-1.0), reads=["ba"], writes=["ba"])
        P.add("act", CALL("activation", out=ba[r_, 16:20], in_=ba[r_, 16:20], func=AF.Ln, scale=1.0, bias=G.onesF[r_, 0:1]), reads=["ba", "cst"], writes=["ba"])
        P.add("dve", CALL("scalar_tensor_tensor", out=ba[r_, 12:16], in0=ba[r_, 12:16], scalar=0.0, in1=ba[r_, 16:20], op0=ALU.max, op1=ALU.add), reads=["ba"], writes=["ba"])
        P.add("dve", CALL("tensor_tensor", out=ba[r_, 8:12], in0=ba[r_, 12:16], in1=G.alog_b[r_, :], op=ALU.mult), reads=["ba", "alog"], writes=["ba"])
        _delta_tile(G, B, l, j, np_, tsl, sample, extra)

    if extra:
        P.ps_lo = 2
    for j_, (x_ap_, np__, xkey_) in enumerate(tiles):
        per_tile(j_, x_ap_, np__, xkey_)
    if not sample:
        P.ps_lo = 0

    if sample:
        merge_branch(0)
        merge_branch(2)
    merge_branch(1)

    mkeys = [f"mT{c}" for c in range(8)]
    for q in range(4):
        wo, wok = G.load_w(G.w_o[l], 8, q * 256, 256, B.stg, B.wbf)
        for j, (x_ap, np_, xkey) in enumerate(tiles):
            tsl = slice(j * 128, j * 128 + np_)
            b = P.ps()
            for k in range(8):
                P.add("pe", CALL("matmul", PS(b, 256)[0:np_, :], lhsT=B.mT[:, k, tsl], rhs=wo[:, k, :], start=(k == 0), stop=(k == 7)),
                      reads=mkeys + [wok], writes=[pk(b)])
            xo = x_ap[:, q * 256:(q + 1) * 256]
            P.add("dve", CALL("tensor_tensor", out=xo, in0=xo, in1=PS(b, 256)[0:np_, :], op=ALU.add), reads=[pk(b), xkey], writes=[xkey])


def _sample_prep(G, B, l):
    P, PS, PSB, pk = G.P, G.PS, G.PSB, G.pk
    I_ = G.identF
    P.add("pool", CALL("memset", B.xqm[:, :, :], 0.0), writes=["xqm"])
    P.add("sp", CALL("dma_start", out=B.ld1536[0:48, :], in_=G.st_conv[l]), writes=["ld1536"], dma=1, key="ld1536")
    for g in range(3):
        b = P.ps()
        for cc in range(4):
            c = g * 4 + cc
            P.add("pe", CALL("transpose", out=PS(b, 48, cc * 48), in_=B.ld1536[0:48, c * 128:(c + 1) * 128], identity=I_[0:48, 0:48]), reads=["ld1536", "cst"], writes=[pk(b)])
        P.add("act", CALL("activation", out=B.hist_s[:, g * 4:(g + 1) * 4, :, :].rearrange("p c s e -> p (c s e)"), in_=PS(b, 192), func=AF.Copy), reads=[pk(b)], writes=[f"hist_s{g}"])
    for hf in range(2):
        ld = B.ld512[hf]
        P.add("sp", CALL("dma_start", out=ld[0:120, :], in_=G.st_pool[l, hf * 120:(hf + 1) * 120, :]), writes=[f"ld512{hf}"], dma=1, key=f"ld512{hf}")
        b = P.ps()
        for c in range(4):
            P.add("pe", CALL("transpose", out=PS(b, 120, c * 120), in_=ld[0:120, c * 128:(c + 1) * 128], identity=I_[0:120, 0:120]), reads=[f"ld512{hf}", "cst"], writes=[pk(b)])
        for c in range(4):
            P.add("act", CALL("activation", out=B.uTs[:, c, hf * 8:(hf + 1) * 8, 0:15], in_=PS(b, 120, c * 120).rearrange("p (s e) -> p s e", e=15), func=AF.Copy),
                  reads=[pk(b)], writes=[f"uTs{c}"])


def _sample_attn(G, B, l):
    P, PS, PSB, pk = G.P, G.PS, G.PSB, G.pk
    P.ps_lo = 4
    for h in range(4):
        P.add("dve", CALL("tensor_copy", out=B.xqm[:, h, 0:1088].rearrange("p (s r) -> p s r", r=68)[:, :, 0:4], in_=B.xqT[:, h, 0:64].rearrange("p (s i) -> p s i", i=4)),
              reads=[f"xqT{h}"], writes=["xqm"])
    for s in range(16):
        i = s % 2
        st_, stk = B.kvs[i], f"kvs{i}"
        P.add("sp", CALL("dma_start", out=st_[:, :].rearrange("p (c n) -> p c n", c=2), in_=G.c_k[l, s].rearrange("(c p) n -> p c n", p=128)), writes=[stk], dma=1, key=stk)
        P.add("act", CALL("activation", out=B.kvb[i][:, :, :].rearrange("p c n -> p (c n)"), in_=st_[:, :], func=AF.Copy), reads=[stk], writes=[f"kvb{i}"])
        bt = 4 + (s % 4)
        for h in range(4):
            for mc in range(2):
                P.add("pe", CALL("transpose", out=PSB(bt, 128, h * 256 + mc * 128), in_=B.kvb[i][:, mc, h * 128:(h + 1) * 128], identity=G.identB),
                      reads=[f"kvb{i}", "identB"], writes=[pk(bt)])
        P.add("dve", CALL("tensor_copy", out=B.kTs[i][:, :, :].rearrange("p h m -> p (h m)"), in_=PSB(bt, 1024)), reads=[pk(bt)], writes=[f"kTs{i}"])
        for h in range(4):
            P.add("pe", CALL("matmul", PS(h, 256)[0:64, :], lhsT=B.xqm[:, h, s * 64:(s + 1) * 64], rhs=B.kTs[i][:, h, :], start=(s == 0), stop=(s == 15)),
                  reads=["xqm", f"kTs{i}"], writes=[pk(h)])
    pTs = []
    for h in range(4):
        pT, pTk = _attn_softmax(G, B, PS(h, 256)[0:64, :], pk(h), 64, h, None)
        dst = B.pTall[h]
        P.add("pool", CALL("tensor_copy", out=dst[:, :, 0:64], in_=pT[:, :, 0:64]), reads=[pTk], writes=[f"pTall{h}"])
        pTs.append(dst)
    for s in range(16):
        i = s % 2
        st_, stk = B.kvs[i], f"kvs{i}"
        P.add("sp", CALL("dma_start", out=st_[:, :].rearrange("p (c n) -> p c n", c=2), in_=G.c_v[l, s].rearrange("(c p) n -> p c n", p=128)), writes=[stk], dma=1, key=stk)
        P.add("act", CALL("activation", out=B.kvb[i][:, :, :].rearrange("p c n -> p (c n)"), in_=st_[:, :], func=AF.Copy), reads=[stk], writes=[f"kvb{i}"])
        for h in range(4):
            for mc in range(2):
                P.add("pe", CALL("matmul", PS(h, 4, 4 * s), lhsT=B.kvb[i][:, mc, h * 128:(h + 1) * 128], rhs=pTs[h][:, mc, 4 * s:4 * s + 4], start=(mc == 0), stop=(mc == 1)),
                      reads=[f"kvb{i}", f"pTall{h}"], writes=[pk(h)])
    for h in range(4):
        P.add("act", CALL("activation", out=B.yT[:, 8 + h, 0:64], in_=PS(h, 64), func=AF.Copy), reads=[pk(h)], writes=[f"yT{8 + h}_0"])
    P.ps_lo = 0


def mixer_phase(G, l):
    P, AR, PS, pk = G.P, G.AR, G.PS, G.pk
    I_ = G.identF
    P.barrier()
    m0 = AR.mark()
    Bp = _mixer_bufs(G, 256, False)
    P.add("pool", CALL("memset", G.hist[:, :, :], 0.0), writes=[f"hist{c}" for c in range(12)])
    P.add("pool", CALL("memset", G.Sst[:, :, :], 0.0), writes=[f"S{h}" for h in range(4)])
    P.add("sp", CALL("dma_start", out=Bp.stg[0][0][:, 0:512].rearrange("p (g e) -> p g e", g=4), in_=G.w_grp[l].rearrange("g c e -> c g e")), writes=["stg0"], dma=1, key="stg0")
    P.add("act", CALL("activation", out=G.wgrp_b[:, :, :], in_=Bp.stg[0][0][:, 0:512].rearrange("p (g e) -> p g e", g=4), func=AF.Copy), reads=["stg0"], writes=["wgrp_b"])
    _mem_kv(G, Bp, l)
    P.barrier()
    nst = G.n_st if hasattr(G, "n_st") else 8
    for st in range(nst):
        tiles = [(G.xp[:, st * 2 + j, :], 128, f"xp{st * 2 + j}") for j in range(2)]
        _mixer_st(G, Bp, l, st, tiles, False)
    P.barrier()
    b = P.ps()
    for c in range(4):
        P.add("pe", CALL("transpose", out=PS(b, 128, c * 128)[0:15, :], in_=Bp.uT[:, c, 0:15], identity=I_), reads=[f"uT{c}", "cst"], writes=[pk(b)])
    P.add("act", CALL("activation", out=Bp.rowbuf[0:15, 0:512], in_=PS(b, 512)[0:15, :], func=AF.Copy), reads=[pk(b)], writes=["rowbuf"])
    P.add("pool", CALL("dma_start", out=G.o_pool_p[l], in_=Bp.rowbuf[0:15, 0:512]), reads=["rowbuf"], dma=1, key="o_pool_p")
    for g in range(3):
        b = P.ps()
        for cc in range(4):
            c = g * 4 + cc
            P.add("pe", CALL("transpose", out=PS(b, 128, cc * 128)[0:3, :], in_=G.hist[:, c, :], identity=I_), reads=[f"hist{c}", "cst"], writes=[pk(b)])
        P.add("act", CALL("activation", out=Bp.rowbuf[0:3, g * 512:(g + 1) * 512], in_=PS(b, 512)[0:3, :], func=AF.Copy), reads=[pk(b)], writes=["rowbuf"])
    P.add("pool", CALL("dma_start", out=G.o_conv_p[l], in_=Bp.rowbuf[0:3, 0:1536]), reads=["rowbuf"], dma=1, key="o_conv_p")
    P.add("pool", CALL("dma_start", out=G.o_delta_p[l].rearrange("h k v -> k h v"), in_=G.Sst[:, :, :]), reads=[f"S{h}" for h in range(4)], dma=1, key="o_delta_p")
    P.barrier()
    AR.release(m0)
    if getattr(G, "skip_sample", False):
        return
    Bs = _mixer_bufs(G, 64, True)
    _sample_prep(G, Bs, l)
    _mixer_st(G, Bs, l, 0, [(G.xs[0:TS, :], TS, "xs")], True)
    P.barrier()
    for hf in range(2):
        b = P.ps()
        for c in range(4):
            P.add("dve", CALL("tensor_copy", out=Bs.pA[:, 0:120].rearrange("p (s e) -> p s e", e=15), in_=Bs.uTs[:, c, hf * 8:(hf + 1) * 8, 4:19]), reads=[f"uTs{c}"], writes=["pA"])
            P.add("pe", CALL("transpose", out=PS(b, 128, c * 128)[0:120, :], in_=Bs.pA[:, 0:120], identity=I_), reads=["pA", "cst"], writes=[pk(b)])
        P.add("act", CALL("activation", out=Bs.rowbuf[0:120, 0:512], in_=PS(b, 512)[0:120, :], func=AF.Copy), reads=[pk(b)], writes=["rowbuf"])
        P.add("pool", CALL("dma_start", out=G.o_pool_s[l, hf * 120:(hf + 1) * 120, :], in_=Bs.rowbuf[0:120, 0:512]), reads=["rowbuf"], dma=1, key="o_pool_s")
    for g in range(3):
        b = P.ps()
        for cc in range(4):
            c = g * 4 + cc
            P.add("pe", CALL("transpose", out=PS(b, 128, cc * 128)[0:48, :], in_=Bs.cvout[:, c, :, :].rearrange("p s e -> p (s e)"), identity=I_), reads=[f"cvout{g}", "cst"], writes=[pk(b)])
        P.add("act", CALL("activation", out=Bs.ld1536[0:48, g * 512:(g + 1) * 512], in_=PS(b, 512)[0:48, :], func=AF.Copy), reads=[pk(b)], writes=["ld1536"])
    P.add("pool", CALL("dma_start", out=G.o_conv_s[l], in_=Bs.ld1536[0:48, :]), reads=["ld1536"], dma=1, key="o_conv_s")
    P.barrier()
    AR.release(m0)


def convert_tables(G):
    P, AR = G.P, G.AR
    if getattr(G, "skip_convert", False):
        return
    m0 = AR.mark()
    NB = 3
    R = 4
    stg = [AR.alloc(R * 1024) for _ in range(NB)]
    bfb = [AR.alloc(R * 1024, BF16) for _ in range(NB)]
    n = 0
    for src, dst in ((G.p_u, G.uv_bf[:, 0:D]), (G.p_v, G.uv_bf[:, D:2 * D])):
        sv = src.rearrange("(c j p) d -> c p j d", p=128, j=R)
        dv = dst.rearrange("(c j p) d -> c p j d", p=128, j=R)
        nchunk = (DEPTH * NEXP) // (128 * R)
        if hasattr(G, "conv_chunks"):
            nchunk = G.conv_chunks
        for c in range(nchunk):
            i = n % NB
            n += 1
            s3 = stg[i].rearrange("p (j d) -> p j d", j=R)
            b3 = bfb[i].rearrange("p (j d) -> p j d", j=R)
            P.add("sp", CALL("dma_start", out=s3, in_=sv[c]), writes=[f"cstg{i}"], dma=1, key=f"cstg{i}")
            if n % 2 == 0:
                P.add("act", CALL("activation", out=bfb[i][:, :], in_=stg[i][:, :], func=AF.Copy), reads=[f"cstg{i}"], writes=[f"cbf{i}"])
            else:
                P.add("dve", CALL("tensor_copy", out=bfb[i][:, :], in_=stg[i][:, :]), reads=[f"cstg{i}"], writes=[f"cbf{i}"])
            P.add("pool", CALL("dma_start", out=dv[c], in_=b3), reads=[f"cbf{i}"], dma=1, key=f"cbf{i}")
    P.barrier()
    AR.release(m0)


def peer_phase(G, l):
    P, AR, PS, PSB, pk = G.P, G.AR, G.PS, G.PSB, G.pk
    P.barrier()
    P.ps_lo = 2
    m0 = AR.mark()
    NS = 10
    csblk = AR.alloc(NS * 1024)
    csb = csblk.bitcast(BF16)
    cs = [csb[:, i * 2048:(i + 1) * 2048] for i in range(NS)]
    wq = AR.alloc(8 * 2048, BF16).rearrange("p (k n) -> p k n", k=8)
    skT = AR.alloc(16 * 128, BF16).rearrange("p (j n) -> p j n", j=16)
    skb = csb[:, 6 * 2048:7 * 2048].rearrange("p (j n) -> p j n", j=16)
    hnf = AR.alloc(1024)
    hnb2 = [AR.alloc(1024, BF16) for _ in range(2)]
    hnT = AR.alloc(8 * 128, BF16).rearrange("p (k t) -> p k t", k=8)
    qT = AR.alloc(16 * 128, BF16).rearrange("p (j t) -> p j t", j=16)
    s1 = AR.alloc(2048)
    s2 = AR.alloc(2048)
    oh = s2
    s2keys = [f"s2_{j}" for j in range(16)]
    top = AR.alloc(256).rearrange("p (j a) -> p j a", j=16)
    topi = AR.alloc(256, U32).rearrange("p (j a) -> p j a", j=16)
    topif = AR.alloc(256).rearrange("p (h t a) -> p h t a", h=8, t=2)
    best = AR.alloc(128).rearrange("p (h k) -> p h k", h=8)
    pos = AR.alloc(128, U32).rearrange("p (h k) -> p h k", h=8)
    pint = AR.alloc(128, U32).rearrange("p (h k) -> p h k", h=8)
    paf = AR.alloc(128).rearrange("p (h k) -> p h k", h=8)
    pbf = AR.alloc(128).rearrange("p (h k) -> p h k", h=8)
    I1 = AR.alloc(128).rearrange("p (h k) -> p h k", h=8)
    I2 = AR.alloc(128).rearrange("p (h k) -> p h k", h=8)
    idxf = AR.alloc(128)
    idx2 = [AR.alloc(128, I32) for _ in range(2)]
    gate2 = [AR.alloc(128).rearrange("p (h k) -> p h k", h=8) for _ in range(2)]
    gsum = AR.alloc(8)
    av = AR.alloc(128)
    tg = AR.alloc(128)
    ag = AR.alloc(128)
    sg = AR.alloc(128)
    wgt = AR.alloc(128)
    Dg = [AR.alloc(2 * 128, BF16).rearrange("p (j m) -> p j m", j=2) for _ in range(4)]
    jb = AR.alloc(1024, BF16)
    ss = AR.alloc(8)

    stgA = (csblk[:, 0:2048], ["cs0", "cs1"])
    stgB = (csblk[:, 2048:4096], ["cs2", "cs3"])
    for g in range(8):
        sv, skeys = (stgA, stgB)[g % 2]
        svv = sv.rearrange("p (k n) -> p k n", k=8)
        P.add("sp", CALL("dma_start", out=svv, in_=G.w_pq[l].rearrange("(k p) n -> p k n", p=128)[:, :, g * 256:(g + 1) * 256]), writes=skeys, dma=1, key="pq" + skeys[0])
        if g % 2 == 0:
            P.add("act", CALL("activation", out=wq[:, :, g * 256:(g + 1) * 256], in_=svv, func=AF.Copy), reads=skeys, writes=[f"wq{g}"])
        else:
            P.add("dve", CALL("tensor_copy", out=wq[:, :, g * 256:(g + 1) * 256], in_=svv), reads=skeys, writes=[f"wq{g}"])
    wqkeys = [f"wq{g}" for g in range(8)]
    sks = csblk[:, 4096:6144].rearrange("p (j c) -> p j c", j=16)
    P.add("sp", CALL("dma_start", out=sks, in_=G.subk[l].rearrange("j k c -> k j c")), writes=["cs4", "cs5"], dma=1, key="sks")
    P.add("act", CALL("activation", out=skb, in_=sks, func=AF.Copy), reads=["cs4", "cs5"], writes=["cs6"])
    for hf in range(2):
        b = P.ps()
        for jj in range(8):
            j = hf * 8 + jj
            P.add("pe", CALL("transpose", out=PSB(b, 128, jj * 128), in_=skb[:, j, :], identity=G.identB), reads=["cs6", "identB"], writes=[pk(b)])
        P.add("act", CALL("activation", out=skT[:, hf * 8:(hf + 1) * 8, :].rearrange("p j n -> p (j n)"), in_=PSB(b, 1024), func=AF.Copy), reads=[pk(b)], writes=[f"skT{hf}"])

    tiles = [(G.xp[:, t, :], 128, f"xp{t}") for t in range(16)] + [(G.xs[0:TS, :], TS, "xs")]
    if hasattr(G, "peer_tiles"):
        tiles = [tiles[i] for i in G.peer_tiles]
    gst = {"gi": 0, "d": 0}
    nsl = getattr(G, "peer_slots", 128)

    def part_topk(x_ap, np_, xkey, par):
        r_ = slice(0, np_)
        idx = idx2[par]
        ik = f"idx{par}"
        hnb = hnb2[par]
        hk = f"hnb{par}"
        gate = gate2[par]
        gk = f"gate{par}"
        G.rms_rstd(x_ap, np_, xkey, jb, "jb", ss[:, 0:1], "pss")
        P.add("dve", CALL("scalar_tensor_tensor", out=hnf[r_, :], in0=x_ap, scalar=ss[r_, 0:1], in1=G.gb_ffn[r_, :], op0=ALU.mult, op1=ALU.mult),
              reads=[xkey, "pss", "gb_ffn"], writes=["hnf"])
        P.add("act", CALL("activation", out=hnb[r_, :], in_=hnf[r_, :], func=AF.Copy), reads=["hnf"], writes=[hk])
        G.to_featmajor(hnb, np_, hk, hnT, "hnT", slice(0, np_))
        for j in range(16):
            b = P.ps()
            for k in range(8):
                P.add("pe", CALL("matmul", PS(b, np_), lhsT=wq[:, k, j * 128:(j + 1) * 128], rhs=hnT[:, k, r_], start=(k == 0), stop=(k == 7)),
                      reads=["hnT", wqkeys[j // 2]], writes=[pk(b)])
            if j % 2 == 0:
                P.add("act", CALL("activation", out=qT[:, j, r_], in_=PS(b, np_), func=AF.Copy), reads=[pk(b)], writes=[f"qT{j}"])
            else:
                P.add("dve", CALL("tensor_copy", out=qT[:, j, r_], in_=PS(b, np_)), reads=[pk(b)], writes=[f"qT{j}"])
        for q in range(4):
            b = P.ps()
            for jj in range(4):
                j = q * 4 + jj
                P.add("pe", CALL("matmul", PS(b, 128, jj * 128)[r_, :], lhsT=qT[:, j, r_], rhs=skT[:, j, :], start=True, stop=True),
                      reads=[f"qT{j}", f"skT{j // 8}"], writes=[pk(b)])
            P.add("act", CALL("activation", out=s1[r_, q * 512:(q + 1) * 512], in_=PS(b, 512)[r_, :], func=AF.Copy), reads=[pk(b)], writes=[f"s1_{q}"])
        s1v = s1[:, :].rearrange("p (j n) -> p j n", j=16)
        s2v = s2[:, :].rearrange("p (j n) -> p j n", j=16)
        for j in range(16):
            sk_ = f"s1_{j // 4}"
            P.add("dve", CALL("max", out=top[r_, j, 0:8], in_=s1v[r_, j, :]), reads=[sk_], writes=[f"top{j}a"])
            P.add("dve", CALL("max_index", out=topi[r_, j, 0:8], in_max=top[r_, j, 0:8], in_values=s1v[r_, j, :]), reads=[sk_, f"top{j}a"], writes=[f"topi{j}a"])
            P.add("dve", CALL("match_replace", out=s2v[r_, j, :], in_to_replace=top[r_, j, 0:8], in_values=s1v[r_, j, :], imm_value=NEG), reads=[sk_, f"top{j}a"], writes=[f"s2_{j}"])
            P.add("dve", CALL("max", out=top[r_, j, 8:16], in_=s2v[r_, j, :]), reads=[f"s2_{j}"], writes=[f"top{j}b"])
            P.add("dve", CALL("max_index", out=topi[r_, j, 8:16], in_max=top[r_, j, 8:16], in_values=s2v[r_, j, :]), reads=[f"s2_{j}", f"top{j}b"], writes=[f"topi{j}b"])
        allt = [f"top{j}{x}" for j in range(16) for x in "ab"]
        alli = [f"topi{j}{x}" for j in range(16) for x in "ab"]
        P.add("dve", CALL("tensor_copy", out=topif[r_, :, :, :].rearrange("p h t a -> p (h t a)"), in_=topi[r_, :, :].rearrange("p j a -> p (j a)")), reads=alli, writes=["topif"])
        topv = top[:, :, :].rearrange("p (h t) a -> p h t a", t=2)
        cand = s1[:, :].rearrange("p (h a b) -> p h a b", h=8, a=16)
        cand2 = s2[:, :].rearrange("p (h n) -> p h n", h=8)
        candf = s1[:, :].rearrange("p (h n) -> p h n", h=8)
        P.add("dve", CALL("tensor_tensor", out=cand[r_, :, :, :], in0=topv[r_, :, 0, :].unsqueeze(3).to_broadcast([np_, 8, 16, 16]),
                          in1=topv[r_, :, 1, :].unsqueeze(2).to_broadcast([np_, 8, 16, 16]), op=ALU.add),
              reads=allt, writes=[f"s1_{q}" for q in range(4)])
        for h in range(8):
            P.add("dve", CALL("max", out=best[r_, h, 0:8], in_=candf[r_, h, :]), reads=[f"s1_{h // 2}"], writes=[f"best{h}a"])
            P.add("dve", CALL("max_index", out=pos[r_, h, 0:8], in_max=best[r_, h, 0:8], in_values=candf[r_, h, :]), reads=[f"s1_{h // 2}", f"best{h}a"], writes=[f"pos{h}a"])
            P.add("dve", CALL("match_replace", out=cand2[r_, h, :], in_to_replace=best[r_, h, 0:8], in_values=candf[r_, h, :], imm_value=NEG),
                  reads=[f"s1_{h // 2}", f"best{h}a"], writes=[f"s2_{2 * h}", f"s2_{2 * h + 1}"])
            P.add("dve", CALL("max", out=best[r_, h, 8:16], in_=cand2[r_, h, :]), reads=[f"s2_{2 * h}", f"s2_{2 * h + 1}"], writes=[f"best{h}b"])
            P.add("dve", CALL("max_index", out=pos[r_, h, 8:16], in_max=best[r_, h, 8:16], in_values=cand2[r_, h, :]), reads=[f"s2_{2 * h}", f"s2_{2 * h + 1}", f"best{h}b"], writes=[f"pos{h}b"])
        allb = [f"best{h}{x}" for h in range(8) for x in "ab"]
        allp = [f"pos{h}{x}" for h in range(8) for x in "ab"]
        P.add("dve", CALL("tensor_single_scalar", out=pint[r_, :, :], in_=pos[r_, :, :], scalar=4, op=ALU.logical_shift_right), reads=allp, writes=["pint"])
        P.add("dve", CALL("tensor_copy", out=paf[r_, :, :], in_=pint[r_, :, :]), reads=["pint"], writes=["paf"])
        P.add("dve", CALL("tensor_single_scalar", out=pint[r_, :, :], in_=pos[r_, :, :], scalar=15, op=ALU.bitwise_and), reads=allp + ["paf"], writes=["pint"])
        P.add("dve", CALL("tensor_copy", out=pbf[r_, :, :], in_=pint[r_, :, :]), reads=["pint"], writes=["pbf"])
        ohv = oh[:, :].rearrange("p (h k a) -> p h k a", h=8, k=16)
        for (pf, pfk, tsel, Iout, Ik) in ((paf, "paf", 0, I1, "I1"), (pbf, "pbf", 1, I2, "I2")):
            P.add("dve", CALL("tensor_tensor", out=ohv[r_, :, :, :], in0=pf[r_, :, :].unsqueeze(3).to_broadcast([np_, 8, 16, 16]),
                              in1=G.iota16[r_, :].unsqueeze(1).unsqueeze(1).to_broadcast([np_, 8, 16, 16]), op=ALU.is_equal),
                  reads=[pfk, "cst"], writes=s2keys)
            P.add("dve", CALL("tensor_tensor", out=ohv[r_, :, :, :], in0=ohv[r_, :, :, :], in1=topif[r_, :, tsel, :].unsqueeze(2).to_broadcast([np_, 8, 16, 16]), op=ALU.mult),
                  reads=["topif"], writes=s2keys)
            P.add("dve", CALL("tensor_reduce", out=Iout[r_, :, :], in_=ohv[r_, :, :, :], axis=AX.X, op=ALU.add), reads=s2keys, writes=[Ik])
        P.add("dve", CALL("scalar_tensor_tensor", out=idxf[r_, :], in0=I1[r_, :, :].rearrange("p h k -> p (h k)"), scalar=128.0, in1=I2[r_, :, :].rearrange("p h k -> p (h k)"), op0=ALU.mult, op1=ALU.add),
              reads=["I1", "I2"], writes=["idxf"])
        if l > 0:
            P.add("dve", CALL("tensor_scalar", out=idxf[r_, :], in0=idxf[r_, :], scalar1=float(l * NEXP), scalar2=0.0, op0=ALU.add, op1=ALU.add), reads=["idxf"], writes=["idxf"])
        P.add("dve", CALL("tensor_copy", out=idx[r_, :], in_=idxf[r_, :]), reads=["idxf"], writes=[ik])
        P.add("dve", CALL("tensor_tensor", out=gate[r_, :, :], in0=best[r_, :, :], in1=best[r_, :, 0:1].to_broadcast([np_, 8, 16]), op=ALU.subtract), reads=allb, writes=[gk])
        P.add("act", CALL("activation", out=gate[r_, :, :], in_=gate[r_, :, :], func=AF.Exp), reads=[gk], writes=[gk])
        P.add("dve", CALL("tensor_reduce", out=gsum[r_, 0:8], in_=gate[r_, :, :], axis=AX.X, op=ALU.add), reads=[gk], writes=["gsum"])
        P.add("dve", CALL("reciprocal", out=gsum[r_, 0:8], in_=gsum[r_, 0:8]), reads=["gsum"], writes=["gsum"])
        P.add("dve", CALL("tensor_tensor", out=gate[r_, :, :], in0=gate[r_, :, :], in1=gsum[r_, 0:8].unsqueeze(2).to_broadcast([np_, 8, 16]), op=ALU.mult), reads=[gk, "gsum"], writes=[gk])

    def part_pipe(x_ap, np_, xkey, par):
        r_ = slice(0, np_)
        idx = idx2[par]
        ik = f"idx{par}"
        hnb = hnb2[par]
        hk = f"hnb{par}"
        gatef = gate2[par][:, :, :].rearrange("p h k -> p (h k)")
        gk = f"gate{par}"
        ngrp = nsl // 2
        slot_of = {}

        def stage3(g):
            c2 = slice(2 * g, 2 * g + 2)
            P.add("dve", CALL("tensor_tensor", out=wgt[r_, c2], in0=sg[r_, c2], in1=gatef[r_, c2], op=ALU.mult), reads=[f"sg{g}", gk], writes=[f"wgt{g}"])
            d_i = gst["d"] % 4
            gst["d"] += 1
            D_ = Dg[d_i]
            dk = f"Dg{d_i}"
            P.add("dve", CALL("tensor_tensor", out=D_[r_, :, r_], in0=G.identF[r_, r_].unsqueeze(1).to_broadcast([np_, 2, np_]),
                              in1=wgt[r_, c2].unsqueeze(2).to_broadcast([np_, 2, np_]), op=ALU.mult), reads=[f"wgt{g}", "cst"], writes=[dk])
            for j in range(2):
                sl = 2 * g + j
                i = slot_of[sl]
                for hf in range(2):
                    P.add("pe", CALL("matmul", PS(hf, 512)[r_, :], lhsT=D_[r_, j, r_], rhs=cs[i][r_, 1024 + hf * 512:1024 + (hf + 1) * 512], start=(sl == 0), stop=(sl == nsl - 1)),
                          reads=[dk, f"cs{i}"], writes=[pk(hf)])

        for g in range(ngrp + 1):
            if g < ngrp:
                c2 = slice(2 * g, 2 * g + 2)
                for j in range(2):
                    sl = 2 * g + j
                    i = gst["gi"] % NS
                    gst["gi"] += 1
                    slot_of[sl] = i
                    P.add("pool", CALL("indirect_dma_start", out=cs[i][r_, :], out_offset=None, in_=G.uv_bf, in_offset=bass.IndirectOffsetOnAxis(ap=idx[r_, sl:sl + 1], axis=0)),
                          reads=[ik], writes=[f"cs{i}"], dma=1, key=f"cs{i}")
                    P.add("dve", CALL("scalar_tensor_tensor", out=cs[i][r_, 0:1024], in0=cs[i][r_, 0:1024], scalar=1.0, in1=hnb[r_, :], op0=ALU.mult, op1=ALU.mult, accum_out=av[r_, sl:sl + 1]),
                          reads=[hk], writes=[f"cs{i}", f"av{sl}"])
                avk = [f"av{2 * g}", f"av{2 * g + 1}"]
                P.add("act", CALL("activation", out=sg[r_, c2], in_=av[r_, c2], func=AF.Gelu_apprx_tanh), reads=avk, writes=[f"sg{g}"])
            if g >= 1:
                stage3(g - 1)
        for hf in range(2):
            xo = x_ap[:, hf * 512:(hf + 1) * 512]
            P.add("dve", CALL("tensor_tensor", out=xo, in0=xo, in1=PS(hf, 512)[r_, :], op=ALU.add), reads=[pk(hf), xkey], writes=[xkey])

    nt_ = len(tiles)
    part_topk(*tiles[0], 0)
    for t in range(nt_):
        P.begin_record((0, 1))
        part_pipe(*tiles[t], t % 2)
        ra = P.end_record()
        rb = []
        if t + 1 < nt_:
            P.begin_record((2, 3, 4, 5, 6, 7))
            part_topk(*tiles[t + 1], (t + 1) % 2)
            rb = P.end_record()
        P.replay([ra, rb])
    P.barrier()
    P.ps_lo = 0
    AR.release(m0)


def final_phase(G):
    P, AR = G.P, G.AR
    m0 = AR.mark()
    ob = [AR.alloc(1024) for _ in range(2)]
    jb = AR.alloc(1024, BF16)
    ss = AR.alloc(8)
    gb_fin = AR.alloc(1024)
    P.add("sp", CALL("dma_start", out=gb_fin[:, :], in_=G.g_fin.partition_broadcast(128)), writes=["gb_fin"], dma=1, key="gb_fin")
    tiles = [(G.xp[:, t, :], 128, f"xp{t}", G.y_p[t * 128:(t + 1) * 128, :]) for t in range(16)] + [(G.xs[0:TS, :], TS, "xs", G.y_s)]
    for n, (x_ap, np_, xkey, o_ap) in enumerate(tiles):
        r_ = slice(0, np_)
        o = ob[n % 2]
        G.rms_rstd(x_ap, np_, xkey, jb, "jb", ss[:, 0:1], "fss")
        P.add("dve", CALL("scalar_tensor_tensor", out=o[r_, :], in0=x_ap, scalar=ss[r_, 0:1], in1=gb_fin[r_, :], op0=ALU.mult, op1=ALU.mult),
              reads=[xkey, "fss", "gb_fin"], writes=[f"ob{n % 2}"])
        P.add("sp", CALL("dma_start", out=o_ap, in_=o[r_, :]), reads=[f"ob{n % 2}"], dma=1, key=f"ob{n % 2}")
    AR.release(m0)


def build(dbg=(), stop=None, **opts):
    build.opts = opts
    nc = bass.Bass("TRN2", target_bir_lowering=False)
    es = ExitStack()
    with es:
        _build(nc, es, dbg, stop)
    return nc


def _build(nc, es, dbg, stop):
    def din(name, shape, dt=F32):
        return nc.dram_tensor(name, list(shape), dt, kind="ExternalInput").ap()

    def dout(name, shape, dt=F32):
        return nc.dram_tensor(name, list(shape), dt, kind="ExternalOutput").ap()

    x_p = din("x_p", [T, D])
    x_s = din("x_s", [TS, D])
    st_pool = din("st_pool", [DEPTH, NSQ * 15, BW])
    st_conv = din("st_conv", [DEPTH, NSQ * 3, 3 * BW])
    st_delta = din("st_delta", [DEPTH, NSQ, 4, 128, 128])
    c_k = din("c_k", [DEPTH, NSQ, 256, BW])
    c_v = din("c_v", [DEPTH, NSQ, 256, BW])
    memp = din("memp", [256, D])
    g_mix = din("g_mix", [DEPTH, D])
    w_in = din("w_in", [DEPTH, D, IN_COLS])
    w_conv = din("w_conv", [DEPTH, 4, 3 * BW])
    a_log = din("a_log", [DEPTH, 4])
    dt_bias = din("dt_bias", [DEPTH, 4])
    g_dn = din("g_dn", [DEPTH, 128])
    w_grp = din("w_grp", [DEPTH, 4, 128, 128])
    p_scale = din("p_scale", [DEPTH, BW])
    g_mem = din("g_mem", [DEPTH, D])
    w_mkv = din("w_mkv", [DEPTH, D, 2 * BW])
    w_br = din("w_br", [DEPTH, 3, BW, D])
    w_o = din("w_o", [DEPTH, D, D])
    g_ffn = din("g_ffn", [DEPTH, D])
    w_pq = din("w_pq", [DEPTH, D, 2048])
    subk = din("subk", [DEPTH, 16, 128, 128])
    p_u = din("p_u", [DEPTH * NEXP, D])
    p_v = din("p_v", [DEPTH * NEXP, D])
    g_fin = din("g_fin", [D])
    consts = din("consts", [128, 1024])

    y_p = dout("y_p", [T, D])
    y_s = dout("y_s", [TS, D])
    o_pool_p = dout("o_pool_p", [DEPTH, 15, BW])
    o_conv_p = dout("o_conv_p", [DEPTH, 3, 3 * BW])
    o_delta_p = dout("o_delta_p", [DEPTH, 4, 128, 128])
    o_mk = dout("o_mk", [DEPTH, 256, BW])
    o_mv = dout("o_mv", [DEPTH, 256, BW])
    o_pool_s = dout("o_pool_s", [DEPTH, NSQ * 15, BW])
    o_conv_s = dout("o_conv_s", [DEPTH, NSQ * 3, 3 * BW])
    o_delta_s = dout("o_delta_s", [DEPTH, NSQ, 4, 128, 128])

    uv_bf = nc.dram_tensor("uv_bf", [DEPTH * NEXP, 2 * D], BF16, kind="Internal").ap()

    P = Prog(nc)
    dbg_outs = {}

    def sb(name, shape, dt=F32):
        return es.enter_context(nc.sbuf_tensor(name, shape, dt))

    xp = sb("xp", [128, 16, D])
    xs = sb("xs", [128, D])
    cst = sb("cst", [128, 8, 128])
    identB_t = sb("identB", [128, 128], BF16)
    identB = identB_t[:, :]
    gmix_sb = sb("gmix_sb", [128, 8])
    gmem_sb = sb("gmem_sb", [128, 8])
    gb_ffn = sb("gb_ffn", [128, D])
    wconv_sb = sb("wconv_sb", [128, 12, 4])
    alog_b = sb("alog_b", [128, 4])
    dtb_b = sb("dtb_b", [128, 4])
    gdn_sb = sb("gdn_sb", [128, 1])
    psc_sb = sb("psc_sb", [128, 4])
    wgrp_b = sb("wgrp_b", [128, 4, 128], BF16)
    Sst = sb("Sst", [128, 4, 128])
    hist = sb("hist", [128, 12, 3])
    epsb = sb("epsb", [128, 1])
    ARENA_N = 32000
    arena_t = sb("arena", [128, ARENA_N])
    AR = Arena(arena_t, ARENA_N)
    psum = es.enter_context(nc.psum_tensor("psum", [128, 4096], F32))

    identF = cst[:, 0, :]
    onesF = cst[:, 1, :]
    Ltri = cst[:, 2, :]
    SLm = cst[:, 3, :]
    Ltri4 = cst[:, 4, :]
    SL4 = cst[:, 5, :]
    seqmask = cst[:, 6, 0:16]
    rcnt = cst[:, 7, 0:64]
    iota16 = cst[:, 7, 64:80]

    def PS(i, n=512, off=0):
        return psum[:, i * 512 + off:i * 512 + off + n]

    def PSB(i, n=1024, off=0):
        return psum[:, i * 512:(i + 1) * 512].bitcast(BF16)[:, off:off + n]

    def pk(i):
        return f"ps{i}"

    def dump(name, ap, reads, shape, view=None, **kw):
        if name not in dbg:
            return
        o = dout("dbg_" + name, shape, ap.dtype)
        dbg_outs[name] = o
        if view:
            o = o.rearrange(view, **kw)
        P.add("sp", CALL("dma_start", out=o, in_=ap), reads=reads, dma=1, key="dbg_" + name)

    P.add("sp", CALL("dma_start", out=cst[:].rearrange("p a b -> p (a b)"), in_=consts), writes=["cst"], dma=1, key="cst")
    P.add("pool", CALL("memset", epsb[:], EPS), writes=["epsb"])
    P.add("act", CALL("activation", out=identB, in_=identF, func=AF.Copy), reads=["cst"], writes=["identB"])
    for i in range(16):
        P.add("sp", CALL("dma_start", out=xp[:, i, :], in_=x_p[i * 128:(i + 1) * 128, :]), writes=[f"xp{i}"], dma=1, key=f"xp{i}")
    P.add("sp", CALL("dma_start", out=xs[0:TS, :], in_=x_s), writes=["xs"], dma=1, key="xs")

    def rms_rstd(x_ap, np_, xkey, junk, junk_key, ss, sskey):
        P.add("act", CALL("activation", out=junk[0:np_, :], in_=x_ap, func=AF.Square, accum_out=ss[0:np_, :]),
              reads=[xkey], writes=[junk_key, sskey])
        P.add("act", CALL("activation", out=ss[0:np_, :], in_=ss[0:np_, :], func=AF.Sqrt, scale=1.0 / D, bias=epsb[0:np_, :]),
              reads=[sskey, "epsb"], writes=[sskey])
        P.add("dve", CALL("reciprocal", out=ss[0:np_, :], in_=ss[0:np_, :]), reads=[sskey], writes=[sskey])

    def to_featmajor(src_bf, np_, srckey, dst, dstkey_fn, tsl, gsb=None, gkey=None):
        b = P.ps()
        for k in range(8):
            P.add("pe", CALL("transpose", out=PSB(b)[:, k * 128:k * 128 + np_], in_=src_bf[0:np_, k * 128:(k + 1) * 128], identity=identB[0:np_, 0:np_]),
                  reads=[srckey, "identB"], writes=[pk(b)])
        src = PSB(b).rearrange("p (k t) -> p k t", k=8)[:, :, 0:np_]
        if gsb is None:
            P.add("act", CALL("activation", out=dst[:, :, tsl], in_=src, func=AF.Copy), reads=[pk(b)], writes=[dstkey_fn])
        else:
            P.add("dve", CALL("tensor_tensor", out=dst[:, :, tsl], in0=src, in1=gsb[:, :].unsqueeze(2).to_broadcast([128, 8, np_]), op=ALU.mult),
                  reads=[pk(b), gkey], writes=[dstkey_fn])

    wst = {"i": 0, "s": 0}

    def load_w(dram2d, krows, c0, ncols, stg, wbf, caster=None):
        i = wst["i"] % len(wbf)
        wst["i"] += 1
        si = wst["s"] % len(stg)
        wst["s"] += 1
        s_ap, s_key = stg[si]
        b_ap, b_key = wbf[i]
        sv = s_ap[:, 0:krows * ncols].rearrange("p (k n) -> p k n", k=krows)
        bv = b_ap[:, 0:krows * ncols].rearrange("p (k n) -> p k n", k=krows)
        src = dram2d.rearrange("(k p) n -> p k n", p=128)[:, :, c0:c0 + ncols]
        P.add("sp", CALL("dma_start", out=sv, in_=src), writes=[s_key], dma=1, key=s_key)
        eng = caster or ("act" if (wst["i"] % 2 == 0) else "dve")
        if eng == "act":
            P.add("act", CALL("activation", out=bv, in_=sv, func=AF.Copy), reads=[s_key], writes=[b_key])
        else:
            P.add(eng, CALL("tensor_copy", out=bv, in_=sv), reads=[s_key], writes=[b_key])
        return bv, b_key

    def layer_params(l):
        P.add("sp", CALL("dma_start", out=gmix_sb[:], in_=g_mix[l].rearrange("(k p) -> p k", p=128), allow_slow_non_contiguous=True), writes=["gmix"], dma=1, key="gmix")
        P.add("sp", CALL("dma_start", out=gmem_sb[:], in_=g_mem[l].rearrange("(k p) -> p k", p=128), allow_slow_non_contiguous=True), writes=["gmem"], dma=1, key="gmem")
        P.add("sp", lambda e: [e.dma_start(out=wconv_sb[:, :, j], in_=w_conv[l, j].rearrange("(c p) -> p c", p=128), allow_slow_non_contiguous=True) for j in range(4)],
              writes=["wconv"], dma=4, key="wconv")
        P.add("sp", CALL("dma_start", out=gdn_sb[:], in_=g_dn[l].rearrange("(p o) -> p o", o=1), allow_slow_non_contiguous=True), writes=["gdn"], dma=1, key="gdn")
        P.add("sp", CALL("dma_start", out=psc_sb[:], in_=p_scale[l].rearrange("(g p) -> p g", p=128), allow_slow_non_contiguous=True), writes=["psc"], dma=1, key="psc")
        P.add("sp", CALL("dma_start", out=gb_ffn[:], in_=g_ffn[l].partition_broadcast(128)), writes=["gb_ffn"], dma=1, key="gb_ffn")
        P.add("sp", CALL("dma_start", out=alog_b[:], in_=a_log[l].partition_broadcast(128)), writes=["alog"], dma=1, key="alog")
        P.add("sp", CALL("dma_start", out=dtb_b[:], in_=dt_bias[l].partition_broadcast(128)), writes=["dtb"], dma=1, key="dtb")
        P.add("act", CALL("activation", out=alog_b[:], in_=alog_b[:], func=AF.Exp), reads=["alog"], writes=["alog"])
        P.add("dve", CALL("tensor_scalar", out=alog_b[:], in0=alog_b[:], scalar1=-1.0, scalar2=0.0, op0=ALU.mult, op1=ALU.add), reads=["alog"], writes=["alog"])

    G = type("G", (), {})()
    for k_, v_ in list(locals().items()):
        setattr(G, k_, v_)
    for k_, v_ in build.opts.items():
        setattr(G, k_, v_)

    convert_tables(G)
    for l in range(DEPTH):
        layer_params(l)
        mixer_phase(G, l)
        dump(f"xp_m{l}", xp[:, :, :], [f"xp{i}" for i in range(16)], [T, D], "(t p) d -> p t d", p=128)
        dump(f"xs_m{l}", xs[0:TS, :], ["xs"], [TS, D])
        if stop == ("mixer", l):
            break
        peer_phase(G, l)
        dump(f"xp_p{l}", xp[:, :, :], [f"xp{i}" for i in range(16)], [T, D], "(t p) d -> p t d", p=128)
        dump(f"xs_p{l}", xs[0:TS, :], ["xs"], [TS, D])
        if stop == ("peer", l):
            break
    else:
        final_phase(G)
    P.emit(es, maxops=build.opts.get('maxops'))
    G.P = P
    build.last = G


def _shard_inputs(inp):
    f = lambda a: np.ascontiguousarray(np.asarray(a, dtype=np.float32))
    shared = dict(
        g_mix=f(inp["g_mix"]), w_in=f(inp["w_in"]), w_conv=f(inp["w_conv"]), a_log=f(inp["a_log"]), dt_bias=f(inp["dt_bias"]),
        g_dn=f(inp["g_dn_out"]), w_grp=f(inp["w_pool_grp"]), p_scale=f(inp["pool_scale"]), g_mem=f(inp["g_mem"]),
        w_mkv=f(inp["w_mem_kv"]), w_br=f(inp["w_branch"]), w_o=f(inp["w_o"]), g_ffn=f(inp["g_ffn"]), w_pq=f(inp["w_peer_q"]),
        subk=f(inp["peer_subkeys"]).reshape(DEPTH, 16, 128, 128), p_u=f(inp["peer_u"]).reshape(DEPTH * NEXP, D),
        p_v=f(inp["peer_v"]).reshape(DEPTH * NEXP, D), g_fin=f(inp["g_final"]), consts=make_consts())
    maps = []
    for c in range(NCORES):
        sl = slice(c * NSQ, (c + 1) * NSQ)
        m = dict(shared)
        m["x_p"] = f(inp["x_prompt"][c])
        m["x_s"] = f(inp["x_sample"][sl]).reshape(TS, D)
        m["st_pool"] = f(inp["state_pool"][:, sl]).reshape(DEPTH, NSQ * 15, BW)
        m["st_conv"] = f(inp["state_conv"][:, sl]).reshape(DEPTH, NSQ * 3, 3 * BW)
        m["st_delta"] = f(inp["state_delta"][:, sl])
        m["c_k"] = f(inp["cache_mem_k"][:, sl]).reshape(DEPTH, NSQ, 256, BW)
        m["c_v"] = f(inp["cache_mem_v"][:, sl]).reshape(DEPTH, NSQ, 256, BW)
        m["memp"] = f(inp["mem_prompt"][c])
        maps.append(m)
    return maps


def _gather_outputs(res):
    R = res.results
    cat = lambda k, ax: np.concatenate([np.asarray(r[k]) for r in R], axis=ax)
    stk = lambda k, ax: np.stack([np.asarray(r[k]) for r in R], axis=ax)
    y_p = stk("y_p", 0)
    y_s = cat("y_s", 0).reshape(NCORES * NSQ, 4, D)
    pool_p = stk("o_pool_p", 1)
    conv_p = stk("o_conv_p", 1)
    delta_p = stk("o_delta_p", 1)
    mk = stk("o_mk", 1).reshape(DEPTH, NCORES, 256, 4, 128)
    mv = stk("o_mv", 1).reshape(DEPTH, NCORES, 256, 4, 128)
    pool_s = cat("o_pool_s", 1).reshape(DEPTH, NCORES * NSQ, 15, BW)
    conv_s = cat("o_conv_s", 1).reshape(DEPTH, NCORES * NSQ, 3, 3 * BW)
    delta_s = cat("o_delta_s", 1)
    outs = (y_p, y_s, pool_p, conv_p, delta_p, mk, mv, pool_s, conv_s, delta_s)
    return tuple(np.ascontiguousarray(o, dtype=np.float32) for o in outs)


def kernel(**inputs):
    maps = _shard_inputs(inputs)
    nc = build()
    res = run_bass_kernel_spmd(nc, maps, core_ids=list(range(NCORES)))
    return _gather_outputs(res)
```
